# Optimizing a Trainium2 kernel written in Bass

```python
import math
import jax, jax.numpy as jnp
from jax import lax
import numpy as np

D_MODEL = 1024
BATCH = 4
SEQ = 8192
DEPTH = 1

CHUNK = 64
MIX_WIDTH = D_MODEL
GDN_HEADS = 4
GDN_DK = D_MODEL // 8
GDN_DV = D_MODEL // 8
GDN_WIDTH = GDN_HEADS * GDN_DV
CONV_K = 4
DIFF_HEADS = 4
DIFF_D = D_MODEL // 16
DIFF_WIDTH = DIFF_HEADS * 2 * DIFF_D
ROPE_THETA = 500000.0
ROT_DIM = DIFF_D // 4
Q_BLOCK = 128
NORM_EPS = 1e-6
SPLIT_SIZES = (GDN_HEADS * GDN_DK, GDN_HEADS * GDN_DK, GDN_WIDTH,
               GDN_HEADS, GDN_HEADS, GDN_WIDTH,
               DIFF_WIDTH, DIFF_WIDTH, DIFF_WIDTH, DIFF_WIDTH)
IN_WIDTH = sum(SPLIT_SIZES)

kernel_name = 'hybrid_gdn_diffattn_parallel_heads'


def rms_norm(x, g):
    xf = x.astype(jnp.float32)
    y = xf * lax.rsqrt(jnp.mean(xf * xf, axis=-1, keepdims=True) + NORM_EPS)
    return (y * g.astype(jnp.float32)).astype(x.dtype)


def l2_norm(x):
    return x * lax.rsqrt(jnp.sum(x * x, axis=-1, keepdims=True) + NORM_EPS)


def causal_dwconv(u, w):
    return lax.conv_general_dilated(u, w[:, None, :].astype(u.dtype), window_strides=(1,),
                                    padding=[(CONV_K - 1, 0)],
                                    dimension_numbers=('NWC', 'WIO', 'NWC'),
                                    feature_group_count=u.shape[-1])


def gated_delta_rule_chunked(q, k, v, beta, g):
    b, s, h, dk = q.shape
    dv = v.shape[-1]
    n = s // CHUNK

    def to_chunks(t):
        return t.reshape(b, n, CHUNK, h, -1).transpose(0, 3, 1, 2, 4)

    q, k, v = to_chunks(q), to_chunks(k), to_chunks(v)
    beta = beta.reshape(b, n, CHUNK, h).transpose(0, 3, 1, 2)
    gc = jnp.cumsum(g.reshape(b, n, CHUNK, h).transpose(0, 3, 1, 2), axis=-1)
    incl = jnp.tril(jnp.ones((CHUNK, CHUNK), dtype=bool))
    strict = jnp.tril(jnp.ones((CHUNK, CHUNK), dtype=bool), k=-1)
    decay = jnp.exp(jnp.where(incl, gc[..., :, None] - gc[..., None, :], -jnp.inf))
    kb = k * beta[..., None]
    vb = v * beta[..., None]
    m = jnp.where(strict, jnp.einsum('bhnid,bhnjd->bhnij', kb, k) * decay, 0.0)
    lhs = m + jnp.eye(CHUNK, dtype=m.dtype)
    rhs = jnp.concatenate([vb, kb * jnp.exp(gc)[..., None]], axis=-1)
    sol = lax.linalg.triangular_solve(lhs, rhs, left_side=True, lower=True, unit_diagonal=True)
    u, w = sol[..., :dv], sol[..., dv:]
    a_intra = jnp.einsum('bhnid,bhnjd->bhnij', q, k) * decay
    q_dec = q * jnp.exp(gc)[..., None]
    k_dec = k * jnp.exp(gc[..., -1:] - gc)[..., None]
    g_last = jnp.exp(gc[..., -1])
    xs = tuple(jnp.moveaxis(t, 2, 0) for t in (w, u, q_dec, k_dec, a_intra, g_last))

    def step(state, inp):
        w_n, u_n, qd_n, kd_n, a_n, gl_n = inp
        v_new = u_n - jnp.einsum('bhcd,bhde->bhce', w_n, state)
        o = jnp.einsum('bhcd,bhde->bhce', qd_n, state) + jnp.einsum('bhcs,bhse->bhce', a_n, v_new)
        state = state * gl_n[..., None, None] + jnp.einsum('bhcd,bhce->bhde', kd_n, v_new)
        return state, o

    state0 = jnp.zeros((b, h, dk, dv), jnp.float32)
    _, o = lax.scan(step, state0, xs)
    return o.transpose(1, 0, 3, 2, 4).reshape(b, s, h, dv)


def gdn_branch(q, k, v, beta_pre, decay_pre, conv_w, a_log, dt_bias, out_g):
    b, s, _ = q.shape
    qkv = jax.nn.silu(causal_dwconv(jnp.concatenate([q, k, v], axis=-1), conv_w)).astype(jnp.float32)
    nk = GDN_HEADS * GDN_DK
    qf = l2_norm(qkv[..., :nk].reshape(b, s, GDN_HEADS, GDN_DK)) * (GDN_DK ** -0.5)
    kf = l2_norm(qkv[..., nk:2 * nk].reshape(b, s, GDN_HEADS, GDN_DK))
    vf = qkv[..., 2 * nk:].reshape(b, s, GDN_HEADS, GDN_DV)
    beta = jax.nn.sigmoid(beta_pre.astype(jnp.float32))
    g = -jnp.exp(a_log.astype(jnp.float32)) * jax.nn.softplus(
        decay_pre.astype(jnp.float32) + dt_bias.astype(jnp.float32))
    o = gated_delta_rule_chunked(qf, kf, vf, beta, g)
    o = rms_norm(o, out_g)
    return o.reshape(b, s, GDN_WIDTH)


def apply_partial_rope(x, cos, sin):
    half = ROT_DIM // 2
    xr = x[..., :ROT_DIM].astype(jnp.float32)
    x1, x2 = xr[..., :half], xr[..., half:]
    rot = jnp.concatenate([x1 * cos - x2 * sin, x2 * cos + x1 * sin], axis=-1)
    return jnp.concatenate([rot.astype(x.dtype), x[..., ROT_DIM:]], axis=-1)


def diff_attention_branch(q, k, v, positions, q_norm_g, k_norm_g, lam_q1, lam_k1, lam_q2, lam_k2,
                          subln_g, lambda_init):
    b, s, _ = q.shape
    q = rms_norm(q.reshape(b, s, DIFF_HEADS, 2, DIFF_D), q_norm_g)
    k = rms_norm(k.reshape(b, s, DIFF_HEADS, 2, DIFF_D), k_norm_g)
    v = v.reshape(b, s, DIFF_HEADS, 2 * DIFF_D)
    inv_freq = ROPE_THETA ** (-jnp.arange(0, ROT_DIM, 2, dtype=jnp.float32) / ROT_DIM)
    ang = positions.astype(jnp.float32)[..., None] * inv_freq
    cos = jnp.cos(ang)[:, :, None, None, :]
    sin = jnp.sin(ang)[:, :, None, None, :]
    q = apply_partial_rope(q, cos, sin)
    k = apply_partial_rope(k, cos, sin)
    lam = (jnp.exp(jnp.sum(lam_q1.astype(jnp.float32) * lam_k1.astype(jnp.float32)))
           - jnp.exp(jnp.sum(lam_q2.astype(jnp.float32) * lam_k2.astype(jnp.float32)))
           + lambda_init)
    qh = q.transpose(0, 2, 3, 1, 4)
    kh = k.transpose(0, 2, 3, 1, 4)
    vh = v.transpose(0, 2, 1, 3).astype(jnp.float32)
    nq = s // Q_BLOCK
    qblocks = qh.reshape(b, DIFF_HEADS, 2, nq, Q_BLOCK, DIFF_D).transpose(3, 0, 1, 2, 4, 5)
    key_chunk = jnp.arange(s) // CHUNK
    scale = DIFF_D ** -0.5

    def block(args):
        qb, bi = args
        q_chunk = (bi * Q_BLOCK + jnp.arange(Q_BLOCK)) // CHUNK
        mask = key_chunk[None, :] <= q_chunk[:, None]
        sc = jnp.einsum('bhiqd,bhikd->bhiqk', qb, kh).astype(jnp.float32) * scale
        p = jax.nn.softmax(jnp.where(mask, sc, -jnp.inf), axis=-1)
        a = p[:, :, 0] - lam * p[:, :, 1]
        return jnp.einsum('bhqk,bhke->bhqe', a, vh)

    o = lax.map(block, (qblocks, jnp.arange(nq)))
    o = o.transpose(1, 0, 3, 2, 4).reshape(b, s, DIFF_HEADS, 2 * DIFF_D)
    o = rms_norm(o, subln_g) * (1.0 - lambda_init)
    return o.reshape(b, s, DIFF_WIDTH)


def hybrid_layer(x, c, positions, lambda_init, norm_g, w_ada, b_ada, w_in, conv_w, a_log, dt_bias,
                 gdn_norm_g, q_norm_g, k_norm_g, lam_q1, lam_k1, lam_q2, lam_k2, subln_g, w_out):
    mod = jax.nn.silu(c) @ w_ada + b_ada
    shift, scale, gate = jnp.split(mod, 3, axis=-1)
    h = rms_norm(x, norm_g) * (1.0 + scale[:, None, :]) + shift[:, None, :]
    proj = h @ w_in
    idx = [int(i) for i in np.cumsum(SPLIT_SIZES)[:-1]]
    a_q, a_k, a_v, a_beta, a_decay, a_gate, b_q, b_k, b_v, b_gate = jnp.split(proj, idx, axis=-1)
    o_a = gdn_branch(a_q, a_k, a_v, a_beta, a_decay, conv_w, a_log, dt_bias, gdn_norm_g)
    o_b = diff_attention_branch(b_q, b_k, b_v, positions, q_norm_g, k_norm_g,
                                lam_q1, lam_k1, lam_q2, lam_k2, subln_g, lambda_init)
    mixed = jnp.concatenate([o_a.astype(x.dtype) * jax.nn.silu(a_gate),
                             o_b.astype(x.dtype) * jax.nn.silu(b_gate)], axis=-1)
    return x + gate[:, None, :] * (mixed @ w_out)


def setup_inputs(seed: int = 0) -> dict:
    key = jax.random.key(seed)
    ks = jax.random.split(key, 20)
    f32 = jnp.float32
    L, D = DEPTH, D_MODEL
    x = jax.random.normal(ks[0], (BATCH, SEQ, D), f32)
    c = jax.random.normal(ks[1], (BATCH, D), f32)
    positions = jnp.broadcast_to(jnp.arange(SEQ, dtype=jnp.int32)[None, :], (BATCH, SEQ))
    norm_g = 1.0 + 0.01 * jax.random.normal(ks[2], (L, D), f32)
    w_ada = 0.5 * D ** -0.5 * jax.random.normal(ks[3], (L, D, 3 * D), f32)
    b_ada = 0.01 * jax.random.normal(ks[4], (L, 3 * D), f32)
    w_in = D ** -0.5 * jax.random.normal(ks[5], (L, D, IN_WIDTH), f32)
    conv_w = CONV_K ** -0.5 * jax.random.normal(ks[6], (L, CONV_K, 2 * GDN_HEADS * GDN_DK + GDN_WIDTH), f32)
    a_log = jnp.log(jax.random.uniform(ks[7], (L, GDN_HEADS), f32, 1.0, 16.0))
    dt = jnp.exp(jax.random.uniform(ks[8], (L, GDN_HEADS), f32, math.log(0.001), math.log(0.1)))
    dt_bias = dt + jnp.log(-jnp.expm1(-dt))
    gdn_norm_g = 1.0 + 0.01 * jax.random.normal(ks[9], (L, GDN_DV), f32)
    q_norm_g = 1.0 + 0.01 * jax.random.normal(ks[10], (L, DIFF_D), f32)
    k_norm_g = 1.0 + 0.01 * jax.random.normal(ks[11], (L, DIFF_D), f32)
    lambda_q1 = 0.1 * jax.random.normal(ks[12], (L, DIFF_D), f32)
    lambda_k1 = 0.1 * jax.random.normal(ks[13], (L, DIFF_D), f32)
    lambda_q2 = 0.1 * jax.random.normal(ks[14], (L, DIFF_D), f32)
    lambda_k2 = 0.1 * jax.random.normal(ks[15], (L, DIFF_D), f32)
    subln_g = 1.0 + 0.01 * jax.random.normal(ks[16], (L, 2 * DIFF_D), f32)
    w_out = MIX_WIDTH ** -0.5 * jax.random.normal(ks[17], (L, MIX_WIDTH, D), f32)
    return {'x': x, 'c': c, 'positions': positions, 'norm_g': norm_g, 'w_ada': w_ada,
            'b_ada': b_ada, 'w_in': w_in, 'conv_w': conv_w, 'a_log': a_log, 'dt_bias': dt_bias,
            'gdn_norm_g': gdn_norm_g, 'q_norm_g': q_norm_g, 'k_norm_g': k_norm_g,
            'lambda_q1': lambda_q1, 'lambda_k1': lambda_k1, 'lambda_q2': lambda_q2,
            'lambda_k2': lambda_k2, 'subln_g': subln_g, 'w_out': w_out}


def reference(x, c, positions, norm_g, w_ada, b_ada, w_in, conv_w, a_log, dt_bias, gdn_norm_g,
              q_norm_g, k_norm_g, lambda_q1, lambda_k1, lambda_q2, lambda_k2, subln_g, w_out):
    for l in range(DEPTH):
        lambda_init = 0.8 - 0.6 * math.exp(-0.3 * l)
        x = hybrid_layer(x, c, positions, lambda_init, norm_g[l], w_ada[l], b_ada[l], w_in[l],
                         conv_w[l], a_log[l], dt_bias[l], gdn_norm_g[l], q_norm_g[l], k_norm_g[l],
                         lambda_q1[l], lambda_k1[l], lambda_q2[l], lambda_k2[l], subln_g[l], w_out[l])
    return x
```

```python
import math
from contextlib import ExitStack

import numpy as np
import ml_dtypes

import concourse.bass as bass
import concourse.mybir as mybir
from concourse.bass_utils import run_bass_kernel_spmd

F32 = mybir.dt.float32
BF16 = mybir.dt.bfloat16
I32 = mybir.dt.int32
ACT = mybir.ActivationFunctionType
ALU = mybir.AluOpType
AX = mybir.AxisListType


class KB:
    NDMA = 40

    def __init__(self, nc):
        self.nc = nc
        self.eng = {"pe": nc.tensor, "act": nc.scalar, "dve": nc.vector, "pool": nc.gpsimd, "sp": nc.sync}
        self.stack = ExitStack()
        self.sem = {}
        self.cnt = {k: 0 for k in self.eng}
        self.waited = {k: {} for k in self.eng}
        self.last_w = {}
        self.readers = {}
        self.dsem = []
        self.dcnt = []
        self.dnext = 0
        self.nins = 0
        self.phase = 0
        self.excl = set()

    def ctx(self):
        for k in ("pe", "act", "dve", "pool"):
            self.sem[k] = self.stack.enter_context(self.nc.semaphore("s_" + k))
        for i in range(self.NDMA):
            self.dsem.append(self.stack.enter_context(self.nc.semaphore(f"s_dma{i}")))
            self.dcnt.append(0)
        return self.stack

    def sb(self, name, shape, dtype):
        return self.stack.enter_context(self.nc.sbuf_tensor(f"{name}_{self.phase}", shape, dtype))

    def ps(self, name, shape, dtype):
        self.excl.add(name)
        return self.stack.enter_context(self.nc.psum_tensor(f"{name}_{self.phase}", shape, dtype))

    def _handle(self, s):
        return self.dsem[s[1]] if isinstance(s, tuple) else self.sem[s]

    def _wait(self, en, deps):
        e = self.eng[en]
        need = {}
        for s, v in deps:
            if s == "pe" and en == "pe":
                continue
            if v > need.get(s, 0):
                need[s] = v
        for s, v in need.items():
            if self.waited[en].get(s, 0) < v:
                e.wait_ge(self._handle(s), v)
                self.waited[en][s] = v
                self.nins += 1

    def _deps(self, r, w):
        d = []
        for k in r:
            if k in self.last_w:
                d.append(self.last_w[k])
        for k in w:
            if k in self.last_w:
                d.append(self.last_w[k])
            d.extend(self.readers.get(k, ()))
        return d

    def _record(self, r, w, tag):
        for k in r:
            self.readers.setdefault(k, []).append(tag)
        for k in w:
            self.last_w[k] = tag
            self.readers[k] = []

    def _skip(self):
        import os
        lim = os.environ.get("KB_LIMIT")
        if not lim:
            return False
        ph, n = (int(v) for v in lim.split(":"))
        if self.phase != ph:
            return False
        self.pops = getattr(self, "pops", 0) + 1
        return self.pops > n

    def op(self, en, fn, r=(), w=(), inc=True):
        if self._skip():
            return None
        w = list(w) + [k for k in r if k in self.excl and k not in w]
        self._wait(en, self._deps(r, w))
        ins = fn(self.eng[en])
        self.nins += 1
        if inc:
            ins.then_inc(self.sem[en], 1)
            self.cnt[en] += 1
            tag = (en, self.cnt[en])
        else:
            tag = (en, self.cnt[en] + 1)
        self._record(r, w, tag)
        return ins

    def pe(self, fn, r=(), w=(), inc=True):
        return self.op("pe", fn, r, w, inc)

    def act(self, fn, r=(), w=()):
        return self.op("act", fn, r, w)

    def dve(self, fn, r=(), w=()):
        return self.op("dve", fn, r, w)

    def pool(self, fn, r=(), w=()):
        return self.op("pool", fn, r, w)

    def dma(self, out, in_, r=(), w=(), q="sp", **kw):
        if self._skip():
            return None
        deps = self._deps(r, w)
        i = self.dnext
        self.dnext = (self.dnext + 1) % self.NDMA
        if self.dcnt[i] > 0:
            deps.append((("d", i), 16 * self.dcnt[i]))
        self._wait(q, deps)
        ins = self.eng[q].dma_start(out=out, in_=in_, **kw)
        ins.then_inc(self.dsem[i], 16)
        self.nins += 1
        self.dcnt[i] += 1
        self._record(r, w, (("d", i), 16 * self.dcnt[i]))
        return ins

    def finish(self):
        deps = [(k, self.cnt[k]) for k in ("pe", "act", "dve", "pool") if self.cnt[k] > 0]
        deps += [(("d", i), 16 * c) for i, c in enumerate(self.dcnt) if c > 0]
        self._wait("sp", deps)

    def barrier(self):
        deps = [(k, self.cnt[k]) for k in ("pe", "act", "dve", "pool") if self.cnt[k] > 0]
        deps += [(("d", i), 16 * c) for i, c in enumerate(self.dcnt) if c > 0]
        for en in ("pe", "act", "dve", "pool", "sp"):
            self._wait(en, [d for d in deps if not (d[0] == "pe" and en == "pe")] + ([("pe", self.cnt["pe"])] if False else []))
        self.last_w = {}
        self.readers = {}

    def push(self):
        self._saved = getattr(self, "_saved", [])
        self._saved.append(self.stack)
        self.stack = ExitStack()
        self.stack.__enter__()
        self.phase += 1

    def pop(self):
        self.barrier()
        self.stack.__exit__(None, None, None)
        self.stack = self._saved.pop()


T = 8192
D = 1024
NT = T // 128
NG = T // 512
NCOL = 4104
EPS = 1e-6
LAMBDA_INIT = 0.8 - 0.6 * math.exp(0.0)
NFM = 20
TM0 = NFM * 128
TWO_PI = 2.0 * math.pi
MAGIC = 12582912.0
CW1 = 6.28125
CW2 = 0.0019350051879882812
CW3 = TWO_PI - CW1 - CW2


def build_program(dbg=(), upto=9):
    nc = bass.Bass("TRN2", target_bir_lowering=False)

    def din(name, shape, dt=F32):
        return nc.dram_tensor(name, list(shape), dt, kind="ExternalInput").ap()

    def dscr(name, shape, dt=F32):
        return nc.dram_tensor(name, list(shape), dt, kind="ExternalOutput" if name in dbg else "Internal").ap()

    x = din("x", [T, D])
    xres = din("xres", [T, 512])
    cT = din("cT", [128, 8])
    posT = din("posT", [128, NT], I32)
    wada = din("wada", [D, 2048])
    wadag = din("wadag", [D, 512])
    badac = din("badac", [128, 16])
    badag = din("badag", [128, 512])
    normgc = din("normgc", [128, 8])
    win = din("win", [D, NCOL])
    convc = din("convc", [128, 12, 4])
    alog = din("alog", [128, 4])
    dtb = din("dtb", [128, 4])
    gng = din("gng", [128, 1])
    slg = din("slg", [128, 1])
    gqk = din("gqk", [128, 1024])
    lamv = din("lamv", [128, 4, 64])
    wout = din("wout", [D, 512])
    c_identf = din("c_identf", [128, 128])
    c_identb = din("c_identb", [128, 128], BF16)
    c_onesb = din("c_onesb", [128, 128], BF16)
    c_masks = din("c_masks", [128, 7, 128])
    c_negblk = din("c_negblk", [128, 128])
    c_invf = din("c_invf", [128, 8])
    out = nc.dram_tensor("out", [T, 512], F32, kind="ExternalOutput").ap()

    gqT = dscr("gqT", [4, 128, T])
    gkT = dscr("gkT", [4, 128, T])
    gktok = dscr("gktok", [4, T, 128])
    gvtok = dscr("gvtok", [4, T, 128])
    sgT = dscr("sgT", [8, 128, T])
    dqT = dscr("dqT", [4, 128, T], BF16)
    dkT = dscr("dkT", [4, 128, T], BF16)
    dv = dscr("dv", [4, T, 128], BF16)
    mixT = dscr("mixT", [8, 128, T], BF16)

    kb = KB(nc)
    with kb.ctx():
        identf = kb.sb("identf", [128, 128], F32)
        identb = kb.sb("identb", [128, 128], BF16)
        onesb = kb.sb("onesb", [128, 128], BF16)
        masks = kb.sb("masks", [128, 7, 128], F32)
        negblk = kb.sb("negblk", [128, 128], F32)
        onesf, ublk, uincl, negustr, onesblk = (masks[:, i, :] for i in range(5))
        sel = [masks[:, 5, :], masks[:, 6, :]]
        bgres = kb.sb("bgres", [128, NT, 8], F32)
        cosr = kb.sb("cosr", [128, NT, 8], F32)
        sinr = kb.sb("sinr", [128, NT, 8], F32)
        gaterep = kb.sb("gaterep", [128, 512], F32)
        mscale = kb.sb("mscale", [128, 8], F32)
        shiftc = kb.sb("shiftc", [128, 8], F32)
        negA = kb.sb("negA", [128, 4], F32)
        dtbs = kb.sb("dtbs", [128, 4], F32)
        gngs = kb.sb("gngs", [128, 1], F32)
        slgs = kb.sb("slgs", [128, 1], F32)
        neglam = kb.sb("neglam", [128, 1], F32)
        negC = kb.sb("negC", [128, 1], F32)
        gqks = kb.sb("gqks", [128, 1024], F32)
        convs = kb.sb("convs", [128, 12, 4], F32)
        wobf = kb.sb("wobf", [128, 8, 512], BF16)

        epst = kb.sb("epst", [128, 1], F32)
        onet = kb.sb("onet", [128, 1], F32)
        CONSTK = ["const"]
        kb.pool(lambda e: e.memset(epst[:], EPS), w=CONSTK)
        kb.pool(lambda e: e.memset(onet[:], 1.0), w=CONSTK)
        for dst, src in ((identf, c_identf), (identb, c_identb), (onesb, c_onesb), (masks, c_masks), (negblk, c_negblk),
                         (negA, alog), (dtbs, dtb), (gngs, gng), (slgs, slg), (gqks, gqk), (convs, convc)):
            kb.dma(dst[:], src, w=CONSTK)

        kb.push()
        sc = kb.sb("sc", [128, 8], F32)
        screp = kb.sb("screp", [128, 8, 128], F32)
        wch = [kb.sb(f"wch{i}", [128, 8, 512], F32) for i in range(2)]
        modcol = kb.sb("modcol", [128, 16], F32)
        bcol = kb.sb("bcol", [128, 16], F32)
        ngc = kb.sb("ngc", [128, 8], F32)
        bgate = kb.sb("bgate", [128, 512], F32)
        lams = kb.sb("lams", [128, 4, 64], F32)
        lprod = kb.sb("lprod", [128, 2, 64], F32)
        lsum = kb.sb("lsum", [128, 2], F32)
        posi = kb.sb("posi", [128, NT], I32)
        posf = kb.sb("posf", [128, NT], F32)
        invf = kb.sb("invf", [128, 8], F32)
        ang = kb.sb("ang", [128, NT, 8], F32)
        tk = kb.sb("tk", [128, NT, 8], F32)
        rr = kb.sb("rr", [128, NT, 8], F32)
        gmax = kb.sb("gmax", [128, 2], F32)
        psm = kb.ps("psm", [128, 512], F32)
        psg = kb.ps("psg", [128, 512], F32)

        kb.dma(sc[:], cT, w=["sc"])
        kb.dma(bcol[:], badac, w=["bcol"])
        kb.dma(ngc[:], normgc, w=["ngc"])
        kb.dma(bgate[:], badag, w=["bgate"])
        kb.dma(lams[:], lamv, w=["lams"])
        kb.dma(posi[:], posT, w=["posi"])
        kb.dma(invf[:], c_invf, w=["invf"])
        kb.act(lambda e: e.activation(out=sc[:], in_=sc[:], func=ACT.Silu), r=["sc"], w=["sc"])
        kb.dve(lambda e: e.tensor_copy(out=screp[:], in_=sc[:].unsqueeze(2).to_broadcast([128, 8, 128])), r=["sc"], w=["screp"])
        wada_v = wada.rearrange("(kc p) n -> p kc n", p=128)
        for cc in range(4):
            wt = wch[cc % 2]
            kb.dma(wt[:], wada_v[:, :, cc * 512:(cc + 1) * 512], w=[f"wch{cc%2}"])
            for jj in range(4):
                j = cc * 4 + jj
                for kc in range(8):
                    kb.pe(lambda e: e.matmul(psm[:, 2 * j:2 * j + 2], lhsT=wt[:, kc, jj * 128:(jj + 1) * 128], rhs=screp[:, kc, 0:2],
                                             start=(kc == 0), stop=(kc == 7)),
                          r=[f"wch{cc%2}", "screp"], w=["psm"], inc=(kc == 7))
        kb.dve(lambda e: e.tensor_tensor(out=modcol[:], in0=psm[:, 0:32].rearrange("p (j two) -> p j two", two=2)[:, :, 0], in1=bcol[:], op=ALU.add),
               r=["psm", "bcol"], w=["modcol"])
        kb.dve(lambda e: e.tensor_copy(out=shiftc[:], in_=modcol[:, 0:8]), r=["modcol"], w=["shiftc"])
        kb.dve(lambda e: e.scalar_tensor_tensor(out=mscale[:], in0=modcol[:, 8:16], scalar=1.0, in1=ngc[:], op0=ALU.add, op1=ALU.mult),
               r=["modcol", "ngc"], w=["mscale"])
        wt = wch[0]
        kb.dma(wt[:], wadag.rearrange("(kc p) n -> p kc n", p=128), w=["wch0"])
        for kc in range(8):
            kb.pe(lambda e: e.matmul(psg[:], lhsT=screp[:, kc, :], rhs=wt[:, kc, :], start=(kc == 0), stop=(kc == 7)),
                  r=["wch0", "screp"], w=["psg"], inc=(kc == 7))
        kb.dve(lambda e: e.tensor_tensor(out=gaterep[:], in0=psg[:], in1=bgate[:], op=ALU.add), r=["psg", "bgate"], w=["gaterep"])
        wt = wch[1]
        kb.dma(wt[:], wout.rearrange("(kc p) n -> p kc n", p=128), w=["wch1"])
        kb.dve(lambda e: e.tensor_copy(out=wobf[:], in_=wt[:]), r=["wch1"], w=["wobf"])
        kb.dve(lambda e: e.tensor_tensor(out=lprod[:, 0, :], in0=lams[:, 0, :], in1=lams[:, 1, :], op=ALU.mult), r=["lams"], w=["lprod"])
        kb.dve(lambda e: e.tensor_tensor(out=lprod[:, 1, :], in0=lams[:, 2, :], in1=lams[:, 3, :], op=ALU.mult), r=["lams", "lprod"], w=["lprod"])
        kb.dve(lambda e: e.tensor_reduce(out=lsum[:], in_=lprod[:], axis=AX.X, op=ALU.add), r=["lprod"], w=["lsum"])
        kb.act(lambda e: e.activation(out=lsum[:], in_=lsum[:], func=ACT.Exp), r=["lsum"], w=["lsum"])
        kb.dve(lambda e: e.scalar_tensor_tensor(out=neglam[:], in0=lsum[:, 1:2], scalar=-LAMBDA_INIT, in1=lsum[:, 0:1], op0=ALU.add, op1=ALU.subtract),
               r=["lsum"], w=["neglam"])
        kb.act(lambda e: e.activation(out=negA[:], in_=negA[:], func=ACT.Exp), r=CONSTK, w=["negA"])
        kb.dve(lambda e: e.tensor_scalar(out=negA[:], in0=negA[:], scalar1=-1.0, scalar2=None, op0=ALU.mult), r=["negA"], w=["negA"])
        kb.dve(lambda e: e.tensor_scalar(out=slgs[:], in0=slgs[:], scalar1=1.0 - LAMBDA_INIT, scalar2=None, op0=ALU.mult), r=CONSTK, w=["slgs"])
        kb.dve(lambda e: e.tensor_reduce(out=gmax[:], in_=gqks[:].rearrange("p (a n) -> p a n", a=2), axis=AX.X, op=ALU.max, apply_absolute_value=True),
               r=CONSTK, w=["gmax"])
        kb.dve(lambda e: e.scalar_tensor_tensor(out=negC[:], in0=gmax[:, 0:1], scalar=-8.0, in1=gmax[:, 1:2], op0=ALU.mult, op1=ALU.mult),
               r=["gmax"], w=["negC"])
        kb.dve(lambda e: e.tensor_copy(out=posf[:], in_=posi[:]), r=["posi"], w=["posf"])
        kb.dve(lambda e: e.tensor_tensor(out=ang[:], in0=posf[:].unsqueeze(2).to_broadcast([128, NT, 8]),
                                         in1=invf[:].unsqueeze(1).to_broadcast([128, NT, 8]), op=ALU.mult), r=["posf", "invf"], w=["ang"])
        for (dst, off, name) in ((sinr, 0.0, "sinr"), (cosr, 0.25, "cosr")):
            kb.dve(lambda e: e.tensor_scalar(out=tk[:], in0=ang[:], scalar1=1.0 / TWO_PI, scalar2=off, op0=ALU.mult, op1=ALU.add), r=["ang"], w=["tk"])
            kb.dve(lambda e: e.tensor_scalar(out=tk[:], in0=tk[:], scalar1=MAGIC, scalar2=None, op0=ALU.add), r=["tk"], w=["tk"])
            kb.dve(lambda e: e.tensor_scalar(out=tk[:], in0=tk[:], scalar1=-MAGIC, scalar2=None, op0=ALU.add), r=["tk"], w=["tk"])
            kb.dve(lambda e: e.scalar_tensor_tensor(out=rr[:], in0=tk[:], scalar=-CW1, in1=ang[:], op0=ALU.mult, op1=ALU.add), r=["ang", "tk"], w=["rr"])
            kb.dve(lambda e: e.scalar_tensor_tensor(out=rr[:], in0=tk[:], scalar=-CW2, in1=rr[:], op0=ALU.mult, op1=ALU.add), r=["rr", "tk"], w=["rr"])
            kb.dve(lambda e: e.scalar_tensor_tensor(out=rr[:], in0=tk[:], scalar=-CW3, in1=rr[:], op0=ALU.mult, op1=ALU.add), r=["rr", "tk"], w=["rr"])
            kb.dve(lambda e: e.tensor_scalar(out=rr[:], in0=rr[:], scalar1=off * TWO_PI, scalar2=math.pi, op0=ALU.add, op1=ALU.min), r=["rr"], w=["rr"])
            kb.dve(lambda e: e.tensor_scalar(out=rr[:], in0=rr[:], scalar1=-math.pi, scalar2=None, op0=ALU.max), r=["rr"], w=["rr"])
            kb.act(lambda e: e.activation(out=dst[:], in_=rr[:], func=ACT.Sin), r=["rr"], w=[name])
        kb.pop()
        kb.push()
        wbf = kb.sb("wbf", [128, 8, NCOL], BF16)
        wcb = [kb.sb(f"wcb{i}", [128, 8, 128], F32) for i in range(2)]
        xt = [kb.sb(f"xt{i}", [128, 1024], F32) for i in range(2)]
        xs = [kb.sb(f"xs{i}", [128, 1024], F32) for i in range(2)]
        junk = kb.sb("junk", [128, 1024], BF16)
        st = [kb.sb(f"st{i}", [128, 4], F32) for i in range(2)]
        hT = [kb.sb(f"hT{i}", [128, 8, 512], BF16) for i in range(2)]
        cbuf = [kb.sb(f"cbuf{i}", [128, 515], F32) for i in range(12)]
        acc = [kb.sb(f"acc{i}", [128, 512], F32) for i in range(2)]
        yb = [kb.sb(f"yb{i}", [128, 512], F32) for i in range(2)]
        sqb = kb.sb("sqb", [128, 512], F32)
        sdb = kb.sb("sdb", [128, 512], F32)
        ynb = [kb.sb(f"ynb{i}", [128, 512], F32) for i in range(2)]
        trb = [kb.sb(f"trb{i}", [128, 512], F32) for i in range(2)]
        sgt = [kb.sb(f"sgt{i}", [128, 512], F32) for i in range(2)]
        qk = kb.sb("qk", [128, 1024], F32)
        ssq = kb.sb("ssq", [128, 16], F32)
        xn = kb.sb("xn", [128, 1024], F32)
        qkb = kb.sb("qkb", [128, 1024], BF16)
        rt = [kb.sb(f"rt{i}", [128, 16, 8], F32) for i in range(4)]
        vb = [kb.sb(f"vb{i}", [128, 512], BF16) for i in range(2)]
        qkTg = [kb.sb("qkTg0", [128, 8, 512], BF16)]
        tmp4 = kb.sb("tmp4", [128, 4], F32)
        psT = kb.ps("psT", [128, 512], F32)
        psF = kb.ps("psF", [128, 512], F32)
        psQ = kb.ps("psQ", [128, 512], F32)
        psK = kb.ps("psK", [128, 512], F32)
        psV = kb.ps("psV", [128, 512], F32)
        psN = kb.ps("psN", [128, 512], F32)
        psXf = kb.ps("psXf", [128, 512], F32)
        psXb = kb.ps("psXb", [128, 1024], BF16)

        win_v = win.rearrange("(kc p) n -> p kc n", p=128)
        nch = (NCOL + 127) // 128
        for ci in range(nch):
            c0, c1 = ci * 128, min(NCOL, ci * 128 + 128)
            wt = wcb[ci % 2]
            kb.dma(wt[:, :, 0:c1 - c0], win_v[:, :, c0:c1], w=[f"wcb{ci%2}"])
            if ci % 2 == 0:
                kb.dve(lambda e: e.tensor_copy(out=wbf[:, :, c0:c1], in_=wt[:, :, 0:c1 - c0]), r=[f"wcb{ci%2}"], w=["wbf"])
            else:
                kb.pool(lambda e: e.tensor_copy(out=wbf[:, :, c0:c1], in_=wt[:, :, 0:c1 - c0]), r=[f"wcb{ci%2}"], w=["wbf"])
        for c in range(12):
            kb.pool(lambda e: e.memset(cbuf[c][:, 0:3], 0.0), w=[f"cbuf{c}"])

        dqT_v = dqT.rearrange("h p t -> p h t")
        dkT_v = dkT.rearrange("h p t -> p h t")
        dv_v = dv.rearrange("h t e -> t h e")
        nfm = 0
        import os as _os
        if _os.environ.get('SKIP_P1'):
            kb.pool(lambda e: e.memset(bgres[:], -0.05), w=["bgres"])
            for h in range(4):
                kb.dma(gqT[h, :, 0:512], x[0:128, 0:512], w=["gqkT"])
                kb.dma(gkT[h, :, 0:512], x[0:128, 0:512], w=["gqkT"])
                kb.dma(gktok[h, 0:512, :].rearrange("(a p) e -> p a e", p=128), x[0:128, 0:512].rearrange("p (a e) -> p a e", a=4), w=["gtok"])
                kb.dma(gvtok[h, 0:512, :].rearrange("(a p) e -> p a e", p=128), x[0:128, 0:512].rearrange("p (a e) -> p a e", a=4), w=["gtok"])
                kb.dma(sgT[h, :, 0:512], x[0:128, 0:512], w=["sgT"])
        for g in range(0 if _os.environ.get('SKIP_P1') else NG):
            hTg = hT[g % 2]
            hk = f"hT{g%2}"
            for a in range(4):
                tt = 4 * g + a
                i2 = tt % 2
                kb.dma(xt[i2][:], x[tt * 128:(tt + 1) * 128, :], w=[f"xt{i2}"])
                s = st[i2]
                kb.act(lambda e: e.activation(out=junk[:], in_=xt[i2][:], func=ACT.Square, accum_out=s[:, 0:1]), r=[f"xt{i2}"], w=["junk", f"st{i2}"])
                kb.dve(lambda e: e.tensor_scalar(out=s[:, 1:2], in0=s[:, 0:1], scalar1=1.0 / D, scalar2=EPS, op0=ALU.mult, op1=ALU.add), r=[f"st{i2}"], w=[f"st{i2}"])
                kb.act(lambda e: e.activation(out=s[:, 2:3], in_=s[:, 1:2], func=ACT.Sqrt), r=[f"st{i2}"], w=[f"st{i2}"])
                kb.dve(lambda e: e.reciprocal(out=s[:, 3:4], in_=s[:, 2:3]), r=[f"st{i2}"], w=[f"st{i2}"])
                kb.act(lambda e: e.activation(out=xs[i2][:], in_=xt[i2][:], func=ACT.Copy, scale=s[:, 3:4]), r=[f"xt{i2}", f"st{i2}"], w=[f"xs{i2}"])
                for half in range(2):
                    for q4 in range(4):
                        kc = half * 4 + q4
                        kb.pe(lambda e: e.transpose(out=psT[:, q4 * 128:(q4 + 1) * 128], in_=xs[i2][:, kc * 128:(kc + 1) * 128], identity=identf[:]),
                              r=[f"xs{i2}", "const"], w=["psT"], inc=(q4 == 3))
                    for q4 in range(4):
                        kc = half * 4 + q4
                        kb.dve(lambda e: e.tensor_scalar(out=hTg[:, kc, a * 128:(a + 1) * 128], in0=psT[:, q4 * 128:(q4 + 1) * 128],
                                                         scalar1=mscale[:, kc:kc + 1], scalar2=shiftc[:, kc:kc + 1], op0=ALU.mult, op1=ALU.add),
                               r=["psT", "mscale", "shiftc"], w=[hk])
            for c in range(NFM):
                for kc in range(8):
                    kb.pe(lambda e: e.matmul(psF[:], lhsT=wbf[:, kc, c * 128:(c + 1) * 128], rhs=hTg[:, kc, :], start=(kc == 0), stop=(kc == 7)),
                          r=["wbf", hk], w=["psF"], inc=(kc == 7))
                tsl = slice(g * 512, (g + 1) * 512)
                if c >= 12:
                    sg = sgt[c % 2]
                    kb.act(lambda e: e.activation(out=sg[:], in_=psF[:], func=ACT.Silu), r=["psF"], w=[f"sgt{c%2}"])
                    kb.dma(sgT[c - 12, :, tsl], sg[:], r=[f"sgt{c%2}"], w=["sgT"], q="pool")
                    continue
                h, kind = c // 3, c % 3
                cb = cbuf[c]
                ck = f"cbuf{c}"
                kb.act(lambda e: e.activation(out=cb[:, 3:515], in_=psF[:], func=ACT.Copy), r=["psF"], w=[ck])
                ac = acc[nfm % 2]
                ak = f"acc{nfm%2}"
                y = yb[nfm % 2]
                yk = f"yb{nfm%2}"
                kb.dve(lambda e: e.tensor_scalar(out=ac[:], in0=cb[:, 0:512], scalar1=convs[:, c, 0:1], scalar2=None, op0=ALU.mult), r=[ck, "const"], w=[ak])
                for j in range(1, 4):
                    kb.dve(lambda e: e.scalar_tensor_tensor(out=ac[:], in0=cb[:, j:j + 512], scalar=convs[:, c, j:j + 1], in1=ac[:], op0=ALU.mult, op1=ALU.add),
                           r=[ck, ak, "const"], w=[ak])
                kb.pool(lambda e: e.tensor_copy(out=cb[:, 0:3], in_=cb[:, 512:515]), r=[ck], w=[ck])
                kb.act(lambda e: e.activation(out=y[:], in_=ac[:], func=ACT.Silu), r=[ak], w=[yk])
                src, srck = y, yk
                if kind < 2:
                    kb.pool(lambda e: e.tensor_tensor(out=sqb[:], in0=y[:], in1=y[:], op=ALU.mult), r=[yk], w=["sqb"])
                    kb.pe(lambda e: e.matmul(psN[:], lhsT=onesf, rhs=sqb[:], start=True, stop=True), r=["sqb", "const"], w=["psN"])
                    kb.act(lambda e: e.activation(out=sdb[:], in_=psN[:], func=ACT.Sqrt, bias=epst[:]), r=["psN", "const"], w=["sdb"])
                    kb.dve(lambda e: e.reciprocal(out=sdb[:], in_=sdb[:]), r=["sdb"], w=["sdb"])
                    yn = ynb[nfm % 2]
                    ynk = f"ynb{nfm%2}"
                    cq = (128.0 ** -0.5) if kind == 0 else 1.0
                    kb.dve(lambda e: e.scalar_tensor_tensor(out=yn[:], in0=y[:], scalar=cq, in1=sdb[:], op0=ALU.mult, op1=ALU.mult), r=[yk, "sdb"], w=[ynk])
                    kb.dma((gqT if kind == 0 else gkT)[h, :, tsl], yn[:], r=[ynk], w=["gqkT"], q="pool")
                    src, srck = yn, ynk
                if kind >= 1:
                    for a in range(4):
                        kb.pe(lambda e: e.transpose(out=psXf[:, a * 128:(a + 1) * 128], in_=src[:, a * 128:(a + 1) * 128], identity=identf[:]),
                              r=[srck, "const"], w=["psXf"], inc=(a == 3))
                    tr = trb[nfm % 2]
                    trk = f"trb{nfm%2}"
                    kb.dve(lambda e: e.tensor_copy(out=tr[:], in_=psXf[:]), r=["psXf"], w=[trk])
                    dst = (gktok if kind == 1 else gvtok)[h, g * 512:(g + 1) * 512, :].rearrange("(a p) e -> p a e", p=128)
                    kb.dma(dst, tr[:].rearrange("p (a e) -> p a e", a=4), r=[trk], w=["gtok"], q="pool")
                nfm += 1
            qTg = qkTg[0]
            qTk = "qkTg0"
            for a in range(4):
                tt = 4 * g + a
                lhs = lambda kc: hTg[:, kc, a * 128:(a + 1) * 128]
                for (pst, pk, c0, n) in ((psQ, "psQ", TM0, 512), (psK, "psK", TM0 + 512, 512), (psV, "psV", TM0 + 1024, 512), (psXf, "psXf", TM0 + 1536, 8)):
                    for kc in range(8):
                        kb.pe(lambda e: e.matmul(pst[:, 0:n], lhsT=lhs(kc), rhs=wbf[:, kc, c0:c0 + n], start=(kc == 0), stop=(kc == 7)),
                              r=["wbf", hk], w=[pk], inc=(kc == 7))
                kb.act(lambda e: e.activation(out=bgres[:, tt, 0:4], in_=psXf[:, 0:4], func=ACT.Sigmoid), r=["psXf"], w=["bgres"])
                for h in range(4):
                    kb.act(lambda e: e.activation(out=tmp4[:, h:h + 1], in_=psXf[:, 4 + h:5 + h], func=ACT.Exp, bias=dtbs[:, h:h + 1]), r=["psXf", "const"], w=["tmp4"])
                kb.act(lambda e: e.activation(out=tmp4[:], in_=tmp4[:], func=ACT.Ln, bias=onet[:]), r=["tmp4", "const"], w=["tmp4"])
                kb.dve(lambda e: e.tensor_tensor(out=bgres[:, tt, 4:8], in0=tmp4[:], in1=negA[:], op=ALU.mult), r=["tmp4", "negA"], w=["bgres"])
                v_ = vb[a % 2]
                vk = f"vb{a%2}"
                kb.act(lambda e: e.activation(out=v_[:], in_=psV[:], func=ACT.Copy), r=["psV"], w=[vk])
                kb.dma(dv_v[tt * 128:(tt + 1) * 128, :, :], v_[:].rearrange("p (h e) -> p h e", h=4), r=[vk], w=["dv"], q="pool")
                kb.dve(lambda e: e.tensor_copy(out=qk[:, 0:512], in_=psQ[:]), r=["psQ"], w=["qk"])
                kb.act(lambda e: e.activation(out=qk[:, 512:1024], in_=psK[:], func=ACT.Copy), r=["psK"], w=["qk"])
                kb.pool(lambda e: e.tensor_tensor(out=xn[:], in0=qk[:], in1=qk[:], op=ALU.mult), r=["qk"], w=["xn"])
                kb.dve(lambda e: e.tensor_reduce(out=ssq[:], in_=xn[:].rearrange("p (s d) -> p s d", d=64), axis=AX.X, op=ALU.add), r=["xn"], w=["ssq"])
                kb.dve(lambda e: e.tensor_scalar(out=ssq[:], in0=ssq[:], scalar1=1.0 / 64, scalar2=EPS, op0=ALU.mult, op1=ALU.add), r=["ssq"], w=["ssq"])
                kb.act(lambda e: e.activation(out=ssq[:], in_=ssq[:], func=ACT.Sqrt), r=["ssq"], w=["ssq"])
                kb.dve(lambda e: e.reciprocal(out=ssq[:], in_=ssq[:]), r=["ssq"], w=["ssq"])
                qk3 = qk[:].rearrange("p (s d) -> p s d", d=64)
                xn3 = xn[:].rearrange("p (s d) -> p s d", d=64)
                kb.dve(lambda e: e.tensor_tensor(out=xn3, in0=qk3, in1=ssq[:].unsqueeze(2).to_broadcast([128, 16, 64]), op=ALU.mult), r=["qk", "ssq"], w=["xn"])
                kb.pool(lambda e: e.tensor_tensor(out=xn[:], in0=xn[:], in1=gqks[:], op=ALU.mult), r=["xn", "const"], w=["xn"])
                kb.act(lambda e: e.activation(out=qkb[:], in_=xn[:], func=ACT.Copy), r=["xn"], w=["qkb"])
                qkb3 = qkb[:].rearrange("p (s d) -> p s d", d=64)
                cb_ = cosr[:, tt, :].unsqueeze(1).to_broadcast([128, 16, 8])
                sb_ = sinr[:, tt, :].unsqueeze(1).to_broadcast([128, 16, 8])
                x1, x2 = xn3[:, :, 0:8], xn3[:, :, 8:16]
                kb.dve(lambda e: e.tensor_tensor(out=rt[0][:], in0=x1, in1=cb_, op=ALU.mult), r=["xn", "cosr"], w=["rt0"])
                kb.dve(lambda e: e.tensor_tensor(out=rt[1][:], in0=x2, in1=sb_, op=ALU.mult), r=["xn", "sinr"], w=["rt1"])
                kb.dve(lambda e: e.tensor_tensor(out=rt[2][:], in0=x2, in1=cb_, op=ALU.mult), r=["xn", "cosr"], w=["rt2"])
                kb.dve(lambda e: e.tensor_tensor(out=rt[3][:], in0=x1, in1=sb_, op=ALU.mult), r=["xn", "sinr"], w=["rt3"])
                kb.dve(lambda e: e.tensor_tensor(out=qkb3[:, :, 0:8], in0=rt[0][:], in1=rt[1][:], op=ALU.subtract), r=["rt0", "rt1", "qkb"], w=["qkb"])
                kb.dve(lambda e: e.tensor_tensor(out=qkb3[:, :, 8:16], in0=rt[2][:], in1=rt[3][:], op=ALU.add), r=["rt2", "rt3", "qkb"], w=["qkb"])
                for bi in range(8):
                    kb.pe(lambda e: e.transpose(out=psXb[:, bi * 128:(bi + 1) * 128], in_=qkb[:, bi * 128:(bi + 1) * 128], identity=identb[:]),
                          r=["qkb", "const"], w=["psXb"], inc=(bi == 7))
                kb.dve(lambda e: e.tensor_copy(out=qTg[:, :, a * 128:(a + 1) * 128], in_=psXb[:].rearrange("p (b t) -> p b t", b=8)), r=["psXb"], w=[qTk])
            kb.dma(dqT_v[:, :, g * 512:(g + 1) * 512], qTg[:, 0:4, :], r=[qTk], w=["dqT"], q="pool")
            kb.dma(dkT_v[:, :, g * 512:(g + 1) * 512], qTg[:, 4:8, :], r=[qTk], w=["dkT"], q="pool")
        kb.pop()
        kb.push()
        Q = lambda t, i: t[:, i * 128:(i + 1) * 128]
        names = ["kTp", "qTp", "ktk", "vtk", "Ug", "E", "Ei", "Es", "aT", "Ya", "YaT", "Yb", "YbT", "Wa", "Wb", "dg", "qdT", "kdec", "rhs2", "vn"]
        gb = {}
        for h in range(4):
            for par in range(2):
                for n in names:
                    gb[(n, h, par)] = kb.sb(f"{n}{h}{par}", [128, 128], F32)
                gb[("sm", h, par)] = kb.sb(f"sm{h}{par}", [128, 8], F32)
                gb[("smc", h, par)] = kb.sb(f"smc{h}{par}", [128, 8], F32)
                gb[("g2", h, par)] = kb.sb(f"g2{h}{par}", [128, 2], F32)
        Sst = [[kb.sb(f"S{h}{i}", [128, 128], F32) for i in range(2)] for h in range(4)]
        obuf = [kb.sb(f"obuf{h}", [128, 512], F32) for h in range(4)]
        gsq = kb.sb("gsq", [128, 512], F32)
        gsd = kb.sb("gsd", [128, 512], F32)
        gon = kb.sb("gon", [128, 512], F32)
        gsg = [kb.sb(f"gsg{i}", [128, 512], F32) for i in range(2)]
        gmx = [kb.sb(f"gmx{i}", [128, 512], BF16) for i in range(2)]
        pA = kb.ps("pA", [128, 512], F32)
        pB = kb.ps("pB", [128, 512], F32)
        pC = kb.ps("pC", [128, 512], F32)
        pKS = kb.ps("pKS", [128, 512], F32)
        pVN = kb.ps("pVN", [128, 512], F32)
        pSS = kb.ps("pSS", [128, 512], F32)
        pO = kb.ps("pO", [128, 512], F32)
        pP = kb.ps("pP", [128, 512], F32)
        for h in range(4):
            kb.pool(lambda e: e.memset(Sst[h][0][:], 0.0), w=[f"S{h}0"])
        scur = [0, 0, 0, 0]
        npost = 0
        import os as _os
        for p in range(int(_os.environ.get('GDN_PAIRS', NT))):
            par = p % 2
            tok = slice(p * 128, (p + 1) * 128)
            for h in range(4):
                B = lambda n: gb[(n, h, par)]
                K_ = lambda n: f"{n}{h}{par}"
                gcol = bgres[:, p, 4 + h:5 + h]
                beta = bgres[:, p, h:h + 1]
                kb.dma(B("kTp")[:], gkT[h, :, tok], r=["gqkT"], w=[K_("kTp")])
                kb.dma(B("qTp")[:], gqT[h, :, tok], r=["gqkT"], w=[K_("qTp")])
                kb.dma(B("ktk")[:], gktok[h, tok, :], r=["gtok"], w=[K_("ktk")])
                kb.dma(B("vtk")[:], gvtok[h, tok, :], r=["gtok"], w=[K_("vtk")])
                kb.dve(lambda e: e.tensor_scalar(out=B("Ug")[:], in0=ublk, scalar1=gcol, scalar2=None, op0=ALU.mult), r=["const", "bgres"], w=[K_("Ug")])
                kb.dve(lambda e: e.tensor_copy(out=B("g2")[:], in_=gcol.to_broadcast([128, 2])), r=["bgres"], w=[K_("g2")])
                kb.pe(lambda e: e.matmul(Q(pA, 0), lhsT=B("Ug")[:], rhs=onesblk, start=True, stop=False), r=[K_("Ug"), "const"], w=["pA"], inc=False)
                kb.pe(lambda e: e.matmul(Q(pA, 0), lhsT=negblk[:], rhs=B("Ug")[:], start=False, stop=True), r=[K_("Ug"), "const"], w=["pA"])
                kb.pe(lambda e: e.matmul(pB[:, 0:2], lhsT=B("Ug")[:], rhs=onesf[:, 0:2], start=True, stop=True), r=[K_("Ug"), "const"], w=["pB"], inc=False)
                kb.pe(lambda e: e.matmul(pB[:, 2:4], lhsT=onesblk, rhs=B("g2")[:], start=True, stop=True), r=[K_("g2"), "const"], w=["pB"], inc=False)
                kb.pe(lambda e: e.matmul(pB[:, 4:6], lhsT=sel[0], rhs=B("g2")[:], start=True, stop=True), r=[K_("g2"), "const"], w=["pB"], inc=False)
                kb.pe(lambda e: e.matmul(pB[:, 6:8], lhsT=sel[1], rhs=B("g2")[:], start=True, stop=True), r=[K_("g2"), "const"], w=["pB"])
                sm, smc = B("sm"), B("smc")
                kb.act(lambda e: e.activation(out=smc[:], in_=pB[:, 0:8], func=ACT.Copy), r=["pB"], w=[K_("smc")])
                kb.act(lambda e: e.activation(out=sm[:, 0:1], in_=smc[:, 0:1], func=ACT.Exp), r=[K_("smc")], w=[K_("sm")])
                kb.dve(lambda e: e.tensor_scalar(out=sm[:, 1:2], in0=sm[:, 0:1], scalar1=-1.0, scalar2=None, op0=ALU.mult), r=[K_("sm")], w=[K_("sm")])
                kb.dve(lambda e: e.tensor_tensor(out=sm[:, 2:3], in0=smc[:, 2:3], in1=smc[:, 0:1], op=ALU.subtract), r=[K_("smc"), K_("sm")], w=[K_("sm")])
                kb.act(lambda e: e.activation(out=sm[:, 2:3], in_=sm[:, 2:3], func=ACT.Exp), r=[K_("sm")], w=[K_("sm")])
                kb.act(lambda e: e.activation(out=sm[:, 4:8], in_=smc[:, 4:8], func=ACT.Exp), r=[K_("smc"), K_("sm")], w=[K_("sm")])
                kb.dve(lambda e: e.tensor_scalar(out=B("E")[:], in0=Q(pA, 0), scalar1=-1.0, scalar2=0.0, op0=ALU.mult, op1=ALU.min), r=["pA"], w=[K_("E")])
                kb.act(lambda e: e.activation(out=B("E")[:], in_=B("E")[:], func=ACT.Exp), r=[K_("E")], w=[K_("E")])
                kb.pool(lambda e: e.tensor_tensor(out=B("Ei")[:], in0=B("E")[:], in1=uincl, op=ALU.mult), r=[K_("E"), "const"], w=[K_("Ei")])
                kb.pool(lambda e: e.tensor_tensor(out=B("Es")[:], in0=B("E")[:], in1=negustr, op=ALU.mult), r=[K_("E"), "const"], w=[K_("Es")])
                kb.pe(lambda e: e.matmul(Q(pA, 1), lhsT=B("kTp")[:], rhs=B("kTp")[:], start=True, stop=True), r=[K_("kTp")], w=["pA"])
                kb.pe(lambda e: e.matmul(Q(pA, 2), lhsT=B("kTp")[:], rhs=B("qTp")[:], start=True, stop=True), r=[K_("kTp"), K_("qTp")], w=["pA"])
                kb.dve(lambda e: e.tensor_tensor(out=B("aT")[:], in0=Q(pA, 2), in1=B("Ei")[:], op=ALU.mult), r=["pA", K_("Ei")], w=[K_("aT")])
                kb.dve(lambda e: e.scalar_tensor_tensor(out=B("Ya")[:], in0=Q(pA, 1), scalar=beta, in1=B("Es")[:], op0=ALU.mult, op1=ALU.mult),
                       r=["pA", K_("Es"), "bgres"], w=[K_("Ya")])
                kb.pe(lambda e: e.transpose(out=Q(pB, 1), in_=B("Ya")[:], identity=identf[:]), r=[K_("Ya"), "const"], w=["pB"])
                kb.act(lambda e: e.activation(out=B("YaT")[:], in_=Q(pB, 1), func=ACT.Copy), r=["pB"], w=[K_("YaT")])
                kb.dve(lambda e: e.tensor_tensor(out=B("Wa")[:], in0=B("Ya")[:], in1=identf[:], op=ALU.add), r=[K_("Ya"), "const"], w=[K_("Wa")])
                Y, YT, W = ("Ya", "YaT", "Wa")
                for k in range(1, 6):
                    Yn, YTn, Wn = ("Yb", "YbT", "Wb") if Y == "Ya" else ("Ya", "YaT", "Wa")
                    if k < 5:
                        kb.pe(lambda e: e.matmul(Q(pB, 2), lhsT=B(YT)[:], rhs=B(Y)[:], start=True, stop=True), r=[K_(Y), K_(YT)], w=["pB"])
                    kb.pe(lambda e: e.matmul(Q(pB, 3), lhsT=B(Y)[:], rhs=B(YT)[:], start=True, stop=True), r=[K_(Y), K_(YT)], w=["pB"])
                    if k < 5:
                        kb.act(lambda e: e.activation(out=B(Yn)[:], in_=Q(pB, 2), func=ACT.Copy), r=["pB"], w=[K_(Yn)])
                    kb.dve(lambda e: e.tensor_copy(out=B(YTn)[:], in_=Q(pB, 3)), r=["pB"], w=[K_(YTn)] + (["pB"] if _os.environ.get("SER") else []))
                    kb.pe(lambda e: e.matmul(Q(pC, 0), lhsT=B(YTn)[:], rhs=B(W)[:], start=True, stop=True), r=[K_(YTn), K_(W)], w=["pC"])
                    kb.dve(lambda e: e.tensor_tensor(out=B(Wn)[:], in0=Q(pC, 0), in1=B(W)[:], op=ALU.add), r=["pC", K_(W)], w=[K_(Wn)])
                    Y, YT, W = Yn, YTn, Wn
                gb[("Wfin", h, par)] = W
                kb.dve(lambda e: e.tensor_scalar(out=B("dg")[:], in0=identf[:], scalar1=sm[:, 0:1], scalar2=None, op0=ALU.mult), r=["const", K_("sm")], w=[K_("dg")])
                kb.pe(lambda e: e.matmul(Q(pA, 3), lhsT=onesf, rhs=B("dg")[:], start=True, stop=True), r=[K_("dg"), "const"], w=["pA"])
                kb.dve(lambda e: e.tensor_tensor(out=B("qdT")[:], in0=Q(pA, 3), in1=B("qTp")[:], op=ALU.mult), r=["pA", K_("qTp")], w=[K_("qdT")])
                kb.act(lambda e: e.activation(out=B("kdec")[:], in_=B("ktk")[:], func=ACT.Copy, scale=sm[:, 2:3]), r=[K_("ktk"), K_("sm")], w=[K_("kdec")])
            for c in range(2 if int(_os.environ.get('GDN_SCAN', 1)) else 0):
                hs = slice(64 * c, 64 * c + 64)
                for h in range(4):
                    B = lambda n: gb[(n, h, par)]
                    K_ = lambda n: f"{n}{h}{par}"
                    S, Sk = Sst[h][scur[h]], f"S{h}{scur[h]}"
                    kb.pe(lambda e: e.matmul(Q(pKS, h), lhsT=B("kTp")[:], rhs=S[:], start=True, stop=True), r=[K_("kTp"), Sk], w=["pKS"])
                for h in range(4):
                    B = lambda n: gb[(n, h, par)]
                    K_ = lambda n: f"{n}{h}{par}"
                    kb.dve(lambda e: e.scalar_tensor_tensor(out=B("rhs2")[hs, :], in0=Q(pKS, h)[hs, :], scalar=B("sm")[hs, 1:2], in1=B("vtk")[hs, :],
                                                            op0=ALU.mult, op1=ALU.add), r=["pKS", K_("sm"), K_("vtk")], w=[K_("rhs2")])
                for h in range(4):
                    B = lambda n: gb[(n, h, par)]
                    K_ = lambda n: f"{n}{h}{par}"
                    Wf = gb[("Wfin", h, par)]
                    kb.pe(lambda e: e.matmul(Q(pVN, h), lhsT=B(Wf)[hs, :], rhs=B("rhs2")[hs, :], start=True, stop=True), r=[K_(Wf), K_("rhs2")], w=["pVN"])
                for h in range(4):
                    B = lambda n: gb[(n, h, par)]
                    K_ = lambda n: f"{n}{h}{par}"
                    kb.dve(lambda e: e.tensor_scalar(out=B("vn")[hs, :], in0=Q(pVN, h)[hs, :], scalar1=bgres[hs, p, h:h + 1], scalar2=None, op0=ALU.mult),
                           r=["pVN", "bgres"], w=[K_("vn")])
                for h in range(4):
                    B = lambda n: gb[(n, h, par)]
                    K_ = lambda n: f"{n}{h}{par}"
                    S, Sk = Sst[h][scur[h]], f"S{h}{scur[h]}"
                    oc = slice(h * 128 + 64 * c, h * 128 + 64 * c + 64)
                    kb.pe(lambda e: e.matmul(pO[:, oc], lhsT=S[:], rhs=B("qdT")[:, hs], start=True, stop=False), r=[Sk, K_("qdT")], w=["pO"], inc=False)
                    kb.pe(lambda e: e.matmul(pO[:, oc], lhsT=B("vn")[hs, :], rhs=B("aT")[hs, hs], start=False, stop=True), r=[K_("vn"), K_("aT")], w=["pO"])
                    kb.pe(lambda e: e.matmul(Q(pSS, h), lhsT=B("kdec")[hs, :], rhs=B("vn")[hs, :], start=True, stop=True), r=[K_("kdec"), K_("vn")], w=["pSS"])
                for h in range(4):
                    B = lambda n: gb[(n, h, par)]
                    K_ = lambda n: f"{n}{h}{par}"
                    S, Sk = Sst[h][scur[h]], f"S{h}{scur[h]}"
                    Sn, Snk = Sst[h][1 - scur[h]], f"S{h}{1-scur[h]}"
                    kb.dve(lambda e: e.scalar_tensor_tensor(out=Sn[:], in0=S[:], scalar=B("sm")[:, 4 + 2 * c:5 + 2 * c], in1=Q(pSS, h), op0=ALU.mult, op1=ALU.add),
                           r=[Sk, K_("sm"), "pSS"], w=[Snk])
                    scur[h] = 1 - scur[h]
            for h in range(4):
                kb.act(lambda e: e.activation(out=obuf[h][:, (p % 4) * 128:(p % 4 + 1) * 128], in_=Q(pO, h), func=ACT.Copy), r=["pO"], w=[f"obuf{h}"])
            if p % 4 == 3:
                g = p // 4
                tsl = slice(g * 512, (g + 1) * 512)
                for h in range(4):
                    i2 = npost % 2
                    kb.dma(gsg[i2][:], sgT[h, :, tsl], r=["sgT"], w=[f"gsg{i2}"])
                    kb.pool(lambda e: e.tensor_tensor(out=gsq[:], in0=obuf[h][:], in1=obuf[h][:], op=ALU.mult), r=[f"obuf{h}"], w=["gsq"])
                    kb.pe(lambda e: e.matmul(pP[:], lhsT=onesf, rhs=gsq[:], start=True, stop=True), r=["gsq", "const"], w=["pP"])
                    kb.act(lambda e: e.activation(out=gsd[:], in_=pP[:], func=ACT.Sqrt, bias=epst[:], scale=1.0 / 128), r=["pP", "const"], w=["gsd"])
                    kb.dve(lambda e: e.reciprocal(out=gsd[:], in_=gsd[:]), r=["gsd"], w=["gsd"])
                    kb.dve(lambda e: e.tensor_tensor(out=gon[:], in0=obuf[h][:], in1=gsd[:], op=ALU.mult), r=[f"obuf{h}", "gsd"], w=["gon"])
                    kb.dve(lambda e: e.scalar_tensor_tensor(out=gmx[i2][:], in0=gon[:], scalar=gngs[:, 0:1], in1=gsg[i2][:], op0=ALU.mult, op1=ALU.mult),
                           r=["gon", "const", f"gsg{i2}"], w=[f"gmx{i2}"])
                    kb.dma(mixT[h, :, tsl], gmx[i2][:], r=[f"gmx{i2}"], w=["mixT"], q="pool")
                    npost += 1
        kb.pop()
        if upto <= 2:
            kb.finish()
            return nc, kb
        kb.push()
        akT = [kb.sb(f"akT{i}", [128, T], BF16) for i in range(2)]
        aqT = [kb.sb(f"aqT{i}", [128, T], BF16) for i in range(2)]
        avv = [kb.sb(f"avv{i}", [128, NT, 128], BF16) for i in range(2)]
        pTb = [[kb.sb(f"pT{i}{j}", [128, 512], BF16) for j in range(2)] for i in range(2)]
        arl = [kb.sb(f"arl{i}", [128, 512], F32) for i in range(2)]
        ao = [kb.sb(f"ao{i}", [128, 512], F32) for i in range(2)]
        aod = kb.sb("aod", [128, 512], F32)
        asq = kb.sb("asq", [128, 512], F32)
        asd = kb.sb("asd", [128, 512], F32)
        aon = kb.sb("aon", [128, 512], F32)
        asg = [kb.sb(f"asg{i}", [128, 512], F32) for i in range(2)]
        amx = [kb.sb(f"amx{i}", [128, 512], BF16) for i in range(2)]
        pS = [[kb.ps(f"pS{i}{j}", [128, 512], F32) for j in range(2)] for i in range(2)]
        pOa = [kb.ps(f"pOa{i}", [128, 512], F32) for i in range(2)]
        pLa = [kb.ps(f"pLa{i}", [128, 512], F32) for i in range(2)]
        nstep = 0
        for h in range(4):
            hb = h % 2
            for part in range(2):
                sl = slice(part * (T // 2), (part + 1) * (T // 2))
                kb.dma(akT[hb][:, sl], dkT[h, :, sl], r=["dkT"], w=[f"akT{hb}"])
                kb.dma(aqT[hb][:, sl], dqT[h, :, sl], r=["dqT"], w=[f"aqT{hb}"])
            dvh = dv[h].rearrange("(t p) e -> p t e", p=128)
            for part in range(4):
                kb.dma(avv[hb][:, part * 16:(part + 1) * 16, :], dvh[:, part * 16:(part + 1) * 16, :], r=["dv"], w=[f"avv{hb}"])
            for g in range(NG):
                nk = 4 * g + 4
                for kt in range(nk):
                    j = kt - 4 * g
                    qlo = 128 * max(j, 0)
                    qs = slice(qlo, 512)
                    sb2 = nstep % 2
                    for i in range(2):
                        rows = slice(64 * i, 64 * i + 64)
                        kb.pe(lambda e: e.matmul(pS[i][sb2][:, qs], lhsT=akT[hb][rows, kt * 128:(kt + 1) * 128], rhs=aqT[hb][rows, g * 512 + qlo:(g + 1) * 512],
                                                 start=True, stop=True), r=[f"akT{hb}", f"aqT{hb}"], w=[f"pS{i}{sb2}"])
                    for i in range(2):
                        pt = pTb[i][sb2]
                        kb.act(lambda e: e.activation(out=pt[:, qs], in_=pS[i][sb2][:, qs], func=ACT.Exp, bias=negC[:], scale=0.125),
                               r=[f"pS{i}{sb2}", "negC"], w=[f"pT{i}{sb2}"])
                        if j >= 0:
                            kb.pool(lambda e: e.memset(pt[64:128, qlo:qlo + 64], 0.0), r=[], w=[f"pT{i}{sb2}"])
                    for i in range(2):
                        pt = pTb[i][sb2]
                        kb.pe(lambda e: e.matmul(pOa[i][:, qs], lhsT=avv[hb][:, kt, :], rhs=pt[:, qs], start=(kt == 0), stop=(kt == nk - 1)),
                              r=[f"avv{hb}", f"pT{i}{sb2}"], w=[f"pOa{i}"], inc=False)
                        kb.pe(lambda e: e.matmul(pLa[i][:, qs], lhsT=onesb[:], rhs=pt[:, qs], start=(kt == 0), stop=(kt == nk - 1)),
                              r=["const", f"pT{i}{sb2}"], w=[f"pLa{i}"])
                    nstep += 1
                tsl = slice(g * 512, (g + 1) * 512)
                i2 = g % 2
                kb.dma(asg[i2][:], sgT[4 + h, :, tsl], r=["sgT"], w=[f"asg{i2}"])
                for i in range(2):
                    kb.dve(lambda e: e.reciprocal(out=arl[i][:], in_=pLa[i][:]), r=[f"pLa{i}"], w=[f"arl{i}"])
                    kb.dve(lambda e: e.tensor_tensor(out=ao[i][:], in0=pOa[i][:], in1=arl[i][:], op=ALU.mult), r=[f"pOa{i}", f"arl{i}"], w=[f"ao{i}"])
                kb.dve(lambda e: e.scalar_tensor_tensor(out=aod[:], in0=ao[1][:], scalar=neglam[:, 0:1], in1=ao[0][:], op0=ALU.mult, op1=ALU.add),
                       r=["ao0", "ao1", "neglam"], w=["aod"])
                kb.pool(lambda e: e.tensor_tensor(out=asq[:], in0=aod[:], in1=aod[:], op=ALU.mult), r=["aod"], w=["asq"])
                pN_ = pS[0][0]
                kb.pe(lambda e: e.matmul(pN_[:], lhsT=onesf, rhs=asq[:], start=True, stop=True), r=["asq", "const"], w=["pS00"])
                kb.act(lambda e: e.activation(out=asd[:], in_=pN_[:], func=ACT.Sqrt, bias=epst[:], scale=1.0 / 128), r=["pS00", "const"], w=["asd"])
                kb.dve(lambda e: e.reciprocal(out=asd[:], in_=asd[:]), r=["asd"], w=["asd"])
                kb.dve(lambda e: e.tensor_tensor(out=aon[:], in0=aod[:], in1=asd[:], op=ALU.mult), r=["aod", "asd"], w=["aon"])
                kb.dve(lambda e: e.scalar_tensor_tensor(out=amx[i2][:], in0=aon[:], scalar=slgs[:, 0:1], in1=asg[i2][:], op0=ALU.mult, op1=ALU.mult),
                       r=["aon", "slgs", f"asg{i2}"], w=[f"amx{i2}"])
                kb.dma(mixT[4 + h, :, tsl], amx[i2][:], r=[f"amx{i2}"], w=["mixT"], q="pool")
        kb.pop()
        if upto <= 3:
            kb.finish()
            return nc, kb
        kb.push()
        mt = [kb.sb(f"mt{i}", [128, 8, 512], BF16) for i in range(2)]
        xr = [kb.sb(f"xr{i}", [128, 512], F32) for i in range(2)]
        y1 = [kb.sb(f"y1{i}", [128, 512], F32) for i in range(2)]
        y2 = [kb.sb(f"y2{i}", [128, 512], F32) for i in range(2)]
        pY = [kb.ps(f"pY{i}", [128, 512], F32) for i in range(2)]
        mixT_v = mixT.rearrange("m p t -> p m t")
        for g in range(NG):
            m_ = mt[g % 2]
            mk = f"mt{g%2}"
            kb.dma(m_[:], mixT_v[:, :, g * 512:(g + 1) * 512], r=["mixT"], w=[mk])
            for a in range(4):
                tt = 4 * g + a
                i2 = tt % 2
                kb.dma(xr[i2][:], xres[tt * 128:(tt + 1) * 128, :], w=[f"xr{i2}"])
                for m in range(8):
                    kb.pe(lambda e: e.matmul(pY[i2][:], lhsT=m_[:, m, a * 128:(a + 1) * 128], rhs=wobf[:, m, :], start=(m == 0), stop=(m == 7)),
                          r=[mk, "wobf"], w=[f"pY{i2}"], inc=(m == 7))
                kb.dve(lambda e: e.tensor_tensor(out=y1[i2][:], in0=pY[i2][:], in1=gaterep[:], op=ALU.mult), r=[f"pY{i2}", "gaterep"], w=[f"y1{i2}"])
                kb.pool(lambda e: e.tensor_tensor(out=y2[i2][:], in0=y1[i2][:], in1=xr[i2][:], op=ALU.add), r=[f"y1{i2}", f"xr{i2}"], w=[f"y2{i2}"])
                kb.dma(out[tt * 128:(tt + 1) * 128, :], y2[i2][:], r=[f"y2{i2}"], w=["out"], q="pool")
        kb.pop()
        kb.finish()
    return nc, kb


def _consts():
    i = np.arange(128)
    same = (i[:, None] // 64) == (i[None, :] // 64)
    onesf = np.ones((128, 128), np.float32)
    ublk = (same & (i[:, None] <= i[None, :])).astype(np.float32)
    uincl = ublk.copy()
    negustr = -(same & (i[None, :] > i[:, None])).astype(np.float32)
    onesblk = same.astype(np.float32)
    sel0 = np.repeat((i < 64).astype(np.float32)[:, None], 128, 1)
    sel1 = np.repeat((i >= 64).astype(np.float32)[:, None], 128, 1)
    masks = np.stack([onesf, ublk, uincl, negustr, onesblk, sel0, sel1], 1).astype(np.float32)
    invf = (np.float32(500000.0) ** (-np.arange(0, 16, 2, dtype=np.float32) / np.float32(16))).astype(np.float32)
    return {
        "c_identf": np.eye(128, dtype=np.float32),
        "c_identb": np.eye(128, dtype=np.float32).astype(ml_dtypes.bfloat16),
        "c_onesb": np.ones((128, 128), np.float32).astype(ml_dtypes.bfloat16),
        "c_masks": np.ascontiguousarray(masks),
        "c_negblk": -onesblk,
        "c_invf": np.ascontiguousarray(np.broadcast_to(invf[None, :], (128, 8))),
    }


def _rep(v, n=128):
    return np.ascontiguousarray(np.broadcast_to(np.asarray(v, np.float32).reshape(1, -1), (n, np.asarray(v).size)))


def prep_core(inp, b, half):
    f = lambda a: np.ascontiguousarray(np.asarray(a, np.float32))
    w_in = np.asarray(inp["w_in"][0], np.float32)
    off = np.cumsum([0, 512, 512, 512, 4, 4, 512, 512, 512, 512, 512])
    aq, ak, av, abeta, adec, agate, bq, bk, bv, bgate = [np.arange(off[i], off[i + 1]) for i in range(10)]
    cols = []
    for h in range(4):
        cols += [aq[h * 128:(h + 1) * 128], ak[h * 128:(h + 1) * 128], av[h * 128:(h + 1) * 128]]
    cols += [agate, bgate, bq, bk, bv, abeta, adec]
    cols = np.concatenate(cols)
    conv_w = np.asarray(inp["conv_w"][0], np.float32)
    convc = np.zeros((128, 12, 4), np.float32)
    for h in range(4):
        for kind in range(3):
            ch = kind * 512 + h * 128 + np.arange(128)
            convc[:, 3 * h + kind, :] = conv_w[:, ch].T
    w_ada = np.asarray(inp["w_ada"][0], np.float32)
    b_ada = np.asarray(inp["b_ada"][0], np.float32)
    gsl = slice(2048 + half * 512, 2048 + (half + 1) * 512)
    d = {
        "x": f(inp["x"][b]),
        "xres": f(inp["x"][b][:, half * 512:(half + 1) * 512]),
        "cT": f(np.asarray(inp["c"][b]).reshape(8, 128).T),
        "posT": np.ascontiguousarray(np.asarray(inp["positions"][b], np.int32).reshape(NT, 128).T),
        "wada": f(w_ada[:, 0:2048]),
        "wadag": f(w_ada[:, gsl]),
        "badac": f(b_ada[0:2048].reshape(16, 128).T),
        "badag": _rep(b_ada[gsl]),
        "normgc": f(np.asarray(inp["norm_g"][0]).reshape(8, 128).T),
        "win": f(w_in[:, cols]),
        "convc": convc,
        "alog": _rep(inp["a_log"][0]),
        "dtb": _rep(inp["dt_bias"][0]),
        "gng": f(np.asarray(inp["gdn_norm_g"][0]).reshape(128, 1)),
        "slg": f(np.asarray(inp["subln_g"][0]).reshape(128, 1)),
        "gqk": _rep(np.concatenate([np.tile(np.asarray(inp["q_norm_g"][0]), 8), np.tile(np.asarray(inp["k_norm_g"][0]), 8)])),
        "lamv": np.ascontiguousarray(np.broadcast_to(np.stack([np.asarray(inp[k][0], np.float32) for k in
                                                               ("lambda_q1", "lambda_k1", "lambda_q2", "lambda_k2")])[None], (128, 4, 64))),
        "wout": f(np.asarray(inp["w_out"][0])[:, half * 512:(half + 1) * 512]),
    }
    d.update(_consts())
    return d


_NC_CACHE = {}


def kernel(**inputs):
    if "nc" not in _NC_CACHE:
        _NC_CACHE["nc"] = build_program()[0]
    nc = _NC_CACHE["nc"]
    in_maps = [prep_core(inputs, core // 2, core % 2) for core in range(8)]
    res = run_bass_kernel_spmd(nc, in_maps, core_ids=list(range(8)))
    out = np.empty((4, T, D), np.float32)
    for core in range(8):
        out[core // 2, :, (core % 2) * 512:(core % 2 + 1) * 512] = res.results[core]["out"]
    return out
```

```python
import math
from contextlib import ExitStack

import numpy as np
import ml_dtypes

import concourse.bass as bass
import concourse.mybir as mybir
from concourse.bass_utils import run_bass_kernel_spmd

F32 = mybir.dt.float32
BF16 = mybir.dt.bfloat16
I32 = mybir.dt.int32
ACT = mybir.ActivationFunctionType
ALU = mybir.AluOpType
AX = mybir.AxisListType


class KB:
    NDMA = 40

    def __init__(self, nc):
        self.nc = nc
        self.eng = {"pe": nc.tensor, "act": nc.scalar, "dve": nc.vector, "pool": nc.gpsimd, "sp": nc.sync}
        self.stack = ExitStack()
        self.sem = {}
        self.cnt = {k: 0 for k in self.eng}
        self.waited = {k: {} for k in self.eng}
        self.last_w = {}
        self.readers = {}
        self.dsem = []
        self.dcnt = []
        self.dnext = 0
        self.nins = 0
        self.phase = 0
        self.excl = set()

    def ctx(self):
        for k in ("pe", "act", "dve", "pool"):
            self.sem[k] = self.stack.enter_context(self.nc.semaphore("s_" + k))
        for i in range(self.NDMA):
            self.dsem.append(self.stack.enter_context(self.nc.semaphore(f"s_dma{i}")))
            self.dcnt.append(0)
        return self.stack

    def sb(self, name, shape, dtype):
        return self.stack.enter_context(self.nc.sbuf_tensor(f"{name}_{self.phase}", shape, dtype))

    def ps(self, name, shape, dtype):
        self.excl.add(name)
        return self.stack.enter_context(self.nc.psum_tensor(f"{name}_{self.phase}", shape, dtype))

    def _handle(self, s):
        return self.dsem[s[1]] if isinstance(s, tuple) else self.sem[s]

    def _wait(self, en, deps):
        e = self.eng[en]
        need = {}
        for s, v in deps:
            if s == "pe" and en == "pe":
                continue
            if v > need.get(s, 0):
                need[s] = v
        for s, v in need.items():
            if self.waited[en].get(s, 0) < v:
                e.wait_ge(self._handle(s), v)
                self.waited[en][s] = v
                self.nins += 1

    def _deps(self, r, w):
        d = []
        for k in r:
            if k in self.last_w:
                d.append(self.last_w[k])
        for k in w:
            if k in self.last_w:
                d.append(self.last_w[k])
            d.extend(self.readers.get(k, ()))
        return d

    def _record(self, r, w, tag):
        for k in r:
            self.readers.setdefault(k, []).append(tag)
        for k in w:
            self.last_w[k] = tag
            self.readers[k] = []

    def _skip(self):
        import os
        lim = os.environ.get("KB_LIMIT")
        if not lim:
            return False
        ph, n = (int(v) for v in lim.split(":"))
        if self.phase != ph:
            return False
        self.pops = getattr(self, "pops", 0) + 1
        return self.pops > n

    def op(self, en, fn, r=(), w=(), inc=True):
        if self._skip():
            return None
        w = list(w) + [k for k in r if k in self.excl and k not in w]
        self._wait(en, self._deps(r, w))
        ins = fn(self.eng[en])
        self.nins += 1
        if inc:
            ins.then_inc(self.sem[en], 1)
            self.cnt[en] += 1
            tag = (en, self.cnt[en])
        else:
            tag = (en, self.cnt[en] + 1)
        self._record(r, w, tag)
        return ins

    def pe(self, fn, r=(), w=(), inc=True):
        return self.op("pe", fn, r, w, inc)

    def act(self, fn, r=(), w=()):
        return self.op("act", fn, r, w)

    def dve(self, fn, r=(), w=()):
        return self.op("dve", fn, r, w)

    def pool(self, fn, r=(), w=()):
        return self.op("pool", fn, r, w)

    def dma(self, out, in_, r=(), w=(), q="sp", **kw):
        if self._skip():
            return None
        deps = self._deps(r, w)
        i = self.dnext
        self.dnext = (self.dnext + 1) % self.NDMA
        if self.dcnt[i] > 0:
            deps.append((("d", i), 16 * self.dcnt[i]))
        self._wait(q, deps)
        ins = self.eng[q].dma_start(out=out, in_=in_, **kw)
        ins.then_inc(self.dsem[i], 16)
        self.nins += 1
        self.dcnt[i] += 1
        self._record(r, w, (("d", i), 16 * self.dcnt[i]))
        return ins

    def finish(self):
        deps = [(k, self.cnt[k]) for k in ("pe", "act", "dve", "pool") if self.cnt[k] > 0]
        deps += [(("d", i), 16 * c) for i, c in enumerate(self.dcnt) if c > 0]
        self._wait("sp", deps)

    def barrier(self):
        deps = [(k, self.cnt[k]) for k in ("pe", "act", "dve", "pool") if self.cnt[k] > 0]
        deps += [(("d", i), 16 * c) for i, c in enumerate(self.dcnt) if c > 0]
        for en in ("pe", "act", "dve", "pool", "sp"):
            self._wait(en, [d for d in deps if not (d[0] == "pe" and en == "pe")] + ([("pe", self.cnt["pe"])] if False else []))
        self.last_w = {}
        self.readers = {}

    def push(self):
        self._saved = getattr(self, "_saved", [])
        self._saved.append(self.stack)
        self.stack = ExitStack()
        self.stack.__enter__()
        self.phase += 1

    def pop(self):
        self.barrier()
        self.stack.__exit__(None, None, None)
        self.stack = self._saved.pop()


T = 8192
D = 1024
NT = T // 128
NG = T // 512
NCOL = 4104
EPS = 1e-6
LAMBDA_INIT = 0.8 - 0.6 * math.exp(0.0)
NFM = 20
TM0 = NFM * 128
TWO_PI = 2.0 * math.pi
MAGIC = 12582912.0
CW1 = 6.28125
CW2 = 0.0019350051879882812
CW3 = TWO_PI - CW1 - CW2


def build_program(dbg=(), upto=9):
    nc = bass.Bass("TRN2", target_bir_lowering=False)

    def din(name, shape, dt=F32):
        return nc.dram_tensor(name, list(shape), dt, kind="ExternalInput").ap()

    def dscr(name, shape, dt=F32):
        return nc.dram_tensor(name, list(shape), dt, kind="ExternalOutput" if name in dbg else "Internal").ap()

    x = din("x", [T, D])
    xres = din("xres", [T, 512])
    cT = din("cT", [128, 8])
    posT = din("posT", [128, NT], I32)
    wada = din("wada", [D, 2048])
    wadag = din("wadag", [D, 512])
    badac = din("badac", [128, 16])
    badag = din("badag", [128, 512])
    normgc = din("normgc", [128, 8])
    win = din("win", [D, NCOL])
    convc = din("convc", [128, 12, 4])
    alog = din("alog", [128, 4])
    dtb = din("dtb", [128, 4])
    gng = din("gng", [128, 1])
    slg = din("slg", [128, 1])
    gqk = din("gqk", [128, 1024])
    lamv = din("lamv", [128, 4, 64])
    wout = din("wout", [D, 512])
    c_identf = din("c_identf", [128, 128])
    c_identb = din("c_identb", [128, 128], BF16)
    c_onesb = din("c_onesb", [128, 128], BF16)
    c_masks = din("c_masks", [128, 7, 128])
    c_negblk = din("c_negblk", [128, 128])
    c_invf = din("c_invf", [128, 8])
    out = nc.dram_tensor("out", [T, 512], F32, kind="ExternalOutput").ap()

    gqT = dscr("gqT", [4, 128, T])
    gkT = dscr("gkT", [4, 128, T])
    gktok = dscr("gktok", [4, T, 128])
    gvtok = dscr("gvtok", [4, T, 128])
    sgT = dscr("sgT", [8, 128, T])
    dqT = dscr("dqT", [4, 128, T], BF16)
    dkT = dscr("dkT", [4, 128, T], BF16)
    dv = dscr("dv", [4, T, 128], BF16)
    mixT = dscr("mixT", [8, 128, T], BF16)

    kb = KB(nc)
    with kb.ctx():
        identf = kb.sb("identf", [128, 128], F32)
        identb = kb.sb("identb", [128, 128], BF16)
        onesb = kb.sb("onesb", [128, 128], BF16)
        masks = kb.sb("masks", [128, 7, 128], F32)
        negblk = kb.sb("negblk", [128, 128], F32)
        onesf, ublk, uincl, negustr, onesblk = (masks[:, i, :] for i in range(5))
        sel = [masks[:, 5, :], masks[:, 6, :]]
        bgres = kb.sb("bgres", [128, NT, 8], F32)
        cosr = kb.sb("cosr", [128, NT, 8], F32)
        sinr = kb.sb("sinr", [128, NT, 8], F32)
        gaterep = kb.sb("gaterep", [128, 512], F32)
        mscale = kb.sb("mscale", [128, 8], F32)
        shiftc = kb.sb("shiftc", [128, 8], F32)
        negA = kb.sb("negA", [128, 4], F32)
        dtbs = kb.sb("dtbs", [128, 4], F32)
        gngs = kb.sb("gngs", [128, 1], F32)
        slgs = kb.sb("slgs", [128, 1], F32)
        neglam = kb.sb("neglam", [128, 1], F32)
        negC = kb.sb("negC", [128, 1], F32)
        gqks = kb.sb("gqks", [128, 1024], F32)
        convs = kb.sb("convs", [128, 12, 4], F32)
        wobf = kb.sb("wobf", [128, 8, 512], BF16)

        epst = kb.sb("epst", [128, 1], F32)
        onet = kb.sb("onet", [128, 1], F32)
        CONSTK = ["const"]
        kb.pool(lambda e: e.memset(epst[:], EPS), w=CONSTK)
        kb.pool(lambda e: e.memset(onet[:], 1.0), w=CONSTK)
        for dst, src in ((identf, c_identf), (identb, c_identb), (onesb, c_onesb), (masks, c_masks), (negblk, c_negblk),
                         (negA, alog), (dtbs, dtb), (gngs, gng), (slgs, slg), (gqks, gqk), (convs, convc)):
            kb.dma(dst[:], src, w=CONSTK)

        kb.push()
        sc = kb.sb("sc", [128, 8], F32)
        screp = kb.sb("screp", [128, 8, 128], F32)
        wch = [kb.sb(f"wch{i}", [128, 8, 512], F32) for i in range(2)]
        modcol = kb.sb("modcol", [128, 16], F32)
        bcol = kb.sb("bcol", [128, 16], F32)
        ngc = kb.sb("ngc", [128, 8], F32)
        bgate = kb.sb("bgate", [128, 512], F32)
        lams = kb.sb("lams", [128, 4, 64], F32)
        lprod = kb.sb("lprod", [128, 2, 64], F32)
        lsum = kb.sb("lsum", [128, 2], F32)
        posi = kb.sb("posi", [128, NT], I32)
        posf = kb.sb("posf", [128, NT], F32)
        invf = kb.sb("invf", [128, 8], F32)
        ang = kb.sb("ang", [128, NT, 8], F32)
        tk = kb.sb("tk", [128, NT, 8], F32)
        rr = kb.sb("rr", [128, NT, 8], F32)
        gmax = kb.sb("gmax", [128, 2], F32)
        psm = kb.ps("psm", [128, 512], F32)
        psg = kb.ps("psg", [128, 512], F32)

        kb.dma(sc[:], cT, w=["sc"])
        kb.dma(bcol[:], badac, w=["bcol"])
        kb.dma(ngc[:], normgc, w=["ngc"])
        kb.dma(bgate[:], badag, w=["bgate"])
        kb.dma(lams[:], lamv, w=["lams"])
        kb.dma(posi[:], posT, w=["posi"])
        kb.dma(invf[:], c_invf, w=["invf"])
        kb.act(lambda e: e.activation(out=sc[:], in_=sc[:], func=ACT.Silu), r=["sc"], w=["sc"])
        kb.dve(lambda e: e.tensor_copy(out=screp[:], in_=sc[:].unsqueeze(2).to_broadcast([128, 8, 128])), r=["sc"], w=["screp"])
        wada_v = wada.rearrange("(kc p) n -> p kc n", p=128)
        for cc in range(4):
            wt = wch[cc % 2]
            kb.dma(wt[:], wada_v[:, :, cc * 512:(cc + 1) * 512], w=[f"wch{cc%2}"])
            for jj in range(4):
                j = cc * 4 + jj
                for kc in range(8):
                    kb.pe(lambda e: e.matmul(psm[:, 2 * j:2 * j + 2], lhsT=wt[:, kc, jj * 128:(jj + 1) * 128], rhs=screp[:, kc, 0:2],
                                             start=(kc == 0), stop=(kc == 7)),
                          r=[f"wch{cc%2}", "screp"], w=["psm"], inc=(kc == 7))
        kb.dve(lambda e: e.tensor_tensor(out=modcol[:], in0=psm[:, 0:32].rearrange("p (j two) -> p j two", two=2)[:, :, 0], in1=bcol[:], op=ALU.add),
               r=["psm", "bcol"], w=["modcol"])
        kb.dve(lambda e: e.tensor_copy(out=shiftc[:], in_=modcol[:, 0:8]), r=["modcol"], w=["shiftc"])
        kb.dve(lambda e: e.scalar_tensor_tensor(out=mscale[:], in0=modcol[:, 8:16], scalar=1.0, in1=ngc[:], op0=ALU.add, op1=ALU.mult),
               r=["modcol", "ngc"], w=["mscale"])
        wt = wch[0]
        kb.dma(wt[:], wadag.rearrange("(kc p) n -> p kc n", p=128), w=["wch0"])
        for kc in range(8):
            kb.pe(lambda e: e.matmul(psg[:], lhsT=screp[:, kc, :], rhs=wt[:, kc, :], start=(kc == 0), stop=(kc == 7)),
                  r=["wch0", "screp"], w=["psg"], inc=(kc == 7))
        kb.dve(lambda e: e.tensor_tensor(out=gaterep[:], in0=psg[:], in1=bgate[:], op=ALU.add), r=["psg", "bgate"], w=["gaterep"])
        wt = wch[1]
        kb.dma(wt[:], wout.rearrange("(kc p) n -> p kc n", p=128), w=["wch1"])
        kb.dve(lambda e: e.tensor_copy(out=wobf[:], in_=wt[:]), r=["wch1"], w=["wobf"])
        kb.dve(lambda e: e.tensor_tensor(out=lprod[:, 0, :], in0=lams[:, 0, :], in1=lams[:, 1, :], op=ALU.mult), r=["lams"], w=["lprod"])
        kb.dve(lambda e: e.tensor_tensor(out=lprod[:, 1, :], in0=lams[:, 2, :], in1=lams[:, 3, :], op=ALU.mult), r=["lams", "lprod"], w=["lprod"])
        kb.dve(lambda e: e.tensor_reduce(out=lsum[:], in_=lprod[:], axis=AX.X, op=ALU.add), r=["lprod"], w=["lsum"])
        kb.act(lambda e: e.activation(out=lsum[:], in_=lsum[:], func=ACT.Exp), r=["lsum"], w=["lsum"])
        kb.dve(lambda e: e.scalar_tensor_tensor(out=neglam[:], in0=lsum[:, 1:2], scalar=-LAMBDA_INIT, in1=lsum[:, 0:1], op0=ALU.add, op1=ALU.subtract),
               r=["lsum"], w=["neglam"])
        kb.act(lambda e: e.activation(out=negA[:], in_=negA[:], func=ACT.Exp), r=CONSTK, w=["negA"])
        kb.dve(lambda e: e.tensor_scalar(out=negA[:], in0=negA[:], scalar1=-1.0, scalar2=None, op0=ALU.mult), r=["negA"], w=["negA"])
        kb.dve(lambda e: e.tensor_scalar(out=slgs[:], in0=slgs[:], scalar1=1.0 - LAMBDA_INIT, scalar2=None, op0=ALU.mult), r=CONSTK, w=["slgs"])
        kb.dve(lambda e: e.tensor_reduce(out=gmax[:], in_=gqks[:].rearrange("p (a n) -> p a n", a=2), axis=AX.X, op=ALU.max, apply_absolute_value=True),
               r=CONSTK, w=["gmax"])
        kb.dve(lambda e: e.scalar_tensor_tensor(out=negC[:], in0=gmax[:, 0:1], scalar=-8.0, in1=gmax[:, 1:2], op0=ALU.mult, op1=ALU.mult),
               r=["gmax"], w=["negC"])
        kb.dve(lambda e: e.tensor_copy(out=posf[:], in_=posi[:]), r=["posi"], w=["posf"])
        kb.dve(lambda e: e.tensor_tensor(out=ang[:], in0=posf[:].unsqueeze(2).to_broadcast([128, NT, 8]),
                                         in1=invf[:].unsqueeze(1).to_broadcast([128, NT, 8]), op=ALU.mult), r=["posf", "invf"], w=["ang"])
        for (dst, off, name) in ((sinr, 0.0, "sinr"), (cosr, 0.25, "cosr")):
            kb.dve(lambda e: e.tensor_scalar(out=tk[:], in0=ang[:], scalar1=1.0 / TWO_PI, scalar2=off, op0=ALU.mult, op1=ALU.add), r=["ang"], w=["tk"])
            kb.dve(lambda e: e.tensor_scalar(out=tk[:], in0=tk[:], scalar1=MAGIC, scalar2=None, op0=ALU.add), r=["tk"], w=["tk"])
            kb.dve(lambda e: e.tensor_scalar(out=tk[:], in0=tk[:], scalar1=-MAGIC, scalar2=None, op0=ALU.add), r=["tk"], w=["tk"])
            kb.dve(lambda e: e.scalar_tensor_tensor(out=rr[:], in0=tk[:], scalar=-CW1, in1=ang[:], op0=ALU.mult, op1=ALU.add), r=["ang", "tk"], w=["rr"])
            kb.dve(lambda e: e.scalar_tensor_tensor(out=rr[:], in0=tk[:], scalar=-CW2, in1=rr[:], op0=ALU.mult, op1=ALU.add), r=["rr", "tk"], w=["rr"])
            kb.dve(lambda e: e.scalar_tensor_tensor(out=rr[:], in0=tk[:], scalar=-CW3, in1=rr[:], op0=ALU.mult, op1=ALU.add), r=["rr", "tk"], w=["rr"])
            kb.dve(lambda e: e.tensor_scalar(out=rr[:], in0=rr[:], scalar1=off * TWO_PI, scalar2=math.pi, op0=ALU.add, op1=ALU.min), r=["rr"], w=["rr"])
            kb.dve(lambda e: e.tensor_scalar(out=rr[:], in0=rr[:], scalar1=-math.pi, scalar2=None, op0=ALU.max), r=["rr"], w=["rr"])
            kb.act(lambda e: e.activation(out=dst[:], in_=rr[:], func=ACT.Sin), r=["rr"], w=[name])
        kb.pop()
        kb.push()
        wbf = kb.sb("wbf", [128, 8, NCOL], BF16)
        wcb = [kb.sb(f"wcb{i}", [128, 8, 128], F32) for i in range(2)]
        xt = [kb.sb(f"xt{i}", [128, 1024], F32) for i in range(2)]
        xs = [kb.sb(f"xs{i}", [128, 1024], F32) for i in range(2)]
        junk = kb.sb("junk", [128, 1024], BF16)
        st = [kb.sb(f"st{i}", [128, 4], F32) for i in range(2)]
        hT = [kb.sb(f"hT{i}", [128, 8, 512], BF16) for i in range(2)]
        cbuf = [kb.sb(f"cbuf{i}", [128, 515], F32) for i in range(12)]
        acc = [kb.sb(f"acc{i}", [128, 512], F32) for i in range(2)]
        yb = [kb.sb(f"yb{i}", [128, 512], F32) for i in range(2)]
        sqb = kb.sb("sqb", [128, 512], F32)
        sdb = kb.sb("sdb", [128, 512], F32)
        ynb = [kb.sb(f"ynb{i}", [128, 512], F32) for i in range(2)]
        trb = [kb.sb(f"trb{i}", [128, 512], F32) for i in range(2)]
        sgt = [kb.sb(f"sgt{i}", [128, 512], F32) for i in range(2)]
        qk = kb.sb("qk", [128, 1024], F32)
        ssq = kb.sb("ssq", [128, 16], F32)
        xn = kb.sb("xn", [128, 1024], F32)
        qkb = kb.sb("qkb", [128, 1024], BF16)
        rt = [kb.sb(f"rt{i}", [128, 16, 8], F32) for i in range(4)]
        vb = [kb.sb(f"vb{i}", [128, 512], BF16) for i in range(2)]
        qkTg = [kb.sb("qkTg0", [128, 8, 512], BF16)]
        tmp4 = kb.sb("tmp4", [128, 4], F32)
        psT = kb.ps("psT", [128, 512], F32)
        psF = kb.ps("psF", [128, 512], F32)
        psQ = kb.ps("psQ", [128, 512], F32)
        psK = kb.ps("psK", [128, 512], F32)
        psV = kb.ps("psV", [128, 512], F32)
        psN = kb.ps("psN", [128, 512], F32)
        psXf = kb.ps("psXf", [128, 512], F32)
        psXb = kb.ps("psXb", [128, 1024], BF16)

        win_v = win.rearrange("(kc p) n -> p kc n", p=128)
        nch = (NCOL + 127) // 128
        for ci in range(nch):
            c0, c1 = ci * 128, min(NCOL, ci * 128 + 128)
            wt = wcb[ci % 2]
            kb.dma(wt[:, :, 0:c1 - c0], win_v[:, :, c0:c1], w=[f"wcb{ci%2}"])
            if ci % 2 == 0:
                kb.dve(lambda e: e.tensor_copy(out=wbf[:, :, c0:c1], in_=wt[:, :, 0:c1 - c0]), r=[f"wcb{ci%2}"], w=["wbf"])
            else:
                kb.pool(lambda e: e.tensor_copy(out=wbf[:, :, c0:c1], in_=wt[:, :, 0:c1 - c0]), r=[f"wcb{ci%2}"], w=["wbf"])
        for c in range(12):
            kb.pool(lambda e: e.memset(cbuf[c][:, 0:3], 0.0), w=[f"cbuf{c}"])

        dqT_v = dqT.rearrange("h p t -> p h t")
        dkT_v = dkT.rearrange("h p t -> p h t")
        dv_v = dv.rearrange("h t e -> t h e")
        nfm = 0
        import os as _os
        if _os.environ.get('SKIP_P1'):
            kb.pool(lambda e: e.memset(bgres[:], -0.05), w=["bgres"])
            for h in range(4):
                kb.dma(gqT[h, :, 0:512], x[0:128, 0:512], w=["gqkT"])
                kb.dma(gkT[h, :, 0:512], x[0:128, 0:512], w=["gqkT"])
                kb.dma(gktok[h, 0:512, :].rearrange("(a p) e -> p a e", p=128), x[0:128, 0:512].rearrange("p (a e) -> p a e", a=4), w=["gtok"])
                kb.dma(gvtok[h, 0:512, :].rearrange("(a p) e -> p a e", p=128), x[0:128, 0:512].rearrange("p (a e) -> p a e", a=4), w=["gtok"])
                kb.dma(sgT[h, :, 0:512], x[0:128, 0:512], w=["sgT"])
        for g in range(0 if _os.environ.get('SKIP_P1') else NG):
            hTg = hT[g % 2]
            hk = f"hT{g%2}"
            for a in range(4):
                tt = 4 * g + a
                i2 = tt % 2
                kb.dma(xt[i2][:], x[tt * 128:(tt + 1) * 128, :], w=[f"xt{i2}"])
                s = st[i2]
                kb.act(lambda e: e.activation(out=junk[:], in_=xt[i2][:], func=ACT.Square, accum_out=s[:, 0:1]), r=[f"xt{i2}"], w=["junk", f"st{i2}"])
                kb.dve(lambda e: e.tensor_scalar(out=s[:, 1:2], in0=s[:, 0:1], scalar1=1.0 / D, scalar2=EPS, op0=ALU.mult, op1=ALU.add), r=[f"st{i2}"], w=[f"st{i2}"])
                kb.act(lambda e: e.activation(out=s[:, 2:3], in_=s[:, 1:2], func=ACT.Sqrt), r=[f"st{i2}"], w=[f"st{i2}"])
                kb.dve(lambda e: e.reciprocal(out=s[:, 3:4], in_=s[:, 2:3]), r=[f"st{i2}"], w=[f"st{i2}"])
                kb.act(lambda e: e.activation(out=xs[i2][:], in_=xt[i2][:], func=ACT.Copy, scale=s[:, 3:4]), r=[f"xt{i2}", f"st{i2}"], w=[f"xs{i2}"])
                for half in range(2):
                    for q4 in range(4):
                        kc = half * 4 + q4
                        kb.pe(lambda e: e.transpose(out=psT[:, q4 * 128:(q4 + 1) * 128], in_=xs[i2][:, kc * 128:(kc + 1) * 128], identity=identf[:]),
                              r=[f"xs{i2}", "const"], w=["psT"], inc=(q4 == 3))
                    for q4 in range(4):
                        kc = half * 4 + q4
                        kb.dve(lambda e: e.tensor_scalar(out=hTg[:, kc, a * 128:(a + 1) * 128], in0=psT[:, q4 * 128:(q4 + 1) * 128],
                                                         scalar1=mscale[:, kc:kc + 1], scalar2=shiftc[:, kc:kc + 1], op0=ALU.mult, op1=ALU.add),
                               r=["psT", "mscale", "shiftc"], w=[hk])
            for c in range(NFM):
                for kc in range(8):
                    kb.pe(lambda e: e.matmul(psF[:], lhsT=wbf[:, kc, c * 128:(c + 1) * 128], rhs=hTg[:, kc, :], start=(kc == 0), stop=(kc == 7)),
                          r=["wbf", hk], w=["psF"], inc=(kc == 7))
                tsl = slice(g * 512, (g + 1) * 512)
                if c >= 12:
                    sg = sgt[c % 2]
                    kb.act(lambda e: e.activation(out=sg[:], in_=psF[:], func=ACT.Silu), r=["psF"], w=[f"sgt{c%2}"])
                    kb.dma(sgT[c - 12, :, tsl], sg[:], r=[f"sgt{c%2}"], w=["sgT"], q="pool")
                    continue
                h, kind = c // 3, c % 3
                cb = cbuf[c]
                ck = f"cbuf{c}"
                kb.act(lambda e: e.activation(out=cb[:, 3:515], in_=psF[:], func=ACT.Copy), r=["psF"], w=[ck])
                ac = acc[nfm % 2]
                ak = f"acc{nfm%2}"
                y = yb[nfm % 2]
                yk = f"yb{nfm%2}"
                kb.dve(lambda e: e.tensor_scalar(out=ac[:], in0=cb[:, 0:512], scalar1=convs[:, c, 0:1], scalar2=None, op0=ALU.mult), r=[ck, "const"], w=[ak])
                for j in range(1, 4):
                    kb.dve(lambda e: e.scalar_tensor_tensor(out=ac[:], in0=cb[:, j:j + 512], scalar=convs[:, c, j:j + 1], in1=ac[:], op0=ALU.mult, op1=ALU.add),
                           r=[ck, ak, "const"], w=[ak])
                kb.pool(lambda e: e.tensor_copy(out=cb[:, 0:3], in_=cb[:, 512:515]), r=[ck], w=[ck])
                kb.act(lambda e: e.activation(out=y[:], in_=ac[:], func=ACT.Silu), r=[ak], w=[yk])
                src, srck = y, yk
                if kind < 2:
                    kb.pool(lambda e: e.tensor_tensor(out=sqb[:], in0=y[:], in1=y[:], op=ALU.mult), r=[yk], w=["sqb"])
                    kb.pe(lambda e: e.matmul(psN[:], lhsT=onesf, rhs=sqb[:], start=True, stop=True), r=["sqb", "const"], w=["psN"])
                    kb.act(lambda e: e.activation(out=sdb[:], in_=psN[:], func=ACT.Sqrt, bias=epst[:]), r=["psN", "const"], w=["sdb"])
                    kb.dve(lambda e: e.reciprocal(out=sdb[:], in_=sdb[:]), r=["sdb"], w=["sdb"])
                    yn = ynb[nfm % 2]
                    ynk = f"ynb{nfm%2}"
                    cq = (128.0 ** -0.5) if kind == 0 else 1.0
                    kb.dve(lambda e: e.scalar_tensor_tensor(out=yn[:], in0=y[:], scalar=cq, in1=sdb[:], op0=ALU.mult, op1=ALU.mult), r=[yk, "sdb"], w=[ynk])
                    kb.dma((gqT if kind == 0 else gkT)[h, :, tsl], yn[:], r=[ynk], w=["gqkT"], q="pool")
                    src, srck = yn, ynk
                if kind >= 1:
                    for a in range(4):
                        kb.pe(lambda e: e.transpose(out=psXf[:, a * 128:(a + 1) * 128], in_=src[:, a * 128:(a + 1) * 128], identity=identf[:]),
                              r=[srck, "const"], w=["psXf"], inc=(a == 3))
                    tr = trb[nfm % 2]
                    trk = f"trb{nfm%2}"
                    kb.dve(lambda e: e.tensor_copy(out=tr[:], in_=psXf[:]), r=["psXf"], w=[trk])
                    dst = (gktok if kind == 1 else gvtok)[h, g * 512:(g + 1) * 512, :].rearrange("(a p) e -> p a e", p=128)
                    kb.dma(dst, tr[:].rearrange("p (a e) -> p a e", a=4), r=[trk], w=["gtok"], q="pool")
                nfm += 1
            qTg = qkTg[0]
            qTk = "qkTg0"
            for a in range(4):
                tt = 4 * g + a
                lhs = lambda kc: hTg[:, kc, a * 128:(a + 1) * 128]
                for (pst, pk, c0, n) in ((psQ, "psQ", TM0, 512), (psK, "psK", TM0 + 512, 512), (psV, "psV", TM0 + 1024, 512), (psXf, "psXf", TM0 + 1536, 8)):
                    for kc in range(8):
                        kb.pe(lambda e: e.matmul(pst[:, 0:n], lhsT=lhs(kc), rhs=wbf[:, kc, c0:c0 + n], start=(kc == 0), stop=(kc == 7)),
                              r=["wbf", hk], w=[pk], inc=(kc == 7))
                kb.act(lambda e: e.activation(out=bgres[:, tt, 0:4], in_=psXf[:, 0:4], func=ACT.Sigmoid), r=["psXf"], w=["bgres"])
                for h in range(4):
                    kb.act(lambda e: e.activation(out=tmp4[:, h:h + 1], in_=psXf[:, 4 + h:5 + h], func=ACT.Exp, bias=dtbs[:, h:h + 1]), r=["psXf", "const"], w=["tmp4"])
                kb.act(lambda e: e.activation(out=tmp4[:], in_=tmp4[:], func=ACT.Ln, bias=onet[:]), r=["tmp4", "const"], w=["tmp4"])
                kb.dve(lambda e: e.tensor_tensor(out=bgres[:, tt, 4:8], in0=tmp4[:], in1=negA[:], op=ALU.mult), r=["tmp4", "negA"], w=["bgres"])
                v_ = vb[a % 2]
                vk = f"vb{a%2}"
                kb.act(lambda e: e.activation(out=v_[:], in_=psV[:], func=ACT.Copy), r=["psV"], w=[vk])
                kb.dma(dv_v[tt * 128:(tt + 1) * 128, :, :], v_[:].rearrange("p (h e) -> p h e", h=4), r=[vk], w=["dv"], q="pool")
                kb.dve(lambda e: e.tensor_copy(out=qk[:, 0:512], in_=psQ[:]), r=["psQ"], w=["qk"])
                kb.act(lambda e: e.activation(out=qk[:, 512:1024], in_=psK[:], func=ACT.Copy), r=["psK"], w=["qk"])
                kb.pool(lambda e: e.tensor_tensor(out=xn[:], in0=qk[:], in1=qk[:], op=ALU.mult), r=["qk"], w=["xn"])
                kb.dve(lambda e: e.tensor_reduce(out=ssq[:], in_=xn[:].rearrange("p (s d) -> p s d", d=64), axis=AX.X, op=ALU.add), r=["xn"], w=["ssq"])
                kb.dve(lambda e: e.tensor_scalar(out=ssq[:], in0=ssq[:], scalar1=1.0 / 64, scalar2=EPS, op0=ALU.mult, op1=ALU.add), r=["ssq"], w=["ssq"])
                kb.act(lambda e: e.activation(out=ssq[:], in_=ssq[:], func=ACT.Sqrt), r=["ssq"], w=["ssq"])
                kb.dve(lambda e: e.reciprocal(out=ssq[:], in_=ssq[:]), r=["ssq"], w=["ssq"])
                qk3 = qk[:].rearrange("p (s d) -> p s d", d=64)
                xn3 = xn[:].rearrange("p (s d) -> p s d", d=64)
                kb.dve(lambda e: e.tensor_tensor(out=xn3, in0=qk3, in1=ssq[:].unsqueeze(2).to_broadcast([128, 16, 64]), op=ALU.mult), r=["qk", "ssq"], w=["xn"])
                kb.pool(lambda e: e.tensor_tensor(out=xn[:], in0=xn[:], in1=gqks[:], op=ALU.mult), r=["xn", "const"], w=["xn"])
                kb.act(lambda e: e.activation(out=qkb[:], in_=xn[:], func=ACT.Copy), r=["xn"], w=["qkb"])
                qkb3 = qkb[:].rearrange("p (s d) -> p s d", d=64)
                cb_ = cosr[:, tt, :].unsqueeze(1).to_broadcast([128, 16, 8])
                sb_ = sinr[:, tt, :].unsqueeze(1).to_broadcast([128, 16, 8])
                x1, x2 = xn3[:, :, 0:8], xn3[:, :, 8:16]
                kb.dve(lambda e: e.tensor_tensor(out=rt[0][:], in0=x1, in1=cb_, op=ALU.mult), r=["xn", "cosr"], w=["rt0"])
                kb.dve(lambda e: e.tensor_tensor(out=rt[1][:], in0=x2, in1=sb_, op=ALU.mult), r=["xn", "sinr"], w=["rt1"])
                kb.dve(lambda e: e.tensor_tensor(out=rt[2][:], in0=x2, in1=cb_, op=ALU.mult), r=["xn", "cosr"], w=["rt2"])
                kb.dve(lambda e: e.tensor_tensor(out=rt[3][:], in0=x1, in1=sb_, op=ALU.mult), r=["xn", "sinr"], w=["rt3"])
                kb.dve(lambda e: e.tensor_tensor(out=qkb3[:, :, 0:8], in0=rt[0][:], in1=rt[1][:], op=ALU.subtract), r=["rt0", "rt1", "qkb"], w=["qkb"])
                kb.dve(lambda e: e.tensor_tensor(out=qkb3[:, :, 8:16], in0=rt[2][:], in1=rt[3][:], op=ALU.add), r=["rt2", "rt3", "qkb"], w=["qkb"])
                for bi in range(8):
                    kb.pe(lambda e: e.transpose(out=psXb[:, bi * 128:(bi + 1) * 128], in_=qkb[:, bi * 128:(bi + 1) * 128], identity=identb[:]),
                          r=["qkb", "const"], w=["psXb"], inc=(bi == 7))
                kb.dve(lambda e: e.tensor_copy(out=qTg[:, :, a * 128:(a + 1) * 128], in_=psXb[:].rearrange("p (b t) -> p b t", b=8)), r=["psXb"], w=[qTk])
            kb.dma(dqT_v[:, :, g * 512:(g + 1) * 512], qTg[:, 0:4, :], r=[qTk], w=["dqT"], q="pool")
            kb.dma(dkT_v[:, :, g * 512:(g + 1) * 512], qTg[:, 4:8, :], r=[qTk], w=["dkT"], q="pool")
        kb.pop()
        kb.push()
        Q = lambda t, i: t[:, i * 128:(i + 1) * 128]
        names = ["kTp", "qTp", "ktk", "vtk", "Ug", "E", "Ei", "Es", "aT", "Ya", "YaT", "Yb", "YbT", "Wa", "Wb", "dg", "qdT", "kdec", "rhs2", "vn"]
        gb = {}
        for h in range(4):
            for par in range(2):
                for n in names:
                    gb[(n, h, par)] = kb.sb(f"{n}{h}{par}", [128, 128], F32)
                gb[("sm", h, par)] = kb.sb(f"sm{h}{par}", [128, 8], F32)
                gb[("smc", h, par)] = kb.sb(f"smc{h}{par}", [128, 8], F32)
                gb[("g2", h, par)] = kb.sb(f"g2{h}{par}", [128, 2], F32)
        Sst = [[kb.sb(f"S{h}{i}", [128, 128], F32) for i in range(2)] for h in range(4)]
        obuf = [kb.sb(f"obuf{h}", [128, 512], F32) for h in range(4)]
        gsq = kb.sb("gsq", [128, 512], F32)
        gsd = kb.sb("gsd", [128, 512], F32)
        gon = kb.sb("gon", [128, 512], F32)
        gsg = [kb.sb(f"gsg{i}", [128, 512], F32) for i in range(2)]
        gmx = [kb.sb(f"gmx{i}", [128, 512], BF16) for i in range(2)]
        pA = kb.ps("pA", [128, 512], F32)
        pB = kb.ps("pB", [128, 512], F32)
        pC = kb.ps("pC", [128, 512], F32)
        pKS = kb.ps("pKS", [128, 512], F32)
        pVN = kb.ps("pVN", [128, 512], F32)
        pSS = kb.ps("pSS", [128, 512], F32)
        pO = kb.ps("pO", [128, 512], F32)
        pP = kb.ps("pP", [128, 512], F32)
        for h in range(4):
            kb.pool(lambda e: e.memset(Sst[h][0][:], 0.0), w=[f"S{h}0"])
        scur = [0, 0, 0, 0]
        H4 = range(4)

        def pre_stages(p):
            par = p % 2
            tok = slice(p * 128, (p + 1) * 128)
            B = lambda n, h: gb[(n, h, par)]
            K_ = lambda n, h: f"{n}{h}{par}"
            gcol = lambda h: bgres[:, p, 4 + h:5 + h]
            beta = lambda h: bgres[:, p, h:h + 1]
            st = []

            def s_load():
                for h in H4:
                    kb.dma(B("kTp", h)[:], gkT[h, :, tok], r=["gqkT"], w=[K_("kTp", h)])
                    kb.dma(B("qTp", h)[:], gqT[h, :, tok], r=["gqkT"], w=[K_("qTp", h)])
                    kb.dma(B("ktk", h)[:], gktok[h, tok, :], r=["gtok"], w=[K_("ktk", h)])
                    kb.dma(B("vtk", h)[:], gvtok[h, tok, :], r=["gtok"], w=[K_("vtk", h)])
            st.append(s_load)

            def s_ug():
                for h in H4:
                    kb.dve(lambda e: e.tensor_scalar(out=B("Ug", h)[:], in0=ublk, scalar1=gcol(h), scalar2=None, op0=ALU.mult), r=["const", "bgres"], w=[K_("Ug", h)])
                    kb.dve(lambda e: e.tensor_copy(out=B("g2", h)[:], in_=gcol(h).to_broadcast([128, 2])), r=["bgres"], w=[K_("g2", h)])
            st.append(s_ug)

            def s_g():
                for h in H4:
                    kb.pe(lambda e: e.matmul(Q(pA, h), lhsT=B("Ug", h)[:], rhs=onesblk, start=True, stop=False), r=[K_("Ug", h), "const"], w=["pA"], inc=False)
                    kb.pe(lambda e: e.matmul(Q(pA, h), lhsT=negblk[:], rhs=B("Ug", h)[:], start=False, stop=True), r=[K_("Ug", h), "const"], w=["pA"])
                for h in H4:
                    o8 = 8 * h
                    kb.pe(lambda e: e.matmul(pB[:, o8:o8 + 2], lhsT=B("Ug", h)[:], rhs=onesf[:, 0:2], start=True, stop=True), r=[K_("Ug", h), "const"], w=["pB"], inc=False)
                    kb.pe(lambda e: e.matmul(pB[:, o8 + 2:o8 + 4], lhsT=onesblk, rhs=B("g2", h)[:], start=True, stop=True), r=[K_("g2", h), "const"], w=["pB"], inc=False)
                    kb.pe(lambda e: e.matmul(pB[:, o8 + 4:o8 + 6], lhsT=sel[0], rhs=B("g2", h)[:], start=True, stop=True), r=[K_("g2", h), "const"], w=["pB"], inc=False)
                    kb.pe(lambda e: e.matmul(pB[:, o8 + 6:o8 + 8], lhsT=sel[1], rhs=B("g2", h)[:], start=True, stop=True), r=[K_("g2", h), "const"], w=["pB"])
            st.append(s_g)

            def s_e1():
                for h in H4:
                    kb.act(lambda e: e.activation(out=B("smc", h)[:], in_=pB[:, 8 * h:8 * h + 8], func=ACT.Copy), r=["pB"], w=[K_("smc", h)])
                    kb.dve(lambda e: e.tensor_scalar(out=B("E", h)[:], in0=Q(pA, h), scalar1=-1.0, scalar2=0.0, op0=ALU.mult, op1=ALU.min), r=["pA"], w=[K_("E", h)])
            st.append(s_e1)

            def s_e2():
                for h in H4:
                    sm, smc = B("sm", h), B("smc", h)
                    kb.act(lambda e: e.activation(out=sm[:, 0:1], in_=smc[:, 0:1], func=ACT.Exp), r=[K_("smc", h)], w=[K_("sm", h)])
                    kb.act(lambda e: e.activation(out=B("E", h)[:], in_=B("E", h)[:], func=ACT.Exp), r=[K_("E", h)], w=[K_("E", h)])
                    kb.dve(lambda e: e.tensor_tensor(out=sm[:, 2:3], in0=smc[:, 2:3], in1=smc[:, 0:1], op=ALU.subtract), r=[K_("smc", h), K_("sm", h)], w=[K_("sm", h)])
            st.append(s_e2)

            def s_kk():
                for h in H4:
                    kb.pe(lambda e: e.matmul(Q(pC, h), lhsT=B("kTp", h)[:], rhs=B("kTp", h)[:], start=True, stop=True), r=[K_("kTp", h)], w=["pC"])
                for h in H4:
                    kb.pe(lambda e: e.matmul(Q(pA, h), lhsT=B("kTp", h)[:], rhs=B("qTp", h)[:], start=True, stop=True), r=[K_("kTp", h), K_("qTp", h)], w=["pA"])
            st.append(s_kk)

            def s_e3():
                for h in H4:
                    sm, smc = B("sm", h), B("smc", h)
                    kb.dve(lambda e: e.tensor_scalar(out=sm[:, 1:2], in0=sm[:, 0:1], scalar1=-1.0, scalar2=None, op0=ALU.mult), r=[K_("sm", h)], w=[K_("sm", h)])
                    kb.act(lambda e: e.activation(out=sm[:, 2:3], in_=sm[:, 2:3], func=ACT.Exp), r=[K_("sm", h)], w=[K_("sm", h)])
                    kb.act(lambda e: e.activation(out=sm[:, 4:8], in_=smc[:, 4:8], func=ACT.Exp), r=[K_("smc", h), K_("sm", h)], w=[K_("sm", h)])
                    kb.pool(lambda e: e.tensor_tensor(out=B("Ei", h)[:], in0=B("E", h)[:], in1=uincl, op=ALU.mult), r=[K_("E", h), "const"], w=[K_("Ei", h)])
                    kb.pool(lambda e: e.tensor_tensor(out=B("Es", h)[:], in0=B("E", h)[:], in1=negustr, op=ALU.mult), r=[K_("E", h), "const"], w=[K_("Es", h)])
            st.append(s_e3)

            def s_ya():
                for h in H4:
                    kb.dve(lambda e: e.scalar_tensor_tensor(out=B("Ya", h)[:], in0=Q(pC, h), scalar=beta(h), in1=B("Es", h)[:], op0=ALU.mult, op1=ALU.mult),
                           r=["pC", K_("Es", h), "bgres"], w=[K_("Ya", h)])
                for h in H4:
                    kb.dve(lambda e: e.tensor_tensor(out=B("aT", h)[:], in0=Q(pA, h), in1=B("Ei", h)[:], op=ALU.mult), r=["pA", K_("Ei", h)], w=[K_("aT", h)])
            st.append(s_ya)

            def s_tr():
                for h in H4:
                    kb.pe(lambda e: e.transpose(out=Q(pB, h), in_=B("Ya", h)[:], identity=identf[:]), r=[K_("Ya", h), "const"], w=["pB"])
            st.append(s_tr)

            def s_tr2():
                for h in H4:
                    kb.act(lambda e: e.activation(out=B("YaT", h)[:], in_=Q(pB, h), func=ACT.Copy), r=["pB"], w=[K_("YaT", h)])
                    kb.dve(lambda e: e.tensor_tensor(out=B("Wa", h)[:], in0=B("Ya", h)[:], in1=identf[:], op=ALU.add), r=[K_("Ya", h), "const"], w=[K_("Wa", h)])
            st.append(s_tr2)

            cur = ["Ya", "YaT", "Wa"]
            for k in range(1, 6):
                Y, YT, W = cur
                Yn, YTn, Wn = ("Yb", "YbT", "Wb") if Y == "Ya" else ("Ya", "YaT", "Wa")

                def s_sq(k=k, Y=Y, YT=YT):
                    if k < 5:
                        for h in H4:
                            kb.pe(lambda e: e.matmul(Q(pA, h), lhsT=B(YT, h)[:], rhs=B(Y, h)[:], start=True, stop=True), r=[K_(Y, h), K_(YT, h)], w=["pA"])
                    for h in H4:
                        kb.pe(lambda e: e.matmul(Q(pC, h), lhsT=B(Y, h)[:], rhs=B(YT, h)[:], start=True, stop=True), r=[K_(Y, h), K_(YT, h)], w=["pC"])
                st.append(s_sq)

                def s_ev(k=k, Yn=Yn, YTn=YTn):
                    for h in H4:
                        if k < 5:
                            kb.act(lambda e: e.activation(out=B(Yn, h)[:], in_=Q(pA, h), func=ACT.Copy), r=["pA"], w=[K_(Yn, h)])
                        kb.dve(lambda e: e.tensor_copy(out=B(YTn, h)[:], in_=Q(pC, h)), r=["pC"], w=[K_(YTn, h)])
                st.append(s_ev)

                def s_w(YTn=YTn, W=W):
                    for h in H4:
                        kb.pe(lambda e: e.matmul(Q(pB, h), lhsT=B(YTn, h)[:], rhs=B(W, h)[:], start=True, stop=True), r=[K_(YTn, h), K_(W, h)], w=["pB"])
                st.append(s_w)

                def s_w2(W=W, Wn=Wn):
                    for h in H4:
                        kb.dve(lambda e: e.tensor_tensor(out=B(Wn, h)[:], in0=Q(pB, h), in1=B(W, h)[:], op=ALU.add), r=["pB", K_(W, h)], w=[K_(Wn, h)])
                st.append(s_w2)
                cur = [Yn, YTn, Wn]
            for h in H4:
                gb[("Wfin", h, par)] = cur[2]

            def s_dg():
                for h in H4:
                    sm = B("sm", h)
                    kb.dve(lambda e: e.tensor_scalar(out=B("dg", h)[:], in0=identf[:], scalar1=sm[:, 0:1], scalar2=None, op0=ALU.mult), r=["const", K_("sm", h)], w=[K_("dg", h)])
                    kb.act(lambda e: e.activation(out=B("kdec", h)[:], in_=B("ktk", h)[:], func=ACT.Copy, scale=sm[:, 2:3]), r=[K_("ktk", h), K_("sm", h)], w=[K_("kdec", h)])
            st.append(s_dg)

            def s_eg():
                for h in H4:
                    kb.pe(lambda e: e.matmul(Q(pA, h), lhsT=onesf, rhs=B("dg", h)[:], start=True, stop=True), r=[K_("dg", h), "const"], w=["pA"])
            st.append(s_eg)

            def s_qd():
                for h in H4:
                    kb.dve(lambda e: e.tensor_tensor(out=B("qdT", h)[:], in0=Q(pA, h), in1=B("qTp", h)[:], op=ALU.mult), r=["pA", K_("qTp", h)], w=[K_("qdT", h)])
            st.append(s_qd)
            return st

        def scan_stages(p):
            par = p % 2
            B = lambda n, h: gb[(n, h, par)]
            K_ = lambda n, h: f"{n}{h}{par}"
            st = []
            for c in range(2):
                hs = slice(64 * c, 64 * c + 64)

                def s1(hs=hs):
                    for h in H4:
                        S, Sk = Sst[h][scur[h]], f"S{h}{scur[h]}"
                        kb.pe(lambda e: e.matmul(Q(pKS, h), lhsT=B("kTp", h)[:], rhs=S[:], start=True, stop=True), r=[K_("kTp", h), Sk], w=["pKS"])
                st.append(s1)

                def s2(hs=hs):
                    for h in H4:
                        kb.dve(lambda e: e.scalar_tensor_tensor(out=B("rhs2", h)[hs, :], in0=Q(pKS, h)[hs, :], scalar=B("sm", h)[hs, 1:2], in1=B("vtk", h)[hs, :],
                                                                op0=ALU.mult, op1=ALU.add), r=["pKS", K_("sm", h), K_("vtk", h)], w=[K_("rhs2", h)])
                st.append(s2)

                def s3(hs=hs):
                    for h in H4:
                        Wf = gb[("Wfin", h, par)]
                        kb.pe(lambda e: e.matmul(Q(pVN, h), lhsT=B(Wf, h)[hs, :], rhs=B("rhs2", h)[hs, :], start=True, stop=True), r=[K_(Wf, h), K_("rhs2", h)], w=["pVN"])
                st.append(s3)

                def s4(hs=hs):
                    for h in H4:
                        kb.dve(lambda e: e.tensor_scalar(out=B("vn", h)[hs, :], in0=Q(pVN, h)[hs, :], scalar1=bgres[hs, p, h:h + 1], scalar2=None, op0=ALU.mult),
                               r=["pVN", "bgres"], w=[K_("vn", h)])
                st.append(s4)

                def s5(hs=hs, c=c):
                    for h in H4:
                        S, Sk = Sst[h][scur[h]], f"S{h}{scur[h]}"
                        oc = slice(h * 128 + 64 * c, h * 128 + 64 * c + 64)
                        kb.pe(lambda e: e.matmul(pO[:, oc], lhsT=S[:], rhs=B("qdT", h)[:, hs], start=True, stop=False), r=[Sk, K_("qdT", h)], w=["pO"], inc=False)
                        kb.pe(lambda e: e.matmul(pO[:, oc], lhsT=B("vn", h)[hs, :], rhs=B("aT", h)[hs, hs], start=False, stop=True), r=[K_("vn", h), K_("aT", h)], w=["pO"])
                        kb.pe(lambda e: e.matmul(Q(pSS, h), lhsT=B("kdec", h)[hs, :], rhs=B("vn", h)[hs, :], start=True, stop=True), r=[K_("kdec", h), K_("vn", h)], w=["pSS"])
                st.append(s5)

                def s6(c=c):
                    for h in H4:
                        S, Sk = Sst[h][scur[h]], f"S{h}{scur[h]}"
                        Sn, Snk = Sst[h][1 - scur[h]], f"S{h}{1-scur[h]}"
                        kb.dve(lambda e: e.scalar_tensor_tensor(out=Sn[:], in0=S[:], scalar=B("sm", h)[:, 4 + 2 * c:5 + 2 * c], in1=Q(pSS, h), op0=ALU.mult, op1=ALU.add),
                               r=[Sk, K_("sm", h), "pSS"], w=[Snk])
                        scur[h] = 1 - scur[h]
                st.append(s6)

            def s7():
                for h in H4:
                    kb.act(lambda e: e.activation(out=obuf[h][:, (p % 4) * 128:(p % 4 + 1) * 128], in_=Q(pO, h), func=ACT.Copy), r=["pO"], w=[f"obuf{h}"])
            st.append(s7)
            return st

        npost = [0]

        def post(g):
            tsl = slice(g * 512, (g + 1) * 512)
            for h in H4:
                i2 = npost[0] % 2
                kb.dma(gsg[i2][:], sgT[h, :, tsl], r=["sgT"], w=[f"gsg{i2}"])
                kb.pool(lambda e: e.tensor_tensor(out=gsq[:], in0=obuf[h][:], in1=obuf[h][:], op=ALU.mult), r=[f"obuf{h}"], w=["gsq"])
                kb.pe(lambda e: e.matmul(pP[:], lhsT=onesf, rhs=gsq[:], start=True, stop=True), r=["gsq", "const"], w=["pP"])
                kb.act(lambda e: e.activation(out=gsd[:], in_=pP[:], func=ACT.Sqrt, bias=epst[:], scale=1.0 / 128), r=["pP", "const"], w=["gsd"])
                kb.dve(lambda e: e.reciprocal(out=gsd[:], in_=gsd[:]), r=["gsd"], w=["gsd"])
                kb.dve(lambda e: e.tensor_tensor(out=gon[:], in0=obuf[h][:], in1=gsd[:], op=ALU.mult), r=[f"obuf{h}", "gsd"], w=["gon"])
                kb.dve(lambda e: e.scalar_tensor_tensor(out=gmx[i2][:], in0=gon[:], scalar=gngs[:, 0:1], in1=gsg[i2][:], op0=ALU.mult, op1=ALU.mult),
                       r=["gon", "const", f"gsg{i2}"], w=[f"gmx{i2}"])
                kb.dma(mixT[h, :, tsl], gmx[i2][:], r=[f"gmx{i2}"], w=["mixT"], q="pool")
                npost[0] += 1

        for f in pre_stages(0):
            f()
        for p in range(NT):
            sc_ = scan_stages(p)
            pr_ = pre_stages(p + 1) if p + 1 < NT else []
            ratio = (len(pr_) + len(sc_) - 1) // len(sc_)
            pi = 0
            for f in sc_:
                f()
                for _ in range(ratio):
                    if pi < len(pr_):
                        pr_[pi]()
                        pi += 1
            while pi < len(pr_):
                pr_[pi]()
                pi += 1
            if p % 4 == 3:
                post(p // 4)
        kb.pop()
        if upto <= 2:
            kb.finish()
            return nc, kb
        kb.push()
        akT = [kb.sb(f"akT{i}", [128, T], BF16) for i in range(2)]
        aqT = [kb.sb(f"aqT{i}", [128, T], BF16) for i in range(2)]
        avv = [kb.sb(f"avv{i}", [128, NT, 128], BF16) for i in range(2)]
        pTb = [[kb.sb(f"pT{i}{j}", [128, 512], BF16) for j in range(2)] for i in range(2)]
        arl = [kb.sb(f"arl{i}", [128, 512], F32) for i in range(2)]
        lacc = [kb.sb(f"lacc{i}", [128, 512], F32) for i in range(2)]
        ao = [kb.sb(f"ao{i}", [128, 512], F32) for i in range(2)]
        aod = kb.sb("aod", [128, 512], F32)
        asq = kb.sb("asq", [128, 512], F32)
        asd = kb.sb("asd", [128, 512], F32)
        aon = kb.sb("aon", [128, 512], F32)
        asg = [kb.sb(f"asg{i}", [128, 512], F32) for i in range(2)]
        amx = [kb.sb(f"amx{i}", [128, 512], BF16) for i in range(2)]
        pS = [[kb.ps(f"pS{i}{j}", [128, 512], F32) for j in range(2)] for i in range(2)]
        pOa = [kb.ps(f"pOa{i}", [128, 512], F32) for i in range(2)]
        pLa = [kb.ps(f"pLa{i}", [128, 512], F32) for i in range(2)]
        nstep = 0
        for h in range(4):
            hb = h % 2
            for part in range(2):
                sl = slice(part * (T // 2), (part + 1) * (T // 2))
                kb.dma(akT[hb][:, sl], dkT[h, :, sl], r=["dkT"], w=[f"akT{hb}"])
                kb.dma(aqT[hb][:, sl], dqT[h, :, sl], r=["dqT"], w=[f"aqT{hb}"])
            dvh = dv[h].rearrange("(t p) e -> p t e", p=128)
            for part in range(4):
                kb.dma(avv[hb][:, part * 16:(part + 1) * 16, :], dvh[:, part * 16:(part + 1) * 16, :], r=["dv"], w=[f"avv{hb}"])
            for g in range(NG):
                nk = 4 * g + 4
                for kt in range(nk):
                    j = kt - 4 * g
                    qlo = 128 * max(j, 0)
                    qs = slice(qlo, 512)
                    sb2 = nstep % 2
                    for i in range(2):
                        rows = slice(64 * i, 64 * i + 64)
                        kb.pe(lambda e: e.matmul(pS[i][sb2][:, qs], lhsT=akT[hb][rows, kt * 128:(kt + 1) * 128], rhs=aqT[hb][rows, g * 512 + qlo:(g + 1) * 512],
                                                 start=True, stop=True), r=[f"akT{hb}", f"aqT{hb}"], w=[f"pS{i}{sb2}"])
                    for i in range(2):
                        pt = pTb[i][sb2]
                        kb.act(lambda e: e.activation(out=pt[:, qs], in_=pS[i][sb2][:, qs], func=ACT.Exp, bias=negC[:], scale=0.125),
                               r=[f"pS{i}{sb2}", "negC"], w=[f"pT{i}{sb2}"])
                        if j >= 0:
                            kb.pool(lambda e: e.memset(pt[64:128, qlo:qlo + 64], 0.0), r=[], w=[f"pT{i}{sb2}"])
                    for i in range(2):
                        pt = pTb[i][sb2]
                        kb.pe(lambda e: e.matmul(pOa[i][:, qs], lhsT=avv[hb][:, kt, :], rhs=pt[:, qs], start=(kt == 0), stop=(kt == nk - 1)),
                              r=[f"avv{hb}", f"pT{i}{sb2}"], w=[f"pOa{i}"])
                        ae = kb.pool if i == 0 else kb.dve
                        if kt == 0:
                            ae(lambda e: e.tensor_copy(out=lacc[i][:, qs], in_=pt[:, qs]), r=[f"pT{i}{sb2}"], w=[f"lacc{i}"])
                        else:
                            ae(lambda e: e.tensor_tensor(out=lacc[i][:, qs], in0=lacc[i][:, qs], in1=pt[:, qs], op=ALU.add), r=[f"pT{i}{sb2}", f"lacc{i}"], w=[f"lacc{i}"])
                    nstep += 1
                tsl = slice(g * 512, (g + 1) * 512)
                i2 = g % 2
                kb.dma(asg[i2][:], sgT[4 + h, :, tsl], r=["sgT"], w=[f"asg{i2}"])
                for i in range(2):
                    kb.pe(lambda e: e.matmul(pLa[i][:], lhsT=onesf, rhs=lacc[i][:], start=True, stop=True), r=["const", f"lacc{i}"], w=[f"pLa{i}"])
                for i in range(2):
                    kb.dve(lambda e: e.reciprocal(out=arl[i][:], in_=pLa[i][:]), r=[f"pLa{i}"], w=[f"arl{i}"])
                    kb.dve(lambda e: e.tensor_tensor(out=ao[i][:], in0=pOa[i][:], in1=arl[i][:], op=ALU.mult), r=[f"pOa{i}", f"arl{i}"], w=[f"ao{i}"])
                kb.dve(lambda e: e.scalar_tensor_tensor(out=aod[:], in0=ao[1][:], scalar=neglam[:, 0:1], in1=ao[0][:], op0=ALU.mult, op1=ALU.add),
                       r=["ao0", "ao1", "neglam"], w=["aod"])
                kb.pool(lambda e: e.tensor_tensor(out=asq[:], in0=aod[:], in1=aod[:], op=ALU.mult), r=["aod"], w=["asq"])
                pN_ = pS[0][0]
                kb.pe(lambda e: e.matmul(pN_[:], lhsT=onesf, rhs=asq[:], start=True, stop=True), r=["asq", "const"], w=["pS00"])
                kb.act(lambda e: e.activation(out=asd[:], in_=pN_[:], func=ACT.Sqrt, bias=epst[:], scale=1.0 / 128), r=["pS00", "const"], w=["asd"])
                kb.dve(lambda e: e.reciprocal(out=asd[:], in_=asd[:]), r=["asd"], w=["asd"])
                kb.dve(lambda e: e.tensor_tensor(out=aon[:], in0=aod[:], in1=asd[:], op=ALU.mult), r=["aod", "asd"], w=["aon"])
                kb.dve(lambda e: e.scalar_tensor_tensor(out=amx[i2][:], in0=aon[:], scalar=slgs[:, 0:1], in1=asg[i2][:], op0=ALU.mult, op1=ALU.mult),
                       r=["aon", "slgs", f"asg{i2}"], w=[f"amx{i2}"])
                kb.dma(mixT[4 + h, :, tsl], amx[i2][:], r=[f"amx{i2}"], w=["mixT"], q="pool")
        kb.pop()
        if upto <= 3:
            kb.finish()
            return nc, kb
        kb.push()
        mt = [kb.sb(f"mt{i}", [128, 8, 512], BF16) for i in range(2)]
        xr = [kb.sb(f"xr{i}", [128, 512], F32) for i in range(2)]
        y1 = [kb.sb(f"y1{i}", [128, 512], F32) for i in range(2)]
        y2 = [kb.sb(f"y2{i}", [128, 512], F32) for i in range(2)]
        pY = [kb.ps(f"pY{i}", [128, 512], F32) for i in range(2)]
        mixT_v = mixT.rearrange("m p t -> p m t")
        for g in range(NG):
            m_ = mt[g % 2]
            mk = f"mt{g%2}"
            kb.dma(m_[:], mixT_v[:, :, g * 512:(g + 1) * 512], r=["mixT"], w=[mk])
            for a in range(4):
                tt = 4 * g + a
                i2 = tt % 2
                kb.dma(xr[i2][:], xres[tt * 128:(tt + 1) * 128, :], w=[f"xr{i2}"])
                for m in range(8):
                    kb.pe(lambda e: e.matmul(pY[i2][:], lhsT=m_[:, m, a * 128:(a + 1) * 128], rhs=wobf[:, m, :], start=(m == 0), stop=(m == 7)),
                          r=[mk, "wobf"], w=[f"pY{i2}"], inc=(m == 7))
                kb.dve(lambda e: e.tensor_tensor(out=y1[i2][:], in0=pY[i2][:], in1=gaterep[:], op=ALU.mult), r=[f"pY{i2}", "gaterep"], w=[f"y1{i2}"])
                kb.pool(lambda e: e.tensor_tensor(out=y2[i2][:], in0=y1[i2][:], in1=xr[i2][:], op=ALU.add), r=[f"y1{i2}", f"xr{i2}"], w=[f"y2{i2}"])
                kb.dma(out[tt * 128:(tt + 1) * 128, :], y2[i2][:], r=[f"y2{i2}"], w=["out"], q="pool")
        kb.pop()
        kb.finish()
    return nc, kb


def _consts():
    i = np.arange(128)
    same = (i[:, None] // 64) == (i[None, :] // 64)
    onesf = np.ones((128, 128), np.float32)
    ublk = (same & (i[:, None] <= i[None, :])).astype(np.float32)
    uincl = ublk.copy()
    negustr = -(same & (i[None, :] > i[:, None])).astype(np.float32)
    onesblk = same.astype(np.float32)
    sel0 = np.repeat((i < 64).astype(np.float32)[:, None], 128, 1)
    sel1 = np.repeat((i >= 64).astype(np.float32)[:, None], 128, 1)
    masks = np.stack([onesf, ublk, uincl, negustr, onesblk, sel0, sel1], 1).astype(np.float32)
    invf = (np.float32(500000.0) ** (-np.arange(0, 16, 2, dtype=np.float32) / np.float32(16))).astype(np.float32)
    return {
        "c_identf": np.eye(128, dtype=np.float32),
        "c_identb": np.eye(128, dtype=np.float32).astype(ml_dtypes.bfloat16),
        "c_onesb": np.ones((128, 128), np.float32).astype(ml_dtypes.bfloat16),
        "c_masks": np.ascontiguousarray(masks),
        "c_negblk": -onesblk,
        "c_invf": np.ascontiguousarray(np.broadcast_to(invf[None, :], (128, 8))),
    }


def _rep(v, n=128):
    return np.ascontiguousarray(np.broadcast_to(np.asarray(v, np.float32).reshape(1, -1), (n, np.asarray(v).size)))


def prep_core(inp, b, half):
    f = lambda a: np.ascontiguousarray(np.asarray(a, np.float32))
    w_in = np.asarray(inp["w_in"][0], np.float32)
    off = np.cumsum([0, 512, 512, 512, 4, 4, 512, 512, 512, 512, 512])
    aq, ak, av, abeta, adec, agate, bq, bk, bv, bgate = [np.arange(off[i], off[i + 1]) for i in range(10)]
    cols = []
    for h in range(4):
        cols += [aq[h * 128:(h + 1) * 128], ak[h * 128:(h + 1) * 128], av[h * 128:(h + 1) * 128]]
    cols += [agate, bgate, bq, bk, bv, abeta, adec]
    cols = np.concatenate(cols)
    conv_w = np.asarray(inp["conv_w"][0], np.float32)
    convc = np.zeros((128, 12, 4), np.float32)
    for h in range(4):
        for kind in range(3):
            ch = kind * 512 + h * 128 + np.arange(128)
            convc[:, 3 * h + kind, :] = conv_w[:, ch].T
    w_ada = np.asarray(inp["w_ada"][0], np.float32)
    b_ada = np.asarray(inp["b_ada"][0], np.float32)
    gsl = slice(2048 + half * 512, 2048 + (half + 1) * 512)
    d = {
        "x": f(inp["x"][b]),
        "xres": f(inp["x"][b][:, half * 512:(half + 1) * 512]),
        "cT": f(np.asarray(inp["c"][b]).reshape(8, 128).T),
        "posT": np.ascontiguousarray(np.asarray(inp["positions"][b], np.int32).reshape(NT, 128).T),
        "wada": f(w_ada[:, 0:2048]),
        "wadag": f(w_ada[:, gsl]),
        "badac": f(b_ada[0:2048].reshape(16, 128).T),
        "badag": _rep(b_ada[gsl]),
        "normgc": f(np.asarray(inp["norm_g"][0]).reshape(8, 128).T),
        "win": f(w_in[:, cols]),
        "convc": convc,
        "alog": _rep(inp["a_log"][0]),
        "dtb": _rep(inp["dt_bias"][0]),
        "gng": f(np.asarray(inp["gdn_norm_g"][0]).reshape(128, 1)),
        "slg": f(np.asarray(inp["subln_g"][0]).reshape(128, 1)),
        "gqk": _rep(np.concatenate([np.tile(np.asarray(inp["q_norm_g"][0]), 8), np.tile(np.asarray(inp["k_norm_g"][0]), 8)])),
        "lamv": np.ascontiguousarray(np.broadcast_to(np.stack([np.asarray(inp[k][0], np.float32) for k in
                                                               ("lambda_q1", "lambda_k1", "lambda_q2", "lambda_k2")])[None], (128, 4, 64))),
        "wout": f(np.asarray(inp["w_out"][0])[:, half * 512:(half + 1) * 512]),
    }
    d.update(_consts())
    return d


_NC_CACHE = {}


def kernel(**inputs):
    if "nc" not in _NC_CACHE:
        _NC_CACHE["nc"] = build_program()[0]
    nc = _NC_CACHE["nc"]
    in_maps = [prep_core(inputs, core // 2, core % 2) for core in range(8)]
    res = run_bass_kernel_spmd(nc, in_maps, core_ids=list(range(8)))
    out = np.empty((4, T, D), np.float32)
    for core in range(8):
        out[core // 2, :, (core % 2) * 512:(core % 2 + 1) * 512] = res.results[core]["out"]
    return out
```

```python
import math
from contextlib import ExitStack

import numpy as np
import ml_dtypes

import concourse.bass as bass
import concourse.mybir as mybir
from concourse.bass_utils import run_bass_kernel_spmd

F32 = mybir.dt.float32
BF16 = mybir.dt.bfloat16
I32 = mybir.dt.int32
ACT = mybir.ActivationFunctionType
ALU = mybir.AluOpType
AX = mybir.AxisListType


class KB:
    NDMA = 40

    def __init__(self, nc):
        self.nc = nc
        self.eng = {"pe": nc.tensor, "act": nc.scalar, "dve": nc.vector, "pool": nc.gpsimd, "sp": nc.sync}
        self.stack = ExitStack()
        self.sem = {}
        self.cnt = {k: 0 for k in self.eng}
        self.waited = {k: {} for k in self.eng}
        self.last_w = {}
        self.readers = {}
        self.dsem = []
        self.dcnt = []
        self.dnext = 0
        self.nins = 0
        self.phase = 0
        self.excl = set()

    def ctx(self):
        for k in ("pe", "act", "dve", "pool"):
            self.sem[k] = self.stack.enter_context(self.nc.semaphore("s_" + k))
        for i in range(self.NDMA):
            self.dsem.append(self.stack.enter_context(self.nc.semaphore(f"s_dma{i}")))
            self.dcnt.append(0)
        return self.stack

    def sb(self, name, shape, dtype):
        return self.stack.enter_context(self.nc.sbuf_tensor(f"{name}_{self.phase}", shape, dtype))

    def ps(self, name, shape, dtype):
        self.excl.add(name)
        return self.stack.enter_context(self.nc.psum_tensor(f"{name}_{self.phase}", shape, dtype))

    def _handle(self, s):
        return self.dsem[s[1]] if isinstance(s, tuple) else self.sem[s]

    def _wait(self, en, deps):
        e = self.eng[en]
        need = {}
        for s, v in deps:
            if s == "pe" and en == "pe":
                continue
            if v > need.get(s, 0):
                need[s] = v
        for s, v in need.items():
            if self.waited[en].get(s, 0) < v:
                e.wait_ge(self._handle(s), v)
                self.waited[en][s] = v
                self.nins += 1

    def _deps(self, r, w):
        d = []
        for k in r:
            if k in self.last_w:
                d.append(self.last_w[k])
        for k in w:
            if k in self.last_w:
                d.append(self.last_w[k])
            d.extend(self.readers.get(k, ()))
        return d

    def _record(self, r, w, tag):
        for k in r:
            self.readers.setdefault(k, []).append(tag)
        for k in w:
            self.last_w[k] = tag
            self.readers[k] = []

    def _skip(self):
        import os
        lim = os.environ.get("KB_LIMIT")
        if not lim:
            return False
        ph, n = (int(v) for v in lim.split(":"))
        if self.phase != ph:
            return False
        self.pops = getattr(self, "pops", 0) + 1
        return self.pops > n

    def op(self, en, fn, r=(), w=(), inc=True):
        if self._skip():
            return None
        w = list(w) + [k for k in r if k in self.excl and k not in w]
        self._wait(en, self._deps(r, w))
        ins = fn(self.eng[en])
        self.nins += 1
        if inc:
            ins.then_inc(self.sem[en], 1)
            self.cnt[en] += 1
            tag = (en, self.cnt[en])
        else:
            tag = (en, self.cnt[en] + 1)
        self._record(r, w, tag)
        return ins

    def pe(self, fn, r=(), w=(), inc=True):
        return self.op("pe", fn, r, w, inc)

    def act(self, fn, r=(), w=()):
        return self.op("act", fn, r, w)

    def dve(self, fn, r=(), w=()):
        return self.op("dve", fn, r, w)

    def pool(self, fn, r=(), w=()):
        return self.op("pool", fn, r, w)

    def dma(self, out, in_, r=(), w=(), q="sp", **kw):
        if self._skip():
            return None
        deps = self._deps(r, w)
        i = self.dnext
        self.dnext = (self.dnext + 1) % self.NDMA
        if self.dcnt[i] > 0:
            deps.append((("d", i), 16 * self.dcnt[i]))
        self._wait(q, deps)
        ins = self.eng[q].dma_start(out=out, in_=in_, **kw)
        ins.then_inc(self.dsem[i], 16)
        self.nins += 1
        self.dcnt[i] += 1
        self._record(r, w, (("d", i), 16 * self.dcnt[i]))
        return ins

    def finish(self):
        deps = [(k, self.cnt[k]) for k in ("pe", "act", "dve", "pool") if self.cnt[k] > 0]
        deps += [(("d", i), 16 * c) for i, c in enumerate(self.dcnt) if c > 0]
        self._wait("sp", deps)

    def barrier(self):
        deps = [(k, self.cnt[k]) for k in ("pe", "act", "dve", "pool") if self.cnt[k] > 0]
        deps += [(("d", i), 16 * c) for i, c in enumerate(self.dcnt) if c > 0]
        for en in ("pe", "act", "dve", "pool", "sp"):
            self._wait(en, [d for d in deps if not (d[0] == "pe" and en == "pe")] + ([("pe", self.cnt["pe"])] if False else []))
        self.last_w = {}
        self.readers = {}

    def push(self):
        self._saved = getattr(self, "_saved", [])
        self._saved.append(self.stack)
        self.stack = ExitStack()
        self.stack.__enter__()
        self.phase += 1

    def pop(self):
        self.barrier()
        self.stack.__exit__(None, None, None)
        self.stack = self._saved.pop()


T = 8192
D = 1024
NT = T // 128
NG = T // 512
NCOL = 4104
EPS = 1e-6
LAMBDA_INIT = 0.8 - 0.6 * math.exp(0.0)
NFM = 20
TM0 = NFM * 128
TWO_PI = 2.0 * math.pi
MAGIC = 12582912.0
CW1 = 6.28125
CW2 = 0.0019350051879882812
CW3 = TWO_PI - CW1 - CW2


def build_program(dbg=(), upto=9):
    nc = bass.Bass("TRN2", target_bir_lowering=False)

    def din(name, shape, dt=F32):
        return nc.dram_tensor(name, list(shape), dt, kind="ExternalInput").ap()

    def dscr(name, shape, dt=F32):
        return nc.dram_tensor(name, list(shape), dt, kind="ExternalOutput" if name in dbg else "Internal").ap()

    x = din("x", [T, D])
    xres = din("xres", [T, 512])
    cT = din("cT", [128, 8])
    posT = din("posT", [128, NT], I32)
    wada = din("wada", [D, 2048])
    wadag = din("wadag", [D, 512])
    badac = din("badac", [128, 16])
    badag = din("badag", [128, 512])
    normgc = din("normgc", [128, 8])
    win = din("win", [D, NCOL])
    convc = din("convc", [128, 12, 4])
    alog = din("alog", [128, 4])
    dtb = din("dtb", [128, 4])
    gng = din("gng", [128, 1])
    slg = din("slg", [128, 1])
    gqk = din("gqk", [128, 1024])
    lamv = din("lamv", [128, 4, 64])
    wout = din("wout", [D, 512])
    c_identf = din("c_identf", [128, 128])
    c_identb = din("c_identb", [128, 128], BF16)
    c_onesb = din("c_onesb", [128, 128], BF16)
    c_masks = din("c_masks", [128, 7, 128])
    c_negblk = din("c_negblk", [128, 128])
    c_invf = din("c_invf", [128, 8])
    out = nc.dram_tensor("out", [T, 512], F32, kind="ExternalOutput").ap()

    gqT = dscr("gqT", [4, 128, T])
    gkT = dscr("gkT", [4, 128, T])
    gktok = dscr("gktok", [4, T, 128])
    gvtok = dscr("gvtok", [4, T, 128])
    sgT = dscr("sgT", [8, 128, T])
    dqT = dscr("dqT", [4, 128, T], BF16)
    dkT = dscr("dkT", [4, 128, T], BF16)
    dv = dscr("dv", [4, T, 128], BF16)
    mixT = dscr("mixT", [8, 128, T], BF16)

    kb = KB(nc)
    with kb.ctx():
        identf = kb.sb("identf", [128, 128], F32)
        identb = kb.sb("identb", [128, 128], BF16)
        onesb = kb.sb("onesb", [128, 128], BF16)
        masks = kb.sb("masks", [128, 7, 128], F32)
        negblk = kb.sb("negblk", [128, 128], F32)
        onesf, ublk, uincl, negustr, onesblk = (masks[:, i, :] for i in range(5))
        sel = [masks[:, 5, :], masks[:, 6, :]]
        bgres = kb.sb("bgres", [128, NT, 8], F32)
        cosr = kb.sb("cosr", [128, NT, 8], F32)
        sinr = kb.sb("sinr", [128, NT, 8], F32)
        gaterep = kb.sb("gaterep", [128, 512], F32)
        mscale = kb.sb("mscale", [128, 8], F32)
        shiftc = kb.sb("shiftc", [128, 8], F32)
        negA = kb.sb("negA", [128, 4], F32)
        dtbs = kb.sb("dtbs", [128, 4], F32)
        gngs = kb.sb("gngs", [128, 1], F32)
        slgs = kb.sb("slgs", [128, 1], F32)
        neglam = kb.sb("neglam", [128, 1], F32)
        negC = kb.sb("negC", [128, 1], F32)
        gqks = kb.sb("gqks", [128, 1024], F32)
        convs = kb.sb("convs", [128, 12, 4], F32)
        wobf = kb.sb("wobf", [128, 8, 512], BF16)

        epst = kb.sb("epst", [128, 1], F32)
        onet = kb.sb("onet", [128, 1], F32)
        CONSTK = ["const"]
        kb.pool(lambda e: e.memset(epst[:], EPS), w=CONSTK)
        kb.pool(lambda e: e.memset(onet[:], 1.0), w=CONSTK)
        for dst, src in ((identf, c_identf), (identb, c_identb), (onesb, c_onesb), (masks, c_masks), (negblk, c_negblk),
                         (negA, alog), (dtbs, dtb), (gngs, gng), (slgs, slg), (gqks, gqk), (convs, convc)):
            kb.dma(dst[:], src, w=CONSTK)

        kb.push()
        sc = kb.sb("sc", [128, 8], F32)
        screp = kb.sb("screp", [128, 8, 128], F32)
        wch = [kb.sb(f"wch{i}", [128, 8, 512], F32) for i in range(2)]
        modcol = kb.sb("modcol", [128, 16], F32)
        bcol = kb.sb("bcol", [128, 16], F32)
        ngc = kb.sb("ngc", [128, 8], F32)
        bgate = kb.sb("bgate", [128, 512], F32)
        lams = kb.sb("lams", [128, 4, 64], F32)
        lprod = kb.sb("lprod", [128, 2, 64], F32)
        lsum = kb.sb("lsum", [128, 2], F32)
        posi = kb.sb("posi", [128, NT], I32)
        posf = kb.sb("posf", [128, NT], F32)
        invf = kb.sb("invf", [128, 8], F32)
        ang = kb.sb("ang", [128, NT, 8], F32)
        tk = kb.sb("tk", [128, NT, 8], F32)
        rr = kb.sb("rr", [128, NT, 8], F32)
        gmax = kb.sb("gmax", [128, 2], F32)
        psm = kb.ps("psm", [128, 512], F32)
        psg = kb.ps("psg", [128, 512], F32)

        kb.dma(sc[:], cT, w=["sc"])
        kb.dma(bcol[:], badac, w=["bcol"])
        kb.dma(ngc[:], normgc, w=["ngc"])
        kb.dma(bgate[:], badag, w=["bgate"])
        kb.dma(lams[:], lamv, w=["lams"])
        kb.dma(posi[:], posT, w=["posi"])
        kb.dma(invf[:], c_invf, w=["invf"])
        kb.act(lambda e: e.activation(out=sc[:], in_=sc[:], func=ACT.Silu), r=["sc"], w=["sc"])
        kb.dve(lambda e: e.tensor_copy(out=screp[:], in_=sc[:].unsqueeze(2).to_broadcast([128, 8, 128])), r=["sc"], w=["screp"])
        wada_v = wada.rearrange("(kc p) n -> p kc n", p=128)
        for cc in range(4):
            wt = wch[cc % 2]
            kb.dma(wt[:], wada_v[:, :, cc * 512:(cc + 1) * 512], w=[f"wch{cc%2}"])
            for jj in range(4):
                j = cc * 4 + jj
                for kc in range(8):
                    kb.pe(lambda e: e.matmul(psm[:, 2 * j:2 * j + 2], lhsT=wt[:, kc, jj * 128:(jj + 1) * 128], rhs=screp[:, kc, 0:2],
                                             start=(kc == 0), stop=(kc == 7)),
                          r=[f"wch{cc%2}", "screp"], w=["psm"], inc=(kc == 7))
        kb.dve(lambda e: e.tensor_tensor(out=modcol[:], in0=psm[:, 0:32].rearrange("p (j two) -> p j two", two=2)[:, :, 0], in1=bcol[:], op=ALU.add),
               r=["psm", "bcol"], w=["modcol"])
        kb.dve(lambda e: e.tensor_copy(out=shiftc[:], in_=modcol[:, 0:8]), r=["modcol"], w=["shiftc"])
        kb.dve(lambda e: e.scalar_tensor_tensor(out=mscale[:], in0=modcol[:, 8:16], scalar=1.0, in1=ngc[:], op0=ALU.add, op1=ALU.mult),
               r=["modcol", "ngc"], w=["mscale"])
        wt = wch[0]
        kb.dma(wt[:], wadag.rearrange("(kc p) n -> p kc n", p=128), w=["wch0"])
        for kc in range(8):
            kb.pe(lambda e: e.matmul(psg[:], lhsT=screp[:, kc, :], rhs=wt[:, kc, :], start=(kc == 0), stop=(kc == 7)),
                  r=["wch0", "screp"], w=["psg"], inc=(kc == 7))
        kb.dve(lambda e: e.tensor_tensor(out=gaterep[:], in0=psg[:], in1=bgate[:], op=ALU.add), r=["psg", "bgate"], w=["gaterep"])
        wt = wch[1]
        kb.dma(wt[:], wout.rearrange("(kc p) n -> p kc n", p=128), w=["wch1"])
        kb.dve(lambda e: e.tensor_copy(out=wobf[:], in_=wt[:]), r=["wch1"], w=["wobf"])
        kb.dve(lambda e: e.tensor_tensor(out=lprod[:, 0, :], in0=lams[:, 0, :], in1=lams[:, 1, :], op=ALU.mult), r=["lams"], w=["lprod"])
        kb.dve(lambda e: e.tensor_tensor(out=lprod[:, 1, :], in0=lams[:, 2, :], in1=lams[:, 3, :], op=ALU.mult), r=["lams", "lprod"], w=["lprod"])
        kb.dve(lambda e: e.tensor_reduce(out=lsum[:], in_=lprod[:], axis=AX.X, op=ALU.add), r=["lprod"], w=["lsum"])
        kb.act(lambda e: e.activation(out=lsum[:], in_=lsum[:], func=ACT.Exp), r=["lsum"], w=["lsum"])
        kb.dve(lambda e: e.scalar_tensor_tensor(out=neglam[:], in0=lsum[:, 1:2], scalar=-LAMBDA_INIT, in1=lsum[:, 0:1], op0=ALU.add, op1=ALU.subtract),
               r=["lsum"], w=["neglam"])
        kb.act(lambda e: e.activation(out=negA[:], in_=negA[:], func=ACT.Exp), r=CONSTK, w=["negA"])
        kb.dve(lambda e: e.tensor_scalar(out=negA[:], in0=negA[:], scalar1=-1.0, scalar2=None, op0=ALU.mult), r=["negA"], w=["negA"])
        kb.dve(lambda e: e.tensor_scalar(out=slgs[:], in0=slgs[:], scalar1=1.0 - LAMBDA_INIT, scalar2=None, op0=ALU.mult), r=CONSTK, w=["slgs"])
        kb.dve(lambda e: e.tensor_reduce(out=gmax[:], in_=gqks[:].rearrange("p (a n) -> p a n", a=2), axis=AX.X, op=ALU.max, apply_absolute_value=True),
               r=CONSTK, w=["gmax"])
        kb.dve(lambda e: e.scalar_tensor_tensor(out=negC[:], in0=gmax[:, 0:1], scalar=-8.0, in1=gmax[:, 1:2], op0=ALU.mult, op1=ALU.mult),
               r=["gmax"], w=["negC"])
        kb.dve(lambda e: e.tensor_copy(out=posf[:], in_=posi[:]), r=["posi"], w=["posf"])
        kb.dve(lambda e: e.tensor_tensor(out=ang[:], in0=posf[:].unsqueeze(2).to_broadcast([128, NT, 8]),
                                         in1=invf[:].unsqueeze(1).to_broadcast([128, NT, 8]), op=ALU.mult), r=["posf", "invf"], w=["ang"])
        for (dst, off, name) in ((sinr, 0.0, "sinr"), (cosr, 0.25, "cosr")):
            kb.dve(lambda e: e.tensor_scalar(out=tk[:], in0=ang[:], scalar1=1.0 / TWO_PI, scalar2=off, op0=ALU.mult, op1=ALU.add), r=["ang"], w=["tk"])
            kb.dve(lambda e: e.tensor_scalar(out=tk[:], in0=tk[:], scalar1=MAGIC, scalar2=None, op0=ALU.add), r=["tk"], w=["tk"])
            kb.dve(lambda e: e.tensor_scalar(out=tk[:], in0=tk[:], scalar1=-MAGIC, scalar2=None, op0=ALU.add), r=["tk"], w=["tk"])
            kb.dve(lambda e: e.scalar_tensor_tensor(out=rr[:], in0=tk[:], scalar=-CW1, in1=ang[:], op0=ALU.mult, op1=ALU.add), r=["ang", "tk"], w=["rr"])
            kb.dve(lambda e: e.scalar_tensor_tensor(out=rr[:], in0=tk[:], scalar=-CW2, in1=rr[:], op0=ALU.mult, op1=ALU.add), r=["rr", "tk"], w=["rr"])
            kb.dve(lambda e: e.scalar_tensor_tensor(out=rr[:], in0=tk[:], scalar=-CW3, in1=rr[:], op0=ALU.mult, op1=ALU.add), r=["rr", "tk"], w=["rr"])
            kb.dve(lambda e: e.tensor_scalar(out=rr[:], in0=rr[:], scalar1=off * TWO_PI, scalar2=math.pi, op0=ALU.add, op1=ALU.min), r=["rr"], w=["rr"])
            kb.dve(lambda e: e.tensor_scalar(out=rr[:], in0=rr[:], scalar1=-math.pi, scalar2=None, op0=ALU.max), r=["rr"], w=["rr"])
            kb.act(lambda e: e.activation(out=dst[:], in_=rr[:], func=ACT.Sin), r=["rr"], w=[name])
        kb.pop()
        kb.push()
        wbf = kb.sb("wbf", [128, 8, NCOL], BF16)
        wcb = [kb.sb(f"wcb{i}", [128, 8, 128], F32) for i in range(2)]
        xt = [kb.sb(f"xt{i}", [128, 1024], F32) for i in range(2)]
        xs = [kb.sb(f"xs{i}", [128, 1024], F32) for i in range(2)]
        junk = kb.sb("junk", [128, 1024], BF16)
        st = [kb.sb(f"st{i}", [128, 4], F32) for i in range(2)]
        hT = [kb.sb(f"hT{i}", [128, 8, 512], BF16) for i in range(2)]
        cbuf = [kb.sb(f"cbuf{i}", [128, 515], F32) for i in range(12)]
        acc = [kb.sb(f"acc{i}", [128, 512], F32) for i in range(2)]
        yb = [kb.sb(f"yb{i}", [128, 512], F32) for i in range(2)]
        sqb = kb.sb("sqb", [128, 512], F32)
        sdb = kb.sb("sdb", [128, 512], F32)
        ynb = [kb.sb(f"ynb{i}", [128, 512], F32) for i in range(2)]
        trb = [kb.sb(f"trb{i}", [128, 512], F32) for i in range(2)]
        sgt = [kb.sb(f"sgt{i}", [128, 512], F32) for i in range(2)]
        qk = kb.sb("qk", [128, 1024], F32)
        ssq = kb.sb("ssq", [128, 16], F32)
        xn = kb.sb("xn", [128, 1024], F32)
        qkb = kb.sb("qkb", [128, 1024], BF16)
        rt = [kb.sb(f"rt{i}", [128, 16, 8], F32) for i in range(4)]
        vb = [kb.sb(f"vb{i}", [128, 512], BF16) for i in range(2)]
        qkTg = [kb.sb("qkTg0", [128, 8, 512], BF16)]
        tmp4 = kb.sb("tmp4", [128, 4], F32)
        psT = kb.ps("psT", [128, 512], F32)
        psF = kb.ps("psF", [128, 512], F32)
        psQ = kb.ps("psQ", [128, 512], F32)
        psK = kb.ps("psK", [128, 512], F32)
        psV = kb.ps("psV", [128, 512], F32)
        psN = kb.ps("psN", [128, 512], F32)
        psXf = kb.ps("psXf", [128, 512], F32)
        psXb = kb.ps("psXb", [128, 1024], BF16)

        win_v = win.rearrange("(kc p) n -> p kc n", p=128)
        nch = (NCOL + 127) // 128
        for ci in range(nch):
            c0, c1 = ci * 128, min(NCOL, ci * 128 + 128)
            wt = wcb[ci % 2]
            kb.dma(wt[:, :, 0:c1 - c0], win_v[:, :, c0:c1], w=[f"wcb{ci%2}"])
            if ci % 2 == 0:
                kb.dve(lambda e: e.tensor_copy(out=wbf[:, :, c0:c1], in_=wt[:, :, 0:c1 - c0]), r=[f"wcb{ci%2}"], w=["wbf"])
            else:
                kb.pool(lambda e: e.tensor_copy(out=wbf[:, :, c0:c1], in_=wt[:, :, 0:c1 - c0]), r=[f"wcb{ci%2}"], w=["wbf"])
        for c in range(12):
            kb.pool(lambda e: e.memset(cbuf[c][:, 0:3], 0.0), w=[f"cbuf{c}"])

        dqT_v = dqT.rearrange("h p t -> p h t")
        dkT_v = dkT.rearrange("h p t -> p h t")
        dv_v = dv.rearrange("h t e -> t h e")
        nfm = 0
        import os as _os
        if _os.environ.get('SKIP_P1'):
            kb.pool(lambda e: e.memset(bgres[:], -0.05), w=["bgres"])
            for h in range(4):
                kb.dma(gqT[h, :, 0:512], x[0:128, 0:512], w=["gqkT"])
                kb.dma(gkT[h, :, 0:512], x[0:128, 0:512], w=["gqkT"])
                kb.dma(gktok[h, 0:512, :].rearrange("(a p) e -> p a e", p=128), x[0:128, 0:512].rearrange("p (a e) -> p a e", a=4), w=["gtok"])
                kb.dma(gvtok[h, 0:512, :].rearrange("(a p) e -> p a e", p=128), x[0:128, 0:512].rearrange("p (a e) -> p a e", a=4), w=["gtok"])
                kb.dma(sgT[h, :, 0:512], x[0:128, 0:512], w=["sgT"])
        for g in range(0 if _os.environ.get('SKIP_P1') else NG):
            hTg = hT[g % 2]
            hk = f"hT{g%2}"
            for a in range(4):
                tt = 4 * g + a
                i2 = tt % 2
                kb.dma(xt[i2][:], x[tt * 128:(tt + 1) * 128, :], w=[f"xt{i2}"])
                s = st[i2]
                kb.act(lambda e: e.activation(out=junk[:], in_=xt[i2][:], func=ACT.Square, accum_out=s[:, 0:1]), r=[f"xt{i2}"], w=["junk", f"st{i2}"])
                kb.dve(lambda e: e.tensor_scalar(out=s[:, 1:2], in0=s[:, 0:1], scalar1=1.0 / D, scalar2=EPS, op0=ALU.mult, op1=ALU.add), r=[f"st{i2}"], w=[f"st{i2}"])
                kb.act(lambda e: e.activation(out=s[:, 2:3], in_=s[:, 1:2], func=ACT.Sqrt), r=[f"st{i2}"], w=[f"st{i2}"])
                kb.dve(lambda e: e.reciprocal(out=s[:, 3:4], in_=s[:, 2:3]), r=[f"st{i2}"], w=[f"st{i2}"])
                kb.act(lambda e: e.activation(out=xs[i2][:], in_=xt[i2][:], func=ACT.Copy, scale=s[:, 3:4]), r=[f"xt{i2}", f"st{i2}"], w=[f"xs{i2}"])
                for half in range(2):
                    for q4 in range(4):
                        kc = half * 4 + q4
                        kb.pe(lambda e: e.transpose(out=psT[:, q4 * 128:(q4 + 1) * 128], in_=xs[i2][:, kc * 128:(kc + 1) * 128], identity=identf[:]),
                              r=[f"xs{i2}", "const"], w=["psT"], inc=(q4 == 3))
                    for q4 in range(4):
                        kc = half * 4 + q4
                        kb.dve(lambda e: e.tensor_scalar(out=hTg[:, kc, a * 128:(a + 1) * 128], in0=psT[:, q4 * 128:(q4 + 1) * 128],
                                                         scalar1=mscale[:, kc:kc + 1], scalar2=shiftc[:, kc:kc + 1], op0=ALU.mult, op1=ALU.add),
                               r=["psT", "mscale", "shiftc"], w=[hk])
            for c in range(NFM):
                for kc in range(8):
                    kb.pe(lambda e: e.matmul(psF[:], lhsT=wbf[:, kc, c * 128:(c + 1) * 128], rhs=hTg[:, kc, :], start=(kc == 0), stop=(kc == 7)),
                          r=["wbf", hk], w=["psF"], inc=(kc == 7))
                tsl = slice(g * 512, (g + 1) * 512)
                if c >= 12:
                    sg = sgt[c % 2]
                    kb.act(lambda e: e.activation(out=sg[:], in_=psF[:], func=ACT.Silu), r=["psF"], w=[f"sgt{c%2}"])
                    kb.dma(sgT[c - 12, :, tsl], sg[:], r=[f"sgt{c%2}"], w=["sgT"], q="pool")
                    continue
                h, kind = c // 3, c % 3
                cb = cbuf[c]
                ck = f"cbuf{c}"
                kb.act(lambda e: e.activation(out=cb[:, 3:515], in_=psF[:], func=ACT.Copy), r=["psF"], w=[ck])
                ac = acc[nfm % 2]
                ak = f"acc{nfm%2}"
                y = yb[nfm % 2]
                yk = f"yb{nfm%2}"
                kb.dve(lambda e: e.tensor_scalar(out=ac[:], in0=cb[:, 0:512], scalar1=convs[:, c, 0:1], scalar2=None, op0=ALU.mult), r=[ck, "const"], w=[ak])
                for j in range(1, 4):
                    kb.dve(lambda e: e.scalar_tensor_tensor(out=ac[:], in0=cb[:, j:j + 512], scalar=convs[:, c, j:j + 1], in1=ac[:], op0=ALU.mult, op1=ALU.add),
                           r=[ck, ak, "const"], w=[ak])
                kb.pool(lambda e: e.tensor_copy(out=cb[:, 0:3], in_=cb[:, 512:515]), r=[ck], w=[ck])
                kb.act(lambda e: e.activation(out=y[:], in_=ac[:], func=ACT.Silu), r=[ak], w=[yk])
                src, srck = y, yk
                if kind < 2:
                    kb.pool(lambda e: e.tensor_tensor(out=sqb[:], in0=y[:], in1=y[:], op=ALU.mult), r=[yk], w=["sqb"])
                    kb.pe(lambda e: e.matmul(psN[:], lhsT=onesf, rhs=sqb[:], start=True, stop=True), r=["sqb", "const"], w=["psN"])
                    kb.act(lambda e: e.activation(out=sdb[:], in_=psN[:], func=ACT.Sqrt, bias=epst[:]), r=["psN", "const"], w=["sdb"])
                    kb.dve(lambda e: e.reciprocal(out=sdb[:], in_=sdb[:]), r=["sdb"], w=["sdb"])
                    yn = ynb[nfm % 2]
                    ynk = f"ynb{nfm%2}"
                    cq = (128.0 ** -0.5) if kind == 0 else 1.0
                    kb.dve(lambda e: e.scalar_tensor_tensor(out=yn[:], in0=y[:], scalar=cq, in1=sdb[:], op0=ALU.mult, op1=ALU.mult), r=[yk, "sdb"], w=[ynk])
                    kb.dma((gqT if kind == 0 else gkT)[h, :, tsl], yn[:], r=[ynk], w=["gqkT"], q="pool")
                    src, srck = yn, ynk
                if kind >= 1:
                    for a in range(4):
                        kb.pe(lambda e: e.transpose(out=psXf[:, a * 128:(a + 1) * 128], in_=src[:, a * 128:(a + 1) * 128], identity=identf[:]),
                              r=[srck, "const"], w=["psXf"], inc=(a == 3))
                    tr = trb[nfm % 2]
                    trk = f"trb{nfm%2}"
                    kb.dve(lambda e: e.tensor_copy(out=tr[:], in_=psXf[:]), r=["psXf"], w=[trk])
                    dst = (gktok if kind == 1 else gvtok)[h, g * 512:(g + 1) * 512, :].rearrange("(a p) e -> p a e", p=128)
                    kb.dma(dst, tr[:].rearrange("p (a e) -> p a e", a=4), r=[trk], w=["gtok"], q="pool")
                nfm += 1
            qTg = qkTg[0]
            qTk = "qkTg0"
            for a in range(4):
                tt = 4 * g + a
                lhs = lambda kc: hTg[:, kc, a * 128:(a + 1) * 128]
                for (pst, pk, c0, n) in ((psQ, "psQ", TM0, 512), (psK, "psK", TM0 + 512, 512), (psV, "psV", TM0 + 1024, 512), (psXf, "psXf", TM0 + 1536, 8)):
                    for kc in range(8):
                        kb.pe(lambda e: e.matmul(pst[:, 0:n], lhsT=lhs(kc), rhs=wbf[:, kc, c0:c0 + n], start=(kc == 0), stop=(kc == 7)),
                              r=["wbf", hk], w=[pk], inc=(kc == 7))
                kb.act(lambda e: e.activation(out=bgres[:, tt, 0:4], in_=psXf[:, 0:4], func=ACT.Sigmoid), r=["psXf"], w=["bgres"])
                for h in range(4):
                    kb.act(lambda e: e.activation(out=tmp4[:, h:h + 1], in_=psXf[:, 4 + h:5 + h], func=ACT.Exp, bias=dtbs[:, h:h + 1]), r=["psXf", "const"], w=["tmp4"])
                kb.act(lambda e: e.activation(out=tmp4[:], in_=tmp4[:], func=ACT.Ln, bias=onet[:]), r=["tmp4", "const"], w=["tmp4"])
                kb.dve(lambda e: e.tensor_tensor(out=bgres[:, tt, 4:8], in0=tmp4[:], in1=negA[:], op=ALU.mult), r=["tmp4", "negA"], w=["bgres"])
                v_ = vb[a % 2]
                vk = f"vb{a%2}"
                kb.act(lambda e: e.activation(out=v_[:], in_=psV[:], func=ACT.Copy), r=["psV"], w=[vk])
                kb.dma(dv_v[tt * 128:(tt + 1) * 128, :, :], v_[:].rearrange("p (h e) -> p h e", h=4), r=[vk], w=["dv"], q="pool")
                kb.dve(lambda e: e.tensor_copy(out=qk[:, 0:512], in_=psQ[:]), r=["psQ"], w=["qk"])
                kb.act(lambda e: e.activation(out=qk[:, 512:1024], in_=psK[:], func=ACT.Copy), r=["psK"], w=["qk"])
                kb.pool(lambda e: e.tensor_tensor(out=xn[:], in0=qk[:], in1=qk[:], op=ALU.mult), r=["qk"], w=["xn"])
                kb.dve(lambda e: e.tensor_reduce(out=ssq[:], in_=xn[:].rearrange("p (s d) -> p s d", d=64), axis=AX.X, op=ALU.add), r=["xn"], w=["ssq"])
                kb.dve(lambda e: e.tensor_scalar(out=ssq[:], in0=ssq[:], scalar1=1.0 / 64, scalar2=EPS, op0=ALU.mult, op1=ALU.add), r=["ssq"], w=["ssq"])
                kb.act(lambda e: e.activation(out=ssq[:], in_=ssq[:], func=ACT.Sqrt), r=["ssq"], w=["ssq"])
                kb.dve(lambda e: e.reciprocal(out=ssq[:], in_=ssq[:]), r=["ssq"], w=["ssq"])
                qk3 = qk[:].rearrange("p (s d) -> p s d", d=64)
                xn3 = xn[:].rearrange("p (s d) -> p s d", d=64)
                kb.dve(lambda e: e.tensor_tensor(out=xn3, in0=qk3, in1=ssq[:].unsqueeze(2).to_broadcast([128, 16, 64]), op=ALU.mult), r=["qk", "ssq"], w=["xn"])
                kb.pool(lambda e: e.tensor_tensor(out=xn[:], in0=xn[:], in1=gqks[:], op=ALU.mult), r=["xn", "const"], w=["xn"])
                kb.act(lambda e: e.activation(out=qkb[:], in_=xn[:], func=ACT.Copy), r=["xn"], w=["qkb"])
                qkb3 = qkb[:].rearrange("p (s d) -> p s d", d=64)
                cb_ = cosr[:, tt, :].unsqueeze(1).to_broadcast([128, 16, 8])
                sb_ = sinr[:, tt, :].unsqueeze(1).to_broadcast([128, 16, 8])
                x1, x2 = xn3[:, :, 0:8], xn3[:, :, 8:16]
                kb.dve(lambda e: e.tensor_tensor(out=rt[0][:], in0=x1, in1=cb_, op=ALU.mult), r=["xn", "cosr"], w=["rt0"])
                kb.dve(lambda e: e.tensor_tensor(out=rt[1][:], in0=x2, in1=sb_, op=ALU.mult), r=["xn", "sinr"], w=["rt1"])
                kb.dve(lambda e: e.tensor_tensor(out=rt[2][:], in0=x2, in1=cb_, op=ALU.mult), r=["xn", "cosr"], w=["rt2"])
                kb.dve(lambda e: e.tensor_tensor(out=rt[3][:], in0=x1, in1=sb_, op=ALU.mult), r=["xn", "sinr"], w=["rt3"])
                kb.dve(lambda e: e.tensor_tensor(out=qkb3[:, :, 0:8], in0=rt[0][:], in1=rt[1][:], op=ALU.subtract), r=["rt0", "rt1", "qkb"], w=["qkb"])
                kb.dve(lambda e: e.tensor_tensor(out=qkb3[:, :, 8:16], in0=rt[2][:], in1=rt[3][:], op=ALU.add), r=["rt2", "rt3", "qkb"], w=["qkb"])
                for bi in range(8):
                    kb.pe(lambda e: e.transpose(out=psXb[:, bi * 128:(bi + 1) * 128], in_=qkb[:, bi * 128:(bi + 1) * 128], identity=identb[:]),
                          r=["qkb", "const"], w=["psXb"], inc=(bi == 7))
                kb.dve(lambda e: e.tensor_copy(out=qTg[:, :, a * 128:(a + 1) * 128], in_=psXb[:].rearrange("p (b t) -> p b t", b=8)), r=["psXb"], w=[qTk])
            kb.dma(dqT_v[:, :, g * 512:(g + 1) * 512], qTg[:, 0:4, :], r=[qTk], w=["dqT"], q="pool")
            kb.dma(dkT_v[:, :, g * 512:(g + 1) * 512], qTg[:, 4:8, :], r=[qTk], w=["dkT"], q="pool")
        kb.pop()
        kb.push()
        Q = lambda t, i: t[:, i * 128:(i + 1) * 128]
        names = ["kTp", "qTp", "ktk", "vtk", "Ug", "E", "Ei", "Es", "aT", "Ya", "YaT", "Yb", "YbT", "Wa", "Wb", "dg", "qdT", "kdec", "rhs2", "vn"]
        gb = {}
        for h in range(4):
            for par in range(2):
                for n in names:
                    gb[(n, h, par)] = kb.sb(f"{n}{h}{par}", [128, 128], F32)
                gb[("sm", h, par)] = kb.sb(f"sm{h}{par}", [128, 8], F32)
                gb[("smc", h, par)] = kb.sb(f"smc{h}{par}", [128, 8], F32)
                gb[("g2", h, par)] = kb.sb(f"g2{h}{par}", [128, 2], F32)
        Sst = [[kb.sb(f"S{h}{i}", [128, 128], F32) for i in range(2)] for h in range(4)]
        obuf = [kb.sb(f"obuf{h}", [128, 512], F32) for h in range(4)]
        gsq = kb.sb("gsq", [128, 512], F32)
        gsd = kb.sb("gsd", [128, 512], F32)
        gon = kb.sb("gon", [128, 512], F32)
        gsg = [kb.sb(f"gsg{i}", [128, 512], F32) for i in range(2)]
        gmx = [kb.sb(f"gmx{i}", [128, 512], BF16) for i in range(2)]
        pA = kb.ps("pA", [128, 512], F32)
        pB = kb.ps("pB", [128, 512], F32)
        pC = kb.ps("pC", [128, 512], F32)
        pKS = kb.ps("pKS", [128, 512], F32)
        pVN = kb.ps("pVN", [128, 512], F32)
        pSS = kb.ps("pSS", [128, 512], F32)
        pO = kb.ps("pO", [128, 512], F32)
        pP = kb.ps("pP", [128, 512], F32)
        for h in range(4):
            kb.pool(lambda e: e.memset(Sst[h][0][:], 0.0), w=[f"S{h}0"])
        scur = [0, 0, 0, 0]
        H4 = range(4)

        def pre_stages(p):
            par = p % 2
            tok = slice(p * 128, (p + 1) * 128)
            B = lambda n, h: gb[(n, h, par)]
            K_ = lambda n, h: f"{n}{h}{par}"
            gcol = lambda h: bgres[:, p, 4 + h:5 + h]
            beta = lambda h: bgres[:, p, h:h + 1]
            st = []

            def s_load():
                for h in H4:
                    kb.dma(B("kTp", h)[:], gkT[h, :, tok], r=["gqkT"], w=[K_("kTp", h)])
                    kb.dma(B("qTp", h)[:], gqT[h, :, tok], r=["gqkT"], w=[K_("qTp", h)])
                    kb.dma(B("ktk", h)[:], gktok[h, tok, :], r=["gtok"], w=[K_("ktk", h)])
                    kb.dma(B("vtk", h)[:], gvtok[h, tok, :], r=["gtok"], w=[K_("vtk", h)])
            st.append(s_load)

            def s_ug():
                for h in H4:
                    kb.dve(lambda e: e.tensor_scalar(out=B("Ug", h)[:], in0=ublk, scalar1=gcol(h), scalar2=None, op0=ALU.mult), r=["const", "bgres"], w=[K_("Ug", h)])
                    kb.dve(lambda e: e.tensor_copy(out=B("g2", h)[:], in_=gcol(h).to_broadcast([128, 2])), r=["bgres"], w=[K_("g2", h)])
            st.append(s_ug)

            def s_g():
                for h in H4:
                    kb.pe(lambda e: e.matmul(Q(pA, h), lhsT=B("Ug", h)[:], rhs=onesblk, start=True, stop=False), r=[K_("Ug", h), "const"], w=["pA"], inc=False)
                    kb.pe(lambda e: e.matmul(Q(pA, h), lhsT=negblk[:], rhs=B("Ug", h)[:], start=False, stop=True), r=[K_("Ug", h), "const"], w=["pA"])
                for h in H4:
                    o8 = 8 * h
                    kb.pe(lambda e: e.matmul(pB[:, o8:o8 + 2], lhsT=B("Ug", h)[:], rhs=onesf[:, 0:2], start=True, stop=True), r=[K_("Ug", h), "const"], w=["pB"], inc=False)
                    kb.pe(lambda e: e.matmul(pB[:, o8 + 2:o8 + 4], lhsT=onesblk, rhs=B("g2", h)[:], start=True, stop=True), r=[K_("g2", h), "const"], w=["pB"], inc=False)
                    kb.pe(lambda e: e.matmul(pB[:, o8 + 4:o8 + 6], lhsT=sel[0], rhs=B("g2", h)[:], start=True, stop=True), r=[K_("g2", h), "const"], w=["pB"], inc=False)
                    kb.pe(lambda e: e.matmul(pB[:, o8 + 6:o8 + 8], lhsT=sel[1], rhs=B("g2", h)[:], start=True, stop=True), r=[K_("g2", h), "const"], w=["pB"])
            st.append(s_g)

            def s_e1():
                for h in H4:
                    kb.act(lambda e: e.activation(out=B("smc", h)[:], in_=pB[:, 8 * h:8 * h + 8], func=ACT.Copy), r=["pB"], w=[K_("smc", h)])
                    kb.dve(lambda e: e.tensor_scalar(out=B("E", h)[:], in0=Q(pA, h), scalar1=-1.0, scalar2=0.0, op0=ALU.mult, op1=ALU.min), r=["pA"], w=[K_("E", h)])
            st.append(s_e1)

            def s_e2():
                for h in H4:
                    sm, smc = B("sm", h), B("smc", h)
                    kb.act(lambda e: e.activation(out=sm[:, 0:1], in_=smc[:, 0:1], func=ACT.Exp), r=[K_("smc", h)], w=[K_("sm", h)])
                    kb.act(lambda e: e.activation(out=B("E", h)[:], in_=B("E", h)[:], func=ACT.Exp), r=[K_("E", h)], w=[K_("E", h)])
                    kb.dve(lambda e: e.tensor_tensor(out=sm[:, 2:3], in0=smc[:, 2:3], in1=smc[:, 0:1], op=ALU.subtract), r=[K_("smc", h), K_("sm", h)], w=[K_("sm", h)])
            st.append(s_e2)

            def s_kk():
                for h in H4:
                    kb.pe(lambda e: e.matmul(Q(pC, h), lhsT=B("kTp", h)[:], rhs=B("kTp", h)[:], start=True, stop=True), r=[K_("kTp", h)], w=["pC"])
                for h in H4:
                    kb.pe(lambda e: e.matmul(Q(pA, h), lhsT=B("kTp", h)[:], rhs=B("qTp", h)[:], start=True, stop=True), r=[K_("kTp", h), K_("qTp", h)], w=["pA"])
            st.append(s_kk)

            def s_e3():
                for h in H4:
                    sm, smc = B("sm", h), B("smc", h)
                    kb.dve(lambda e: e.tensor_scalar(out=sm[:, 1:2], in0=sm[:, 0:1], scalar1=-1.0, scalar2=None, op0=ALU.mult), r=[K_("sm", h)], w=[K_("sm", h)])
                    kb.act(lambda e: e.activation(out=sm[:, 2:3], in_=sm[:, 2:3], func=ACT.Exp), r=[K_("sm", h)], w=[K_("sm", h)])
                    kb.act(lambda e: e.activation(out=sm[:, 4:8], in_=smc[:, 4:8], func=ACT.Exp), r=[K_("smc", h), K_("sm", h)], w=[K_("sm", h)])
                    kb.pool(lambda e: e.tensor_tensor(out=B("Ei", h)[:], in0=B("E", h)[:], in1=uincl, op=ALU.mult), r=[K_("E", h), "const"], w=[K_("Ei", h)])
                    kb.pool(lambda e: e.tensor_tensor(out=B("Es", h)[:], in0=B("E", h)[:], in1=negustr, op=ALU.mult), r=[K_("E", h), "const"], w=[K_("Es", h)])
            st.append(s_e3)

            def s_ya():
                for h in H4:
                    kb.dve(lambda e: e.scalar_tensor_tensor(out=B("Ya", h)[:], in0=Q(pC, h), scalar=beta(h), in1=B("Es", h)[:], op0=ALU.mult, op1=ALU.mult),
                           r=["pC", K_("Es", h), "bgres"], w=[K_("Ya", h)])
                for h in H4:
                    kb.dve(lambda e: e.tensor_tensor(out=B("aT", h)[:], in0=Q(pA, h), in1=B("Ei", h)[:], op=ALU.mult), r=["pA", K_("Ei", h)], w=[K_("aT", h)])
            st.append(s_ya)

            def s_tr():
                for h in H4:
                    kb.pe(lambda e: e.transpose(out=Q(pB, h), in_=B("Ya", h)[:], identity=identf[:]), r=[K_("Ya", h), "const"], w=["pB"])
            st.append(s_tr)

            def s_tr2():
                for h in H4:
                    kb.act(lambda e: e.activation(out=B("YaT", h)[:], in_=Q(pB, h), func=ACT.Copy), r=["pB"], w=[K_("YaT", h)])
                    kb.dve(lambda e: e.tensor_tensor(out=B("Wa", h)[:], in0=B("Ya", h)[:], in1=identf[:], op=ALU.add), r=[K_("Ya", h), "const"], w=[K_("Wa", h)])
            st.append(s_tr2)

            cur = ["Ya", "YaT", "Wa"]
            for k in range(1, 6):
                Y, YT, W = cur
                Yn, YTn, Wn = ("Yb", "YbT", "Wb") if Y == "Ya" else ("Ya", "YaT", "Wa")

                def s_sq(k=k, Y=Y, YT=YT):
                    if k < 5:
                        for h in H4:
                            kb.pe(lambda e: e.matmul(Q(pA, h), lhsT=B(YT, h)[:], rhs=B(Y, h)[:], start=True, stop=True), r=[K_(Y, h), K_(YT, h)], w=["pA"])
                    for h in H4:
                        kb.pe(lambda e: e.matmul(Q(pC, h), lhsT=B(Y, h)[:], rhs=B(YT, h)[:], start=True, stop=True), r=[K_(Y, h), K_(YT, h)], w=["pC"])
                st.append(s_sq)

                def s_ev(k=k, Yn=Yn, YTn=YTn):
                    for h in H4:
                        if k < 5:
                            kb.act(lambda e: e.activation(out=B(Yn, h)[:], in_=Q(pA, h), func=ACT.Copy), r=["pA"], w=[K_(Yn, h)])
                        kb.dve(lambda e: e.tensor_copy(out=B(YTn, h)[:], in_=Q(pC, h)), r=["pC"], w=[K_(YTn, h)])
                st.append(s_ev)

                def s_w(YTn=YTn, W=W):
                    for h in H4:
                        kb.pe(lambda e: e.matmul(Q(pB, h), lhsT=B(YTn, h)[:], rhs=B(W, h)[:], start=True, stop=True), r=[K_(YTn, h), K_(W, h)], w=["pB"])
                st.append(s_w)

                def s_w2(W=W, Wn=Wn):
                    for h in H4:
                        kb.dve(lambda e: e.tensor_tensor(out=B(Wn, h)[:], in0=Q(pB, h), in1=B(W, h)[:], op=ALU.add), r=["pB", K_(W, h)], w=[K_(Wn, h)])
                st.append(s_w2)
                cur = [Yn, YTn, Wn]
            for h in H4:
                gb[("Wfin", h, par)] = cur[2]

            def s_dg():
                for h in H4:
                    sm = B("sm", h)
                    kb.dve(lambda e: e.tensor_scalar(out=B("dg", h)[:], in0=identf[:], scalar1=sm[:, 0:1], scalar2=None, op0=ALU.mult), r=["const", K_("sm", h)], w=[K_("dg", h)])
                    kb.act(lambda e: e.activation(out=B("kdec", h)[:], in_=B("ktk", h)[:], func=ACT.Copy, scale=sm[:, 2:3]), r=[K_("ktk", h), K_("sm", h)], w=[K_("kdec", h)])
            st.append(s_dg)

            def s_eg():
                for h in H4:
                    kb.pe(lambda e: e.matmul(Q(pA, h), lhsT=onesf, rhs=B("dg", h)[:], start=True, stop=True), r=[K_("dg", h), "const"], w=["pA"])
            st.append(s_eg)

            def s_qd():
                for h in H4:
                    kb.dve(lambda e: e.tensor_tensor(out=B("qdT", h)[:], in0=Q(pA, h), in1=B("qTp", h)[:], op=ALU.mult), r=["pA", K_("qTp", h)], w=[K_("qdT", h)])
            st.append(s_qd)
            return st

        def scan_stages(p):
            par = p % 2
            B = lambda n, h: gb[(n, h, par)]
            K_ = lambda n, h: f"{n}{h}{par}"
            st = []
            for c in range(2):
                hs = slice(64 * c, 64 * c + 64)

                def s1(hs=hs):
                    for h in H4:
                        S, Sk = Sst[h][scur[h]], f"S{h}{scur[h]}"
                        kb.pe(lambda e: e.matmul(Q(pKS, h), lhsT=B("kTp", h)[:], rhs=S[:], start=True, stop=True), r=[K_("kTp", h), Sk], w=["pKS"])
                st.append(s1)

                def s2(hs=hs):
                    for h in H4:
                        kb.dve(lambda e: e.scalar_tensor_tensor(out=B("rhs2", h)[hs, :], in0=Q(pKS, h)[hs, :], scalar=B("sm", h)[hs, 1:2], in1=B("vtk", h)[hs, :],
                                                                op0=ALU.mult, op1=ALU.add), r=["pKS", K_("sm", h), K_("vtk", h)], w=[K_("rhs2", h)])
                st.append(s2)

                def s3(hs=hs):
                    for h in H4:
                        Wf = gb[("Wfin", h, par)]
                        kb.pe(lambda e: e.matmul(Q(pVN, h), lhsT=B(Wf, h)[hs, :], rhs=B("rhs2", h)[hs, :], start=True, stop=True), r=[K_(Wf, h), K_("rhs2", h)], w=["pVN"])
                st.append(s3)

                def s4(hs=hs):
                    for h in H4:
                        kb.dve(lambda e: e.tensor_scalar(out=B("vn", h)[hs, :], in0=Q(pVN, h)[hs, :], scalar1=bgres[hs, p, h:h + 1], scalar2=None, op0=ALU.mult),
                               r=["pVN", "bgres"], w=[K_("vn", h)])
                st.append(s4)

                def s5(hs=hs, c=c):
                    for h in H4:
                        S, Sk = Sst[h][scur[h]], f"S{h}{scur[h]}"
                        oc = slice(h * 128 + 64 * c, h * 128 + 64 * c + 64)
                        kb.pe(lambda e: e.matmul(pO[:, oc], lhsT=S[:], rhs=B("qdT", h)[:, hs], start=True, stop=False), r=[Sk, K_("qdT", h)], w=["pO"], inc=False)
                        kb.pe(lambda e: e.matmul(pO[:, oc], lhsT=B("vn", h)[hs, :], rhs=B("aT", h)[hs, hs], start=False, stop=True), r=[K_("vn", h), K_("aT", h)], w=["pO"])
                        kb.pe(lambda e: e.matmul(Q(pSS, h), lhsT=B("kdec", h)[hs, :], rhs=B("vn", h)[hs, :], start=True, stop=True), r=[K_("kdec", h), K_("vn", h)], w=["pSS"])
                st.append(s5)

                def s6(c=c):
                    for h in H4:
                        S, Sk = Sst[h][scur[h]], f"S{h}{scur[h]}"
                        Sn, Snk = Sst[h][1 - scur[h]], f"S{h}{1-scur[h]}"
                        kb.dve(lambda e: e.scalar_tensor_tensor(out=Sn[:], in0=S[:], scalar=B("sm", h)[:, 4 + 2 * c:5 + 2 * c], in1=Q(pSS, h), op0=ALU.mult, op1=ALU.add),
                               r=[Sk, K_("sm", h), "pSS"], w=[Snk])
                        scur[h] = 1 - scur[h]
                st.append(s6)

            def s7():
                for h in H4:
                    kb.act(lambda e: e.activation(out=obuf[h][:, (p % 4) * 128:(p % 4 + 1) * 128], in_=Q(pO, h), func=ACT.Copy), r=["pO"], w=[f"obuf{h}"])
            st.append(s7)
            return st

        npost = [0]

        def post(g):
            tsl = slice(g * 512, (g + 1) * 512)
            for h in H4:
                i2 = npost[0] % 2
                kb.dma(gsg[i2][:], sgT[h, :, tsl], r=["sgT"], w=[f"gsg{i2}"])
                kb.pool(lambda e: e.tensor_tensor(out=gsq[:], in0=obuf[h][:], in1=obuf[h][:], op=ALU.mult), r=[f"obuf{h}"], w=["gsq"])
                kb.pe(lambda e: e.matmul(pP[:], lhsT=onesf, rhs=gsq[:], start=True, stop=True), r=["gsq", "const"], w=["pP"])
                kb.act(lambda e: e.activation(out=gsd[:], in_=pP[:], func=ACT.Sqrt, bias=epst[:], scale=1.0 / 128), r=["pP", "const"], w=["gsd"])
                kb.dve(lambda e: e.reciprocal(out=gsd[:], in_=gsd[:]), r=["gsd"], w=["gsd"])
                kb.dve(lambda e: e.tensor_tensor(out=gon[:], in0=obuf[h][:], in1=gsd[:], op=ALU.mult), r=[f"obuf{h}", "gsd"], w=["gon"])
                kb.dve(lambda e: e.scalar_tensor_tensor(out=gmx[i2][:], in0=gon[:], scalar=gngs[:, 0:1], in1=gsg[i2][:], op0=ALU.mult, op1=ALU.mult),
                       r=["gon", "const", f"gsg{i2}"], w=[f"gmx{i2}"])
                kb.dma(mixT[h, :, tsl], gmx[i2][:], r=[f"gmx{i2}"], w=["mixT"], q="pool")
                npost[0] += 1

        for f in pre_stages(0):
            f()
        for p in range(NT):
            sc_ = scan_stages(p)
            pr_ = pre_stages(p + 1) if p + 1 < NT else []
            ratio = (len(pr_) + len(sc_) - 1) // len(sc_)
            pi = 0
            for f in sc_:
                f()
                for _ in range(ratio):
                    if pi < len(pr_):
                        pr_[pi]()
                        pi += 1
            while pi < len(pr_):
                pr_[pi]()
                pi += 1
            if p % 4 == 3:
                post(p // 4)
        kb.pop()
        if upto <= 2:
            kb.finish()
            return nc, kb
        kb.push()
        akT = [kb.sb(f"akT{i}", [128, T], BF16) for i in range(2)]
        aqT = [kb.sb(f"aqT{i}", [128, T], BF16) for i in range(2)]
        avv = [kb.sb(f"avv{i}", [128, NT, 128], BF16) for i in range(2)]
        pTb = [[kb.sb(f"pT{i}{j}", [128, 512], BF16) for j in range(2)] for i in range(2)]
        arl = [kb.sb(f"arl{i}", [128, 512], F32) for i in range(2)]
        lacc = [kb.sb(f"lacc{i}", [128, 512], F32) for i in range(2)]
        ao = [kb.sb(f"ao{i}", [128, 512], F32) for i in range(2)]
        aod = kb.sb("aod", [128, 512], F32)
        asq = kb.sb("asq", [128, 512], F32)
        asd = kb.sb("asd", [128, 512], F32)
        aon = kb.sb("aon", [128, 512], F32)
        asg = [kb.sb(f"asg{i}", [128, 512], F32) for i in range(2)]
        amx = [kb.sb(f"amx{i}", [128, 512], BF16) for i in range(2)]
        pS = [[kb.ps(f"pS{i}{j}", [128, 512], F32) for j in range(2)] for i in range(2)]
        pOa = [kb.ps(f"pOa{i}", [128, 512], F32) for i in range(2)]
        pLa = [kb.ps(f"pLa{i}", [128, 512], F32) for i in range(2)]
        nstep = 0
        for h in range(4):
            hb = h % 2
            for part in range(2):
                sl = slice(part * (T // 2), (part + 1) * (T // 2))
                kb.dma(akT[hb][:, sl], dkT[h, :, sl], r=["dkT"], w=[f"akT{hb}"])
                kb.dma(aqT[hb][:, sl], dqT[h, :, sl], r=["dqT"], w=[f"aqT{hb}"])
            dvh = dv[h].rearrange("(t p) e -> p t e", p=128)
            for part in range(4):
                kb.dma(avv[hb][:, part * 16:(part + 1) * 16, :], dvh[:, part * 16:(part + 1) * 16, :], r=["dv"], w=[f"avv{hb}"])
            steps = [(g, kt) for g in range(NG) for kt in range(4 * g + 4)]

            def emit_S(n):
                g, kt = steps[n]
                j = kt - 4 * g
                qlo = 128 * max(j, 0)
                qs = slice(qlo, 512)
                sb2 = n % 2
                for i in range(2):
                    rows = slice(64 * i, 64 * i + 64)
                    kb.pe(lambda e: e.matmul(pS[i][sb2][:, qs], lhsT=akT[hb][rows, kt * 128:(kt + 1) * 128], rhs=aqT[hb][rows, g * 512 + qlo:(g + 1) * 512],
                                             start=True, stop=True), r=[f"akT{hb}", f"aqT{hb}"], w=[f"pS{i}{sb2}"])

            def emit_rest(n):
                g, kt = steps[n]
                nk = 4 * g + 4
                j = kt - 4 * g
                qlo = 128 * max(j, 0)
                qs = slice(qlo, 512)
                sb2 = n % 2
                for i in range(2):
                    pt = pTb[i][sb2]
                    kb.act(lambda e: e.activation(out=pt[:, qs], in_=pS[i][sb2][:, qs], func=ACT.Exp, bias=negC[:], scale=0.125),
                           r=[f"pS{i}{sb2}", "negC"], w=[f"pT{i}{sb2}"])
                    if j >= 0:
                        kb.pool(lambda e: e.memset(pt[64:128, qlo:qlo + 64], 0.0), r=[], w=[f"pT{i}{sb2}"])
                for i in range(2):
                    pt = pTb[i][sb2]
                    kb.pe(lambda e: e.matmul(pOa[i][:, qs], lhsT=avv[hb][:, kt, :], rhs=pt[:, qs], start=(kt == 0), stop=(kt == nk - 1)),
                          r=[f"avv{hb}", f"pT{i}{sb2}"], w=[f"pOa{i}"])
                    ae = kb.pool if i == 0 else kb.dve
                    if kt == 0:
                        ae(lambda e: e.tensor_copy(out=lacc[i][:, qs], in_=pt[:, qs]), r=[f"pT{i}{sb2}"], w=[f"lacc{i}"])
                    else:
                        ae(lambda e: e.tensor_tensor(out=lacc[i][:, qs], in0=lacc[i][:, qs], in1=pt[:, qs], op=ALU.add), r=[f"pT{i}{sb2}", f"lacc{i}"], w=[f"lacc{i}"])
                if kt != nk - 1:
                    return
                tsl = slice(g * 512, (g + 1) * 512)
                i2 = g % 2
                kb.dma(asg[i2][:], sgT[4 + h, :, tsl], r=["sgT"], w=[f"asg{i2}"])
                for i in range(2):
                    kb.pe(lambda e: e.matmul(pLa[i][:], lhsT=onesf, rhs=lacc[i][:], start=True, stop=True), r=["const", f"lacc{i}"], w=[f"pLa{i}"])
                for i in range(2):
                    kb.dve(lambda e: e.reciprocal(out=arl[i][:], in_=pLa[i][:]), r=[f"pLa{i}"], w=[f"arl{i}"])
                    kb.dve(lambda e: e.tensor_tensor(out=ao[i][:], in0=pOa[i][:], in1=arl[i][:], op=ALU.mult), r=[f"pOa{i}", f"arl{i}"], w=[f"ao{i}"])
                kb.dve(lambda e: e.scalar_tensor_tensor(out=aod[:], in0=ao[1][:], scalar=neglam[:, 0:1], in1=ao[0][:], op0=ALU.mult, op1=ALU.add),
                       r=["ao0", "ao1", "neglam"], w=["aod"])
                kb.pool(lambda e: e.tensor_tensor(out=asq[:], in0=aod[:], in1=aod[:], op=ALU.mult), r=["aod"], w=["asq"])
                pN_ = pLa[0]
                kb.pe(lambda e: e.matmul(pN_[:], lhsT=onesf, rhs=asq[:], start=True, stop=True), r=["asq", "const"], w=["pLa0"])
                kb.act(lambda e: e.activation(out=asd[:], in_=pN_[:], func=ACT.Sqrt, bias=epst[:], scale=1.0 / 128), r=["pLa0", "const"], w=["asd"])
                kb.dve(lambda e: e.reciprocal(out=asd[:], in_=asd[:]), r=["asd"], w=["asd"])
                kb.dve(lambda e: e.tensor_tensor(out=aon[:], in0=aod[:], in1=asd[:], op=ALU.mult), r=["aod", "asd"], w=["aon"])
                kb.dve(lambda e: e.scalar_tensor_tensor(out=amx[i2][:], in0=aon[:], scalar=slgs[:, 0:1], in1=asg[i2][:], op0=ALU.mult, op1=ALU.mult),
                       r=["aon", "slgs", f"asg{i2}"], w=[f"amx{i2}"])
                kb.dma(mixT[4 + h, :, tsl], amx[i2][:], r=[f"amx{i2}"], w=["mixT"], q="pool")

            emit_S(0)
            for n in range(len(steps)):
                if n + 1 < len(steps):
                    emit_S(n + 1)
                emit_rest(n)
        kb.pop()
        if upto <= 3:
            kb.finish()
            return nc, kb
        kb.push()
        mt = [kb.sb(f"mt{i}", [128, 8, 512], BF16) for i in range(2)]
        xr = [kb.sb(f"xr{i}", [128, 512], F32) for i in range(2)]
        y1 = [kb.sb(f"y1{i}", [128, 512], F32) for i in range(2)]
        y2 = [kb.sb(f"y2{i}", [128, 512], F32) for i in range(2)]
        pY = [kb.ps(f"pY{i}", [128, 512], F32) for i in range(2)]
        mixT_v = mixT.rearrange("m p t -> p m t")
        for g in range(NG):
            m_ = mt[g % 2]
            mk = f"mt{g%2}"
            kb.dma(m_[:], mixT_v[:, :, g * 512:(g + 1) * 512], r=["mixT"], w=[mk])
            for a in range(4):
                tt = 4 * g + a
                i2 = tt % 2
                kb.dma(xr[i2][:], xres[tt * 128:(tt + 1) * 128, :], w=[f"xr{i2}"])
                for m in range(8):
                    kb.pe(lambda e: e.matmul(pY[i2][:], lhsT=m_[:, m, a * 128:(a + 1) * 128], rhs=wobf[:, m, :], start=(m == 0), stop=(m == 7)),
                          r=[mk, "wobf"], w=[f"pY{i2}"], inc=(m == 7))
                kb.dve(lambda e: e.tensor_tensor(out=y1[i2][:], in0=pY[i2][:], in1=gaterep[:], op=ALU.mult), r=[f"pY{i2}", "gaterep"], w=[f"y1{i2}"])
                kb.pool(lambda e: e.tensor_tensor(out=y2[i2][:], in0=y1[i2][:], in1=xr[i2][:], op=ALU.add), r=[f"y1{i2}", f"xr{i2}"], w=[f"y2{i2}"])
                kb.dma(out[tt * 128:(tt + 1) * 128, :], y2[i2][:], r=[f"y2{i2}"], w=["out"], q="pool")
        kb.pop()
        kb.finish()
    return nc, kb


def _consts():
    i = np.arange(128)
    same = (i[:, None] // 64) == (i[None, :] // 64)
    onesf = np.ones((128, 128), np.float32)
    ublk = (same & (i[:, None] <= i[None, :])).astype(np.float32)
    uincl = ublk.copy()
    negustr = -(same & (i[None, :] > i[:, None])).astype(np.float32)
    onesblk = same.astype(np.float32)
    sel0 = np.repeat((i < 64).astype(np.float32)[:, None], 128, 1)
    sel1 = np.repeat((i >= 64).astype(np.float32)[:, None], 128, 1)
    masks = np.stack([onesf, ublk, uincl, negustr, onesblk, sel0, sel1], 1).astype(np.float32)
    invf = (np.float32(500000.0) ** (-np.arange(0, 16, 2, dtype=np.float32) / np.float32(16))).astype(np.float32)
    return {
        "c_identf": np.eye(128, dtype=np.float32),
        "c_identb": np.eye(128, dtype=np.float32).astype(ml_dtypes.bfloat16),
        "c_onesb": np.ones((128, 128), np.float32).astype(ml_dtypes.bfloat16),
        "c_masks": np.ascontiguousarray(masks),
        "c_negblk": -onesblk,
        "c_invf": np.ascontiguousarray(np.broadcast_to(invf[None, :], (128, 8))),
    }


def _rep(v, n=128):
    return np.ascontiguousarray(np.broadcast_to(np.asarray(v, np.float32).reshape(1, -1), (n, np.asarray(v).size)))


def prep_core(inp, b, half):
    f = lambda a: np.ascontiguousarray(np.asarray(a, np.float32))
    w_in = np.asarray(inp["w_in"][0], np.float32)
    off = np.cumsum([0, 512, 512, 512, 4, 4, 512, 512, 512, 512, 512])
    aq, ak, av, abeta, adec, agate, bq, bk, bv, bgate = [np.arange(off[i], off[i + 1]) for i in range(10)]
    cols = []
    for h in range(4):
        cols += [aq[h * 128:(h + 1) * 128], ak[h * 128:(h + 1) * 128], av[h * 128:(h + 1) * 128]]
    cols += [agate, bgate, bq, bk, bv, abeta, adec]
    cols = np.concatenate(cols)
    conv_w = np.asarray(inp["conv_w"][0], np.float32)
    convc = np.zeros((128, 12, 4), np.float32)
    for h in range(4):
        for kind in range(3):
            ch = kind * 512 + h * 128 + np.arange(128)
            convc[:, 3 * h + kind, :] = conv_w[:, ch].T
    w_ada = np.asarray(inp["w_ada"][0], np.float32)
    b_ada = np.asarray(inp["b_ada"][0], np.float32)
    gsl = slice(2048 + half * 512, 2048 + (half + 1) * 512)
    d = {
        "x": f(inp["x"][b]),
        "xres": f(inp["x"][b][:, half * 512:(half + 1) * 512]),
        "cT": f(np.asarray(inp["c"][b]).reshape(8, 128).T),
        "posT": np.ascontiguousarray(np.asarray(inp["positions"][b], np.int32).reshape(NT, 128).T),
        "wada": f(w_ada[:, 0:2048]),
        "wadag": f(w_ada[:, gsl]),
        "badac": f(b_ada[0:2048].reshape(16, 128).T),
        "badag": _rep(b_ada[gsl]),
        "normgc": f(np.asarray(inp["norm_g"][0]).reshape(8, 128).T),
        "win": f(w_in[:, cols]),
        "convc": convc,
        "alog": _rep(inp["a_log"][0]),
        "dtb": _rep(inp["dt_bias"][0]),
        "gng": f(np.asarray(inp["gdn_norm_g"][0]).reshape(128, 1)),
        "slg": f(np.asarray(inp["subln_g"][0]).reshape(128, 1)),
        "gqk": _rep(np.concatenate([np.tile(np.asarray(inp["q_norm_g"][0]), 8), np.tile(np.asarray(inp["k_norm_g"][0]), 8)])),
        "lamv": np.ascontiguousarray(np.broadcast_to(np.stack([np.asarray(inp[k][0], np.float32) for k in
                                                               ("lambda_q1", "lambda_k1", "lambda_q2", "lambda_k2")])[None], (128, 4, 64))),
        "wout": f(np.asarray(inp["w_out"][0])[:, half * 512:(half + 1) * 512]),
    }
    d.update(_consts())
    return d


_NC_CACHE = {}


def kernel(**inputs):
    if "nc" not in _NC_CACHE:
        _NC_CACHE["nc"] = build_program()[0]
    nc = _NC_CACHE["nc"]
    in_maps = [prep_core(inputs, core // 2, core % 2) for core in range(8)]
    res = run_bass_kernel_spmd(nc, in_maps, core_ids=list(range(8)))
    out = np.empty((4, T, D), np.float32)
    for core in range(8):
        out[core // 2, :, (core % 2) * 512:(core % 2 + 1) * 512] = res.results[core]["out"]
    return out
```

```python
import math
from contextlib import ExitStack

import numpy as np
import ml_dtypes

import concourse.bass as bass
import concourse.mybir as mybir
from concourse.bass_utils import run_bass_kernel_spmd

F32 = mybir.dt.float32
BF16 = mybir.dt.bfloat16
I32 = mybir.dt.int32
ACT = mybir.ActivationFunctionType
ALU = mybir.AluOpType
AX = mybir.AxisListType


class KB:
    NDMA = 40

    def __init__(self, nc):
        self.nc = nc
        self.eng = {"pe": nc.tensor, "act": nc.scalar, "dve": nc.vector, "pool": nc.gpsimd, "sp": nc.sync}
        self.stack = ExitStack()
        self.sem = {}
        self.cnt = {k: 0 for k in self.eng}
        self.waited = {k: {} for k in self.eng}
        self.last_w = {}
        self.readers = {}
        self.dsem = []
        self.dcnt = []
        self.dnext = 0
        self.nins = 0
        self.phase = 0
        self.excl = set()

    def ctx(self):
        for k in ("pe", "act", "dve", "pool"):
            self.sem[k] = self.stack.enter_context(self.nc.semaphore("s_" + k))
        for i in range(self.NDMA):
            self.dsem.append(self.stack.enter_context(self.nc.semaphore(f"s_dma{i}")))
            self.dcnt.append(0)
        return self.stack

    def sb(self, name, shape, dtype):
        return self.stack.enter_context(self.nc.sbuf_tensor(f"{name}_{self.phase}", shape, dtype))

    def ps(self, name, shape, dtype):
        self.excl.add(name)
        return self.stack.enter_context(self.nc.psum_tensor(f"{name}_{self.phase}", shape, dtype))

    def _handle(self, s):
        return self.dsem[s[1]] if isinstance(s, tuple) else self.sem[s]

    def _wait(self, en, deps):
        e = self.eng[en]
        need = {}
        for s, v in deps:
            if s == "pe" and en == "pe":
                continue
            if v > need.get(s, 0):
                need[s] = v
        for s, v in need.items():
            if self.waited[en].get(s, 0) < v:
                e.wait_ge(self._handle(s), v)
                self.waited[en][s] = v
                self.nins += 1

    def _deps(self, r, w):
        d = []
        for k in r:
            if k in self.last_w:
                d.append(self.last_w[k])
        for k in w:
            if k in self.last_w:
                d.append(self.last_w[k])
            d.extend(self.readers.get(k, ()))
        return d

    def _record(self, r, w, tag):
        for k in r:
            self.readers.setdefault(k, []).append(tag)
        for k in w:
            self.last_w[k] = tag
            self.readers[k] = []

    def _skip(self):
        import os
        lim = os.environ.get("KB_LIMIT")
        if not lim:
            return False
        ph, n = (int(v) for v in lim.split(":"))
        if self.phase != ph:
            return False
        self.pops = getattr(self, "pops", 0) + 1
        return self.pops > n

    def op(self, en, fn, r=(), w=(), inc=True):
        if self._skip():
            return None
        w = list(w) + [k for k in r if k in self.excl and k not in w]
        self._wait(en, self._deps(r, w))
        ins = fn(self.eng[en])
        self.nins += 1
        if inc:
            ins.then_inc(self.sem[en], 1)
            self.cnt[en] += 1
            tag = (en, self.cnt[en])
        else:
            tag = (en, self.cnt[en] + 1)
        self._record(r, w, tag)
        return ins

    def pe(self, fn, r=(), w=(), inc=True):
        return self.op("pe", fn, r, w, inc)

    def act(self, fn, r=(), w=()):
        return self.op("act", fn, r, w)

    def dve(self, fn, r=(), w=()):
        return self.op("dve", fn, r, w)

    def pool(self, fn, r=(), w=()):
        return self.op("pool", fn, r, w)

    def dma(self, out, in_, r=(), w=(), q="sp", **kw):
        if self._skip():
            return None
        deps = self._deps(r, w)
        i = self.dnext
        self.dnext = (self.dnext + 1) % self.NDMA
        if self.dcnt[i] > 0:
            deps.append((("d", i), 16 * self.dcnt[i]))
        self._wait(q, deps)
        ins = self.eng[q].dma_start(out=out, in_=in_, **kw)
        ins.then_inc(self.dsem[i], 16)
        self.nins += 1
        self.dcnt[i] += 1
        self._record(r, w, (("d", i), 16 * self.dcnt[i]))
        return ins

    def finish(self):
        deps = [(k, self.cnt[k]) for k in ("pe", "act", "dve", "pool") if self.cnt[k] > 0]
        deps += [(("d", i), 16 * c) for i, c in enumerate(self.dcnt) if c > 0]
        self._wait("sp", deps)

    def barrier(self):
        deps = [(k, self.cnt[k]) for k in ("pe", "act", "dve", "pool") if self.cnt[k] > 0]
        deps += [(("d", i), 16 * c) for i, c in enumerate(self.dcnt) if c > 0]
        for en in ("pe", "act", "dve", "pool", "sp"):
            self._wait(en, [d for d in deps if not (d[0] == "pe" and en == "pe")] + ([("pe", self.cnt["pe"])] if False else []))
        self.last_w = {}
        self.readers = {}

    def push(self):
        self._saved = getattr(self, "_saved", [])
        self._saved.append(self.stack)
        self.stack = ExitStack()
        self.stack.__enter__()
        self.phase += 1

    def pop(self):
        self.barrier()
        self.stack.__exit__(None, None, None)
        self.stack = self._saved.pop()


T = 8192
D = 1024
NT = T // 128
NG = T // 512
NCOL = 4104
EPS = 1e-6
LAMBDA_INIT = 0.8 - 0.6 * math.exp(0.0)
NFM = 20
TM0 = NFM * 128
TWO_PI = 2.0 * math.pi
MAGIC = 12582912.0
CW1 = 6.28125
CW2 = 0.0019350051879882812
CW3 = TWO_PI - CW1 - CW2


def build_program(dbg=(), upto=9):
    nc = bass.Bass("TRN2", target_bir_lowering=False)

    def din(name, shape, dt=F32):
        return nc.dram_tensor(name, list(shape), dt, kind="ExternalInput").ap()

    def dscr(name, shape, dt=F32):
        return nc.dram_tensor(name, list(shape), dt, kind="ExternalOutput" if name in dbg else "Internal").ap()

    x = din("x", [T, D])
    xres = din("xres", [T, 512])
    cT = din("cT", [128, 8])
    posT = din("posT", [128, NT], I32)
    wada = din("wada", [D, 2048])
    wadag = din("wadag", [D, 512])
    badac = din("badac", [128, 16])
    badag = din("badag", [128, 512])
    normgc = din("normgc", [128, 8])
    win = din("win", [D, NCOL])
    convc = din("convc", [128, 12, 4])
    alog = din("alog", [128, 4])
    dtb = din("dtb", [128, 4])
    gng = din("gng", [128, 1])
    slg = din("slg", [128, 1])
    gqk = din("gqk", [128, 1024])
    lamv = din("lamv", [128, 4, 64])
    wout = din("wout", [D, 512])
    c_identf = din("c_identf", [128, 128])
    c_identb = din("c_identb", [128, 128], BF16)
    c_onesb = din("c_onesb", [128, 128], BF16)
    c_masks = din("c_masks", [128, 7, 128])
    c_negblk = din("c_negblk", [128, 128])
    c_invf = din("c_invf", [128, 8])
    out = nc.dram_tensor("out", [T, 512], F32, kind="ExternalOutput").ap()

    gqT = dscr("gqT", [4, 128, T])
    gkT = dscr("gkT", [4, 128, T])
    gktok = dscr("gktok", [4, T, 128])
    gvtok = dscr("gvtok", [4, T, 128])
    sgT = dscr("sgT", [8, 128, T])
    dqT = dscr("dqT", [4, 128, T], BF16)
    dkT = dscr("dkT", [4, 128, T], BF16)
    dv = dscr("dv", [4, T, 128], BF16)
    mixT = dscr("mixT", [8, 128, T], BF16)

    kb = KB(nc)
    with kb.ctx():
        identf = kb.sb("identf", [128, 128], F32)
        identb = kb.sb("identb", [128, 128], BF16)
        onesb = kb.sb("onesb", [128, 128], BF16)
        masks = kb.sb("masks", [128, 7, 128], F32)
        negblk = kb.sb("negblk", [128, 128], F32)
        onesf, ublk, uincl, negustr, onesblk = (masks[:, i, :] for i in range(5))
        sel = [masks[:, 5, :], masks[:, 6, :]]
        bgres = kb.sb("bgres", [128, NT, 8], F32)
        cosr = kb.sb("cosr", [128, NT, 8], F32)
        sinr = kb.sb("sinr", [128, NT, 8], F32)
        gaterep = kb.sb("gaterep", [128, 512], F32)
        mscale = kb.sb("mscale", [128, 8], F32)
        shiftc = kb.sb("shiftc", [128, 8], F32)
        negA = kb.sb("negA", [128, 4], F32)
        dtbs = kb.sb("dtbs", [128, 4], F32)
        gngs = kb.sb("gngs", [128, 1], F32)
        slgs = kb.sb("slgs", [128, 1], F32)
        neglam = kb.sb("neglam", [128, 1], F32)
        negC = kb.sb("negC", [128, 1], F32)
        gqks = kb.sb("gqks", [128, 1024], F32)
        convs = kb.sb("convs", [128, 12, 4], F32)
        wobf = kb.sb("wobf", [128, 8, 512], BF16)

        epst = kb.sb("epst", [128, 1], F32)
        onet = kb.sb("onet", [128, 1], F32)
        CONSTK = ["const"]
        kb.pool(lambda e: e.memset(epst[:], EPS), w=CONSTK)
        kb.pool(lambda e: e.memset(onet[:], 1.0), w=CONSTK)
        for dst, src in ((identf, c_identf), (identb, c_identb), (onesb, c_onesb), (masks, c_masks), (negblk, c_negblk),
                         (negA, alog), (dtbs, dtb), (gngs, gng), (slgs, slg), (gqks, gqk), (convs, convc)):
            kb.dma(dst[:], src, w=CONSTK)

        kb.push()
        sc = kb.sb("sc", [128, 8], F32)
        screp = kb.sb("screp", [128, 8, 128], F32)
        wch = [kb.sb(f"wch{i}", [128, 8, 512], F32) for i in range(2)]
        modcol = kb.sb("modcol", [128, 16], F32)
        bcol = kb.sb("bcol", [128, 16], F32)
        ngc = kb.sb("ngc", [128, 8], F32)
        bgate = kb.sb("bgate", [128, 512], F32)
        lams = kb.sb("lams", [128, 4, 64], F32)
        lprod = kb.sb("lprod", [128, 2, 64], F32)
        lsum = kb.sb("lsum", [128, 2], F32)
        posi = kb.sb("posi", [128, NT], I32)
        posf = kb.sb("posf", [128, NT], F32)
        invf = kb.sb("invf", [128, 8], F32)
        ang = kb.sb("ang", [128, NT, 8], F32)
        tk = kb.sb("tk", [128, NT, 8], F32)
        rr = kb.sb("rr", [128, NT, 8], F32)
        gmax = kb.sb("gmax", [128, 2], F32)
        psm = kb.ps("psm", [128, 512], F32)
        psg = kb.ps("psg", [128, 512], F32)

        kb.dma(sc[:], cT, w=["sc"])
        kb.dma(bcol[:], badac, w=["bcol"])
        kb.dma(ngc[:], normgc, w=["ngc"])
        kb.dma(bgate[:], badag, w=["bgate"])
        kb.dma(lams[:], lamv, w=["lams"])
        kb.dma(posi[:], posT, w=["posi"])
        kb.dma(invf[:], c_invf, w=["invf"])
        kb.act(lambda e: e.activation(out=sc[:], in_=sc[:], func=ACT.Silu), r=["sc"], w=["sc"])
        kb.dve(lambda e: e.tensor_copy(out=screp[:], in_=sc[:].unsqueeze(2).to_broadcast([128, 8, 128])), r=["sc"], w=["screp"])
        wada_v = wada.rearrange("(kc p) n -> p kc n", p=128)
        for cc in range(4):
            wt = wch[cc % 2]
            kb.dma(wt[:], wada_v[:, :, cc * 512:(cc + 1) * 512], w=[f"wch{cc%2}"])
            for jj in range(4):
                j = cc * 4 + jj
                for kc in range(8):
                    kb.pe(lambda e: e.matmul(psm[:, 2 * j:2 * j + 2], lhsT=wt[:, kc, jj * 128:(jj + 1) * 128], rhs=screp[:, kc, 0:2],
                                             start=(kc == 0), stop=(kc == 7)),
                          r=[f"wch{cc%2}", "screp"], w=["psm"], inc=(kc == 7))
        kb.dve(lambda e: e.tensor_tensor(out=modcol[:], in0=psm[:, 0:32].rearrange("p (j two) -> p j two", two=2)[:, :, 0], in1=bcol[:], op=ALU.add),
               r=["psm", "bcol"], w=["modcol"])
        kb.dve(lambda e: e.tensor_copy(out=shiftc[:], in_=modcol[:, 0:8]), r=["modcol"], w=["shiftc"])
        kb.dve(lambda e: e.scalar_tensor_tensor(out=mscale[:], in0=modcol[:, 8:16], scalar=1.0, in1=ngc[:], op0=ALU.add, op1=ALU.mult),
               r=["modcol", "ngc"], w=["mscale"])
        wt = wch[0]
        kb.dma(wt[:], wadag.rearrange("(kc p) n -> p kc n", p=128), w=["wch0"])
        for kc in range(8):
            kb.pe(lambda e: e.matmul(psg[:], lhsT=screp[:, kc, :], rhs=wt[:, kc, :], start=(kc == 0), stop=(kc == 7)),
                  r=["wch0", "screp"], w=["psg"], inc=(kc == 7))
        kb.dve(lambda e: e.tensor_tensor(out=gaterep[:], in0=psg[:], in1=bgate[:], op=ALU.add), r=["psg", "bgate"], w=["gaterep"])
        wt = wch[1]
        kb.dma(wt[:], wout.rearrange("(kc p) n -> p kc n", p=128), w=["wch1"])
        kb.dve(lambda e: e.tensor_copy(out=wobf[:], in_=wt[:]), r=["wch1"], w=["wobf"])
        kb.dve(lambda e: e.tensor_tensor(out=lprod[:, 0, :], in0=lams[:, 0, :], in1=lams[:, 1, :], op=ALU.mult), r=["lams"], w=["lprod"])
        kb.dve(lambda e: e.tensor_tensor(out=lprod[:, 1, :], in0=lams[:, 2, :], in1=lams[:, 3, :], op=ALU.mult), r=["lams", "lprod"], w=["lprod"])
        kb.dve(lambda e: e.tensor_reduce(out=lsum[:], in_=lprod[:], axis=AX.X, op=ALU.add), r=["lprod"], w=["lsum"])
        kb.act(lambda e: e.activation(out=lsum[:], in_=lsum[:], func=ACT.Exp), r=["lsum"], w=["lsum"])
        kb.dve(lambda e: e.scalar_tensor_tensor(out=neglam[:], in0=lsum[:, 1:2], scalar=-LAMBDA_INIT, in1=lsum[:, 0:1], op0=ALU.add, op1=ALU.subtract),
               r=["lsum"], w=["neglam"])
        kb.act(lambda e: e.activation(out=negA[:], in_=negA[:], func=ACT.Exp), r=CONSTK, w=["negA"])
        kb.dve(lambda e: e.tensor_scalar(out=negA[:], in0=negA[:], scalar1=-1.0, scalar2=None, op0=ALU.mult), r=["negA"], w=["negA"])
        kb.dve(lambda e: e.tensor_scalar(out=slgs[:], in0=slgs[:], scalar1=1.0 - LAMBDA_INIT, scalar2=None, op0=ALU.mult), r=CONSTK, w=["slgs"])
        kb.dve(lambda e: e.tensor_reduce(out=gmax[:], in_=gqks[:].rearrange("p (a n) -> p a n", a=2), axis=AX.X, op=ALU.max, apply_absolute_value=True),
               r=CONSTK, w=["gmax"])
        kb.dve(lambda e: e.scalar_tensor_tensor(out=negC[:], in0=gmax[:, 0:1], scalar=-8.0, in1=gmax[:, 1:2], op0=ALU.mult, op1=ALU.mult),
               r=["gmax"], w=["negC"])
        kb.dve(lambda e: e.tensor_copy(out=posf[:], in_=posi[:]), r=["posi"], w=["posf"])
        kb.dve(lambda e: e.tensor_tensor(out=ang[:], in0=posf[:].unsqueeze(2).to_broadcast([128, NT, 8]),
                                         in1=invf[:].unsqueeze(1).to_broadcast([128, NT, 8]), op=ALU.mult), r=["posf", "invf"], w=["ang"])
        for (dst, off, name) in ((sinr, 0.0, "sinr"), (cosr, 0.25, "cosr")):
            kb.dve(lambda e: e.tensor_scalar(out=tk[:], in0=ang[:], scalar1=1.0 / TWO_PI, scalar2=off, op0=ALU.mult, op1=ALU.add), r=["ang"], w=["tk"])
            kb.dve(lambda e: e.tensor_scalar(out=tk[:], in0=tk[:], scalar1=MAGIC, scalar2=None, op0=ALU.add), r=["tk"], w=["tk"])
            kb.dve(lambda e: e.tensor_scalar(out=tk[:], in0=tk[:], scalar1=-MAGIC, scalar2=None, op0=ALU.add), r=["tk"], w=["tk"])
            kb.dve(lambda e: e.scalar_tensor_tensor(out=rr[:], in0=tk[:], scalar=-CW1, in1=ang[:], op0=ALU.mult, op1=ALU.add), r=["ang", "tk"], w=["rr"])
            kb.dve(lambda e: e.scalar_tensor_tensor(out=rr[:], in0=tk[:], scalar=-CW2, in1=rr[:], op0=ALU.mult, op1=ALU.add), r=["rr", "tk"], w=["rr"])
            kb.dve(lambda e: e.scalar_tensor_tensor(out=rr[:], in0=tk[:], scalar=-CW3, in1=rr[:], op0=ALU.mult, op1=ALU.add), r=["rr", "tk"], w=["rr"])
            kb.dve(lambda e: e.tensor_scalar(out=rr[:], in0=rr[:], scalar1=off * TWO_PI, scalar2=math.pi, op0=ALU.add, op1=ALU.min), r=["rr"], w=["rr"])
            kb.dve(lambda e: e.tensor_scalar(out=rr[:], in0=rr[:], scalar1=-math.pi, scalar2=None, op0=ALU.max), r=["rr"], w=["rr"])
            kb.act(lambda e: e.activation(out=dst[:], in_=rr[:], func=ACT.Sin), r=["rr"], w=[name])
        kb.pop()
        kb.push()
        wbf = kb.sb("wbf", [128, 8, NCOL], BF16)
        wcb = [kb.sb(f"wcb{i}", [128, 8, 128], F32) for i in range(2)]
        xt = [kb.sb(f"xt{i}", [128, 1024], F32) for i in range(2)]
        xs = [kb.sb(f"xs{i}", [128, 1024], F32) for i in range(2)]
        junk = kb.sb("junk", [128, 1024], BF16)
        st = [kb.sb(f"st{i}", [128, 4], F32) for i in range(2)]
        hT = [kb.sb(f"hT{i}", [128, 8, 512], BF16) for i in range(2)]
        cbuf = [kb.sb(f"cbuf{i}", [128, 515], F32) for i in range(12)]
        acc = [kb.sb(f"acc{i}", [128, 512], F32) for i in range(2)]
        yb = [kb.sb(f"yb{i}", [128, 512], F32) for i in range(2)]
        sqb = kb.sb("sqb", [128, 512], F32)
        sdb = kb.sb("sdb", [128, 512], F32)
        ynb = [kb.sb(f"ynb{i}", [128, 512], F32) for i in range(2)]
        trb = [kb.sb(f"trb{i}", [128, 512], F32) for i in range(2)]
        sgt = [kb.sb(f"sgt{i}", [128, 512], F32) for i in range(2)]
        qk = kb.sb("qk", [128, 1024], F32)
        ssq = kb.sb("ssq", [128, 16], F32)
        xn = kb.sb("xn", [128, 1024], F32)
        qkb = kb.sb("qkb", [128, 1024], BF16)
        rt = [kb.sb(f"rt{i}", [128, 16, 8], F32) for i in range(4)]
        vb = [kb.sb(f"vb{i}", [128, 512], BF16) for i in range(2)]
        qkTg = [kb.sb("qkTg0", [128, 8, 512], BF16)]
        tmp4 = kb.sb("tmp4", [128, 4], F32)
        psT = kb.ps("psT", [128, 512], F32)
        psF = kb.ps("psF", [128, 512], F32)
        psQ = kb.ps("psQ", [128, 512], F32)
        psK = kb.ps("psK", [128, 512], F32)
        psV = kb.ps("psV", [128, 512], F32)
        psN = kb.ps("psN", [128, 512], F32)
        psXf = kb.ps("psXf", [128, 512], F32)
        psXb = kb.ps("psXb", [128, 1024], BF16)

        win_v = win.rearrange("(kc p) n -> p kc n", p=128)
        nch = (NCOL + 127) // 128
        for ci in range(nch):
            c0, c1 = ci * 128, min(NCOL, ci * 128 + 128)
            wt = wcb[ci % 2]
            kb.dma(wt[:, :, 0:c1 - c0], win_v[:, :, c0:c1], w=[f"wcb{ci%2}"])
            if ci % 2 == 0:
                kb.dve(lambda e: e.tensor_copy(out=wbf[:, :, c0:c1], in_=wt[:, :, 0:c1 - c0]), r=[f"wcb{ci%2}"], w=["wbf"])
            else:
                kb.pool(lambda e: e.tensor_copy(out=wbf[:, :, c0:c1], in_=wt[:, :, 0:c1 - c0]), r=[f"wcb{ci%2}"], w=["wbf"])
        for c in range(12):
            kb.pool(lambda e: e.memset(cbuf[c][:, 0:3], 0.0), w=[f"cbuf{c}"])

        dqT_v = dqT.rearrange("h p t -> p h t")
        dkT_v = dkT.rearrange("h p t -> p h t")
        dv_v = dv.rearrange("h t e -> t h e")
        nfm = 0
        import os as _os
        if _os.environ.get('SKIP_P1'):
            kb.pool(lambda e: e.memset(bgres[:], -0.05), w=["bgres"])
            for h in range(4):
                kb.dma(gqT[h, :, 0:512], x[0:128, 0:512], w=["gqkT"])
                kb.dma(gkT[h, :, 0:512], x[0:128, 0:512], w=["gqkT"])
                kb.dma(gktok[h, 0:512, :].rearrange("(a p) e -> p a e", p=128), x[0:128, 0:512].rearrange("p (a e) -> p a e", a=4), w=["gtok"])
                kb.dma(gvtok[h, 0:512, :].rearrange("(a p) e -> p a e", p=128), x[0:128, 0:512].rearrange("p (a e) -> p a e", a=4), w=["gtok"])
                kb.dma(sgT[h, :, 0:512], x[0:128, 0:512], w=["sgT"])
        for g in range(0 if _os.environ.get('SKIP_P1') else NG):
            hTg = hT[g % 2]
            hk = f"hT{g%2}"
            for a in range(4):
                tt = 4 * g + a
                i2 = tt % 2
                kb.dma(xt[i2][:], x[tt * 128:(tt + 1) * 128, :], w=[f"xt{i2}"])
                s = st[i2]
                kb.act(lambda e: e.activation(out=junk[:], in_=xt[i2][:], func=ACT.Square, accum_out=s[:, 0:1]), r=[f"xt{i2}"], w=["junk", f"st{i2}"])
                kb.dve(lambda e: e.tensor_scalar(out=s[:, 1:2], in0=s[:, 0:1], scalar1=1.0 / D, scalar2=EPS, op0=ALU.mult, op1=ALU.add), r=[f"st{i2}"], w=[f"st{i2}"])
                kb.act(lambda e: e.activation(out=s[:, 2:3], in_=s[:, 1:2], func=ACT.Sqrt), r=[f"st{i2}"], w=[f"st{i2}"])
                kb.dve(lambda e: e.reciprocal(out=s[:, 3:4], in_=s[:, 2:3]), r=[f"st{i2}"], w=[f"st{i2}"])
                kb.act(lambda e: e.activation(out=xs[i2][:], in_=xt[i2][:], func=ACT.Copy, scale=s[:, 3:4]), r=[f"xt{i2}", f"st{i2}"], w=[f"xs{i2}"])
                for half in range(2):
                    for q4 in range(4):
                        kc = half * 4 + q4
                        kb.pe(lambda e: e.transpose(out=psT[:, q4 * 128:(q4 + 1) * 128], in_=xs[i2][:, kc * 128:(kc + 1) * 128], identity=identf[:]),
                              r=[f"xs{i2}", "const"], w=["psT"], inc=(q4 == 3))
                    for q4 in range(4):
                        kc = half * 4 + q4
                        kb.dve(lambda e: e.tensor_scalar(out=hTg[:, kc, a * 128:(a + 1) * 128], in0=psT[:, q4 * 128:(q4 + 1) * 128],
                                                         scalar1=mscale[:, kc:kc + 1], scalar2=shiftc[:, kc:kc + 1], op0=ALU.mult, op1=ALU.add),
                               r=["psT", "mscale", "shiftc"], w=[hk])
            for c in range(NFM):
                for kc in range(8):
                    kb.pe(lambda e: e.matmul(psF[:], lhsT=wbf[:, kc, c * 128:(c + 1) * 128], rhs=hTg[:, kc, :], start=(kc == 0), stop=(kc == 7)),
                          r=["wbf", hk], w=["psF"], inc=(kc == 7))
                tsl = slice(g * 512, (g + 1) * 512)
                if c >= 12:
                    sg = sgt[c % 2]
                    kb.act(lambda e: e.activation(out=sg[:], in_=psF[:], func=ACT.Silu), r=["psF"], w=[f"sgt{c%2}"])
                    kb.dma(sgT[c - 12, :, tsl], sg[:], r=[f"sgt{c%2}"], w=["sgT"], q="pool")
                    continue
                h, kind = c // 3, c % 3
                cb = cbuf[c]
                ck = f"cbuf{c}"
                kb.act(lambda e: e.activation(out=cb[:, 3:515], in_=psF[:], func=ACT.Copy), r=["psF"], w=[ck])
                ac = acc[nfm % 2]
                ak = f"acc{nfm%2}"
                y = yb[nfm % 2]
                yk = f"yb{nfm%2}"
                kb.dve(lambda e: e.tensor_scalar(out=ac[:], in0=cb[:, 0:512], scalar1=convs[:, c, 0:1], scalar2=None, op0=ALU.mult), r=[ck, "const"], w=[ak])
                for j in range(1, 4):
                    kb.dve(lambda e: e.scalar_tensor_tensor(out=ac[:], in0=cb[:, j:j + 512], scalar=convs[:, c, j:j + 1], in1=ac[:], op0=ALU.mult, op1=ALU.add),
                           r=[ck, ak, "const"], w=[ak])
                kb.pool(lambda e: e.tensor_copy(out=cb[:, 0:3], in_=cb[:, 512:515]), r=[ck], w=[ck])
                kb.act(lambda e: e.activation(out=y[:], in_=ac[:], func=ACT.Silu), r=[ak], w=[yk])
                src, srck = y, yk
                if kind < 2:
                    kb.pool(lambda e: e.tensor_tensor(out=sqb[:], in0=y[:], in1=y[:], op=ALU.mult), r=[yk], w=["sqb"])
                    kb.pe(lambda e: e.matmul(psN[:], lhsT=onesf, rhs=sqb[:], start=True, stop=True), r=["sqb", "const"], w=["psN"])
                    kb.act(lambda e: e.activation(out=sdb[:], in_=psN[:], func=ACT.Sqrt, bias=epst[:]), r=["psN", "const"], w=["sdb"])
                    kb.dve(lambda e: e.reciprocal(out=sdb[:], in_=sdb[:]), r=["sdb"], w=["sdb"])
                    yn = ynb[nfm % 2]
                    ynk = f"ynb{nfm%2}"
                    cq = (128.0 ** -0.5) if kind == 0 else 1.0
                    kb.dve(lambda e: e.scalar_tensor_tensor(out=yn[:], in0=y[:], scalar=cq, in1=sdb[:], op0=ALU.mult, op1=ALU.mult), r=[yk, "sdb"], w=[ynk])
                    kb.dma((gqT if kind == 0 else gkT)[h, :, tsl], yn[:], r=[ynk], w=["gqkT"], q="pool")
                    src, srck = yn, ynk
                if kind >= 1:
                    for a in range(4):
                        kb.pe(lambda e: e.transpose(out=psXf[:, a * 128:(a + 1) * 128], in_=src[:, a * 128:(a + 1) * 128], identity=identf[:]),
                              r=[srck, "const"], w=["psXf"], inc=(a == 3))
                    tr = trb[nfm % 2]
                    trk = f"trb{nfm%2}"
                    kb.dve(lambda e: e.tensor_copy(out=tr[:], in_=psXf[:]), r=["psXf"], w=[trk])
                    dst = (gktok if kind == 1 else gvtok)[h, g * 512:(g + 1) * 512, :].rearrange("(a p) e -> p a e", p=128)
                    kb.dma(dst, tr[:].rearrange("p (a e) -> p a e", a=4), r=[trk], w=["gtok"], q="pool")
                nfm += 1
            qTg = qkTg[0]
            qTk = "qkTg0"
            for a in range(4):
                tt = 4 * g + a
                lhs = lambda kc: hTg[:, kc, a * 128:(a + 1) * 128]
                for (pst, pk, c0, n) in ((psQ, "psQ", TM0, 512), (psK, "psK", TM0 + 512, 512), (psV, "psV", TM0 + 1024, 512), (psXf, "psXf", TM0 + 1536, 8)):
                    for kc in range(8):
                        kb.pe(lambda e: e.matmul(pst[:, 0:n], lhsT=lhs(kc), rhs=wbf[:, kc, c0:c0 + n], start=(kc == 0), stop=(kc == 7)),
                              r=["wbf", hk], w=[pk], inc=(kc == 7))
                kb.act(lambda e: e.activation(out=bgres[:, tt, 0:4], in_=psXf[:, 0:4], func=ACT.Sigmoid), r=["psXf"], w=["bgres"])
                for h in range(4):
                    kb.act(lambda e: e.activation(out=tmp4[:, h:h + 1], in_=psXf[:, 4 + h:5 + h], func=ACT.Exp, bias=dtbs[:, h:h + 1]), r=["psXf", "const"], w=["tmp4"])
                kb.act(lambda e: e.activation(out=tmp4[:], in_=tmp4[:], func=ACT.Ln, bias=onet[:]), r=["tmp4", "const"], w=["tmp4"])
                kb.dve(lambda e: e.tensor_tensor(out=bgres[:, tt, 4:8], in0=tmp4[:], in1=negA[:], op=ALU.mult), r=["tmp4", "negA"], w=["bgres"])
                v_ = vb[a % 2]
                vk = f"vb{a%2}"
                kb.act(lambda e: e.activation(out=v_[:], in_=psV[:], func=ACT.Copy), r=["psV"], w=[vk])
                kb.dma(dv_v[tt * 128:(tt + 1) * 128, :, :], v_[:].rearrange("p (h e) -> p h e", h=4), r=[vk], w=["dv"], q="pool")
                kb.dve(lambda e: e.tensor_copy(out=qk[:, 0:512], in_=psQ[:]), r=["psQ"], w=["qk"])
                kb.act(lambda e: e.activation(out=qk[:, 512:1024], in_=psK[:], func=ACT.Copy), r=["psK"], w=["qk"])
                kb.pool(lambda e: e.tensor_tensor(out=xn[:], in0=qk[:], in1=qk[:], op=ALU.mult), r=["qk"], w=["xn"])
                kb.dve(lambda e: e.tensor_reduce(out=ssq[:], in_=xn[:].rearrange("p (s d) -> p s d", d=64), axis=AX.X, op=ALU.add), r=["xn"], w=["ssq"])
                kb.dve(lambda e: e.tensor_scalar(out=ssq[:], in0=ssq[:], scalar1=1.0 / 64, scalar2=EPS, op0=ALU.mult, op1=ALU.add), r=["ssq"], w=["ssq"])
                kb.act(lambda e: e.activation(out=ssq[:], in_=ssq[:], func=ACT.Sqrt), r=["ssq"], w=["ssq"])
                kb.dve(lambda e: e.reciprocal(out=ssq[:], in_=ssq[:]), r=["ssq"], w=["ssq"])
                qk3 = qk[:].rearrange("p (s d) -> p s d", d=64)
                xn3 = xn[:].rearrange("p (s d) -> p s d", d=64)
                kb.dve(lambda e: e.tensor_tensor(out=xn3, in0=qk3, in1=ssq[:].unsqueeze(2).to_broadcast([128, 16, 64]), op=ALU.mult), r=["qk", "ssq"], w=["xn"])
                kb.pool(lambda e: e.tensor_tensor(out=xn[:], in0=xn[:], in1=gqks[:], op=ALU.mult), r=["xn", "const"], w=["xn"])
                kb.act(lambda e: e.activation(out=qkb[:], in_=xn[:], func=ACT.Copy), r=["xn"], w=["qkb"])
                qkb3 = qkb[:].rearrange("p (s d) -> p s d", d=64)
                cb_ = cosr[:, tt, :].unsqueeze(1).to_broadcast([128, 16, 8])
                sb_ = sinr[:, tt, :].unsqueeze(1).to_broadcast([128, 16, 8])
                x1, x2 = xn3[:, :, 0:8], xn3[:, :, 8:16]
                kb.dve(lambda e: e.tensor_tensor(out=rt[0][:], in0=x1, in1=cb_, op=ALU.mult), r=["xn", "cosr"], w=["rt0"])
                kb.dve(lambda e: e.tensor_tensor(out=rt[1][:], in0=x2, in1=sb_, op=ALU.mult), r=["xn", "sinr"], w=["rt1"])
                kb.dve(lambda e: e.tensor_tensor(out=rt[2][:], in0=x2, in1=cb_, op=ALU.mult), r=["xn", "cosr"], w=["rt2"])
                kb.dve(lambda e: e.tensor_tensor(out=rt[3][:], in0=x1, in1=sb_, op=ALU.mult), r=["xn", "sinr"], w=["rt3"])
                kb.dve(lambda e: e.tensor_tensor(out=qkb3[:, :, 0:8], in0=rt[0][:], in1=rt[1][:], op=ALU.subtract), r=["rt0", "rt1", "qkb"], w=["qkb"])
                kb.dve(lambda e: e.tensor_tensor(out=qkb3[:, :, 8:16], in0=rt[2][:], in1=rt[3][:], op=ALU.add), r=["rt2", "rt3", "qkb"], w=["qkb"])
                for bi in range(8):
                    kb.pe(lambda e: e.transpose(out=psXb[:, bi * 128:(bi + 1) * 128], in_=qkb[:, bi * 128:(bi + 1) * 128], identity=identb[:]),
                          r=["qkb", "const"], w=["psXb"], inc=(bi == 7))
                kb.dve(lambda e: e.tensor_copy(out=qTg[:, :, a * 128:(a + 1) * 128], in_=psXb[:].rearrange("p (b t) -> p b t", b=8)), r=["psXb"], w=[qTk])
            kb.dma(dqT_v[:, :, g * 512:(g + 1) * 512], qTg[:, 0:4, :], r=[qTk], w=["dqT"], q="pool")
            kb.dma(dkT_v[:, :, g * 512:(g + 1) * 512], qTg[:, 4:8, :], r=[qTk], w=["dkT"], q="pool")
        kb.pop()
        kb.push()
        Q = lambda t, i: t[:, i * 128:(i + 1) * 128]
        RR = (lambda ap: ap.bitcast(mybir.dt.float32r)) if not _os.environ.get('NO_F32R') else (lambda ap: ap)
        names = ["kTr", "qTr", "kTp", "qTp", "ktk", "vtk", "Ug", "E", "Ei", "Es", "aT", "Ya", "YaT", "Yb", "YbT", "Wa", "Wb", "dg", "qdT", "kdec", "rhs2", "vn"]
        gb = {}
        for h in range(4):
            for par in range(2):
                for n in names:
                    gb[(n, h, par)] = kb.sb(f"{n}{h}{par}", [128, 128], F32)
                gb[("sm", h, par)] = kb.sb(f"sm{h}{par}", [128, 8], F32)
                gb[("smc", h, par)] = kb.sb(f"smc{h}{par}", [128, 8], F32)
                gb[("g2", h, par)] = kb.sb(f"g2{h}{par}", [128, 2], F32)
        Sst = [[kb.sb(f"S{h}{i}", [128, 128], F32) for i in range(2)] for h in range(4)]
        obuf = [kb.sb(f"obuf{h}", [128, 512], F32) for h in range(4)]
        gsq = kb.sb("gsq", [128, 512], F32)
        gsd = kb.sb("gsd", [128, 512], F32)
        gon = kb.sb("gon", [128, 512], F32)
        gsg = [kb.sb(f"gsg{i}", [128, 512], F32) for i in range(2)]
        gmx = [kb.sb(f"gmx{i}", [128, 512], BF16) for i in range(2)]
        pA = kb.ps("pA", [128, 512], F32)
        pB = kb.ps("pB", [128, 512], F32)
        pC = kb.ps("pC", [128, 512], F32)
        pKS = kb.ps("pKS", [128, 512], F32)
        pVN = kb.ps("pVN", [128, 512], F32)
        pSS = kb.ps("pSS", [128, 512], F32)
        pO = kb.ps("pO", [128, 512], F32)
        pP = kb.ps("pP", [128, 512], F32)
        for h in range(4):
            kb.dve(lambda e: e.tensor_scalar(out=RR(Sst[h][0][:]), in0=identf[:], scalar1=0.0, scalar2=None, op0=ALU.mult), r=["const"], w=[f"S{h}0"])
        scur = [0, 0, 0, 0]
        H4 = range(4)

        def pre_stages(p):
            par = p % 2
            tok = slice(p * 128, (p + 1) * 128)
            B = lambda n, h: gb[(n, h, par)]
            K_ = lambda n, h: f"{n}{h}{par}"
            gcol = lambda h: bgres[:, p, 4 + h:5 + h]
            beta = lambda h: bgres[:, p, h:h + 1]
            st = []

            def s_load():
                for h in H4:
                    kb.dma(B("kTp", h)[:], gkT[h, :, tok], r=["gqkT"], w=[K_("kTp", h)])
                    kb.dma(B("qTp", h)[:], gqT[h, :, tok], r=["gqkT"], w=[K_("qTp", h)])
                    kb.dma(B("ktk", h)[:], gktok[h, tok, :], r=["gtok"], w=[K_("ktk", h)])
                    kb.dma(B("vtk", h)[:], gvtok[h, tok, :], r=["gtok"], w=[K_("vtk", h)])
            st.append(s_load)

            def s_ug():
                for h in H4:
                    kb.pool(lambda e: e.tensor_copy(out=RR(B("kTr", h)[:]), in_=B("kTp", h)[:]), r=[K_("kTp", h)], w=[K_("kTr", h)])
                    kb.pool(lambda e: e.tensor_copy(out=RR(B("qTr", h)[:]), in_=B("qTp", h)[:]), r=[K_("qTp", h)], w=[K_("qTr", h)])
                    kb.dve(lambda e: e.tensor_scalar(out=B("Ug", h)[:], in0=ublk, scalar1=gcol(h), scalar2=None, op0=ALU.mult), r=["const", "bgres"], w=[K_("Ug", h)])
                    kb.dve(lambda e: e.tensor_copy(out=B("g2", h)[:], in_=gcol(h).to_broadcast([128, 2])), r=["bgres"], w=[K_("g2", h)])
            st.append(s_ug)

            def s_g():
                for h in H4:
                    kb.pe(lambda e: e.matmul(Q(pA, h), lhsT=B("Ug", h)[:], rhs=onesblk, start=True, stop=False), r=[K_("Ug", h), "const"], w=["pA"], inc=False)
                    kb.pe(lambda e: e.matmul(Q(pA, h), lhsT=negblk[:], rhs=B("Ug", h)[:], start=False, stop=True), r=[K_("Ug", h), "const"], w=["pA"])
                for h in H4:
                    o8 = 8 * h
                    kb.pe(lambda e: e.matmul(pB[:, o8:o8 + 2], lhsT=B("Ug", h)[:], rhs=onesf[:, 0:2], start=True, stop=True), r=[K_("Ug", h), "const"], w=["pB"], inc=False)
                    kb.pe(lambda e: e.matmul(pB[:, o8 + 2:o8 + 4], lhsT=onesblk, rhs=B("g2", h)[:], start=True, stop=True), r=[K_("g2", h), "const"], w=["pB"], inc=False)
                    kb.pe(lambda e: e.matmul(pB[:, o8 + 4:o8 + 6], lhsT=sel[0], rhs=B("g2", h)[:], start=True, stop=True), r=[K_("g2", h), "const"], w=["pB"], inc=False)
                    kb.pe(lambda e: e.matmul(pB[:, o8 + 6:o8 + 8], lhsT=sel[1], rhs=B("g2", h)[:], start=True, stop=True), r=[K_("g2", h), "const"], w=["pB"])
            st.append(s_g)

            def s_e1():
                for h in H4:
                    kb.act(lambda e: e.activation(out=B("smc", h)[:], in_=pB[:, 8 * h:8 * h + 8], func=ACT.Copy), r=["pB"], w=[K_("smc", h)])
                    kb.dve(lambda e: e.tensor_scalar(out=B("E", h)[:], in0=Q(pA, h), scalar1=-1.0, scalar2=0.0, op0=ALU.mult, op1=ALU.min), r=["pA"], w=[K_("E", h)])
            st.append(s_e1)

            def s_e2():
                for h in H4:
                    sm, smc = B("sm", h), B("smc", h)
                    kb.act(lambda e: e.activation(out=sm[:, 0:1], in_=smc[:, 0:1], func=ACT.Exp), r=[K_("smc", h)], w=[K_("sm", h)])
                    kb.act(lambda e: e.activation(out=B("E", h)[:], in_=B("E", h)[:], func=ACT.Exp), r=[K_("E", h)], w=[K_("E", h)])
                    kb.dve(lambda e: e.tensor_tensor(out=sm[:, 2:3], in0=smc[:, 2:3], in1=smc[:, 0:1], op=ALU.subtract), r=[K_("smc", h), K_("sm", h)], w=[K_("sm", h)])
            st.append(s_e2)

            def s_kk():
                for h in H4:
                    kb.pe(lambda e: e.matmul(Q(pC, h), lhsT=RR(B("kTr", h)[:]), rhs=RR(B("kTr", h)[:]), start=True, stop=True), r=[K_("kTr", h)], w=["pC"])
                for h in H4:
                    kb.pe(lambda e: e.matmul(Q(pA, h), lhsT=RR(B("kTr", h)[:]), rhs=RR(B("qTr", h)[:]), start=True, stop=True), r=[K_("kTr", h), K_("qTr", h)], w=["pA"])
            st.append(s_kk)

            def s_e3():
                for h in H4:
                    sm, smc = B("sm", h), B("smc", h)
                    kb.dve(lambda e: e.tensor_scalar(out=sm[:, 1:2], in0=sm[:, 0:1], scalar1=-1.0, scalar2=None, op0=ALU.mult), r=[K_("sm", h)], w=[K_("sm", h)])
                    kb.act(lambda e: e.activation(out=sm[:, 2:3], in_=sm[:, 2:3], func=ACT.Exp), r=[K_("sm", h)], w=[K_("sm", h)])
                    kb.act(lambda e: e.activation(out=sm[:, 4:8], in_=smc[:, 4:8], func=ACT.Exp), r=[K_("smc", h), K_("sm", h)], w=[K_("sm", h)])
                    kb.pool(lambda e: e.tensor_tensor(out=B("Ei", h)[:], in0=B("E", h)[:], in1=uincl, op=ALU.mult), r=[K_("E", h), "const"], w=[K_("Ei", h)])
                    kb.pool(lambda e: e.tensor_tensor(out=B("Es", h)[:], in0=B("E", h)[:], in1=negustr, op=ALU.mult), r=[K_("E", h), "const"], w=[K_("Es", h)])
            st.append(s_e3)

            def s_ya():
                for h in H4:
                    kb.dve(lambda e: e.scalar_tensor_tensor(out=RR(B("Ya", h)[:]), in0=Q(pC, h), scalar=beta(h), in1=B("Es", h)[:], op0=ALU.mult, op1=ALU.mult),
                           r=["pC", K_("Es", h), "bgres"], w=[K_("Ya", h)])
                for h in H4:
                    kb.dve(lambda e: e.tensor_tensor(out=RR(B("aT", h)[:]), in0=Q(pA, h), in1=B("Ei", h)[:], op=ALU.mult), r=["pA", K_("Ei", h)], w=[K_("aT", h)])
            st.append(s_ya)

            def s_tr():
                for h in H4:
                    kb.pe(lambda e: e.transpose(out=Q(pB, h), in_=B("Ya", h)[:], identity=identf[:]), r=[K_("Ya", h), "const"], w=["pB"])
            st.append(s_tr)

            def s_tr2():
                for h in H4:
                    kb.act(lambda e: e.activation(out=RR(B("YaT", h)[:]), in_=Q(pB, h), func=ACT.Copy), r=["pB"], w=[K_("YaT", h)])
                    kb.dve(lambda e: e.tensor_tensor(out=RR(B("Wa", h)[:]), in0=B("Ya", h)[:], in1=identf[:], op=ALU.add), r=[K_("Ya", h), "const"], w=[K_("Wa", h)])
            st.append(s_tr2)

            cur = ["Ya", "YaT", "Wa"]
            for k in range(1, 6):
                Y, YT, W = cur
                Yn, YTn, Wn = ("Yb", "YbT", "Wb") if Y == "Ya" else ("Ya", "YaT", "Wa")

                def s_sq(k=k, Y=Y, YT=YT):
                    if k < 5:
                        for h in H4:
                            kb.pe(lambda e: e.matmul(Q(pA, h), lhsT=RR(B(YT, h)[:]), rhs=RR(B(Y, h)[:]), start=True, stop=True), r=[K_(Y, h), K_(YT, h)], w=["pA"])
                    for h in H4:
                        kb.pe(lambda e: e.matmul(Q(pC, h), lhsT=RR(B(Y, h)[:]), rhs=RR(B(YT, h)[:]), start=True, stop=True), r=[K_(Y, h), K_(YT, h)], w=["pC"])
                st.append(s_sq)

                def s_ev(k=k, Yn=Yn, YTn=YTn):
                    for h in H4:
                        if k < 5:
                            kb.act(lambda e: e.activation(out=RR(B(Yn, h)[:]), in_=Q(pA, h), func=ACT.Copy), r=["pA"], w=[K_(Yn, h)])
                        kb.dve(lambda e: e.tensor_copy(out=RR(B(YTn, h)[:]), in_=Q(pC, h)), r=["pC"], w=[K_(YTn, h)])
                st.append(s_ev)

                def s_w(YTn=YTn, W=W):
                    for h in H4:
                        kb.pe(lambda e: e.matmul(Q(pB, h), lhsT=RR(B(YTn, h)[:]), rhs=RR(B(W, h)[:]), start=True, stop=True), r=[K_(YTn, h), K_(W, h)], w=["pB"])
                st.append(s_w)

                def s_w2(W=W, Wn=Wn):
                    for h in H4:
                        kb.dve(lambda e: e.tensor_tensor(out=RR(B(Wn, h)[:]), in0=Q(pB, h), in1=B(W, h)[:], op=ALU.add), r=["pB", K_(W, h)], w=[K_(Wn, h)])
                st.append(s_w2)
                cur = [Yn, YTn, Wn]
            for h in H4:
                gb[("Wfin", h, par)] = cur[2]

            def s_dg():
                for h in H4:
                    sm = B("sm", h)
                    kb.dve(lambda e: e.tensor_scalar(out=B("dg", h)[:], in0=identf[:], scalar1=sm[:, 0:1], scalar2=None, op0=ALU.mult), r=["const", K_("sm", h)], w=[K_("dg", h)])
                    kb.act(lambda e: e.activation(out=RR(B("kdec", h)[:]), in_=B("ktk", h)[:], func=ACT.Copy, scale=sm[:, 2:3]), r=[K_("ktk", h), K_("sm", h)], w=[K_("kdec", h)])
            st.append(s_dg)

            def s_eg():
                for h in H4:
                    kb.pe(lambda e: e.matmul(Q(pA, h), lhsT=onesf, rhs=B("dg", h)[:], start=True, stop=True), r=[K_("dg", h), "const"], w=["pA"])
            st.append(s_eg)

            def s_qd():
                for h in H4:
                    kb.dve(lambda e: e.tensor_tensor(out=RR(B("qdT", h)[:]), in0=Q(pA, h), in1=B("qTp", h)[:], op=ALU.mult), r=["pA", K_("qTp", h)], w=[K_("qdT", h)])
            st.append(s_qd)
            return st

        def scan_stages(p):
            par = p % 2
            B = lambda n, h: gb[(n, h, par)]
            K_ = lambda n, h: f"{n}{h}{par}"
            st = []
            for c in range(2):
                hs = slice(64 * c, 64 * c + 64)

                def s1(hs=hs):
                    for h in H4:
                        S, Sk = Sst[h][scur[h]], f"S{h}{scur[h]}"
                        kb.pe(lambda e: e.matmul(Q(pKS, h), lhsT=RR(B("kTr", h)[:]), rhs=RR(S[:]), start=True, stop=True), r=[K_("kTr", h), Sk], w=["pKS"])
                st.append(s1)

                def s2(hs=hs):
                    for h in H4:
                        kb.dve(lambda e: e.scalar_tensor_tensor(out=RR(B("rhs2", h)[hs, :]), in0=Q(pKS, h)[hs, :], scalar=B("sm", h)[hs, 1:2], in1=B("vtk", h)[hs, :],
                                                                op0=ALU.mult, op1=ALU.add), r=["pKS", K_("sm", h), K_("vtk", h)], w=[K_("rhs2", h)])
                st.append(s2)

                def s3(hs=hs):
                    for h in H4:
                        Wf = gb[("Wfin", h, par)]
                        kb.pe(lambda e: e.matmul(Q(pVN, h), lhsT=RR(B(Wf, h)[hs, :]), rhs=RR(B("rhs2", h)[hs, :]), start=True, stop=True), r=[K_(Wf, h), K_("rhs2", h)], w=["pVN"])
                st.append(s3)

                def s4(hs=hs):
                    for h in H4:
                        kb.dve(lambda e: e.tensor_scalar(out=RR(B("vn", h)[hs, :]), in0=Q(pVN, h)[hs, :], scalar1=bgres[hs, p, h:h + 1], scalar2=None, op0=ALU.mult),
                               r=["pVN", "bgres"], w=[K_("vn", h)])
                st.append(s4)

                def s5(hs=hs, c=c):
                    for h in H4:
                        S, Sk = Sst[h][scur[h]], f"S{h}{scur[h]}"
                        oc = slice(h * 128 + 64 * c, h * 128 + 64 * c + 64)
                        kb.pe(lambda e: e.matmul(pO[:, oc], lhsT=RR(S[:]), rhs=RR(B("qdT", h)[:, hs]), start=True, stop=False), r=[Sk, K_("qdT", h)], w=["pO"], inc=False)
                        kb.pe(lambda e: e.matmul(pO[:, oc], lhsT=RR(B("vn", h)[hs, :]), rhs=RR(B("aT", h)[hs, hs]), start=False, stop=True), r=[K_("vn", h), K_("aT", h)], w=["pO"])
                        kb.pe(lambda e: e.matmul(Q(pSS, h), lhsT=RR(B("kdec", h)[hs, :]), rhs=RR(B("vn", h)[hs, :]), start=True, stop=True), r=[K_("kdec", h), K_("vn", h)], w=["pSS"])
                st.append(s5)

                def s6(c=c):
                    for h in H4:
                        S, Sk = Sst[h][scur[h]], f"S{h}{scur[h]}"
                        Sn, Snk = Sst[h][1 - scur[h]], f"S{h}{1-scur[h]}"
                        kb.dve(lambda e: e.scalar_tensor_tensor(out=RR(Sn[:]), in0=S[:], scalar=B("sm", h)[:, 4 + 2 * c:5 + 2 * c], in1=Q(pSS, h), op0=ALU.mult, op1=ALU.add),
                               r=[Sk, K_("sm", h), "pSS"], w=[Snk])
                        scur[h] = 1 - scur[h]
                st.append(s6)

            def s7():
                for h in H4:
                    kb.act(lambda e: e.activation(out=obuf[h][:, (p % 4) * 128:(p % 4 + 1) * 128], in_=Q(pO, h), func=ACT.Copy), r=["pO"], w=[f"obuf{h}"])
            st.append(s7)
            return st

        npost = [0]

        def post(g):
            tsl = slice(g * 512, (g + 1) * 512)
            for h in H4:
                i2 = npost[0] % 2
                kb.dma(gsg[i2][:], sgT[h, :, tsl], r=["sgT"], w=[f"gsg{i2}"])
                kb.pool(lambda e: e.tensor_tensor(out=gsq[:], in0=obuf[h][:], in1=obuf[h][:], op=ALU.mult), r=[f"obuf{h}"], w=["gsq"])
                kb.pe(lambda e: e.matmul(pP[:], lhsT=onesf, rhs=gsq[:], start=True, stop=True), r=["gsq", "const"], w=["pP"])
                kb.act(lambda e: e.activation(out=gsd[:], in_=pP[:], func=ACT.Sqrt, bias=epst[:], scale=1.0 / 128), r=["pP", "const"], w=["gsd"])
                kb.dve(lambda e: e.reciprocal(out=gsd[:], in_=gsd[:]), r=["gsd"], w=["gsd"])
                kb.dve(lambda e: e.tensor_tensor(out=gon[:], in0=obuf[h][:], in1=gsd[:], op=ALU.mult), r=[f"obuf{h}", "gsd"], w=["gon"])
                kb.dve(lambda e: e.scalar_tensor_tensor(out=gmx[i2][:], in0=gon[:], scalar=gngs[:, 0:1], in1=gsg[i2][:], op0=ALU.mult, op1=ALU.mult),
                       r=["gon", "const", f"gsg{i2}"], w=[f"gmx{i2}"])
                kb.dma(mixT[h, :, tsl], gmx[i2][:], r=[f"gmx{i2}"], w=["mixT"], q="pool")
                npost[0] += 1

        for f in pre_stages(0):
            f()
        for p in range(NT):
            sc_ = scan_stages(p)
            pr_ = pre_stages(p + 1) if p + 1 < NT else []
            ratio = (len(pr_) + len(sc_) - 1) // len(sc_)
            pi = 0
            for f in sc_:
                f()
                for _ in range(ratio):
                    if pi < len(pr_):
                        pr_[pi]()
                        pi += 1
            while pi < len(pr_):
                pr_[pi]()
                pi += 1
            if p % 4 == 3:
                post(p // 4)
        kb.pop()
        if upto <= 2:
            kb.finish()
            return nc, kb
        kb.push()
        akT = [kb.sb(f"akT{i}", [128, T], BF16) for i in range(2)]
        aqT = [kb.sb(f"aqT{i}", [128, T], BF16) for i in range(2)]
        avv = [kb.sb(f"avv{i}", [128, NT, 128], BF16) for i in range(2)]
        pTb = [[kb.sb(f"pT{i}{j}", [128, 512], BF16) for j in range(2)] for i in range(2)]
        arl = [kb.sb(f"arl{i}", [128, 512], F32) for i in range(2)]
        lacc = [kb.sb(f"lacc{i}", [128, 512], F32) for i in range(2)]
        ao = [kb.sb(f"ao{i}", [128, 512], F32) for i in range(2)]
        aod = kb.sb("aod", [128, 512], F32)
        asq = kb.sb("asq", [128, 512], F32)
        asd = kb.sb("asd", [128, 512], F32)
        aon = kb.sb("aon", [128, 512], F32)
        asg = [kb.sb(f"asg{i}", [128, 512], F32) for i in range(2)]
        amx = [kb.sb(f"amx{i}", [128, 512], BF16) for i in range(2)]
        pS = [[kb.ps(f"pS{i}{j}", [128, 512], F32) for j in range(2)] for i in range(2)]
        pOa = [kb.ps(f"pOa{i}", [128, 512], F32) for i in range(2)]
        pLa = [kb.ps(f"pLa{i}", [128, 512], F32) for i in range(2)]
        nstep = 0
        for h in range(4):
            hb = h % 2
            for part in range(2):
                sl = slice(part * (T // 2), (part + 1) * (T // 2))
                kb.dma(akT[hb][:, sl], dkT[h, :, sl], r=["dkT"], w=[f"akT{hb}"])
                kb.dma(aqT[hb][:, sl], dqT[h, :, sl], r=["dqT"], w=[f"aqT{hb}"])
            dvh = dv[h].rearrange("(t p) e -> p t e", p=128)
            for part in range(4):
                kb.dma(avv[hb][:, part * 16:(part + 1) * 16, :], dvh[:, part * 16:(part + 1) * 16, :], r=["dv"], w=[f"avv{hb}"])
            steps = [(g, kt) for g in range(NG) for kt in range(4 * g + 4)]

            def emit_S(n):
                g, kt = steps[n]
                j = kt - 4 * g
                qlo = 128 * max(j, 0)
                qs = slice(qlo, 512)
                sb2 = n % 2
                for i in range(2):
                    rows = slice(64 * i, 64 * i + 64)
                    kb.pe(lambda e: e.matmul(pS[i][sb2][:, qs], lhsT=akT[hb][rows, kt * 128:(kt + 1) * 128], rhs=aqT[hb][rows, g * 512 + qlo:(g + 1) * 512],
                                             start=True, stop=True), r=[f"akT{hb}", f"aqT{hb}"], w=[f"pS{i}{sb2}"])

            def emit_rest(n):
                g, kt = steps[n]
                nk = 4 * g + 4
                j = kt - 4 * g
                qlo = 128 * max(j, 0)
                qs = slice(qlo, 512)
                sb2 = n % 2
                for i in range(2):
                    pt = pTb[i][sb2]
                    kb.act(lambda e: e.activation(out=pt[:, qs], in_=pS[i][sb2][:, qs], func=ACT.Exp, bias=negC[:], scale=0.125),
                           r=[f"pS{i}{sb2}", "negC"], w=[f"pT{i}{sb2}"])
                    if j >= 0:
                        kb.pool(lambda e: e.memset(pt[64:128, qlo:qlo + 64], 0.0), r=[], w=[f"pT{i}{sb2}"])
                for i in range(2):
                    pt = pTb[i][sb2]
                    kb.pe(lambda e: e.matmul(pOa[i][:, qs], lhsT=avv[hb][:, kt, :], rhs=pt[:, qs], start=(kt == 0), stop=(kt == nk - 1)),
                          r=[f"avv{hb}", f"pT{i}{sb2}"], w=[f"pOa{i}"])
                    ae = kb.pool if i == 0 else kb.dve
                    if kt == 0:
                        ae(lambda e: e.tensor_copy(out=lacc[i][:, qs], in_=pt[:, qs]), r=[f"pT{i}{sb2}"], w=[f"lacc{i}"])
                    else:
                        ae(lambda e: e.tensor_tensor(out=lacc[i][:, qs], in0=lacc[i][:, qs], in1=pt[:, qs], op=ALU.add), r=[f"pT{i}{sb2}", f"lacc{i}"], w=[f"lacc{i}"])
                if kt != nk - 1:
                    return
                tsl = slice(g * 512, (g + 1) * 512)
                i2 = g % 2
                kb.dma(asg[i2][:], sgT[4 + h, :, tsl], r=["sgT"], w=[f"asg{i2}"])
                for i in range(2):
                    kb.pe(lambda e: e.matmul(pLa[i][:], lhsT=onesf, rhs=lacc[i][:], start=True, stop=True), r=["const", f"lacc{i}"], w=[f"pLa{i}"])
                for i in range(2):
                    kb.dve(lambda e: e.reciprocal(out=arl[i][:], in_=pLa[i][:]), r=[f"pLa{i}"], w=[f"arl{i}"])
                    kb.dve(lambda e: e.tensor_tensor(out=ao[i][:], in0=pOa[i][:], in1=arl[i][:], op=ALU.mult), r=[f"pOa{i}", f"arl{i}"], w=[f"ao{i}"])
                kb.dve(lambda e: e.scalar_tensor_tensor(out=aod[:], in0=ao[1][:], scalar=neglam[:, 0:1], in1=ao[0][:], op0=ALU.mult, op1=ALU.add),
                       r=["ao0", "ao1", "neglam"], w=["aod"])
                kb.pool(lambda e: e.tensor_tensor(out=asq[:], in0=aod[:], in1=aod[:], op=ALU.mult), r=["aod"], w=["asq"])
                pN_ = pLa[0]
                kb.pe(lambda e: e.matmul(pN_[:], lhsT=onesf, rhs=asq[:], start=True, stop=True), r=["asq", "const"], w=["pLa0"])
                kb.act(lambda e: e.activation(out=asd[:], in_=pN_[:], func=ACT.Sqrt, bias=epst[:], scale=1.0 / 128), r=["pLa0", "const"], w=["asd"])
                kb.dve(lambda e: e.reciprocal(out=asd[:], in_=asd[:]), r=["asd"], w=["asd"])
                kb.dve(lambda e: e.tensor_tensor(out=aon[:], in0=aod[:], in1=asd[:], op=ALU.mult), r=["aod", "asd"], w=["aon"])
                kb.dve(lambda e: e.scalar_tensor_tensor(out=amx[i2][:], in0=aon[:], scalar=slgs[:, 0:1], in1=asg[i2][:], op0=ALU.mult, op1=ALU.mult),
                       r=["aon", "slgs", f"asg{i2}"], w=[f"amx{i2}"])
                kb.dma(mixT[4 + h, :, tsl], amx[i2][:], r=[f"amx{i2}"], w=["mixT"], q="pool")

            emit_S(0)
            for n in range(len(steps)):
                if n + 1 < len(steps):
                    emit_S(n + 1)
                emit_rest(n)
        kb.pop()
        if upto <= 3:
            kb.finish()
            return nc, kb
        kb.push()
        mt = [kb.sb(f"mt{i}", [128, 8, 512], BF16) for i in range(2)]
        xr = [kb.sb(f"xr{i}", [128, 512], F32) for i in range(2)]
        y1 = [kb.sb(f"y1{i}", [128, 512], F32) for i in range(2)]
        y2 = [kb.sb(f"y2{i}", [128, 512], F32) for i in range(2)]
        pY = [kb.ps(f"pY{i}", [128, 512], F32) for i in range(2)]
        mixT_v = mixT.rearrange("m p t -> p m t")
        for g in range(NG):
            m_ = mt[g % 2]
            mk = f"mt{g%2}"
            kb.dma(m_[:], mixT_v[:, :, g * 512:(g + 1) * 512], r=["mixT"], w=[mk])
            for a in range(4):
                tt = 4 * g + a
                i2 = tt % 2
                kb.dma(xr[i2][:], xres[tt * 128:(tt + 1) * 128, :], w=[f"xr{i2}"])
                for m in range(8):
                    kb.pe(lambda e: e.matmul(pY[i2][:], lhsT=m_[:, m, a * 128:(a + 1) * 128], rhs=wobf[:, m, :], start=(m == 0), stop=(m == 7)),
                          r=[mk, "wobf"], w=[f"pY{i2}"], inc=(m == 7))
                kb.dve(lambda e: e.tensor_tensor(out=y1[i2][:], in0=pY[i2][:], in1=gaterep[:], op=ALU.mult), r=[f"pY{i2}", "gaterep"], w=[f"y1{i2}"])
                kb.pool(lambda e: e.tensor_tensor(out=y2[i2][:], in0=y1[i2][:], in1=xr[i2][:], op=ALU.add), r=[f"y1{i2}", f"xr{i2}"], w=[f"y2{i2}"])
                kb.dma(out[tt * 128:(tt + 1) * 128, :], y2[i2][:], r=[f"y2{i2}"], w=["out"], q="pool")
        kb.pop()
        kb.finish()
    return nc, kb


def _consts():
    i = np.arange(128)
    same = (i[:, None] // 64) == (i[None, :] // 64)
    onesf = np.ones((128, 128), np.float32)
    ublk = (same & (i[:, None] <= i[None, :])).astype(np.float32)
    uincl = ublk.copy()
    negustr = -(same & (i[None, :] > i[:, None])).astype(np.float32)
    onesblk = same.astype(np.float32)
    sel0 = np.repeat((i < 64).astype(np.float32)[:, None], 128, 1)
    sel1 = np.repeat((i >= 64).astype(np.float32)[:, None], 128, 1)
    masks = np.stack([onesf, ublk, uincl, negustr, onesblk, sel0, sel1], 1).astype(np.float32)
    invf = (np.float32(500000.0) ** (-np.arange(0, 16, 2, dtype=np.float32) / np.float32(16))).astype(np.float32)
    return {
        "c_identf": np.eye(128, dtype=np.float32),
        "c_identb": np.eye(128, dtype=np.float32).astype(ml_dtypes.bfloat16),
        "c_onesb": np.ones((128, 128), np.float32).astype(ml_dtypes.bfloat16),
        "c_masks": np.ascontiguousarray(masks),
        "c_negblk": -onesblk,
        "c_invf": np.ascontiguousarray(np.broadcast_to(invf[None, :], (128, 8))),
    }


def _rep(v, n=128):
    return np.ascontiguousarray(np.broadcast_to(np.asarray(v, np.float32).reshape(1, -1), (n, np.asarray(v).size)))


def prep_core(inp, b, half):
    f = lambda a: np.ascontiguousarray(np.asarray(a, np.float32))
    w_in = np.asarray(inp["w_in"][0], np.float32)
    off = np.cumsum([0, 512, 512, 512, 4, 4, 512, 512, 512, 512, 512])
    aq, ak, av, abeta, adec, agate, bq, bk, bv, bgate = [np.arange(off[i], off[i + 1]) for i in range(10)]
    cols = []
    for h in range(4):
        cols += [aq[h * 128:(h + 1) * 128], ak[h * 128:(h + 1) * 128], av[h * 128:(h + 1) * 128]]
    cols += [agate, bgate, bq, bk, bv, abeta, adec]
    cols = np.concatenate(cols)
    conv_w = np.asarray(inp["conv_w"][0], np.float32)
    convc = np.zeros((128, 12, 4), np.float32)
    for h in range(4):
        for kind in range(3):
            ch = kind * 512 + h * 128 + np.arange(128)
            convc[:, 3 * h + kind, :] = conv_w[:, ch].T
    w_ada = np.asarray(inp["w_ada"][0], np.float32)
    b_ada = np.asarray(inp["b_ada"][0], np.float32)
    gsl = slice(2048 + half * 512, 2048 + (half + 1) * 512)
    d = {
        "x": f(inp["x"][b]),
        "xres": f(inp["x"][b][:, half * 512:(half + 1) * 512]),
        "cT": f(np.asarray(inp["c"][b]).reshape(8, 128).T),
        "posT": np.ascontiguousarray(np.asarray(inp["positions"][b], np.int32).reshape(NT, 128).T),
        "wada": f(w_ada[:, 0:2048]),
        "wadag": f(w_ada[:, gsl]),
        "badac": f(b_ada[0:2048].reshape(16, 128).T),
        "badag": _rep(b_ada[gsl]),
        "normgc": f(np.asarray(inp["norm_g"][0]).reshape(8, 128).T),
        "win": f(w_in[:, cols]),
        "convc": convc,
        "alog": _rep(inp["a_log"][0]),
        "dtb": _rep(inp["dt_bias"][0]),
        "gng": f(np.asarray(inp["gdn_norm_g"][0]).reshape(128, 1)),
        "slg": f(np.asarray(inp["subln_g"][0]).reshape(128, 1)),
        "gqk": _rep(np.concatenate([np.tile(np.asarray(inp["q_norm_g"][0]), 8), np.tile(np.asarray(inp["k_norm_g"][0]), 8)])),
        "lamv": np.ascontiguousarray(np.broadcast_to(np.stack([np.asarray(inp[k][0], np.float32) for k in
                                                               ("lambda_q1", "lambda_k1", "lambda_q2", "lambda_k2")])[None], (128, 4, 64))),
        "wout": f(np.asarray(inp["w_out"][0])[:, half * 512:(half + 1) * 512]),
    }
    d.update(_consts())
    return d


_NC_CACHE = {}


def kernel(**inputs):
    if "nc" not in _NC_CACHE:
        _NC_CACHE["nc"] = build_program()[0]
    nc = _NC_CACHE["nc"]
    in_maps = [prep_core(inputs, core // 2, core % 2) for core in range(8)]
    res = run_bass_kernel_spmd(nc, in_maps, core_ids=list(range(8)))
    out = np.empty((4, T, D), np.float32)
    for core in range(8):
        out[core // 2, :, (core % 2) * 512:(core % 2 + 1) * 512] = res.results[core]["out"]
    return out
```

```python
import math
from contextlib import ExitStack

import numpy as np
import ml_dtypes

import concourse.bass as bass
import concourse.mybir as mybir
from concourse.bass_utils import run_bass_kernel_spmd

F32 = mybir.dt.float32
BF16 = mybir.dt.bfloat16
I32 = mybir.dt.int32
ACT = mybir.ActivationFunctionType
ALU = mybir.AluOpType
AX = mybir.AxisListType


class KB:
    NDMA = 40

    def __init__(self, nc):
        self.nc = nc
        self.eng = {"pe": nc.tensor, "act": nc.scalar, "dve": nc.vector, "pool": nc.gpsimd, "sp": nc.sync}
        self.stack = ExitStack()
        self.sem = {}
        self.cnt = {k: 0 for k in self.eng}
        self.waited = {k: {} for k in self.eng}
        self.last_w = {}
        self.readers = {}
        self.dsem = []
        self.dcnt = []
        self.dnext = 0
        self.nins = 0
        self.phase = 0
        self.excl = set()

    def ctx(self):
        for k in ("pe", "act", "dve", "pool"):
            self.sem[k] = self.stack.enter_context(self.nc.semaphore("s_" + k))
        for i in range(self.NDMA):
            self.dsem.append(self.stack.enter_context(self.nc.semaphore(f"s_dma{i}")))
            self.dcnt.append(0)
        return self.stack

    def sb(self, name, shape, dtype):
        return self.stack.enter_context(self.nc.sbuf_tensor(f"{name}_{self.phase}", shape, dtype))

    def ps(self, name, shape, dtype):
        self.excl.add(name)
        return self.stack.enter_context(self.nc.psum_tensor(f"{name}_{self.phase}", shape, dtype))

    def _handle(self, s):
        return self.dsem[s[1]] if isinstance(s, tuple) else self.sem[s]

    def _wait(self, en, deps):
        e = self.eng[en]
        need = {}
        for s, v in deps:
            if s == "pe" and en == "pe":
                continue
            if v > need.get(s, 0):
                need[s] = v
        for s, v in need.items():
            if self.waited[en].get(s, 0) < v:
                e.wait_ge(self._handle(s), v)
                self.waited[en][s] = v
                self.nins += 1

    def _deps(self, r, w):
        d = []
        for k in r:
            if k in self.last_w:
                d.append(self.last_w[k])
        for k in w:
            if k in self.last_w:
                d.append(self.last_w[k])
            d.extend(self.readers.get(k, ()))
        return d

    def _record(self, r, w, tag):
        for k in r:
            self.readers.setdefault(k, []).append(tag)
        for k in w:
            self.last_w[k] = tag
            self.readers[k] = []

    def _skip(self):
        import os
        lim = os.environ.get("KB_LIMIT")
        if not lim:
            return False
        ph, n = (int(v) for v in lim.split(":"))
        if self.phase != ph:
            return False
        self.pops = getattr(self, "pops", 0) + 1
        return self.pops > n

    def op(self, en, fn, r=(), w=(), inc=True):
        if self._skip():
            return None
        w = list(w) + [k for k in r if k in self.excl and k not in w]
        self._wait(en, self._deps(r, w))
        ins = fn(self.eng[en])
        self.nins += 1
        if inc:
            ins.then_inc(self.sem[en], 1)
            self.cnt[en] += 1
            tag = (en, self.cnt[en])
        else:
            tag = (en, self.cnt[en] + 1)
        self._record(r, w, tag)
        return ins

    def pe(self, fn, r=(), w=(), inc=True):
        return self.op("pe", fn, r, w, inc)

    def act(self, fn, r=(), w=()):
        return self.op("act", fn, r, w)

    def dve(self, fn, r=(), w=()):
        return self.op("dve", fn, r, w)

    def pool(self, fn, r=(), w=()):
        return self.op("pool", fn, r, w)

    def dma(self, out, in_, r=(), w=(), q="sp", **kw):
        if self._skip():
            return None
        deps = self._deps(r, w)
        i = self.dnext
        self.dnext = (self.dnext + 1) % self.NDMA
        if self.dcnt[i] > 0:
            deps.append((("d", i), 16 * self.dcnt[i]))
        self._wait(q, deps)
        ins = self.eng[q].dma_start(out=out, in_=in_, **kw)
        ins.then_inc(self.dsem[i], 16)
        self.nins += 1
        self.dcnt[i] += 1
        self._record(r, w, (("d", i), 16 * self.dcnt[i]))
        return ins

    def finish(self):
        deps = [(k, self.cnt[k]) for k in ("pe", "act", "dve", "pool") if self.cnt[k] > 0]
        deps += [(("d", i), 16 * c) for i, c in enumerate(self.dcnt) if c > 0]
        self._wait("sp", deps)

    def barrier(self):
        deps = [(k, self.cnt[k]) for k in ("pe", "act", "dve", "pool") if self.cnt[k] > 0]
        deps += [(("d", i), 16 * c) for i, c in enumerate(self.dcnt) if c > 0]
        for en in ("pe", "act", "dve", "pool", "sp"):
            self._wait(en, [d for d in deps if not (d[0] == "pe" and en == "pe")] + ([("pe", self.cnt["pe"])] if False else []))
        self.last_w = {}
        self.readers = {}

    def push(self):
        self._saved = getattr(self, "_saved", [])
        self._saved.append(self.stack)
        self.stack = ExitStack()
        self.stack.__enter__()
        self.phase += 1

    def pop(self):
        self.barrier()
        self.stack.__exit__(None, None, None)
        self.stack = self._saved.pop()


T = 8192
D = 1024
NT = T // 128
NG = T // 512
NCOL = 4104
EPS = 1e-6
LAMBDA_INIT = 0.8 - 0.6 * math.exp(0.0)
NFM = 20
TM0 = NFM * 128
TWO_PI = 2.0 * math.pi
MAGIC = 12582912.0
CW1 = 6.28125
CW2 = 0.0019350051879882812
CW3 = TWO_PI - CW1 - CW2


def build_program(dbg=(), upto=9):
    nc = bass.Bass("TRN2", target_bir_lowering=False)

    def din(name, shape, dt=F32):
        return nc.dram_tensor(name, list(shape), dt, kind="ExternalInput").ap()

    def dscr(name, shape, dt=F32):
        return nc.dram_tensor(name, list(shape), dt, kind="ExternalOutput" if name in dbg else "Internal").ap()

    x = din("x", [T, D])
    xres = din("xres", [T, 512])
    cT = din("cT", [128, 8])
    posT = din("posT", [128, NT], I32)
    wada = din("wada", [D, 2048])
    wadag = din("wadag", [D, 512])
    badac = din("badac", [128, 16])
    badag = din("badag", [128, 512])
    normgc = din("normgc", [128, 8])
    win = din("win", [D, NCOL])
    convc = din("convc", [128, 12, 4])
    alog = din("alog", [128, 4])
    dtb = din("dtb", [128, 4])
    gng = din("gng", [128, 1])
    slg = din("slg", [128, 1])
    gqk = din("gqk", [128, 1024])
    lamv = din("lamv", [128, 4, 64])
    wout = din("wout", [D, 512])
    c_identf = din("c_identf", [128, 128])
    c_identb = din("c_identb", [128, 128], BF16)
    c_onesb = din("c_onesb", [128, 128], BF16)
    c_masks = din("c_masks", [128, 7, 128])
    c_negblk = din("c_negblk", [128, 128])
    c_invf = din("c_invf", [128, 8])
    out = nc.dram_tensor("out", [T, 512], F32, kind="ExternalOutput").ap()

    gqT = dscr("gqT", [4, 128, T])
    gkT = dscr("gkT", [4, 128, T])
    gktok = dscr("gktok", [4, T, 128])
    gvtok = dscr("gvtok", [4, T, 128])
    sgT = dscr("sgT", [8, 128, T])
    dqT = dscr("dqT", [4, 128, T], BF16)
    dkT = dscr("dkT", [4, 128, T], BF16)
    dv = dscr("dv", [4, T, 128], BF16)
    mixT = dscr("mixT", [8, 128, T], BF16)

    kb = KB(nc)
    with kb.ctx():
        identf = kb.sb("identf", [128, 128], F32)
        identb = kb.sb("identb", [128, 128], BF16)
        onesb = kb.sb("onesb", [128, 128], BF16)
        masks = kb.sb("masks", [128, 7, 128], F32)
        negblk = kb.sb("negblk", [128, 128], F32)
        onesf, ublk, uincl, negustr, onesblk = (masks[:, i, :] for i in range(5))
        sel = [masks[:, 5, :], masks[:, 6, :]]
        bgres = kb.sb("bgres", [128, NT, 8], F32)
        cosr = kb.sb("cosr", [128, NT, 8], F32)
        sinr = kb.sb("sinr", [128, NT, 8], F32)
        gaterep = kb.sb("gaterep", [128, 512], F32)
        mscale = kb.sb("mscale", [128, 8], F32)
        shiftc = kb.sb("shiftc", [128, 8], F32)
        negA = kb.sb("negA", [128, 4], F32)
        dtbs = kb.sb("dtbs", [128, 4], F32)
        gngs = kb.sb("gngs", [128, 1], F32)
        slgs = kb.sb("slgs", [128, 1], F32)
        neglam = kb.sb("neglam", [128, 1], F32)
        negC = kb.sb("negC", [128, 1], F32)
        gqks = kb.sb("gqks", [128, 1024], F32)
        convs = kb.sb("convs", [128, 12, 4], F32)
        wobf = kb.sb("wobf", [128, 8, 512], BF16)

        epst = kb.sb("epst", [128, 1], F32)
        onet = kb.sb("onet", [128, 1], F32)
        CONSTK = ["const"]
        kb.pool(lambda e: e.memset(epst[:], EPS), w=CONSTK)
        kb.pool(lambda e: e.memset(onet[:], 1.0), w=CONSTK)
        for dst, src in ((identf, c_identf), (identb, c_identb), (onesb, c_onesb), (masks, c_masks), (negblk, c_negblk),
                         (negA, alog), (dtbs, dtb), (gngs, gng), (slgs, slg), (gqks, gqk), (convs, convc)):
            kb.dma(dst[:], src, w=CONSTK)

        kb.push()
        sc = kb.sb("sc", [128, 8], F32)
        screp = kb.sb("screp", [128, 8, 128], F32)
        wch = [kb.sb(f"wch{i}", [128, 8, 512], F32) for i in range(2)]
        modcol = kb.sb("modcol", [128, 16], F32)
        bcol = kb.sb("bcol", [128, 16], F32)
        ngc = kb.sb("ngc", [128, 8], F32)
        bgate = kb.sb("bgate", [128, 512], F32)
        lams = kb.sb("lams", [128, 4, 64], F32)
        lprod = kb.sb("lprod", [128, 2, 64], F32)
        lsum = kb.sb("lsum", [128, 2], F32)
        posi = kb.sb("posi", [128, NT], I32)
        posf = kb.sb("posf", [128, NT], F32)
        invf = kb.sb("invf", [128, 8], F32)
        ang = kb.sb("ang", [128, NT, 8], F32)
        tk = kb.sb("tk", [128, NT, 8], F32)
        rr = kb.sb("rr", [128, NT, 8], F32)
        gmax = kb.sb("gmax", [128, 2], F32)
        psm = kb.ps("psm", [128, 512], F32)
        psg = kb.ps("psg", [128, 512], F32)

        kb.dma(sc[:], cT, w=["sc"])
        kb.dma(bcol[:], badac, w=["bcol"])
        kb.dma(ngc[:], normgc, w=["ngc"])
        kb.dma(bgate[:], badag, w=["bgate"])
        kb.dma(lams[:], lamv, w=["lams"])
        kb.dma(posi[:], posT, w=["posi"])
        kb.dma(invf[:], c_invf, w=["invf"])
        kb.act(lambda e: e.activation(out=sc[:], in_=sc[:], func=ACT.Silu), r=["sc"], w=["sc"])
        kb.dve(lambda e: e.tensor_copy(out=screp[:], in_=sc[:].unsqueeze(2).to_broadcast([128, 8, 128])), r=["sc"], w=["screp"])
        wada_v = wada.rearrange("(kc p) n -> p kc n", p=128)
        for cc in range(4):
            wt = wch[cc % 2]
            kb.dma(wt[:], wada_v[:, :, cc * 512:(cc + 1) * 512], w=[f"wch{cc%2}"])
            for jj in range(4):
                j = cc * 4 + jj
                for kc in range(8):
                    kb.pe(lambda e: e.matmul(psm[:, 2 * j:2 * j + 2], lhsT=wt[:, kc, jj * 128:(jj + 1) * 128], rhs=screp[:, kc, 0:2],
                                             start=(kc == 0), stop=(kc == 7)),
                          r=[f"wch{cc%2}", "screp"], w=["psm"], inc=(kc == 7))
        kb.dve(lambda e: e.tensor_tensor(out=modcol[:], in0=psm[:, 0:32].rearrange("p (j two) -> p j two", two=2)[:, :, 0], in1=bcol[:], op=ALU.add),
               r=["psm", "bcol"], w=["modcol"])
        kb.dve(lambda e: e.tensor_copy(out=shiftc[:], in_=modcol[:, 0:8]), r=["modcol"], w=["shiftc"])
        kb.dve(lambda e: e.scalar_tensor_tensor(out=mscale[:], in0=modcol[:, 8:16], scalar=1.0, in1=ngc[:], op0=ALU.add, op1=ALU.mult),
               r=["modcol", "ngc"], w=["mscale"])
        wt = wch[0]
        kb.dma(wt[:], wadag.rearrange("(kc p) n -> p kc n", p=128), w=["wch0"])
        for kc in range(8):
            kb.pe(lambda e: e.matmul(psg[:], lhsT=screp[:, kc, :], rhs=wt[:, kc, :], start=(kc == 0), stop=(kc == 7)),
                  r=["wch0", "screp"], w=["psg"], inc=(kc == 7))
        kb.dve(lambda e: e.tensor_tensor(out=gaterep[:], in0=psg[:], in1=bgate[:], op=ALU.add), r=["psg", "bgate"], w=["gaterep"])
        wt = wch[1]
        kb.dma(wt[:], wout.rearrange("(kc p) n -> p kc n", p=128), w=["wch1"])
        kb.dve(lambda e: e.tensor_copy(out=wobf[:], in_=wt[:]), r=["wch1"], w=["wobf"])
        kb.dve(lambda e: e.tensor_tensor(out=lprod[:, 0, :], in0=lams[:, 0, :], in1=lams[:, 1, :], op=ALU.mult), r=["lams"], w=["lprod"])
        kb.dve(lambda e: e.tensor_tensor(out=lprod[:, 1, :], in0=lams[:, 2, :], in1=lams[:, 3, :], op=ALU.mult), r=["lams", "lprod"], w=["lprod"])
        kb.dve(lambda e: e.tensor_reduce(out=lsum[:], in_=lprod[:], axis=AX.X, op=ALU.add), r=["lprod"], w=["lsum"])
        kb.act(lambda e: e.activation(out=lsum[:], in_=lsum[:], func=ACT.Exp), r=["lsum"], w=["lsum"])
        kb.dve(lambda e: e.scalar_tensor_tensor(out=neglam[:], in0=lsum[:, 1:2], scalar=-LAMBDA_INIT, in1=lsum[:, 0:1], op0=ALU.add, op1=ALU.subtract),
               r=["lsum"], w=["neglam"])
        kb.act(lambda e: e.activation(out=negA[:], in_=negA[:], func=ACT.Exp), r=CONSTK, w=["negA"])
        kb.dve(lambda e: e.tensor_scalar(out=negA[:], in0=negA[:], scalar1=-1.0, scalar2=None, op0=ALU.mult), r=["negA"], w=["negA"])
        kb.dve(lambda e: e.tensor_scalar(out=slgs[:], in0=slgs[:], scalar1=1.0 - LAMBDA_INIT, scalar2=None, op0=ALU.mult), r=CONSTK, w=["slgs"])
        kb.dve(lambda e: e.tensor_reduce(out=gmax[:], in_=gqks[:].rearrange("p (a n) -> p a n", a=2), axis=AX.X, op=ALU.max, apply_absolute_value=True),
               r=CONSTK, w=["gmax"])
        kb.dve(lambda e: e.scalar_tensor_tensor(out=negC[:], in0=gmax[:, 0:1], scalar=-8.0, in1=gmax[:, 1:2], op0=ALU.mult, op1=ALU.mult),
               r=["gmax"], w=["negC"])
        kb.dve(lambda e: e.tensor_copy(out=posf[:], in_=posi[:]), r=["posi"], w=["posf"])
        kb.dve(lambda e: e.tensor_tensor(out=ang[:], in0=posf[:].unsqueeze(2).to_broadcast([128, NT, 8]),
                                         in1=invf[:].unsqueeze(1).to_broadcast([128, NT, 8]), op=ALU.mult), r=["posf", "invf"], w=["ang"])
        for (dst, off, name) in ((sinr, 0.0, "sinr"), (cosr, 0.25, "cosr")):
            kb.dve(lambda e: e.tensor_scalar(out=tk[:], in0=ang[:], scalar1=1.0 / TWO_PI, scalar2=off, op0=ALU.mult, op1=ALU.add), r=["ang"], w=["tk"])
            kb.dve(lambda e: e.tensor_scalar(out=tk[:], in0=tk[:], scalar1=MAGIC, scalar2=None, op0=ALU.add), r=["tk"], w=["tk"])
            kb.dve(lambda e: e.tensor_scalar(out=tk[:], in0=tk[:], scalar1=-MAGIC, scalar2=None, op0=ALU.add), r=["tk"], w=["tk"])
            kb.dve(lambda e: e.scalar_tensor_tensor(out=rr[:], in0=tk[:], scalar=-CW1, in1=ang[:], op0=ALU.mult, op1=ALU.add), r=["ang", "tk"], w=["rr"])
            kb.dve(lambda e: e.scalar_tensor_tensor(out=rr[:], in0=tk[:], scalar=-CW2, in1=rr[:], op0=ALU.mult, op1=ALU.add), r=["rr", "tk"], w=["rr"])
            kb.dve(lambda e: e.scalar_tensor_tensor(out=rr[:], in0=tk[:], scalar=-CW3, in1=rr[:], op0=ALU.mult, op1=ALU.add), r=["rr", "tk"], w=["rr"])
            kb.dve(lambda e: e.tensor_scalar(out=rr[:], in0=rr[:], scalar1=off * TWO_PI, scalar2=math.pi, op0=ALU.add, op1=ALU.min), r=["rr"], w=["rr"])
            kb.dve(lambda e: e.tensor_scalar(out=rr[:], in0=rr[:], scalar1=-math.pi, scalar2=None, op0=ALU.max), r=["rr"], w=["rr"])
            kb.act(lambda e: e.activation(out=dst[:], in_=rr[:], func=ACT.Sin), r=["rr"], w=[name])
        kb.pop()
        kb.push()
        wbf = kb.sb("wbf", [128, 8, NCOL], BF16)
        wcb = [kb.sb(f"wcb{i}", [128, 8, 64], F32) for i in range(2)]
        xt = [kb.sb(f"xt{i}", [128, 1024], F32) for i in range(2)]
        xs = [kb.sb(f"xs{i}", [128, 1024], F32) for i in range(2)]
        junk = kb.sb("junk", [128, 1024], BF16)
        st = [kb.sb(f"st{i}", [128, 4], F32) for i in range(2)]
        hT = [kb.sb(f"hT{i}", [128, 8, 512], BF16) for i in range(2)]
        cbuf = [kb.sb(f"cbuf{i}", [128, 515], F32) for i in range(12)]
        acc = [kb.sb(f"acc{i}", [128, 512], F32) for i in range(2)]
        yb = [kb.sb(f"yb{i}", [128, 512], F32) for i in range(3)]
        sqb = [kb.sb(f"sqb{i}", [128, 512], F32) for i in range(2)]
        sdb = [kb.sb(f"sdb{i}", [128, 512], F32) for i in range(2)]
        ynb = [kb.sb(f"ynb{i}", [128, 512], F32) for i in range(3)]
        trb = [kb.sb(f"trb{i}", [128, 512], F32) for i in range(2)]
        sgt = [kb.sb(f"sgt{i}", [128, 512], F32) for i in range(2)]
        qk = kb.sb("qk", [128, 1024], F32)
        ssq = kb.sb("ssq", [128, 16], F32)
        xn = kb.sb("xn", [128, 1024], F32)
        qkb = kb.sb("qkb", [128, 1024], BF16)
        rt = [kb.sb(f"rt{i}", [128, 16, 8], F32) for i in range(4)]
        vb = [kb.sb(f"vb{i}", [128, 512], BF16) for i in range(2)]
        qkTg = [kb.sb("qkTg0", [128, 8, 512], BF16)]
        tmp4 = kb.sb("tmp4", [128, 4], F32)
        psT = kb.ps("psT", [128, 512], F32)
        psF = kb.ps("psF", [128, 512], F32)
        psQ = kb.ps("psQ", [128, 512], F32)
        psK = kb.ps("psK", [128, 512], F32)
        psV = kb.ps("psV", [128, 512], F32)
        psN = kb.ps("psN", [128, 512], F32)
        psXf = kb.ps("psXf", [128, 512], F32)
        psXb = kb.ps("psXb", [128, 1024], BF16)

        win_v = win.rearrange("(kc p) n -> p kc n", p=128)
        nch = (NCOL + 63) // 64
        for ci in range(nch):
            c0, c1 = ci * 64, min(NCOL, ci * 64 + 64)
            wt = wcb[ci % 2]
            kb.dma(wt[:, :, 0:c1 - c0], win_v[:, :, c0:c1], w=[f"wcb{ci%2}"])
            if ci % 2 == 0:
                kb.dve(lambda e: e.tensor_copy(out=wbf[:, :, c0:c1], in_=wt[:, :, 0:c1 - c0]), r=[f"wcb{ci%2}"], w=["wbf"])
            else:
                kb.pool(lambda e: e.tensor_copy(out=wbf[:, :, c0:c1], in_=wt[:, :, 0:c1 - c0]), r=[f"wcb{ci%2}"], w=["wbf"])
        for c in range(12):
            kb.pool(lambda e: e.memset(cbuf[c][:, 0:3], 0.0), w=[f"cbuf{c}"])

        dqT_v = dqT.rearrange("h p t -> p h t")
        dkT_v = dkT.rearrange("h p t -> p h t")
        dv_v = dv.rearrange("h t e -> t h e")
        nfm = 0
        import os as _os
        if _os.environ.get('SKIP_P1'):
            kb.pool(lambda e: e.memset(bgres[:], -0.05), w=["bgres"])
            for h in range(4):
                kb.dma(gqT[h, :, 0:512], x[0:128, 0:512], w=["gqkT"])
                kb.dma(gkT[h, :, 0:512], x[0:128, 0:512], w=["gqkT"])
                kb.dma(gktok[h, 0:512, :].rearrange("(a p) e -> p a e", p=128), x[0:128, 0:512].rearrange("p (a e) -> p a e", a=4), w=["gtok"])
                kb.dma(gvtok[h, 0:512, :].rearrange("(a p) e -> p a e", p=128), x[0:128, 0:512].rearrange("p (a e) -> p a e", a=4), w=["gtok"])
                kb.dma(sgT[h, :, 0:512], x[0:128, 0:512], w=["sgT"])
        for g in range(0 if _os.environ.get('SKIP_P1') else NG):
            hTg = hT[g % 2]
            hk = f"hT{g%2}"
            for a in range(4):
                tt = 4 * g + a
                i2 = tt % 2
                kb.dma(xt[i2][:], x[tt * 128:(tt + 1) * 128, :], w=[f"xt{i2}"])
                s = st[i2]
                kb.act(lambda e: e.activation(out=junk[:], in_=xt[i2][:], func=ACT.Square, accum_out=s[:, 0:1]), r=[f"xt{i2}"], w=["junk", f"st{i2}"])
                kb.dve(lambda e: e.tensor_scalar(out=s[:, 1:2], in0=s[:, 0:1], scalar1=1.0 / D, scalar2=EPS, op0=ALU.mult, op1=ALU.add), r=[f"st{i2}"], w=[f"st{i2}"])
                kb.act(lambda e: e.activation(out=s[:, 2:3], in_=s[:, 1:2], func=ACT.Sqrt), r=[f"st{i2}"], w=[f"st{i2}"])
                kb.dve(lambda e: e.reciprocal(out=s[:, 3:4], in_=s[:, 2:3]), r=[f"st{i2}"], w=[f"st{i2}"])
                kb.act(lambda e: e.activation(out=xs[i2][:], in_=xt[i2][:], func=ACT.Copy, scale=s[:, 3:4]), r=[f"xt{i2}", f"st{i2}"], w=[f"xs{i2}"])
                for half in range(2):
                    for q4 in range(4):
                        kc = half * 4 + q4
                        kb.pe(lambda e: e.transpose(out=psT[:, q4 * 128:(q4 + 1) * 128], in_=xs[i2][:, kc * 128:(kc + 1) * 128], identity=identf[:]),
                              r=[f"xs{i2}", "const"], w=["psT"], inc=(q4 == 3))
                    for q4 in range(4):
                        kc = half * 4 + q4
                        kb.dve(lambda e: e.tensor_scalar(out=hTg[:, kc, a * 128:(a + 1) * 128], in0=psT[:, q4 * 128:(q4 + 1) * 128],
                                                         scalar1=mscale[:, kc:kc + 1], scalar2=shiftc[:, kc:kc + 1], op0=ALU.mult, op1=ALU.add),
                               r=["psT", "mscale", "shiftc"], w=[hk])
            pend = []
            tsl = slice(g * 512, (g + 1) * 512)
            for c in range(NFM):
                for kc in range(8):
                    kb.pe(lambda e: e.matmul(psF[:], lhsT=wbf[:, kc, c * 128:(c + 1) * 128], rhs=hTg[:, kc, :], start=(kc == 0), stop=(kc == 7)),
                          r=["wbf", hk], w=["psF"], inc=(kc == 7))
                new2, new3 = None, None
                if c >= 12:
                    sg = sgt[c % 2]
                    kb.act(lambda e: e.activation(out=sg[:], in_=psF[:], func=ACT.Silu), r=["psF"], w=[f"sgt{c%2}"])
                    kb.dma(sgT[c - 12, :, tsl], sg[:], r=[f"sgt{c%2}"], w=["sgT"], q="pool")
                else:
                    h, kind = c // 3, c % 3
                    cb = cbuf[c]
                    ck = f"cbuf{c}"
                    kb.act(lambda e: e.activation(out=cb[:, 3:515], in_=psF[:], func=ACT.Copy), r=["psF"], w=[ck])
                    ac = acc[nfm % 2]
                    ak = f"acc{nfm%2}"
                    y = yb[nfm % 3]
                    yk = f"yb{nfm%3}"
                    kb.dve(lambda e: e.tensor_scalar(out=ac[:], in0=cb[:, 0:512], scalar1=convs[:, c, 0:1], scalar2=None, op0=ALU.mult), r=[ck, "const"], w=[ak])
                    for j in range(1, 4):
                        kb.dve(lambda e: e.scalar_tensor_tensor(out=ac[:], in0=cb[:, j:j + 512], scalar=convs[:, c, j:j + 1], in1=ac[:], op0=ALU.mult, op1=ALU.add),
                               r=[ck, ak, "const"], w=[ak])
                    kb.pool(lambda e: e.tensor_copy(out=cb[:, 0:3], in_=cb[:, 512:515]), r=[ck], w=[ck])
                    kb.act(lambda e: e.activation(out=y[:], in_=ac[:], func=ACT.Silu), r=[ak], w=[yk])
                    sq_, sqk = sqb[nfm % 2], f"sqb{nfm%2}"
                    sd_, sdk = sdb[nfm % 2], f"sdb{nfm%2}"
                    yn, ynk = ynb[nfm % 3], f"ynb{nfm%3}"
                    tr, trk = trb[nfm % 2], f"trb{nfm%2}"
                    if kind < 2:
                        kb.pool(lambda e: e.tensor_tensor(out=sq_[:], in0=y[:], in1=y[:], op=ALU.mult), r=[yk], w=[sqk])

                    def part3(h=h, kind=kind, src=(yn if kind < 2 else y), srck=(ynk if kind < 2 else yk), tr=tr, trk=trk):
                        for a in range(4):
                            kb.pe(lambda e: e.transpose(out=psXf[:, a * 128:(a + 1) * 128], in_=src[:, a * 128:(a + 1) * 128], identity=identf[:]),
                                  r=[srck, "const"], w=["psXf"], inc=(a == 3))
                        kb.dve(lambda e: e.tensor_copy(out=tr[:], in_=psXf[:]), r=["psXf"], w=[trk])
                        dst = (gktok if kind == 1 else gvtok)[h, g * 512:(g + 1) * 512, :].rearrange("(a p) e -> p a e", p=128)
                        kb.dma(dst, tr[:].rearrange("p (a e) -> p a e", a=4), r=[trk], w=["gtok"], q="pool")

                    def part2(h=h, kind=kind, y=y, yk=yk, sq_=sq_, sqk=sqk, sd_=sd_, sdk=sdk, yn=yn, ynk=ynk):
                        kb.pe(lambda e: e.matmul(psN[:], lhsT=onesf, rhs=sq_[:], start=True, stop=True), r=[sqk, "const"], w=["psN"])
                        kb.act(lambda e: e.activation(out=sd_[:], in_=psN[:], func=ACT.Sqrt, bias=epst[:]), r=["psN", "const"], w=[sdk])
                        kb.dve(lambda e: e.reciprocal(out=sd_[:], in_=sd_[:]), r=[sdk], w=[sdk])
                        cq = (128.0 ** -0.5) if kind == 0 else 1.0
                        kb.dve(lambda e: e.scalar_tensor_tensor(out=yn[:], in0=y[:], scalar=cq, in1=sd_[:], op0=ALU.mult, op1=ALU.mult), r=[yk, sdk], w=[ynk])
                        kb.dma((gqT if kind == 0 else gkT)[h, :, tsl], yn[:], r=[ynk], w=["gqkT"], q="pool")

                    if kind < 2:
                        new2 = part2
                    if kind >= 1:
                        new3 = part3
                    nfm += 1
                for item in [it for it in pend if it[0] <= c]:
                    pend.remove(item)
                    item[1]()
                if new2 is not None:
                    pend.append((c + 1, new2))
                if new3 is not None:
                    pend.append((c + 2 if new2 is not None else c + 1, new3))
            for item in sorted(pend, key=lambda it: it[0]):
                item[1]()
            qTg = qkTg[0]
            qTk = "qkTg0"
            for a in range(4):
                tt = 4 * g + a
                lhs = lambda kc: hTg[:, kc, a * 128:(a + 1) * 128]
                for (pst, pk, c0, n) in ((psQ, "psQ", TM0, 512), (psK, "psK", TM0 + 512, 512), (psV, "psV", TM0 + 1024, 512), (psXf, "psXf", TM0 + 1536, 8)):
                    for kc in range(8):
                        kb.pe(lambda e: e.matmul(pst[:, 0:n], lhsT=lhs(kc), rhs=wbf[:, kc, c0:c0 + n], start=(kc == 0), stop=(kc == 7)),
                              r=["wbf", hk], w=[pk], inc=(kc == 7))
                kb.act(lambda e: e.activation(out=bgres[:, tt, 0:4], in_=psXf[:, 0:4], func=ACT.Sigmoid), r=["psXf"], w=["bgres"])
                for h in range(4):
                    kb.act(lambda e: e.activation(out=tmp4[:, h:h + 1], in_=psXf[:, 4 + h:5 + h], func=ACT.Exp, bias=dtbs[:, h:h + 1]), r=["psXf", "const"], w=["tmp4"])
                kb.act(lambda e: e.activation(out=tmp4[:], in_=tmp4[:], func=ACT.Ln, bias=onet[:]), r=["tmp4", "const"], w=["tmp4"])
                kb.dve(lambda e: e.tensor_tensor(out=bgres[:, tt, 4:8], in0=tmp4[:], in1=negA[:], op=ALU.mult), r=["tmp4", "negA"], w=["bgres"])
                v_ = vb[a % 2]
                vk = f"vb{a%2}"
                kb.act(lambda e: e.activation(out=v_[:], in_=psV[:], func=ACT.Copy), r=["psV"], w=[vk])
                kb.dma(dv_v[tt * 128:(tt + 1) * 128, :, :], v_[:].rearrange("p (h e) -> p h e", h=4), r=[vk], w=["dv"], q="pool")
                kb.dve(lambda e: e.tensor_copy(out=qk[:, 0:512], in_=psQ[:]), r=["psQ"], w=["qk"])
                kb.act(lambda e: e.activation(out=qk[:, 512:1024], in_=psK[:], func=ACT.Copy), r=["psK"], w=["qk"])
                kb.pool(lambda e: e.tensor_tensor(out=xn[:], in0=qk[:], in1=qk[:], op=ALU.mult), r=["qk"], w=["xn"])
                kb.dve(lambda e: e.tensor_reduce(out=ssq[:], in_=xn[:].rearrange("p (s d) -> p s d", d=64), axis=AX.X, op=ALU.add), r=["xn"], w=["ssq"])
                kb.dve(lambda e: e.tensor_scalar(out=ssq[:], in0=ssq[:], scalar1=1.0 / 64, scalar2=EPS, op0=ALU.mult, op1=ALU.add), r=["ssq"], w=["ssq"])
                kb.act(lambda e: e.activation(out=ssq[:], in_=ssq[:], func=ACT.Sqrt), r=["ssq"], w=["ssq"])
                kb.dve(lambda e: e.reciprocal(out=ssq[:], in_=ssq[:]), r=["ssq"], w=["ssq"])
                qk3 = qk[:].rearrange("p (s d) -> p s d", d=64)
                xn3 = xn[:].rearrange("p (s d) -> p s d", d=64)
                kb.dve(lambda e: e.tensor_tensor(out=xn3, in0=qk3, in1=ssq[:].unsqueeze(2).to_broadcast([128, 16, 64]), op=ALU.mult), r=["qk", "ssq"], w=["xn"])
                kb.pool(lambda e: e.tensor_tensor(out=xn[:], in0=xn[:], in1=gqks[:], op=ALU.mult), r=["xn", "const"], w=["xn"])
                kb.act(lambda e: e.activation(out=qkb[:], in_=xn[:], func=ACT.Copy), r=["xn"], w=["qkb"])
                qkb3 = qkb[:].rearrange("p (s d) -> p s d", d=64)
                cb_ = cosr[:, tt, :].unsqueeze(1).to_broadcast([128, 16, 8])
                sb_ = sinr[:, tt, :].unsqueeze(1).to_broadcast([128, 16, 8])
                x1, x2 = xn3[:, :, 0:8], xn3[:, :, 8:16]
                kb.dve(lambda e: e.tensor_tensor(out=rt[0][:], in0=x1, in1=cb_, op=ALU.mult), r=["xn", "cosr"], w=["rt0"])
                kb.dve(lambda e: e.tensor_tensor(out=rt[1][:], in0=x2, in1=sb_, op=ALU.mult), r=["xn", "sinr"], w=["rt1"])
                kb.dve(lambda e: e.tensor_tensor(out=rt[2][:], in0=x2, in1=cb_, op=ALU.mult), r=["xn", "cosr"], w=["rt2"])
                kb.dve(lambda e: e.tensor_tensor(out=rt[3][:], in0=x1, in1=sb_, op=ALU.mult), r=["xn", "sinr"], w=["rt3"])
                kb.dve(lambda e: e.tensor_tensor(out=qkb3[:, :, 0:8], in0=rt[0][:], in1=rt[1][:], op=ALU.subtract), r=["rt0", "rt1", "qkb"], w=["qkb"])
                kb.dve(lambda e: e.tensor_tensor(out=qkb3[:, :, 8:16], in0=rt[2][:], in1=rt[3][:], op=ALU.add), r=["rt2", "rt3", "qkb"], w=["qkb"])
                for bi in range(8):
                    kb.pe(lambda e: e.transpose(out=psXb[:, bi * 128:(bi + 1) * 128], in_=qkb[:, bi * 128:(bi + 1) * 128], identity=identb[:]),
                          r=["qkb", "const"], w=["psXb"], inc=(bi == 7))
                kb.dve(lambda e: e.tensor_copy(out=qTg[:, :, a * 128:(a + 1) * 128], in_=psXb[:].rearrange("p (b t) -> p b t", b=8)), r=["psXb"], w=[qTk])
            kb.dma(dqT_v[:, :, g * 512:(g + 1) * 512], qTg[:, 0:4, :], r=[qTk], w=["dqT"], q="pool")
            kb.dma(dkT_v[:, :, g * 512:(g + 1) * 512], qTg[:, 4:8, :], r=[qTk], w=["dkT"], q="pool")
        kb.pop()
        kb.push()
        Q = lambda t, i: t[:, i * 128:(i + 1) * 128]
        RR = (lambda ap: ap.bitcast(mybir.dt.float32r)) if not _os.environ.get('NO_F32R') else (lambda ap: ap)
        names = ["kTr", "qTr", "kTp", "qTp", "ktk", "vtk", "Ug", "E", "Ei", "Es", "aT", "Ya", "YaT", "Yb", "YbT", "Wa", "Wb", "dg", "qdT", "kdec", "rhs2", "vn"]
        gb = {}
        for h in range(4):
            for par in range(2):
                for n in names:
                    gb[(n, h, par)] = kb.sb(f"{n}{h}{par}", [128, 128], F32)
                gb[("sm", h, par)] = kb.sb(f"sm{h}{par}", [128, 8], F32)
                gb[("smc", h, par)] = kb.sb(f"smc{h}{par}", [128, 8], F32)
                gb[("g2", h, par)] = kb.sb(f"g2{h}{par}", [128, 2], F32)
        Sst = [[kb.sb(f"S{h}{i}", [128, 128], F32) for i in range(2)] for h in range(4)]
        obuf = [kb.sb(f"obuf{h}", [128, 512], F32) for h in range(4)]
        gsq = kb.sb("gsq", [128, 512], F32)
        gsd = kb.sb("gsd", [128, 512], F32)
        gon = kb.sb("gon", [128, 512], F32)
        gsg = [kb.sb(f"gsg{i}", [128, 512], F32) for i in range(2)]
        gmx = [kb.sb(f"gmx{i}", [128, 512], BF16) for i in range(2)]
        pA = kb.ps("pA", [128, 512], F32)
        pB = kb.ps("pB", [128, 512], F32)
        pC = kb.ps("pC", [128, 512], F32)
        pKS = kb.ps("pKS", [128, 512], F32)
        pVN = kb.ps("pVN", [128, 512], F32)
        pSS = kb.ps("pSS", [128, 512], F32)
        pO = kb.ps("pO", [128, 512], F32)
        pP = kb.ps("pP", [128, 512], F32)
        for h in range(4):
            kb.dve(lambda e: e.tensor_scalar(out=RR(Sst[h][0][:]), in0=identf[:], scalar1=0.0, scalar2=None, op0=ALU.mult), r=["const"], w=[f"S{h}0"])
        scur = [0, 0, 0, 0]
        H4 = range(4)

        def pre_stages(p):
            par = p % 2
            tok = slice(p * 128, (p + 1) * 128)
            B = lambda n, h: gb[(n, h, par)]
            K_ = lambda n, h: f"{n}{h}{par}"
            gcol = lambda h: bgres[:, p, 4 + h:5 + h]
            beta = lambda h: bgres[:, p, h:h + 1]
            st = []

            def s_load():
                for h in H4:
                    kb.dma(B("kTp", h)[:], gkT[h, :, tok], r=["gqkT"], w=[K_("kTp", h)])
                    kb.dma(B("qTp", h)[:], gqT[h, :, tok], r=["gqkT"], w=[K_("qTp", h)])
                    kb.dma(B("ktk", h)[:], gktok[h, tok, :], r=["gtok"], w=[K_("ktk", h)])
                    kb.dma(B("vtk", h)[:], gvtok[h, tok, :], r=["gtok"], w=[K_("vtk", h)])
            st.append(s_load)

            def s_ug():
                for h in H4:
                    kb.pool(lambda e: e.tensor_copy(out=RR(B("kTr", h)[:]), in_=B("kTp", h)[:]), r=[K_("kTp", h)], w=[K_("kTr", h)])
                    kb.pool(lambda e: e.tensor_copy(out=RR(B("qTr", h)[:]), in_=B("qTp", h)[:]), r=[K_("qTp", h)], w=[K_("qTr", h)])
                    kb.dve(lambda e: e.tensor_scalar(out=B("Ug", h)[:], in0=ublk, scalar1=gcol(h), scalar2=None, op0=ALU.mult), r=["const", "bgres"], w=[K_("Ug", h)])
                    kb.dve(lambda e: e.tensor_copy(out=B("g2", h)[:], in_=gcol(h).to_broadcast([128, 2])), r=["bgres"], w=[K_("g2", h)])
            st.append(s_ug)

            def s_g():
                for h in H4:
                    kb.pe(lambda e: e.matmul(Q(pA, h), lhsT=B("Ug", h)[:], rhs=onesblk, start=True, stop=False), r=[K_("Ug", h), "const"], w=["pA"], inc=False)
                    kb.pe(lambda e: e.matmul(Q(pA, h), lhsT=negblk[:], rhs=B("Ug", h)[:], start=False, stop=True), r=[K_("Ug", h), "const"], w=["pA"])
                for h in H4:
                    o8 = 8 * h
                    kb.pe(lambda e: e.matmul(pB[:, o8:o8 + 2], lhsT=B("Ug", h)[:], rhs=onesf[:, 0:2], start=True, stop=True), r=[K_("Ug", h), "const"], w=["pB"], inc=False)
                    kb.pe(lambda e: e.matmul(pB[:, o8 + 2:o8 + 4], lhsT=onesblk, rhs=B("g2", h)[:], start=True, stop=True), r=[K_("g2", h), "const"], w=["pB"], inc=False)
                    kb.pe(lambda e: e.matmul(pB[:, o8 + 4:o8 + 6], lhsT=sel[0], rhs=B("g2", h)[:], start=True, stop=True), r=[K_("g2", h), "const"], w=["pB"], inc=False)
                    kb.pe(lambda e: e.matmul(pB[:, o8 + 6:o8 + 8], lhsT=sel[1], rhs=B("g2", h)[:], start=True, stop=True), r=[K_("g2", h), "const"], w=["pB"])
            st.append(s_g)

            def s_e1():
                for h in H4:
                    kb.act(lambda e: e.activation(out=B("smc", h)[:], in_=pB[:, 8 * h:8 * h + 8], func=ACT.Copy), r=["pB"], w=[K_("smc", h)])
                    kb.dve(lambda e: e.tensor_scalar(out=B("E", h)[:], in0=Q(pA, h), scalar1=-1.0, scalar2=0.0, op0=ALU.mult, op1=ALU.min), r=["pA"], w=[K_("E", h)])
            st.append(s_e1)

            def s_e2():
                for h in H4:
                    sm, smc = B("sm", h), B("smc", h)
                    kb.act(lambda e: e.activation(out=sm[:, 0:1], in_=smc[:, 0:1], func=ACT.Exp), r=[K_("smc", h)], w=[K_("sm", h)])
                    kb.act(lambda e: e.activation(out=B("E", h)[:], in_=B("E", h)[:], func=ACT.Exp), r=[K_("E", h)], w=[K_("E", h)])
                    kb.dve(lambda e: e.tensor_tensor(out=sm[:, 2:3], in0=smc[:, 2:3], in1=smc[:, 0:1], op=ALU.subtract), r=[K_("smc", h), K_("sm", h)], w=[K_("sm", h)])
            st.append(s_e2)

            def s_kk():
                for h in H4:
                    kb.pe(lambda e: e.matmul(Q(pC, h), lhsT=RR(B("kTr", h)[:]), rhs=RR(B("kTr", h)[:]), start=True, stop=True), r=[K_("kTr", h)], w=["pC"])
                for h in H4:
                    kb.pe(lambda e: e.matmul(Q(pA, h), lhsT=RR(B("kTr", h)[:]), rhs=RR(B("qTr", h)[:]), start=True, stop=True), r=[K_("kTr", h), K_("qTr", h)], w=["pA"])
            st.append(s_kk)

            def s_e3():
                for h in H4:
                    sm, smc = B("sm", h), B("smc", h)
                    kb.dve(lambda e: e.tensor_scalar(out=sm[:, 1:2], in0=sm[:, 0:1], scalar1=-1.0, scalar2=None, op0=ALU.mult), r=[K_("sm", h)], w=[K_("sm", h)])
                    kb.act(lambda e: e.activation(out=sm[:, 2:3], in_=sm[:, 2:3], func=ACT.Exp), r=[K_("sm", h)], w=[K_("sm", h)])
                    kb.act(lambda e: e.activation(out=sm[:, 4:8], in_=smc[:, 4:8], func=ACT.Exp), r=[K_("smc", h), K_("sm", h)], w=[K_("sm", h)])
                    kb.pool(lambda e: e.tensor_tensor(out=B("Ei", h)[:], in0=B("E", h)[:], in1=uincl, op=ALU.mult), r=[K_("E", h), "const"], w=[K_("Ei", h)])
                    kb.pool(lambda e: e.tensor_tensor(out=B("Es", h)[:], in0=B("E", h)[:], in1=negustr, op=ALU.mult), r=[K_("E", h), "const"], w=[K_("Es", h)])
            st.append(s_e3)

            def s_ya():
                for h in H4:
                    kb.dve(lambda e: e.scalar_tensor_tensor(out=RR(B("Ya", h)[:]), in0=Q(pC, h), scalar=beta(h), in1=B("Es", h)[:], op0=ALU.mult, op1=ALU.mult),
                           r=["pC", K_("Es", h), "bgres"], w=[K_("Ya", h)])
                for h in H4:
                    kb.dve(lambda e: e.tensor_tensor(out=RR(B("aT", h)[:]), in0=Q(pA, h), in1=B("Ei", h)[:], op=ALU.mult), r=["pA", K_("Ei", h)], w=[K_("aT", h)])
            st.append(s_ya)

            def s_tr():
                for h in H4:
                    kb.pe(lambda e: e.transpose(out=Q(pB, h), in_=B("Ya", h)[:], identity=identf[:]), r=[K_("Ya", h), "const"], w=["pB"])
            st.append(s_tr)

            def s_tr2():
                for h in H4:
                    kb.act(lambda e: e.activation(out=RR(B("YaT", h)[:]), in_=Q(pB, h), func=ACT.Copy), r=["pB"], w=[K_("YaT", h)])
                    kb.dve(lambda e: e.tensor_tensor(out=RR(B("Wa", h)[:]), in0=B("Ya", h)[:], in1=identf[:], op=ALU.add), r=[K_("Ya", h), "const"], w=[K_("Wa", h)])
            st.append(s_tr2)

            cur = ["Ya", "YaT", "Wa"]
            for k in range(1, 6):
                Y, YT, W = cur
                Yn, YTn, Wn = ("Yb", "YbT", "Wb") if Y == "Ya" else ("Ya", "YaT", "Wa")

                def s_sq(k=k, Y=Y, YT=YT):
                    if k < 5:
                        for h in H4:
                            kb.pe(lambda e: e.matmul(Q(pA, h), lhsT=RR(B(YT, h)[:]), rhs=RR(B(Y, h)[:]), start=True, stop=True), r=[K_(Y, h), K_(YT, h)], w=["pA"])
                    for h in H4:
                        kb.pe(lambda e: e.matmul(Q(pC, h), lhsT=RR(B(Y, h)[:]), rhs=RR(B(YT, h)[:]), start=True, stop=True), r=[K_(Y, h), K_(YT, h)], w=["pC"])
                st.append(s_sq)

                def s_ev(k=k, Yn=Yn, YTn=YTn):
                    for h in H4:
                        if k < 5:
                            kb.act(lambda e: e.activation(out=RR(B(Yn, h)[:]), in_=Q(pA, h), func=ACT.Copy), r=["pA"], w=[K_(Yn, h)])
                        kb.dve(lambda e: e.tensor_copy(out=RR(B(YTn, h)[:]), in_=Q(pC, h)), r=["pC"], w=[K_(YTn, h)])
                st.append(s_ev)

                def s_w(YTn=YTn, W=W):
                    for h in H4:
                        kb.pe(lambda e: e.matmul(Q(pB, h), lhsT=RR(B(YTn, h)[:]), rhs=RR(B(W, h)[:]), start=True, stop=True), r=[K_(YTn, h), K_(W, h)], w=["pB"])
                st.append(s_w)

                def s_w2(W=W, Wn=Wn):
                    for h in H4:
                        kb.dve(lambda e: e.tensor_tensor(out=RR(B(Wn, h)[:]), in0=Q(pB, h), in1=B(W, h)[:], op=ALU.add), r=["pB", K_(W, h)], w=[K_(Wn, h)])
                st.append(s_w2)
                cur = [Yn, YTn, Wn]
            for h in H4:
                gb[("Wfin", h, par)] = cur[2]

            def s_dg():
                for h in H4:
                    sm = B("sm", h)
                    kb.dve(lambda e: e.tensor_scalar(out=B("dg", h)[:], in0=identf[:], scalar1=sm[:, 0:1], scalar2=None, op0=ALU.mult), r=["const", K_("sm", h)], w=[K_("dg", h)])
                    kb.act(lambda e: e.activation(out=RR(B("kdec", h)[:]), in_=B("ktk", h)[:], func=ACT.Copy, scale=sm[:, 2:3]), r=[K_("ktk", h), K_("sm", h)], w=[K_("kdec", h)])
            st.append(s_dg)

            def s_eg():
                for h in H4:
                    kb.pe(lambda e: e.matmul(Q(pA, h), lhsT=onesf, rhs=B("dg", h)[:], start=True, stop=True), r=[K_("dg", h), "const"], w=["pA"])
            st.append(s_eg)

            def s_qd():
                for h in H4:
                    kb.dve(lambda e: e.tensor_tensor(out=RR(B("qdT", h)[:]), in0=Q(pA, h), in1=B("qTp", h)[:], op=ALU.mult), r=["pA", K_("qTp", h)], w=[K_("qdT", h)])
            st.append(s_qd)
            return st

        def scan_stages(p):
            par = p % 2
            B = lambda n, h: gb[(n, h, par)]
            K_ = lambda n, h: f"{n}{h}{par}"
            st = []
            for c in range(2):
                hs = slice(64 * c, 64 * c + 64)

                def s1(hs=hs):
                    for h in H4:
                        S, Sk = Sst[h][scur[h]], f"S{h}{scur[h]}"
                        kb.pe(lambda e: e.matmul(Q(pKS, h), lhsT=RR(B("kTr", h)[:]), rhs=RR(S[:]), start=True, stop=True), r=[K_("kTr", h), Sk], w=["pKS"])
                st.append(s1)

                def s2(hs=hs):
                    for h in H4:
                        kb.dve(lambda e: e.scalar_tensor_tensor(out=RR(B("rhs2", h)[hs, :]), in0=Q(pKS, h)[hs, :], scalar=B("sm", h)[hs, 1:2], in1=B("vtk", h)[hs, :],
                                                                op0=ALU.mult, op1=ALU.add), r=["pKS", K_("sm", h), K_("vtk", h)], w=[K_("rhs2", h)])
                st.append(s2)

                def s3(hs=hs):
                    for h in H4:
                        Wf = gb[("Wfin", h, par)]
                        kb.pe(lambda e: e.matmul(Q(pVN, h), lhsT=RR(B(Wf, h)[hs, :]), rhs=RR(B("rhs2", h)[hs, :]), start=True, stop=True), r=[K_(Wf, h), K_("rhs2", h)], w=["pVN"])
                st.append(s3)

                def s4(hs=hs):
                    for h in H4:
                        kb.dve(lambda e: e.tensor_scalar(out=RR(B("vn", h)[hs, :]), in0=Q(pVN, h)[hs, :], scalar1=bgres[hs, p, h:h + 1], scalar2=None, op0=ALU.mult),
                               r=["pVN", "bgres"], w=[K_("vn", h)])
                st.append(s4)

                def s5(hs=hs, c=c):
                    for h in H4:
                        S, Sk = Sst[h][scur[h]], f"S{h}{scur[h]}"
                        oc = slice(h * 128 + 64 * c, h * 128 + 64 * c + 64)
                        kb.pe(lambda e: e.matmul(pO[:, oc], lhsT=RR(S[:]), rhs=RR(B("qdT", h)[:, hs]), start=True, stop=False), r=[Sk, K_("qdT", h)], w=["pO"], inc=False)
                        kb.pe(lambda e: e.matmul(pO[:, oc], lhsT=RR(B("vn", h)[hs, :]), rhs=RR(B("aT", h)[hs, hs]), start=False, stop=True), r=[K_("vn", h), K_("aT", h)], w=["pO"])
                        kb.pe(lambda e: e.matmul(Q(pSS, h), lhsT=RR(B("kdec", h)[hs, :]), rhs=RR(B("vn", h)[hs, :]), start=True, stop=True), r=[K_("kdec", h), K_("vn", h)], w=["pSS"])
                st.append(s5)

                def s6(c=c):
                    for h in H4:
                        S, Sk = Sst[h][scur[h]], f"S{h}{scur[h]}"
                        Sn, Snk = Sst[h][1 - scur[h]], f"S{h}{1-scur[h]}"
                        kb.dve(lambda e: e.scalar_tensor_tensor(out=RR(Sn[:]), in0=S[:], scalar=B("sm", h)[:, 4 + 2 * c:5 + 2 * c], in1=Q(pSS, h), op0=ALU.mult, op1=ALU.add),
                               r=[Sk, K_("sm", h), "pSS"], w=[Snk])
                        scur[h] = 1 - scur[h]
                st.append(s6)

            def s7():
                for h in H4:
                    kb.act(lambda e: e.activation(out=obuf[h][:, (p % 4) * 128:(p % 4 + 1) * 128], in_=Q(pO, h), func=ACT.Copy), r=["pO"], w=[f"obuf{h}"])
            st.append(s7)
            return st

        npost = [0]

        def post(g):
            tsl = slice(g * 512, (g + 1) * 512)
            for h in H4:
                i2 = npost[0] % 2
                kb.dma(gsg[i2][:], sgT[h, :, tsl], r=["sgT"], w=[f"gsg{i2}"])
                kb.pool(lambda e: e.tensor_tensor(out=gsq[:], in0=obuf[h][:], in1=obuf[h][:], op=ALU.mult), r=[f"obuf{h}"], w=["gsq"])
                kb.pe(lambda e: e.matmul(pP[:], lhsT=onesf, rhs=gsq[:], start=True, stop=True), r=["gsq", "const"], w=["pP"])
                kb.act(lambda e: e.activation(out=gsd[:], in_=pP[:], func=ACT.Sqrt, bias=epst[:], scale=1.0 / 128), r=["pP", "const"], w=["gsd"])
                kb.dve(lambda e: e.reciprocal(out=gsd[:], in_=gsd[:]), r=["gsd"], w=["gsd"])
                kb.dve(lambda e: e.tensor_tensor(out=gon[:], in0=obuf[h][:], in1=gsd[:], op=ALU.mult), r=[f"obuf{h}", "gsd"], w=["gon"])
                kb.dve(lambda e: e.scalar_tensor_tensor(out=gmx[i2][:], in0=gon[:], scalar=gngs[:, 0:1], in1=gsg[i2][:], op0=ALU.mult, op1=ALU.mult),
                       r=["gon", "const", f"gsg{i2}"], w=[f"gmx{i2}"])
                kb.dma(mixT[h, :, tsl], gmx[i2][:], r=[f"gmx{i2}"], w=["mixT"], q="pool")
                npost[0] += 1

        for f in pre_stages(0):
            f()
        for p in range(NT):
            sc_ = scan_stages(p)
            pr_ = pre_stages(p + 1) if p + 1 < NT else []
            ratio = (len(pr_) + len(sc_) - 1) // len(sc_)
            pi = 0
            for f in sc_:
                f()
                for _ in range(ratio):
                    if pi < len(pr_):
                        pr_[pi]()
                        pi += 1
            while pi < len(pr_):
                pr_[pi]()
                pi += 1
            if p % 4 == 3:
                post(p // 4)
        kb.pop()
        if upto <= 2:
            kb.finish()
            return nc, kb
        kb.push()
        akT = [kb.sb(f"akT{i}", [128, T], BF16) for i in range(2)]
        aqT = [kb.sb(f"aqT{i}", [128, T], BF16) for i in range(2)]
        avv = [kb.sb(f"avv{i}", [128, NT, 128], BF16) for i in range(2)]
        pTb = [[kb.sb(f"pT{i}{j}", [128, 512], BF16) for j in range(2)] for i in range(2)]
        arl = [kb.sb(f"arl{i}", [128, 512], F32) for i in range(2)]
        lacc = [kb.sb(f"lacc{i}", [128, 512], F32) for i in range(2)]
        ao = [kb.sb(f"ao{i}", [128, 512], F32) for i in range(2)]
        aod = kb.sb("aod", [128, 512], F32)
        asq = kb.sb("asq", [128, 512], F32)
        asd = kb.sb("asd", [128, 512], F32)
        aon = kb.sb("aon", [128, 512], F32)
        asg = [kb.sb(f"asg{i}", [128, 512], F32) for i in range(2)]
        amx = [kb.sb(f"amx{i}", [128, 512], BF16) for i in range(2)]
        pS = [[kb.ps(f"pS{i}{j}", [128, 512], F32) for j in range(2)] for i in range(2)]
        pOa = [kb.ps(f"pOa{i}", [128, 512], F32) for i in range(2)]
        pLa = [kb.ps(f"pLa{i}", [128, 512], F32) for i in range(2)]
        nstep = 0
        for h in range(4):
            hb = h % 2
            for part in range(2):
                sl = slice(part * (T // 2), (part + 1) * (T // 2))
                kb.dma(akT[hb][:, sl], dkT[h, :, sl], r=["dkT"], w=[f"akT{hb}"])
                kb.dma(aqT[hb][:, sl], dqT[h, :, sl], r=["dqT"], w=[f"aqT{hb}"])
            dvh = dv[h].rearrange("(t p) e -> p t e", p=128)
            for part in range(4):
                kb.dma(avv[hb][:, part * 16:(part + 1) * 16, :], dvh[:, part * 16:(part + 1) * 16, :], r=["dv"], w=[f"avv{hb}"])
            steps = [(g, kt) for g in range(NG) for kt in range(4 * g + 4)]

            def emit_S(n):
                g, kt = steps[n]
                j = kt - 4 * g
                qlo = 128 * max(j, 0)
                qs = slice(qlo, 512)
                sb2 = n % 2
                for i in range(2):
                    rows = slice(64 * i, 64 * i + 64)
                    kb.pe(lambda e: e.matmul(pS[i][sb2][:, qs], lhsT=akT[hb][rows, kt * 128:(kt + 1) * 128], rhs=aqT[hb][rows, g * 512 + qlo:(g + 1) * 512],
                                             start=True, stop=True), r=[f"akT{hb}", f"aqT{hb}"], w=[f"pS{i}{sb2}"])

            def emit_rest(n):
                g, kt = steps[n]
                nk = 4 * g + 4
                j = kt - 4 * g
                qlo = 128 * max(j, 0)
                qs = slice(qlo, 512)
                sb2 = n % 2
                for i in range(2):
                    pt = pTb[i][sb2]
                    kb.act(lambda e: e.activation(out=pt[:, qs], in_=pS[i][sb2][:, qs], func=ACT.Exp, bias=negC[:], scale=0.125),
                           r=[f"pS{i}{sb2}", "negC"], w=[f"pT{i}{sb2}"])
                    if j >= 0:
                        kb.pool(lambda e: e.memset(pt[64:128, qlo:qlo + 64], 0.0), r=[], w=[f"pT{i}{sb2}"])
                for i in range(2):
                    pt = pTb[i][sb2]
                    kb.pe(lambda e: e.matmul(pOa[i][:, qs], lhsT=avv[hb][:, kt, :], rhs=pt[:, qs], start=(kt == 0), stop=(kt == nk - 1)),
                          r=[f"avv{hb}", f"pT{i}{sb2}"], w=[f"pOa{i}"])
                    ae = kb.pool if i == 0 else kb.dve
                    if kt == 0:
                        ae(lambda e: e.tensor_copy(out=lacc[i][:, qs], in_=pt[:, qs]), r=[f"pT{i}{sb2}"], w=[f"lacc{i}"])
                    else:
                        ae(lambda e: e.tensor_tensor(out=lacc[i][:, qs], in0=lacc[i][:, qs], in1=pt[:, qs], op=ALU.add), r=[f"pT{i}{sb2}", f"lacc{i}"], w=[f"lacc{i}"])
                if kt != nk - 1:
                    return
                tsl = slice(g * 512, (g + 1) * 512)
                i2 = g % 2
                kb.dma(asg[i2][:], sgT[4 + h, :, tsl], r=["sgT"], w=[f"asg{i2}"])
                for i in range(2):
                    kb.pe(lambda e: e.matmul(pLa[i][:], lhsT=onesf, rhs=lacc[i][:], start=True, stop=True), r=["const", f"lacc{i}"], w=[f"pLa{i}"])
                for i in range(2):
                    kb.dve(lambda e: e.reciprocal(out=arl[i][:], in_=pLa[i][:]), r=[f"pLa{i}"], w=[f"arl{i}"])
                    kb.dve(lambda e: e.tensor_tensor(out=ao[i][:], in0=pOa[i][:], in1=arl[i][:], op=ALU.mult), r=[f"pOa{i}", f"arl{i}"], w=[f"ao{i}"])
                kb.dve(lambda e: e.scalar_tensor_tensor(out=aod[:], in0=ao[1][:], scalar=neglam[:, 0:1], in1=ao[0][:], op0=ALU.mult, op1=ALU.add),
                       r=["ao0", "ao1", "neglam"], w=["aod"])
                kb.pool(lambda e: e.tensor_tensor(out=asq[:], in0=aod[:], in1=aod[:], op=ALU.mult), r=["aod"], w=["asq"])
                pN_ = pLa[0]
                kb.pe(lambda e: e.matmul(pN_[:], lhsT=onesf, rhs=asq[:], start=True, stop=True), r=["asq", "const"], w=["pLa0"])
                kb.act(lambda e: e.activation(out=asd[:], in_=pN_[:], func=ACT.Sqrt, bias=epst[:], scale=1.0 / 128), r=["pLa0", "const"], w=["asd"])
                kb.dve(lambda e: e.reciprocal(out=asd[:], in_=asd[:]), r=["asd"], w=["asd"])
                kb.dve(lambda e: e.tensor_tensor(out=aon[:], in0=aod[:], in1=asd[:], op=ALU.mult), r=["aod", "asd"], w=["aon"])
                kb.dve(lambda e: e.scalar_tensor_tensor(out=amx[i2][:], in0=aon[:], scalar=slgs[:, 0:1], in1=asg[i2][:], op0=ALU.mult, op1=ALU.mult),
                       r=["aon", "slgs", f"asg{i2}"], w=[f"amx{i2}"])
                kb.dma(mixT[4 + h, :, tsl], amx[i2][:], r=[f"amx{i2}"], w=["mixT"], q="pool")

            emit_S(0)
            for n in range(len(steps)):
                if n + 1 < len(steps):
                    emit_S(n + 1)
                emit_rest(n)
        kb.pop()
        if upto <= 3:
            kb.finish()
            return nc, kb
        kb.push()
        mt = [kb.sb(f"mt{i}", [128, 8, 512], BF16) for i in range(2)]
        xr = [kb.sb(f"xr{i}", [128, 512], F32) for i in range(2)]
        y1 = [kb.sb(f"y1{i}", [128, 512], F32) for i in range(2)]
        y2 = [kb.sb(f"y2{i}", [128, 512], F32) for i in range(2)]
        pY = [kb.ps(f"pY{i}", [128, 512], F32) for i in range(2)]
        mixT_v = mixT.rearrange("m p t -> p m t")
        for g in range(NG):
            m_ = mt[g % 2]
            mk = f"mt{g%2}"
            kb.dma(m_[:], mixT_v[:, :, g * 512:(g + 1) * 512], r=["mixT"], w=[mk])
            for a in range(4):
                tt = 4 * g + a
                i2 = tt % 2
                kb.dma(xr[i2][:], xres[tt * 128:(tt + 1) * 128, :], w=[f"xr{i2}"])
                for m in range(8):
                    kb.pe(lambda e: e.matmul(pY[i2][:], lhsT=m_[:, m, a * 128:(a + 1) * 128], rhs=wobf[:, m, :], start=(m == 0), stop=(m == 7)),
                          r=[mk, "wobf"], w=[f"pY{i2}"], inc=(m == 7))
                kb.dve(lambda e: e.tensor_tensor(out=y1[i2][:], in0=pY[i2][:], in1=gaterep[:], op=ALU.mult), r=[f"pY{i2}", "gaterep"], w=[f"y1{i2}"])
                kb.pool(lambda e: e.tensor_tensor(out=y2[i2][:], in0=y1[i2][:], in1=xr[i2][:], op=ALU.add), r=[f"y1{i2}", f"xr{i2}"], w=[f"y2{i2}"])
                kb.dma(out[tt * 128:(tt + 1) * 128, :], y2[i2][:], r=[f"y2{i2}"], w=["out"], q="pool")
        kb.pop()
        kb.finish()
    return nc, kb


def _consts():
    i = np.arange(128)
    same = (i[:, None] // 64) == (i[None, :] // 64)
    onesf = np.ones((128, 128), np.float32)
    ublk = (same & (i[:, None] <= i[None, :])).astype(np.float32)
    uincl = ublk.copy()
    negustr = -(same & (i[None, :] > i[:, None])).astype(np.float32)
    onesblk = same.astype(np.float32)
    sel0 = np.repeat((i < 64).astype(np.float32)[:, None], 128, 1)
    sel1 = np.repeat((i >= 64).astype(np.float32)[:, None], 128, 1)
    masks = np.stack([onesf, ublk, uincl, negustr, onesblk, sel0, sel1], 1).astype(np.float32)
    invf = (np.float32(500000.0) ** (-np.arange(0, 16, 2, dtype=np.float32) / np.float32(16))).astype(np.float32)
    return {
        "c_identf": np.eye(128, dtype=np.float32),
        "c_identb": np.eye(128, dtype=np.float32).astype(ml_dtypes.bfloat16),
        "c_onesb": np.ones((128, 128), np.float32).astype(ml_dtypes.bfloat16),
        "c_masks": np.ascontiguousarray(masks),
        "c_negblk": -onesblk,
        "c_invf": np.ascontiguousarray(np.broadcast_to(invf[None, :], (128, 8))),
    }


def _rep(v, n=128):
    return np.ascontiguousarray(np.broadcast_to(np.asarray(v, np.float32).reshape(1, -1), (n, np.asarray(v).size)))


def prep_core(inp, b, half):
    f = lambda a: np.ascontiguousarray(np.asarray(a, np.float32))
    w_in = np.asarray(inp["w_in"][0], np.float32)
    off = np.cumsum([0, 512, 512, 512, 4, 4, 512, 512, 512, 512, 512])
    aq, ak, av, abeta, adec, agate, bq, bk, bv, bgate = [np.arange(off[i], off[i + 1]) for i in range(10)]
    cols = []
    for h in range(4):
        cols += [aq[h * 128:(h + 1) * 128], ak[h * 128:(h + 1) * 128], av[h * 128:(h + 1) * 128]]
    cols += [agate, bgate, bq, bk, bv, abeta, adec]
    cols = np.concatenate(cols)
    conv_w = np.asarray(inp["conv_w"][0], np.float32)
    convc = np.zeros((128, 12, 4), np.float32)
    for h in range(4):
        for kind in range(3):
            ch = kind * 512 + h * 128 + np.arange(128)
            convc[:, 3 * h + kind, :] = conv_w[:, ch].T
    w_ada = np.asarray(inp["w_ada"][0], np.float32)
    b_ada = np.asarray(inp["b_ada"][0], np.float32)
    gsl = slice(2048 + half * 512, 2048 + (half + 1) * 512)
    d = {
        "x": f(inp["x"][b]),
        "xres": f(inp["x"][b][:, half * 512:(half + 1) * 512]),
        "cT": f(np.asarray(inp["c"][b]).reshape(8, 128).T),
        "posT": np.ascontiguousarray(np.asarray(inp["positions"][b], np.int32).reshape(NT, 128).T),
        "wada": f(w_ada[:, 0:2048]),
        "wadag": f(w_ada[:, gsl]),
        "badac": f(b_ada[0:2048].reshape(16, 128).T),
        "badag": _rep(b_ada[gsl]),
        "normgc": f(np.asarray(inp["norm_g"][0]).reshape(8, 128).T),
        "win": f(w_in[:, cols]),
        "convc": convc,
        "alog": _rep(inp["a_log"][0]),
        "dtb": _rep(inp["dt_bias"][0]),
        "gng": f(np.asarray(inp["gdn_norm_g"][0]).reshape(128, 1)),
        "slg": f(np.asarray(inp["subln_g"][0]).reshape(128, 1)),
        "gqk": _rep(np.concatenate([np.tile(np.asarray(inp["q_norm_g"][0]), 8), np.tile(np.asarray(inp["k_norm_g"][0]), 8)])),
        "lamv": np.ascontiguousarray(np.broadcast_to(np.stack([np.asarray(inp[k][0], np.float32) for k in
                                                               ("lambda_q1", "lambda_k1", "lambda_q2", "lambda_k2")])[None], (128, 4, 64))),
        "wout": f(np.asarray(inp["w_out"][0])[:, half * 512:(half + 1) * 512]),
    }
    d.update(_consts())
    return d


_NC_CACHE = {}


def kernel(**inputs):
    if "nc" not in _NC_CACHE:
        _NC_CACHE["nc"] = build_program()[0]
    nc = _NC_CACHE["nc"]
    in_maps = [prep_core(inputs, core // 2, core % 2) for core in range(8)]
    res = run_bass_kernel_spmd(nc, in_maps, core_ids=list(range(8)))
    out = np.empty((4, T, D), np.float32)
    for core in range(8):
        out[core // 2, :, (core % 2) * 512:(core % 2 + 1) * 512] = res.results[core]["out"]
    return out
```

```python
import math
from contextlib import ExitStack

import numpy as np
import ml_dtypes

import concourse.bass as bass
import concourse.mybir as mybir
from concourse.bass_utils import run_bass_kernel_spmd

F32 = mybir.dt.float32
BF16 = mybir.dt.bfloat16
I32 = mybir.dt.int32
ACT = mybir.ActivationFunctionType
ALU = mybir.AluOpType
AX = mybir.AxisListType


class KB:
    NDMA = 40

    def __init__(self, nc):
        self.nc = nc
        self.eng = {"pe": nc.tensor, "act": nc.scalar, "dve": nc.vector, "pool": nc.gpsimd, "sp": nc.sync}
        self.stack = ExitStack()
        self.sem = {}
        self.cnt = {k: 0 for k in self.eng}
        self.waited = {k: {} for k in self.eng}
        self.last_w = {}
        self.readers = {}
        self.dsem = []
        self.dcnt = []
        self.dnext = 0
        self.nins = 0
        self.phase = 0
        self.excl = set()

    def ctx(self):
        for k in ("pe", "act", "dve", "pool"):
            self.sem[k] = self.stack.enter_context(self.nc.semaphore("s_" + k))
        for i in range(self.NDMA):
            self.dsem.append(self.stack.enter_context(self.nc.semaphore(f"s_dma{i}")))
            self.dcnt.append(0)
        return self.stack

    def sb(self, name, shape, dtype):
        return self.stack.enter_context(self.nc.sbuf_tensor(f"{name}_{self.phase}", shape, dtype))

    def ps(self, name, shape, dtype):
        self.excl.add(name)
        return self.stack.enter_context(self.nc.psum_tensor(f"{name}_{self.phase}", shape, dtype))

    def _handle(self, s):
        return self.dsem[s[1]] if isinstance(s, tuple) else self.sem[s]

    def _wait(self, en, deps):
        e = self.eng[en]
        need = {}
        for s, v in deps:
            if s == "pe" and en == "pe":
                continue
            if v > need.get(s, 0):
                need[s] = v
        for s, v in need.items():
            if self.waited[en].get(s, 0) < v:
                e.wait_ge(self._handle(s), v)
                self.waited[en][s] = v
                self.nins += 1

    def _deps(self, r, w):
        d = []
        for k in r:
            if k in self.last_w:
                d.append(self.last_w[k])
        for k in w:
            if k in self.last_w:
                d.append(self.last_w[k])
            d.extend(self.readers.get(k, ()))
        return d

    def _record(self, r, w, tag):
        for k in r:
            self.readers.setdefault(k, []).append(tag)
        for k in w:
            self.last_w[k] = tag
            self.readers[k] = []

    def _skip(self):
        import os
        lim = os.environ.get("KB_LIMIT")
        if not lim:
            return False
        ph, n = (int(v) for v in lim.split(":"))
        if self.phase != ph:
            return False
        self.pops = getattr(self, "pops", 0) + 1
        return self.pops > n

    def op(self, en, fn, r=(), w=(), inc=True):
        if self._skip():
            return None
        w = list(w) + [k for k in r if k in self.excl and k not in w]
        self._wait(en, self._deps(r, w))
        ins = fn(self.eng[en])
        self.nins += 1
        if inc:
            ins.then_inc(self.sem[en], 1)
            self.cnt[en] += 1
            tag = (en, self.cnt[en])
        else:
            tag = (en, self.cnt[en] + 1)
        self._record(r, w, tag)
        return ins

    def pe(self, fn, r=(), w=(), inc=True):
        return self.op("pe", fn, r, w, inc)

    def act(self, fn, r=(), w=()):
        return self.op("act", fn, r, w)

    def dve(self, fn, r=(), w=()):
        return self.op("dve", fn, r, w)

    def pool(self, fn, r=(), w=()):
        return self.op("pool", fn, r, w)

    def dma(self, out, in_, r=(), w=(), q="sp", **kw):
        if self._skip():
            return None
        deps = self._deps(r, w)
        i = self.dnext
        self.dnext = (self.dnext + 1) % self.NDMA
        if self.dcnt[i] > 0:
            deps.append((("d", i), 16 * self.dcnt[i]))
        self._wait(q, deps)
        ins = self.eng[q].dma_start(out=out, in_=in_, **kw)
        ins.then_inc(self.dsem[i], 16)
        self.nins += 1
        self.dcnt[i] += 1
        self._record(r, w, (("d", i), 16 * self.dcnt[i]))
        return ins

    def finish(self):
        deps = [(k, self.cnt[k]) for k in ("pe", "act", "dve", "pool") if self.cnt[k] > 0]
        deps += [(("d", i), 16 * c) for i, c in enumerate(self.dcnt) if c > 0]
        self._wait("sp", deps)

    def barrier(self):
        deps = [(k, self.cnt[k]) for k in ("pe", "act", "dve", "pool") if self.cnt[k] > 0]
        deps += [(("d", i), 16 * c) for i, c in enumerate(self.dcnt) if c > 0]
        for en in ("pe", "act", "dve", "pool", "sp"):
            self._wait(en, [d for d in deps if not (d[0] == "pe" and en == "pe")] + ([("pe", self.cnt["pe"])] if False else []))
        self.last_w = {}
        self.readers = {}

    def push(self):
        self._saved = getattr(self, "_saved", [])
        self._saved.append(self.stack)
        self.stack = ExitStack()
        self.stack.__enter__()
        self.phase += 1

    def pop(self):
        self.barrier()
        self.stack.__exit__(None, None, None)
        self.stack = self._saved.pop()


T = 8192
D = 1024
NT = T // 128
NG = T // 512
NCOL = 4104
EPS = 1e-6
LAMBDA_INIT = 0.8 - 0.6 * math.exp(0.0)
NFM = 20
TM0 = NFM * 128
TWO_PI = 2.0 * math.pi
MAGIC = 12582912.0
CW1 = 6.28125
CW2 = 0.0019350051879882812
CW3 = TWO_PI - CW1 - CW2


def build_program(dbg=(), upto=9):
    nc = bass.Bass("TRN2", target_bir_lowering=False)

    def din(name, shape, dt=F32):
        return nc.dram_tensor(name, list(shape), dt, kind="ExternalInput").ap()

    def dscr(name, shape, dt=F32):
        return nc.dram_tensor(name, list(shape), dt, kind="ExternalOutput" if name in dbg else "Internal").ap()

    x = din("x", [T, D])
    xres = din("xres", [T, 512])
    cT = din("cT", [128, 8])
    posT = din("posT", [128, NT], I32)
    wada = din("wada", [D, 2048])
    wadag = din("wadag", [D, 512])
    badac = din("badac", [128, 16])
    badag = din("badag", [128, 512])
    normgc = din("normgc", [128, 8])
    win = din("win", [D, NCOL])
    convc = din("convc", [128, 12, 4])
    alog = din("alog", [128, 4])
    dtb = din("dtb", [128, 4])
    gng = din("gng", [128, 1])
    slg = din("slg", [128, 1])
    gqk = din("gqk", [128, 1024])
    lamv = din("lamv", [128, 4, 64])
    wout = din("wout", [D, 512])
    c_identf = din("c_identf", [128, 128])
    c_identb = din("c_identb", [128, 128], BF16)
    c_onesb = din("c_onesb", [128, 128], BF16)
    c_masks = din("c_masks", [128, 7, 128])
    c_negblk = din("c_negblk", [128, 128])
    c_invf = din("c_invf", [128, 8])
    out = nc.dram_tensor("out", [T, 512], F32, kind="ExternalOutput").ap()

    gqT = dscr("gqT", [4, 128, T])
    gkT = dscr("gkT", [4, 128, T])
    gktok = dscr("gktok", [4, T, 128])
    gvtok = dscr("gvtok", [4, T, 128])
    sgT = dscr("sgT", [8, 128, T])
    dqT = dscr("dqT", [4, 128, T], BF16)
    dkT = dscr("dkT", [4, 128, T], BF16)
    dv = dscr("dv", [4, T, 128], BF16)
    mixT = dscr("mixT", [8, 128, T], BF16)

    kb = KB(nc)
    with kb.ctx():
        identf = kb.sb("identf", [128, 128], F32)
        identb = kb.sb("identb", [128, 128], BF16)
        onesb = kb.sb("onesb", [128, 128], BF16)
        masks = kb.sb("masks", [128, 7, 128], F32)
        negblk = kb.sb("negblk", [128, 128], F32)
        onesf, ublk, uincl, negustr, onesblk = (masks[:, i, :] for i in range(5))
        sel = [masks[:, 5, :], masks[:, 6, :]]
        bgres = kb.sb("bgres", [128, NT, 8], F32)
        cosr = kb.sb("cosr", [128, NT, 8], F32)
        sinr = kb.sb("sinr", [128, NT, 8], F32)
        gaterep = kb.sb("gaterep", [128, 512], F32)
        mscale = kb.sb("mscale", [128, 8], F32)
        shiftc = kb.sb("shiftc", [128, 8], F32)
        negA = kb.sb("negA", [128, 4], F32)
        dtbs = kb.sb("dtbs", [128, 4], F32)
        gngs = kb.sb("gngs", [128, 1], F32)
        slgs = kb.sb("slgs", [128, 1], F32)
        neglam = kb.sb("neglam", [128, 1], F32)
        negC = kb.sb("negC", [128, 1], F32)
        gqks = kb.sb("gqks", [128, 1024], F32)
        convs = kb.sb("convs", [128, 12, 4], F32)
        wobf = kb.sb("wobf", [128, 8, 512], BF16)

        epst = kb.sb("epst", [128, 1], F32)
        onet = kb.sb("onet", [128, 1], F32)
        CONSTK = ["const"]
        kb.pool(lambda e: e.memset(epst[:], EPS), w=CONSTK)
        kb.pool(lambda e: e.memset(onet[:], 1.0), w=CONSTK)
        for dst, src in ((identf, c_identf), (identb, c_identb), (onesb, c_onesb), (masks, c_masks), (negblk, c_negblk),
                         (negA, alog), (dtbs, dtb), (gngs, gng), (slgs, slg), (gqks, gqk), (convs, convc)):
            kb.dma(dst[:], src, w=CONSTK)

        kb.push()
        sc = kb.sb("sc", [128, 8], F32)
        screp = kb.sb("screp", [128, 8, 128], F32)
        wch = [kb.sb(f"wch{i}", [128, 8, 512], F32) for i in range(2)]
        modcol = kb.sb("modcol", [128, 16], F32)
        bcol = kb.sb("bcol", [128, 16], F32)
        ngc = kb.sb("ngc", [128, 8], F32)
        bgate = kb.sb("bgate", [128, 512], F32)
        lams = kb.sb("lams", [128, 4, 64], F32)
        lprod = kb.sb("lprod", [128, 2, 64], F32)
        lsum = kb.sb("lsum", [128, 2], F32)
        posi = kb.sb("posi", [128, NT], I32)
        posf = kb.sb("posf", [128, NT], F32)
        invf = kb.sb("invf", [128, 8], F32)
        ang = kb.sb("ang", [128, NT, 8], F32)
        tk = kb.sb("tk", [128, NT, 8], F32)
        rr = kb.sb("rr", [128, NT, 8], F32)
        gmax = kb.sb("gmax", [128, 2], F32)
        psm = kb.ps("psm", [128, 512], F32)
        psg = kb.ps("psg", [128, 512], F32)

        kb.dma(sc[:], cT, w=["sc"])
        kb.dma(bcol[:], badac, w=["bcol"])
        kb.dma(ngc[:], normgc, w=["ngc"])
        kb.dma(bgate[:], badag, w=["bgate"])
        kb.dma(lams[:], lamv, w=["lams"])
        kb.dma(posi[:], posT, w=["posi"])
        kb.dma(invf[:], c_invf, w=["invf"])
        kb.act(lambda e: e.activation(out=sc[:], in_=sc[:], func=ACT.Silu), r=["sc"], w=["sc"])
        kb.dve(lambda e: e.tensor_copy(out=screp[:], in_=sc[:].unsqueeze(2).to_broadcast([128, 8, 128])), r=["sc"], w=["screp"])
        wada_v = wada.rearrange("(kc p) n -> p kc n", p=128)
        for cc in range(4):
            wt = wch[cc % 2]
            kb.dma(wt[:], wada_v[:, :, cc * 512:(cc + 1) * 512], w=[f"wch{cc%2}"])
            for jj in range(4):
                j = cc * 4 + jj
                for kc in range(8):
                    kb.pe(lambda e: e.matmul(psm[:, 2 * j:2 * j + 2], lhsT=wt[:, kc, jj * 128:(jj + 1) * 128], rhs=screp[:, kc, 0:2],
                                             start=(kc == 0), stop=(kc == 7)),
                          r=[f"wch{cc%2}", "screp"], w=["psm"], inc=(kc == 7))
        kb.dve(lambda e: e.tensor_tensor(out=modcol[:], in0=psm[:, 0:32].rearrange("p (j two) -> p j two", two=2)[:, :, 0], in1=bcol[:], op=ALU.add),
               r=["psm", "bcol"], w=["modcol"])
        kb.dve(lambda e: e.tensor_copy(out=shiftc[:], in_=modcol[:, 0:8]), r=["modcol"], w=["shiftc"])
        kb.dve(lambda e: e.scalar_tensor_tensor(out=mscale[:], in0=modcol[:, 8:16], scalar=1.0, in1=ngc[:], op0=ALU.add, op1=ALU.mult),
               r=["modcol", "ngc"], w=["mscale"])
        wt = wch[0]
        kb.dma(wt[:], wadag.rearrange("(kc p) n -> p kc n", p=128), w=["wch0"])
        for kc in range(8):
            kb.pe(lambda e: e.matmul(psg[:], lhsT=screp[:, kc, :], rhs=wt[:, kc, :], start=(kc == 0), stop=(kc == 7)),
                  r=["wch0", "screp"], w=["psg"], inc=(kc == 7))
        kb.dve(lambda e: e.tensor_tensor(out=gaterep[:], in0=psg[:], in1=bgate[:], op=ALU.add), r=["psg", "bgate"], w=["gaterep"])
        wt = wch[1]
        kb.dma(wt[:], wout.rearrange("(kc p) n -> p kc n", p=128), w=["wch1"])
        kb.dve(lambda e: e.tensor_copy(out=wobf[:], in_=wt[:]), r=["wch1"], w=["wobf"])
        kb.dve(lambda e: e.tensor_tensor(out=lprod[:, 0, :], in0=lams[:, 0, :], in1=lams[:, 1, :], op=ALU.mult), r=["lams"], w=["lprod"])
        kb.dve(lambda e: e.tensor_tensor(out=lprod[:, 1, :], in0=lams[:, 2, :], in1=lams[:, 3, :], op=ALU.mult), r=["lams", "lprod"], w=["lprod"])
        kb.dve(lambda e: e.tensor_reduce(out=lsum[:], in_=lprod[:], axis=AX.X, op=ALU.add), r=["lprod"], w=["lsum"])
        kb.act(lambda e: e.activation(out=lsum[:], in_=lsum[:], func=ACT.Exp), r=["lsum"], w=["lsum"])
        kb.dve(lambda e: e.scalar_tensor_tensor(out=neglam[:], in0=lsum[:, 1:2], scalar=-LAMBDA_INIT, in1=lsum[:, 0:1], op0=ALU.add, op1=ALU.subtract),
               r=["lsum"], w=["neglam"])
        kb.act(lambda e: e.activation(out=negA[:], in_=negA[:], func=ACT.Exp), r=CONSTK, w=["negA"])
        kb.dve(lambda e: e.tensor_scalar(out=negA[:], in0=negA[:], scalar1=-1.0, scalar2=None, op0=ALU.mult), r=["negA"], w=["negA"])
        kb.dve(lambda e: e.tensor_scalar(out=slgs[:], in0=slgs[:], scalar1=1.0 - LAMBDA_INIT, scalar2=None, op0=ALU.mult), r=CONSTK, w=["slgs"])
        kb.dve(lambda e: e.tensor_reduce(out=gmax[:], in_=gqks[:].rearrange("p (a n) -> p a n", a=2), axis=AX.X, op=ALU.max, apply_absolute_value=True),
               r=CONSTK, w=["gmax"])
        kb.dve(lambda e: e.scalar_tensor_tensor(out=negC[:], in0=gmax[:, 0:1], scalar=-8.0, in1=gmax[:, 1:2], op0=ALU.mult, op1=ALU.mult),
               r=["gmax"], w=["negC"])
        kb.dve(lambda e: e.tensor_copy(out=posf[:], in_=posi[:]), r=["posi"], w=["posf"])
        kb.dve(lambda e: e.tensor_tensor(out=ang[:], in0=posf[:].unsqueeze(2).to_broadcast([128, NT, 8]),
                                         in1=invf[:].unsqueeze(1).to_broadcast([128, NT, 8]), op=ALU.mult), r=["posf", "invf"], w=["ang"])
        for (dst, off, name) in ((sinr, 0.0, "sinr"), (cosr, 0.25, "cosr")):
            kb.dve(lambda e: e.tensor_scalar(out=tk[:], in0=ang[:], scalar1=1.0 / TWO_PI, scalar2=off, op0=ALU.mult, op1=ALU.add), r=["ang"], w=["tk"])
            kb.dve(lambda e: e.tensor_scalar(out=tk[:], in0=tk[:], scalar1=MAGIC, scalar2=None, op0=ALU.add), r=["tk"], w=["tk"])
            kb.dve(lambda e: e.tensor_scalar(out=tk[:], in0=tk[:], scalar1=-MAGIC, scalar2=None, op0=ALU.add), r=["tk"], w=["tk"])
            kb.dve(lambda e: e.scalar_tensor_tensor(out=rr[:], in0=tk[:], scalar=-CW1, in1=ang[:], op0=ALU.mult, op1=ALU.add), r=["ang", "tk"], w=["rr"])
            kb.dve(lambda e: e.scalar_tensor_tensor(out=rr[:], in0=tk[:], scalar=-CW2, in1=rr[:], op0=ALU.mult, op1=ALU.add), r=["rr", "tk"], w=["rr"])
            kb.dve(lambda e: e.scalar_tensor_tensor(out=rr[:], in0=tk[:], scalar=-CW3, in1=rr[:], op0=ALU.mult, op1=ALU.add), r=["rr", "tk"], w=["rr"])
            kb.dve(lambda e: e.tensor_scalar(out=rr[:], in0=rr[:], scalar1=off * TWO_PI, scalar2=math.pi, op0=ALU.add, op1=ALU.min), r=["rr"], w=["rr"])
            kb.dve(lambda e: e.tensor_scalar(out=rr[:], in0=rr[:], scalar1=-math.pi, scalar2=None, op0=ALU.max), r=["rr"], w=["rr"])
            kb.act(lambda e: e.activation(out=dst[:], in_=rr[:], func=ACT.Sin), r=["rr"], w=[name])
        kb.pop()
        kb.push()
        wbf = kb.sb("wbf", [128, 8, NCOL], BF16)
        wcb = [kb.sb(f"wcb{i}", [128, 8, 64], F32) for i in range(2)]
        xt = [kb.sb(f"xt{i}", [128, 1024], F32) for i in range(2)]
        xs = [kb.sb(f"xs{i}", [128, 1024], F32) for i in range(2)]
        junk = kb.sb("junk", [128, 1024], BF16)
        st = [kb.sb(f"st{i}", [128, 4], F32) for i in range(2)]
        hT = [kb.sb(f"hT{i}", [128, 8, 512], BF16) for i in range(2)]
        cbuf = [kb.sb(f"cbuf{i}", [128, 515], F32) for i in range(12)]
        acc = [kb.sb(f"acc{i}", [128, 512], F32) for i in range(2)]
        yb = [kb.sb(f"yb{i}", [128, 512], F32) for i in range(3)]
        sqb = [kb.sb(f"sqb{i}", [128, 512], F32) for i in range(2)]
        sdb = [kb.sb(f"sdb{i}", [128, 512], F32) for i in range(2)]
        ynb = [kb.sb(f"ynb{i}", [128, 512], F32) for i in range(3)]
        trb = [kb.sb(f"trb{i}", [128, 512], F32) for i in range(2)]
        sgt = [kb.sb(f"sgt{i}", [128, 512], F32) for i in range(2)]
        qk = kb.sb("qk", [128, 1024], F32)
        ssq = kb.sb("ssq", [128, 16], F32)
        xn = kb.sb("xn", [128, 1024], F32)
        qkb = kb.sb("qkb", [128, 1024], BF16)
        rt = [kb.sb(f"rt{i}", [128, 16, 8], F32) for i in range(4)]
        vb = [kb.sb(f"vb{i}", [128, 512], BF16) for i in range(2)]
        qkTg = [kb.sb("qkTg0", [128, 8, 512], BF16)]
        tmp4 = kb.sb("tmp4", [128, 4], F32)
        psT = kb.ps("psT", [128, 512], F32)
        psF = kb.ps("psF", [128, 512], F32)
        psQ = kb.ps("psQ", [128, 512], F32)
        psK = kb.ps("psK", [128, 512], F32)
        psV = kb.ps("psV", [128, 512], F32)
        psN = kb.ps("psN", [128, 512], F32)
        psXf = kb.ps("psXf", [128, 512], F32)
        psXb = kb.ps("psXb", [128, 1024], BF16)

        win_v = win.rearrange("(kc p) n -> p kc n", p=128)
        nch = (NCOL + 63) // 64
        for ci in range(nch):
            c0, c1 = ci * 64, min(NCOL, ci * 64 + 64)
            wt = wcb[ci % 2]
            kb.dma(wt[:, :, 0:c1 - c0], win_v[:, :, c0:c1], w=[f"wcb{ci%2}"])
            if ci % 2 == 0:
                kb.dve(lambda e: e.tensor_copy(out=wbf[:, :, c0:c1], in_=wt[:, :, 0:c1 - c0]), r=[f"wcb{ci%2}"], w=["wbf"])
            else:
                kb.pool(lambda e: e.tensor_copy(out=wbf[:, :, c0:c1], in_=wt[:, :, 0:c1 - c0]), r=[f"wcb{ci%2}"], w=["wbf"])
        for c in range(12):
            kb.pool(lambda e: e.memset(cbuf[c][:, 0:3], 0.0), w=[f"cbuf{c}"])

        dqT_v = dqT.rearrange("h p t -> p h t")
        dkT_v = dkT.rearrange("h p t -> p h t")
        dv_v = dv.rearrange("h t e -> t h e")
        nfm = 0
        import os as _os
        if _os.environ.get('SKIP_P1'):
            kb.pool(lambda e: e.memset(bgres[:], -0.05), w=["bgres"])
            for h in range(4):
                kb.dma(gqT[h, :, 0:512], x[0:128, 0:512], w=["gqkT"])
                kb.dma(gkT[h, :, 0:512], x[0:128, 0:512], w=["gqkT"])
                kb.dma(gktok[h, 0:512, :].rearrange("(a p) e -> p a e", p=128), x[0:128, 0:512].rearrange("p (a e) -> p a e", a=4), w=["gtok"])
                kb.dma(gvtok[h, 0:512, :].rearrange("(a p) e -> p a e", p=128), x[0:128, 0:512].rearrange("p (a e) -> p a e", a=4), w=["gtok"])
                kb.dma(sgT[h, :, 0:512], x[0:128, 0:512], w=["sgT"])
        for g in range(0 if _os.environ.get('SKIP_P1') else NG):
            hTg = hT[g % 2]
            hk = f"hT{g%2}"
            for a in range(4):
                tt = 4 * g + a
                i2 = tt % 2
                kb.dma(xt[i2][:], x[tt * 128:(tt + 1) * 128, :], w=[f"xt{i2}"])
                s = st[i2]
                kb.act(lambda e: e.activation(out=junk[:], in_=xt[i2][:], func=ACT.Square, accum_out=s[:, 0:1]), r=[f"xt{i2}"], w=["junk", f"st{i2}"])
                kb.dve(lambda e: e.tensor_scalar(out=s[:, 1:2], in0=s[:, 0:1], scalar1=1.0 / D, scalar2=EPS, op0=ALU.mult, op1=ALU.add), r=[f"st{i2}"], w=[f"st{i2}"])
                kb.act(lambda e: e.activation(out=s[:, 2:3], in_=s[:, 1:2], func=ACT.Sqrt), r=[f"st{i2}"], w=[f"st{i2}"])
                kb.dve(lambda e: e.reciprocal(out=s[:, 3:4], in_=s[:, 2:3]), r=[f"st{i2}"], w=[f"st{i2}"])
                kb.act(lambda e: e.activation(out=xs[i2][:], in_=xt[i2][:], func=ACT.Copy, scale=s[:, 3:4]), r=[f"xt{i2}", f"st{i2}"], w=[f"xs{i2}"])
                for half in range(2):
                    for q4 in range(4):
                        kc = half * 4 + q4
                        kb.pe(lambda e: e.transpose(out=psT[:, q4 * 128:(q4 + 1) * 128], in_=xs[i2][:, kc * 128:(kc + 1) * 128], identity=identf[:]),
                              r=[f"xs{i2}", "const"], w=["psT"], inc=(q4 == 3))
                    for q4 in range(4):
                        kc = half * 4 + q4
                        kb.dve(lambda e: e.tensor_scalar(out=hTg[:, kc, a * 128:(a + 1) * 128], in0=psT[:, q4 * 128:(q4 + 1) * 128],
                                                         scalar1=mscale[:, kc:kc + 1], scalar2=shiftc[:, kc:kc + 1], op0=ALU.mult, op1=ALU.add),
                               r=["psT", "mscale", "shiftc"], w=[hk])
            pend = []
            tsl = slice(g * 512, (g + 1) * 512)
            for c in range(NFM):
                for kc in range(8):
                    kb.pe(lambda e: e.matmul(psF[:], lhsT=wbf[:, kc, c * 128:(c + 1) * 128], rhs=hTg[:, kc, :], start=(kc == 0), stop=(kc == 7)),
                          r=["wbf", hk], w=["psF"], inc=(kc == 7))
                new2, new3 = None, None
                if c >= 12:
                    sg = sgt[c % 2]
                    kb.act(lambda e: e.activation(out=sg[:], in_=psF[:], func=ACT.Silu), r=["psF"], w=[f"sgt{c%2}"])
                    kb.dma(sgT[c - 12, :, tsl], sg[:], r=[f"sgt{c%2}"], w=["sgT"], q="pool")
                else:
                    h, kind = c // 3, c % 3
                    cb = cbuf[c]
                    ck = f"cbuf{c}"
                    kb.act(lambda e: e.activation(out=cb[:, 3:515], in_=psF[:], func=ACT.Copy), r=["psF"], w=[ck])
                    ac = acc[nfm % 2]
                    ak = f"acc{nfm%2}"
                    y = yb[nfm % 3]
                    yk = f"yb{nfm%3}"
                    kb.dve(lambda e: e.tensor_scalar(out=ac[:], in0=cb[:, 0:512], scalar1=convs[:, c, 0:1], scalar2=None, op0=ALU.mult), r=[ck, "const"], w=[ak])
                    for j in range(1, 4):
                        kb.dve(lambda e: e.scalar_tensor_tensor(out=ac[:], in0=cb[:, j:j + 512], scalar=convs[:, c, j:j + 1], in1=ac[:], op0=ALU.mult, op1=ALU.add),
                               r=[ck, ak, "const"], w=[ak])
                    kb.pool(lambda e: e.tensor_copy(out=cb[:, 0:3], in_=cb[:, 512:515]), r=[ck], w=[ck])
                    kb.act(lambda e: e.activation(out=y[:], in_=ac[:], func=ACT.Silu), r=[ak], w=[yk])
                    sq_, sqk = sqb[nfm % 2], f"sqb{nfm%2}"
                    sd_, sdk = sdb[nfm % 2], f"sdb{nfm%2}"
                    yn, ynk = ynb[nfm % 3], f"ynb{nfm%3}"
                    tr, trk = trb[nfm % 2], f"trb{nfm%2}"
                    if kind < 2:
                        kb.pool(lambda e: e.tensor_tensor(out=sq_[:], in0=y[:], in1=y[:], op=ALU.mult), r=[yk], w=[sqk])

                    def part3(h=h, kind=kind, src=(yn if kind < 2 else y), srck=(ynk if kind < 2 else yk), tr=tr, trk=trk):
                        for a in range(4):
                            kb.pe(lambda e: e.transpose(out=psXf[:, a * 128:(a + 1) * 128], in_=src[:, a * 128:(a + 1) * 128], identity=identf[:]),
                                  r=[srck, "const"], w=["psXf"], inc=(a == 3))
                        kb.dve(lambda e: e.tensor_copy(out=tr[:], in_=psXf[:]), r=["psXf"], w=[trk])
                        dst = (gktok if kind == 1 else gvtok)[h, g * 512:(g + 1) * 512, :].rearrange("(a p) e -> p a e", p=128)
                        kb.dma(dst, tr[:].rearrange("p (a e) -> p a e", a=4), r=[trk], w=["gtok"], q="pool")

                    def part2(h=h, kind=kind, y=y, yk=yk, sq_=sq_, sqk=sqk, sd_=sd_, sdk=sdk, yn=yn, ynk=ynk):
                        kb.pe(lambda e: e.matmul(psN[:], lhsT=onesf, rhs=sq_[:], start=True, stop=True), r=[sqk, "const"], w=["psN"])
                        kb.act(lambda e: e.activation(out=sd_[:], in_=psN[:], func=ACT.Ln, bias=epst[:]), r=["psN", "const"], w=[sdk])
                        kb.act(lambda e: e.activation(out=sd_[:], in_=sd_[:], func=ACT.Exp, scale=-0.5), r=[sdk], w=[sdk])
                        cq = (128.0 ** -0.5) if kind == 0 else 1.0
                        kb.dve(lambda e: e.scalar_tensor_tensor(out=yn[:], in0=y[:], scalar=cq, in1=sd_[:], op0=ALU.mult, op1=ALU.mult), r=[yk, sdk], w=[ynk])
                        kb.dma((gqT if kind == 0 else gkT)[h, :, tsl], yn[:], r=[ynk], w=["gqkT"], q="pool")

                    if kind < 2:
                        new2 = part2
                    if kind >= 1:
                        new3 = part3
                    nfm += 1
                for item in [it for it in pend if it[0] <= c]:
                    pend.remove(item)
                    item[1]()
                if new2 is not None:
                    pend.append((c + 1, new2))
                if new3 is not None:
                    pend.append((c + 2 if new2 is not None else c + 1, new3))
            for item in sorted(pend, key=lambda it: it[0]):
                item[1]()
            qTg = qkTg[0]
            qTk = "qkTg0"
            for a in range(4):
                tt = 4 * g + a
                lhs = lambda kc: hTg[:, kc, a * 128:(a + 1) * 128]
                for (pst, pk, c0, n) in ((psQ, "psQ", TM0, 512), (psK, "psK", TM0 + 512, 512), (psV, "psV", TM0 + 1024, 512), (psXf, "psXf", TM0 + 1536, 8)):
                    for kc in range(8):
                        kb.pe(lambda e: e.matmul(pst[:, 0:n], lhsT=lhs(kc), rhs=wbf[:, kc, c0:c0 + n], start=(kc == 0), stop=(kc == 7)),
                              r=["wbf", hk], w=[pk], inc=(kc == 7))
                kb.act(lambda e: e.activation(out=bgres[:, tt, 0:4], in_=psXf[:, 0:4], func=ACT.Sigmoid), r=["psXf"], w=["bgres"])
                for h in range(4):
                    kb.act(lambda e: e.activation(out=tmp4[:, h:h + 1], in_=psXf[:, 4 + h:5 + h], func=ACT.Exp, bias=dtbs[:, h:h + 1]), r=["psXf", "const"], w=["tmp4"])
                kb.act(lambda e: e.activation(out=tmp4[:], in_=tmp4[:], func=ACT.Ln, bias=onet[:]), r=["tmp4", "const"], w=["tmp4"])
                kb.dve(lambda e: e.tensor_tensor(out=bgres[:, tt, 4:8], in0=tmp4[:], in1=negA[:], op=ALU.mult), r=["tmp4", "negA"], w=["bgres"])
                v_ = vb[a % 2]
                vk = f"vb{a%2}"
                kb.act(lambda e: e.activation(out=v_[:], in_=psV[:], func=ACT.Copy), r=["psV"], w=[vk])
                kb.dma(dv_v[tt * 128:(tt + 1) * 128, :, :], v_[:].rearrange("p (h e) -> p h e", h=4), r=[vk], w=["dv"], q="pool")
                kb.dve(lambda e: e.tensor_copy(out=qk[:, 0:512], in_=psQ[:]), r=["psQ"], w=["qk"])
                kb.act(lambda e: e.activation(out=qk[:, 512:1024], in_=psK[:], func=ACT.Copy), r=["psK"], w=["qk"])
                kb.pool(lambda e: e.tensor_tensor(out=xn[:], in0=qk[:], in1=qk[:], op=ALU.mult), r=["qk"], w=["xn"])
                kb.dve(lambda e: e.tensor_reduce(out=ssq[:], in_=xn[:].rearrange("p (s d) -> p s d", d=64), axis=AX.X, op=ALU.add), r=["xn"], w=["ssq"])
                kb.dve(lambda e: e.tensor_scalar(out=ssq[:], in0=ssq[:], scalar1=1.0 / 64, scalar2=EPS, op0=ALU.mult, op1=ALU.add), r=["ssq"], w=["ssq"])
                kb.act(lambda e: e.activation(out=ssq[:], in_=ssq[:], func=ACT.Sqrt), r=["ssq"], w=["ssq"])
                kb.dve(lambda e: e.reciprocal(out=ssq[:], in_=ssq[:]), r=["ssq"], w=["ssq"])
                qk3 = qk[:].rearrange("p (s d) -> p s d", d=64)
                xn3 = xn[:].rearrange("p (s d) -> p s d", d=64)
                kb.dve(lambda e: e.tensor_tensor(out=xn3, in0=qk3, in1=ssq[:].unsqueeze(2).to_broadcast([128, 16, 64]), op=ALU.mult), r=["qk", "ssq"], w=["xn"])
                kb.pool(lambda e: e.tensor_tensor(out=xn[:], in0=xn[:], in1=gqks[:], op=ALU.mult), r=["xn", "const"], w=["xn"])
                kb.act(lambda e: e.activation(out=qkb[:], in_=xn[:], func=ACT.Copy), r=["xn"], w=["qkb"])
                qkb3 = qkb[:].rearrange("p (s d) -> p s d", d=64)
                cb_ = cosr[:, tt, :].unsqueeze(1).to_broadcast([128, 16, 8])
                sb_ = sinr[:, tt, :].unsqueeze(1).to_broadcast([128, 16, 8])
                x1, x2 = xn3[:, :, 0:8], xn3[:, :, 8:16]
                kb.dve(lambda e: e.tensor_tensor(out=rt[0][:], in0=x1, in1=cb_, op=ALU.mult), r=["xn", "cosr"], w=["rt0"])
                kb.dve(lambda e: e.tensor_tensor(out=rt[1][:], in0=x2, in1=sb_, op=ALU.mult), r=["xn", "sinr"], w=["rt1"])
                kb.dve(lambda e: e.tensor_tensor(out=rt[2][:], in0=x2, in1=cb_, op=ALU.mult), r=["xn", "cosr"], w=["rt2"])
                kb.dve(lambda e: e.tensor_tensor(out=rt[3][:], in0=x1, in1=sb_, op=ALU.mult), r=["xn", "sinr"], w=["rt3"])
                kb.dve(lambda e: e.tensor_tensor(out=qkb3[:, :, 0:8], in0=rt[0][:], in1=rt[1][:], op=ALU.subtract), r=["rt0", "rt1", "qkb"], w=["qkb"])
                kb.dve(lambda e: e.tensor_tensor(out=qkb3[:, :, 8:16], in0=rt[2][:], in1=rt[3][:], op=ALU.add), r=["rt2", "rt3", "qkb"], w=["qkb"])
                for bi in range(8):
                    kb.pe(lambda e: e.transpose(out=psXb[:, bi * 128:(bi + 1) * 128], in_=qkb[:, bi * 128:(bi + 1) * 128], identity=identb[:]),
                          r=["qkb", "const"], w=["psXb"], inc=(bi == 7))
                kb.dve(lambda e: e.tensor_copy(out=qTg[:, :, a * 128:(a + 1) * 128], in_=psXb[:].rearrange("p (b t) -> p b t", b=8)), r=["psXb"], w=[qTk])
            kb.dma(dqT_v[:, :, g * 512:(g + 1) * 512], qTg[:, 0:4, :], r=[qTk], w=["dqT"], q="pool")
            kb.dma(dkT_v[:, :, g * 512:(g + 1) * 512], qTg[:, 4:8, :], r=[qTk], w=["dkT"], q="pool")
        kb.pop()
        kb.push()
        Q = lambda t, i: t[:, i * 128:(i + 1) * 128]
        RR = (lambda ap: ap.bitcast(mybir.dt.float32r)) if not _os.environ.get('NO_F32R') else (lambda ap: ap)
        names = ["kTr", "qTr", "kTp", "qTp", "ktk", "vtk", "Ug", "E", "Ei", "Es", "aT", "Ya", "YaT", "Yb", "YbT", "Wa", "Wb", "dg", "qdT", "kdec", "rhs2", "vn"]
        gb = {}
        for h in range(4):
            for par in range(2):
                for n in names:
                    gb[(n, h, par)] = kb.sb(f"{n}{h}{par}", [128, 128], F32)
                gb[("sm", h, par)] = kb.sb(f"sm{h}{par}", [128, 8], F32)
                gb[("smc", h, par)] = kb.sb(f"smc{h}{par}", [128, 8], F32)
                gb[("g2", h, par)] = kb.sb(f"g2{h}{par}", [128, 2], F32)
        Sst = [[kb.sb(f"S{h}{i}", [128, 128], F32) for i in range(2)] for h in range(4)]
        obuf = [kb.sb(f"obuf{h}", [128, 512], F32) for h in range(4)]
        gsq = kb.sb("gsq", [128, 512], F32)
        gsd = kb.sb("gsd", [128, 512], F32)
        gon = kb.sb("gon", [128, 512], F32)
        gsg = [kb.sb(f"gsg{i}", [128, 512], F32) for i in range(2)]
        gmx = [kb.sb(f"gmx{i}", [128, 512], BF16) for i in range(2)]
        pA = kb.ps("pA", [128, 512], F32)
        pB = kb.ps("pB", [128, 512], F32)
        pC = kb.ps("pC", [128, 512], F32)
        pKS = kb.ps("pKS", [128, 512], F32)
        pVN = kb.ps("pVN", [128, 512], F32)
        pSS = kb.ps("pSS", [128, 512], F32)
        pO = kb.ps("pO", [128, 512], F32)
        pP = kb.ps("pP", [128, 512], F32)
        for h in range(4):
            kb.dve(lambda e: e.tensor_scalar(out=RR(Sst[h][0][:]), in0=identf[:], scalar1=0.0, scalar2=None, op0=ALU.mult), r=["const"], w=[f"S{h}0"])
        scur = [0, 0, 0, 0]
        H4 = range(4)

        def pre_stages(p):
            par = p % 2
            tok = slice(p * 128, (p + 1) * 128)
            B = lambda n, h: gb[(n, h, par)]
            K_ = lambda n, h: f"{n}{h}{par}"
            gcol = lambda h: bgres[:, p, 4 + h:5 + h]
            beta = lambda h: bgres[:, p, h:h + 1]
            st = []

            def s_load():
                for h in H4:
                    kb.dma(B("kTp", h)[:], gkT[h, :, tok], r=["gqkT"], w=[K_("kTp", h)])
                    kb.dma(B("qTp", h)[:], gqT[h, :, tok], r=["gqkT"], w=[K_("qTp", h)])
                    kb.dma(B("ktk", h)[:], gktok[h, tok, :], r=["gtok"], w=[K_("ktk", h)])
                    kb.dma(B("vtk", h)[:], gvtok[h, tok, :], r=["gtok"], w=[K_("vtk", h)])
            st.append(s_load)

            def s_ug():
                for h in H4:
                    kb.pool(lambda e: e.tensor_copy(out=RR(B("kTr", h)[:]), in_=B("kTp", h)[:]), r=[K_("kTp", h)], w=[K_("kTr", h)])
                    kb.pool(lambda e: e.tensor_copy(out=RR(B("qTr", h)[:]), in_=B("qTp", h)[:]), r=[K_("qTp", h)], w=[K_("qTr", h)])
                    kb.dve(lambda e: e.tensor_scalar(out=B("Ug", h)[:], in0=ublk, scalar1=gcol(h), scalar2=None, op0=ALU.mult), r=["const", "bgres"], w=[K_("Ug", h)])
                    kb.dve(lambda e: e.tensor_copy(out=B("g2", h)[:], in_=gcol(h).to_broadcast([128, 2])), r=["bgres"], w=[K_("g2", h)])
            st.append(s_ug)

            def s_g():
                for h in H4:
                    kb.pe(lambda e: e.matmul(Q(pA, h), lhsT=B("Ug", h)[:], rhs=onesblk, start=True, stop=False), r=[K_("Ug", h), "const"], w=["pA"], inc=False)
                    kb.pe(lambda e: e.matmul(Q(pA, h), lhsT=negblk[:], rhs=B("Ug", h)[:], start=False, stop=True), r=[K_("Ug", h), "const"], w=["pA"])
                for h in H4:
                    o8 = 8 * h
                    kb.pe(lambda e: e.matmul(pB[:, o8:o8 + 2], lhsT=B("Ug", h)[:], rhs=onesf[:, 0:2], start=True, stop=True), r=[K_("Ug", h), "const"], w=["pB"], inc=False)
                    kb.pe(lambda e: e.matmul(pB[:, o8 + 2:o8 + 4], lhsT=onesblk, rhs=B("g2", h)[:], start=True, stop=True), r=[K_("g2", h), "const"], w=["pB"], inc=False)
                    kb.pe(lambda e: e.matmul(pB[:, o8 + 4:o8 + 6], lhsT=sel[0], rhs=B("g2", h)[:], start=True, stop=True), r=[K_("g2", h), "const"], w=["pB"], inc=False)
                    kb.pe(lambda e: e.matmul(pB[:, o8 + 6:o8 + 8], lhsT=sel[1], rhs=B("g2", h)[:], start=True, stop=True), r=[K_("g2", h), "const"], w=["pB"])
            st.append(s_g)

            def s_e1():
                for h in H4:
                    kb.act(lambda e: e.activation(out=B("smc", h)[:], in_=pB[:, 8 * h:8 * h + 8], func=ACT.Copy), r=["pB"], w=[K_("smc", h)])
                    kb.dve(lambda e: e.tensor_scalar(out=B("E", h)[:], in0=Q(pA, h), scalar1=-1.0, scalar2=0.0, op0=ALU.mult, op1=ALU.min), r=["pA"], w=[K_("E", h)])
            st.append(s_e1)

            def s_e2():
                for h in H4:
                    sm, smc = B("sm", h), B("smc", h)
                    kb.act(lambda e: e.activation(out=sm[:, 0:1], in_=smc[:, 0:1], func=ACT.Exp), r=[K_("smc", h)], w=[K_("sm", h)])
                    kb.act(lambda e: e.activation(out=B("E", h)[:], in_=B("E", h)[:], func=ACT.Exp), r=[K_("E", h)], w=[K_("E", h)])
                    kb.dve(lambda e: e.tensor_tensor(out=sm[:, 2:3], in0=smc[:, 2:3], in1=smc[:, 0:1], op=ALU.subtract), r=[K_("smc", h), K_("sm", h)], w=[K_("sm", h)])
            st.append(s_e2)

            def s_kk():
                for h in H4:
                    kb.pe(lambda e: e.matmul(Q(pC, h), lhsT=RR(B("kTr", h)[:]), rhs=RR(B("kTr", h)[:]), start=True, stop=True), r=[K_("kTr", h)], w=["pC"])
                for h in H4:
                    kb.pe(lambda e: e.matmul(Q(pA, h), lhsT=RR(B("kTr", h)[:]), rhs=RR(B("qTr", h)[:]), start=True, stop=True), r=[K_("kTr", h), K_("qTr", h)], w=["pA"])
            st.append(s_kk)

            def s_e3():
                for h in H4:
                    sm, smc = B("sm", h), B("smc", h)
                    kb.dve(lambda e: e.tensor_scalar(out=sm[:, 1:2], in0=sm[:, 0:1], scalar1=-1.0, scalar2=None, op0=ALU.mult), r=[K_("sm", h)], w=[K_("sm", h)])
                    kb.act(lambda e: e.activation(out=sm[:, 2:3], in_=sm[:, 2:3], func=ACT.Exp), r=[K_("sm", h)], w=[K_("sm", h)])
                    kb.act(lambda e: e.activation(out=sm[:, 4:8], in_=smc[:, 4:8], func=ACT.Exp), r=[K_("smc", h), K_("sm", h)], w=[K_("sm", h)])
                    kb.pool(lambda e: e.tensor_tensor(out=B("Ei", h)[:], in0=B("E", h)[:], in1=uincl, op=ALU.mult), r=[K_("E", h), "const"], w=[K_("Ei", h)])
                    kb.pool(lambda e: e.tensor_tensor(out=B("Es", h)[:], in0=B("E", h)[:], in1=negustr, op=ALU.mult), r=[K_("E", h), "const"], w=[K_("Es", h)])
            st.append(s_e3)

            def s_ya():
                for h in H4:
                    kb.dve(lambda e: e.scalar_tensor_tensor(out=RR(B("Ya", h)[:]), in0=Q(pC, h), scalar=beta(h), in1=B("Es", h)[:], op0=ALU.mult, op1=ALU.mult),
                           r=["pC", K_("Es", h), "bgres"], w=[K_("Ya", h)])
                for h in H4:
                    kb.dve(lambda e: e.tensor_tensor(out=RR(B("aT", h)[:]), in0=Q(pA, h), in1=B("Ei", h)[:], op=ALU.mult), r=["pA", K_("Ei", h)], w=[K_("aT", h)])
            st.append(s_ya)

            def s_tr():
                for h in H4:
                    kb.pe(lambda e: e.transpose(out=Q(pB, h), in_=B("Ya", h)[:], identity=identf[:]), r=[K_("Ya", h), "const"], w=["pB"])
            st.append(s_tr)

            def s_tr2():
                for h in H4:
                    kb.act(lambda e: e.activation(out=RR(B("YaT", h)[:]), in_=Q(pB, h), func=ACT.Copy), r=["pB"], w=[K_("YaT", h)])
                    kb.dve(lambda e: e.tensor_tensor(out=RR(B("Wa", h)[:]), in0=B("Ya", h)[:], in1=identf[:], op=ALU.add), r=[K_("Ya", h), "const"], w=[K_("Wa", h)])
            st.append(s_tr2)

            cur = ["Ya", "YaT", "Wa"]
            for k in range(1, 6):
                Y, YT, W = cur
                Yn, YTn, Wn = ("Yb", "YbT", "Wb") if Y == "Ya" else ("Ya", "YaT", "Wa")

                def s_sq(k=k, Y=Y, YT=YT):
                    if k < 5:
                        for h in H4:
                            kb.pe(lambda e: e.matmul(Q(pA, h), lhsT=RR(B(YT, h)[:]), rhs=RR(B(Y, h)[:]), start=True, stop=True), r=[K_(Y, h), K_(YT, h)], w=["pA"])
                    for h in H4:
                        kb.pe(lambda e: e.matmul(Q(pC, h), lhsT=RR(B(Y, h)[:]), rhs=RR(B(YT, h)[:]), start=True, stop=True), r=[K_(Y, h), K_(YT, h)], w=["pC"])
                st.append(s_sq)

                def s_ev(k=k, Yn=Yn, YTn=YTn):
                    for h in H4:
                        if k < 5:
                            kb.act(lambda e: e.activation(out=RR(B(Yn, h)[:]), in_=Q(pA, h), func=ACT.Copy), r=["pA"], w=[K_(Yn, h)])
                        kb.dve(lambda e: e.tensor_copy(out=RR(B(YTn, h)[:]), in_=Q(pC, h)), r=["pC"], w=[K_(YTn, h)])
                st.append(s_ev)

                def s_w(YTn=YTn, W=W):
                    for h in H4:
                        kb.pe(lambda e: e.matmul(Q(pB, h), lhsT=RR(B(YTn, h)[:]), rhs=RR(B(W, h)[:]), start=True, stop=True), r=[K_(YTn, h), K_(W, h)], w=["pB"])
                st.append(s_w)

                def s_w2(W=W, Wn=Wn):
                    for h in H4:
                        kb.dve(lambda e: e.tensor_tensor(out=RR(B(Wn, h)[:]), in0=Q(pB, h), in1=B(W, h)[:], op=ALU.add), r=["pB", K_(W, h)], w=[K_(Wn, h)])
                st.append(s_w2)
                cur = [Yn, YTn, Wn]
            for h in H4:
                gb[("Wfin", h, par)] = cur[2]

            def s_dg():
                for h in H4:
                    sm = B("sm", h)
                    kb.dve(lambda e: e.tensor_scalar(out=B("dg", h)[:], in0=identf[:], scalar1=sm[:, 0:1], scalar2=None, op0=ALU.mult), r=["const", K_("sm", h)], w=[K_("dg", h)])
                    kb.act(lambda e: e.activation(out=RR(B("kdec", h)[:]), in_=B("ktk", h)[:], func=ACT.Copy, scale=sm[:, 2:3]), r=[K_("ktk", h), K_("sm", h)], w=[K_("kdec", h)])
            st.append(s_dg)

            def s_eg():
                for h in H4:
                    kb.pe(lambda e: e.matmul(Q(pA, h), lhsT=onesf, rhs=B("dg", h)[:], start=True, stop=True), r=[K_("dg", h), "const"], w=["pA"])
            st.append(s_eg)

            def s_qd():
                for h in H4:
                    kb.dve(lambda e: e.tensor_tensor(out=RR(B("qdT", h)[:]), in0=Q(pA, h), in1=B("qTp", h)[:], op=ALU.mult), r=["pA", K_("qTp", h)], w=[K_("qdT", h)])
            st.append(s_qd)
            return st

        def scan_stages(p):
            par = p % 2
            B = lambda n, h: gb[(n, h, par)]
            K_ = lambda n, h: f"{n}{h}{par}"
            st = []
            for c in range(2):
                hs = slice(64 * c, 64 * c + 64)

                def s1(hs=hs):
                    for h in H4:
                        S, Sk = Sst[h][scur[h]], f"S{h}{scur[h]}"
                        kb.pe(lambda e: e.matmul(Q(pKS, h), lhsT=RR(B("kTr", h)[:]), rhs=RR(S[:]), start=True, stop=True), r=[K_("kTr", h), Sk], w=["pKS"])
                st.append(s1)

                def s2(hs=hs):
                    for h in H4:
                        kb.dve(lambda e: e.scalar_tensor_tensor(out=RR(B("rhs2", h)[hs, :]), in0=Q(pKS, h)[hs, :], scalar=B("sm", h)[hs, 1:2], in1=B("vtk", h)[hs, :],
                                                                op0=ALU.mult, op1=ALU.add), r=["pKS", K_("sm", h), K_("vtk", h)], w=[K_("rhs2", h)])
                st.append(s2)

                def s3(hs=hs):
                    for h in H4:
                        Wf = gb[("Wfin", h, par)]
                        kb.pe(lambda e: e.matmul(Q(pVN, h), lhsT=RR(B(Wf, h)[hs, :]), rhs=RR(B("rhs2", h)[hs, :]), start=True, stop=True), r=[K_(Wf, h), K_("rhs2", h)], w=["pVN"])
                st.append(s3)

                def s4(hs=hs):
                    for h in H4:
                        kb.dve(lambda e: e.tensor_scalar(out=RR(B("vn", h)[hs, :]), in0=Q(pVN, h)[hs, :], scalar1=bgres[hs, p, h:h + 1], scalar2=None, op0=ALU.mult),
                               r=["pVN", "bgres"], w=[K_("vn", h)])
                st.append(s4)

                def s5(hs=hs, c=c):
                    for h in H4:
                        S, Sk = Sst[h][scur[h]], f"S{h}{scur[h]}"
                        oc = slice(h * 128 + 64 * c, h * 128 + 64 * c + 64)
                        kb.pe(lambda e: e.matmul(pO[:, oc], lhsT=RR(S[:]), rhs=RR(B("qdT", h)[:, hs]), start=True, stop=False), r=[Sk, K_("qdT", h)], w=["pO"], inc=False)
                        kb.pe(lambda e: e.matmul(pO[:, oc], lhsT=RR(B("vn", h)[hs, :]), rhs=RR(B("aT", h)[hs, hs]), start=False, stop=True), r=[K_("vn", h), K_("aT", h)], w=["pO"])
                        kb.pe(lambda e: e.matmul(Q(pSS, h), lhsT=RR(B("kdec", h)[hs, :]), rhs=RR(B("vn", h)[hs, :]), start=True, stop=True), r=[K_("kdec", h), K_("vn", h)], w=["pSS"])
                st.append(s5)

                def s6(c=c):
                    for h in H4:
                        S, Sk = Sst[h][scur[h]], f"S{h}{scur[h]}"
                        Sn, Snk = Sst[h][1 - scur[h]], f"S{h}{1-scur[h]}"
                        kb.dve(lambda e: e.scalar_tensor_tensor(out=RR(Sn[:]), in0=S[:], scalar=B("sm", h)[:, 4 + 2 * c:5 + 2 * c], in1=Q(pSS, h), op0=ALU.mult, op1=ALU.add),
                               r=[Sk, K_("sm", h), "pSS"], w=[Snk])
                        scur[h] = 1 - scur[h]
                st.append(s6)

            def s7():
                for h in H4:
                    kb.act(lambda e: e.activation(out=obuf[h][:, (p % 4) * 128:(p % 4 + 1) * 128], in_=Q(pO, h), func=ACT.Copy), r=["pO"], w=[f"obuf{h}"])
            st.append(s7)
            return st

        npost = [0]

        def post(g):
            tsl = slice(g * 512, (g + 1) * 512)
            for h in H4:
                i2 = npost[0] % 2
                kb.dma(gsg[i2][:], sgT[h, :, tsl], r=["sgT"], w=[f"gsg{i2}"])
                kb.pool(lambda e: e.tensor_tensor(out=gsq[:], in0=obuf[h][:], in1=obuf[h][:], op=ALU.mult), r=[f"obuf{h}"], w=["gsq"])
                kb.pe(lambda e: e.matmul(pP[:], lhsT=onesf, rhs=gsq[:], start=True, stop=True), r=["gsq", "const"], w=["pP"])
                kb.act(lambda e: e.activation(out=gsd[:], in_=pP[:], func=ACT.Ln, bias=epst[:], scale=1.0 / 128), r=["pP", "const"], w=["gsd"])
                kb.act(lambda e: e.activation(out=gsd[:], in_=gsd[:], func=ACT.Exp, scale=-0.5), r=["gsd"], w=["gsd"])
                kb.dve(lambda e: e.tensor_tensor(out=gon[:], in0=obuf[h][:], in1=gsd[:], op=ALU.mult), r=[f"obuf{h}", "gsd"], w=["gon"])
                kb.dve(lambda e: e.scalar_tensor_tensor(out=gmx[i2][:], in0=gon[:], scalar=gngs[:, 0:1], in1=gsg[i2][:], op0=ALU.mult, op1=ALU.mult),
                       r=["gon", "const", f"gsg{i2}"], w=[f"gmx{i2}"])
                kb.dma(mixT[h, :, tsl], gmx[i2][:], r=[f"gmx{i2}"], w=["mixT"], q="pool")
                npost[0] += 1

        for f in pre_stages(0):
            f()
        for p in range(NT):
            sc_ = scan_stages(p)
            pr_ = pre_stages(p + 1) if p + 1 < NT else []
            ratio = (len(pr_) + len(sc_) - 1) // len(sc_)
            pi = 0
            for f in sc_:
                f()
                for _ in range(ratio):
                    if pi < len(pr_):
                        pr_[pi]()
                        pi += 1
            while pi < len(pr_):
                pr_[pi]()
                pi += 1
            if p % 4 == 3:
                post(p // 4)
        kb.pop()
        if upto <= 2:
            kb.finish()
            return nc, kb
        kb.push()
        akT = [kb.sb(f"akT{i}", [128, T], BF16) for i in range(2)]
        aqT = [kb.sb(f"aqT{i}", [128, T], BF16) for i in range(2)]
        avv = [kb.sb(f"avv{i}", [128, NT, 128], BF16) for i in range(2)]
        pTb = [[kb.sb(f"pT{i}{j}", [128, 512], BF16) for j in range(2)] for i in range(2)]
        arl = [kb.sb(f"arl{i}", [128, 512], F32) for i in range(2)]
        lacc = [kb.sb(f"lacc{i}", [128, 512], F32) for i in range(2)]
        ao = [kb.sb(f"ao{i}", [128, 512], F32) for i in range(2)]
        aod = kb.sb("aod", [128, 512], F32)
        asq = kb.sb("asq", [128, 512], F32)
        asd = kb.sb("asd", [128, 512], F32)
        aon = kb.sb("aon", [128, 512], F32)
        asg = [kb.sb(f"asg{i}", [128, 512], F32) for i in range(2)]
        amx = [kb.sb(f"amx{i}", [128, 512], BF16) for i in range(2)]
        pS = [[kb.ps(f"pS{i}{j}", [128, 512], F32) for j in range(2)] for i in range(2)]
        pOa = [kb.ps(f"pOa{i}", [128, 512], F32) for i in range(2)]
        pLa = [kb.ps(f"pLa{i}", [128, 512], F32) for i in range(2)]
        nstep = 0
        for h in range(4):
            hb = h % 2
            for part in range(2):
                sl = slice(part * (T // 2), (part + 1) * (T // 2))
                kb.dma(akT[hb][:, sl], dkT[h, :, sl], r=["dkT"], w=[f"akT{hb}"])
                kb.dma(aqT[hb][:, sl], dqT[h, :, sl], r=["dqT"], w=[f"aqT{hb}"])
            dvh = dv[h].rearrange("(t p) e -> p t e", p=128)
            for part in range(4):
                kb.dma(avv[hb][:, part * 16:(part + 1) * 16, :], dvh[:, part * 16:(part + 1) * 16, :], r=["dv"], w=[f"avv{hb}"])
            steps = [(g, kt) for g in range(NG) for kt in range(4 * g + 4)]

            def emit_S(n):
                g, kt = steps[n]
                j = kt - 4 * g
                qlo = 128 * max(j, 0)
                qs = slice(qlo, 512)
                sb2 = n % 2
                for i in range(2):
                    rows = slice(64 * i, 64 * i + 64)
                    kb.pe(lambda e: e.matmul(pS[i][sb2][:, qs], lhsT=akT[hb][rows, kt * 128:(kt + 1) * 128], rhs=aqT[hb][rows, g * 512 + qlo:(g + 1) * 512],
                                             start=True, stop=True), r=[f"akT{hb}", f"aqT{hb}"], w=[f"pS{i}{sb2}"])

            def emit_rest(n):
                g, kt = steps[n]
                nk = 4 * g + 4
                j = kt - 4 * g
                qlo = 128 * max(j, 0)
                qs = slice(qlo, 512)
                sb2 = n % 2
                for i in range(2):
                    pt = pTb[i][sb2]
                    kb.act(lambda e: e.activation(out=pt[:, qs], in_=pS[i][sb2][:, qs], func=ACT.Exp, bias=negC[:], scale=0.125),
                           r=[f"pS{i}{sb2}", "negC"], w=[f"pT{i}{sb2}"])
                    if j >= 0:
                        kb.pool(lambda e: e.memset(pt[64:128, qlo:qlo + 64], 0.0), r=[], w=[f"pT{i}{sb2}"])
                for i in range(2):
                    pt = pTb[i][sb2]
                    kb.pe(lambda e: e.matmul(pOa[i][:, qs], lhsT=avv[hb][:, kt, :], rhs=pt[:, qs], start=(kt == 0), stop=(kt == nk - 1)),
                          r=[f"avv{hb}", f"pT{i}{sb2}"], w=[f"pOa{i}"])
                    ae = kb.pool if i == 0 else kb.dve
                    if kt == 0:
                        ae(lambda e: e.tensor_copy(out=lacc[i][:, qs], in_=pt[:, qs]), r=[f"pT{i}{sb2}"], w=[f"lacc{i}"])
                    else:
                        ae(lambda e: e.tensor_tensor(out=lacc[i][:, qs], in0=lacc[i][:, qs], in1=pt[:, qs], op=ALU.add), r=[f"pT{i}{sb2}", f"lacc{i}"], w=[f"lacc{i}"])
                if kt != nk - 1:
                    return
                tsl = slice(g * 512, (g + 1) * 512)
                i2 = g % 2
                kb.dma(asg[i2][:], sgT[4 + h, :, tsl], r=["sgT"], w=[f"asg{i2}"])
                for i in range(2):
                    kb.pe(lambda e: e.matmul(pLa[i][:], lhsT=onesf, rhs=lacc[i][:], start=True, stop=True), r=["const", f"lacc{i}"], w=[f"pLa{i}"])
                for i in range(2):
                    kb.act(lambda e: e.activation(out=arl[i][:], in_=pLa[i][:], func=ACT.Ln), r=[f"pLa{i}"], w=[f"arl{i}"])
                    kb.act(lambda e: e.activation(out=arl[i][:], in_=arl[i][:], func=ACT.Exp, scale=-1.0), r=[f"arl{i}"], w=[f"arl{i}"])
                    kb.dve(lambda e: e.tensor_tensor(out=ao[i][:], in0=pOa[i][:], in1=arl[i][:], op=ALU.mult), r=[f"pOa{i}", f"arl{i}"], w=[f"ao{i}"])
                kb.dve(lambda e: e.scalar_tensor_tensor(out=aod[:], in0=ao[1][:], scalar=neglam[:, 0:1], in1=ao[0][:], op0=ALU.mult, op1=ALU.add),
                       r=["ao0", "ao1", "neglam"], w=["aod"])
                kb.pool(lambda e: e.tensor_tensor(out=asq[:], in0=aod[:], in1=aod[:], op=ALU.mult), r=["aod"], w=["asq"])
                pN_ = pLa[0]
                kb.pe(lambda e: e.matmul(pN_[:], lhsT=onesf, rhs=asq[:], start=True, stop=True), r=["asq", "const"], w=["pLa0"])
                kb.act(lambda e: e.activation(out=asd[:], in_=pN_[:], func=ACT.Ln, bias=epst[:], scale=1.0 / 128), r=["pLa0", "const"], w=["asd"])
                kb.act(lambda e: e.activation(out=asd[:], in_=asd[:], func=ACT.Exp, scale=-0.5), r=["asd"], w=["asd"])
                kb.dve(lambda e: e.tensor_tensor(out=aon[:], in0=aod[:], in1=asd[:], op=ALU.mult), r=["aod", "asd"], w=["aon"])
                kb.dve(lambda e: e.scalar_tensor_tensor(out=amx[i2][:], in0=aon[:], scalar=slgs[:, 0:1], in1=asg[i2][:], op0=ALU.mult, op1=ALU.mult),
                       r=["aon", "slgs", f"asg{i2}"], w=[f"amx{i2}"])
                kb.dma(mixT[4 + h, :, tsl], amx[i2][:], r=[f"amx{i2}"], w=["mixT"], q="pool")

            emit_S(0)
            for n in range(len(steps)):
                if n + 1 < len(steps):
                    emit_S(n + 1)
                emit_rest(n)
        kb.pop()
        if upto <= 3:
            kb.finish()
            return nc, kb
        kb.push()
        mt = [kb.sb(f"mt{i}", [128, 8, 512], BF16) for i in range(2)]
        xr = [kb.sb(f"xr{i}", [128, 512], F32) for i in range(2)]
        y1 = [kb.sb(f"y1{i}", [128, 512], F32) for i in range(2)]
        y2 = [kb.sb(f"y2{i}", [128, 512], F32) for i in range(2)]
        pY = [kb.ps(f"pY{i}", [128, 512], F32) for i in range(2)]
        mixT_v = mixT.rearrange("m p t -> p m t")
        for g in range(NG):
            m_ = mt[g % 2]
            mk = f"mt{g%2}"
            kb.dma(m_[:], mixT_v[:, :, g * 512:(g + 1) * 512], r=["mixT"], w=[mk])
            for a in range(4):
                tt = 4 * g + a
                i2 = tt % 2
                kb.dma(xr[i2][:], xres[tt * 128:(tt + 1) * 128, :], w=[f"xr{i2}"])
                for m in range(8):
                    kb.pe(lambda e: e.matmul(pY[i2][:], lhsT=m_[:, m, a * 128:(a + 1) * 128], rhs=wobf[:, m, :], start=(m == 0), stop=(m == 7)),
                          r=[mk, "wobf"], w=[f"pY{i2}"], inc=(m == 7))
                kb.dve(lambda e: e.tensor_tensor(out=y1[i2][:], in0=pY[i2][:], in1=gaterep[:], op=ALU.mult), r=[f"pY{i2}", "gaterep"], w=[f"y1{i2}"])
                kb.pool(lambda e: e.tensor_tensor(out=y2[i2][:], in0=y1[i2][:], in1=xr[i2][:], op=ALU.add), r=[f"y1{i2}", f"xr{i2}"], w=[f"y2{i2}"])
                kb.dma(out[tt * 128:(tt + 1) * 128, :], y2[i2][:], r=[f"y2{i2}"], w=["out"], q="pool")
        kb.pop()
        kb.finish()
    return nc, kb


def _consts():
    i = np.arange(128)
    same = (i[:, None] // 64) == (i[None, :] // 64)
    onesf = np.ones((128, 128), np.float32)
    ublk = (same & (i[:, None] <= i[None, :])).astype(np.float32)
    uincl = ublk.copy()
    negustr = -(same & (i[None, :] > i[:, None])).astype(np.float32)
    onesblk = same.astype(np.float32)
    sel0 = np.repeat((i < 64).astype(np.float32)[:, None], 128, 1)
    sel1 = np.repeat((i >= 64).astype(np.float32)[:, None], 128, 1)
    masks = np.stack([onesf, ublk, uincl, negustr, onesblk, sel0, sel1], 1).astype(np.float32)
    invf = (np.float32(500000.0) ** (-np.arange(0, 16, 2, dtype=np.float32) / np.float32(16))).astype(np.float32)
    return {
        "c_identf": np.eye(128, dtype=np.float32),
        "c_identb": np.eye(128, dtype=np.float32).astype(ml_dtypes.bfloat16),
        "c_onesb": np.ones((128, 128), np.float32).astype(ml_dtypes.bfloat16),
        "c_masks": np.ascontiguousarray(masks),
        "c_negblk": -onesblk,
        "c_invf": np.ascontiguousarray(np.broadcast_to(invf[None, :], (128, 8))),
    }


def _rep(v, n=128):
    return np.ascontiguousarray(np.broadcast_to(np.asarray(v, np.float32).reshape(1, -1), (n, np.asarray(v).size)))


def prep_core(inp, b, half):
    f = lambda a: np.ascontiguousarray(np.asarray(a, np.float32))
    w_in = np.asarray(inp["w_in"][0], np.float32)
    off = np.cumsum([0, 512, 512, 512, 4, 4, 512, 512, 512, 512, 512])
    aq, ak, av, abeta, adec, agate, bq, bk, bv, bgate = [np.arange(off[i], off[i + 1]) for i in range(10)]
    cols = []
    for h in range(4):
        cols += [aq[h * 128:(h + 1) * 128], ak[h * 128:(h + 1) * 128], av[h * 128:(h + 1) * 128]]
    cols += [agate, bgate, bq, bk, bv, abeta, adec]
    cols = np.concatenate(cols)
    conv_w = np.asarray(inp["conv_w"][0], np.float32)
    convc = np.zeros((128, 12, 4), np.float32)
    for h in range(4):
        for kind in range(3):
            ch = kind * 512 + h * 128 + np.arange(128)
            convc[:, 3 * h + kind, :] = conv_w[:, ch].T
    w_ada = np.asarray(inp["w_ada"][0], np.float32)
    b_ada = np.asarray(inp["b_ada"][0], np.float32)
    gsl = slice(2048 + half * 512, 2048 + (half + 1) * 512)
    d = {
        "x": f(inp["x"][b]),
        "xres": f(inp["x"][b][:, half * 512:(half + 1) * 512]),
        "cT": f(np.asarray(inp["c"][b]).reshape(8, 128).T),
        "posT": np.ascontiguousarray(np.asarray(inp["positions"][b], np.int32).reshape(NT, 128).T),
        "wada": f(w_ada[:, 0:2048]),
        "wadag": f(w_ada[:, gsl]),
        "badac": f(b_ada[0:2048].reshape(16, 128).T),
        "badag": _rep(b_ada[gsl]),
        "normgc": f(np.asarray(inp["norm_g"][0]).reshape(8, 128).T),
        "win": f(w_in[:, cols]),
        "convc": convc,
        "alog": _rep(inp["a_log"][0]),
        "dtb": _rep(inp["dt_bias"][0]),
        "gng": f(np.asarray(inp["gdn_norm_g"][0]).reshape(128, 1)),
        "slg": f(np.asarray(inp["subln_g"][0]).reshape(128, 1)),
        "gqk": _rep(np.concatenate([np.tile(np.asarray(inp["q_norm_g"][0]), 8), np.tile(np.asarray(inp["k_norm_g"][0]), 8)])),
        "lamv": np.ascontiguousarray(np.broadcast_to(np.stack([np.asarray(inp[k][0], np.float32) for k in
                                                               ("lambda_q1", "lambda_k1", "lambda_q2", "lambda_k2")])[None], (128, 4, 64))),
        "wout": f(np.asarray(inp["w_out"][0])[:, half * 512:(half + 1) * 512]),
    }
    d.update(_consts())
    return d


_NC_CACHE = {}


def kernel(**inputs):
    if "nc" not in _NC_CACHE:
        _NC_CACHE["nc"] = build_program()[0]
    nc = _NC_CACHE["nc"]
    in_maps = [prep_core(inputs, core // 2, core % 2) for core in range(8)]
    res = run_bass_kernel_spmd(nc, in_maps, core_ids=list(range(8)))
    out = np.empty((4, T, D), np.float32)
    for core in range(8):
        out[core // 2, :, (core % 2) * 512:(core % 2 + 1) * 512] = res.results[core]["out"]
    return out
```

```python
import math
from contextlib import ExitStack

import numpy as np
import ml_dtypes

import concourse.bass as bass
import concourse.mybir as mybir
from concourse.bass_utils import run_bass_kernel_spmd

F32 = mybir.dt.float32
BF16 = mybir.dt.bfloat16
I32 = mybir.dt.int32
ACT = mybir.ActivationFunctionType
ALU = mybir.AluOpType
AX = mybir.AxisListType


class KB:
    NDMA = 40

    def __init__(self, nc):
        self.nc = nc
        self.eng = {"pe": nc.tensor, "act": nc.scalar, "dve": nc.vector, "pool": nc.gpsimd, "sp": nc.sync}
        self.stack = ExitStack()
        self.sem = {}
        self.cnt = {k: 0 for k in self.eng}
        self.waited = {k: {} for k in self.eng}
        self.last_w = {}
        self.readers = {}
        self.dsem = []
        self.dcnt = []
        self.dnext = 0
        self.nins = 0
        self.phase = 0
        self.excl = set()

    def ctx(self):
        for k in ("pe", "act", "dve", "pool"):
            self.sem[k] = self.stack.enter_context(self.nc.semaphore("s_" + k))
        for i in range(self.NDMA):
            self.dsem.append(self.stack.enter_context(self.nc.semaphore(f"s_dma{i}")))
            self.dcnt.append(0)
        return self.stack

    def sb(self, name, shape, dtype):
        return self.stack.enter_context(self.nc.sbuf_tensor(f"{name}_{self.phase}", shape, dtype))

    def ps(self, name, shape, dtype):
        self.excl.add(name)
        return self.stack.enter_context(self.nc.psum_tensor(f"{name}_{self.phase}", shape, dtype))

    def _handle(self, s):
        return self.dsem[s[1]] if isinstance(s, tuple) else self.sem[s]

    def _wait(self, en, deps):
        e = self.eng[en]
        need = {}
        for s, v in deps:
            if s == "pe" and en == "pe":
                continue
            if v > need.get(s, 0):
                need[s] = v
        for s, v in need.items():
            if self.waited[en].get(s, 0) < v:
                e.wait_ge(self._handle(s), v)
                self.waited[en][s] = v
                self.nins += 1

    def _deps(self, r, w):
        d = []
        for k in r:
            if k in self.last_w:
                d.append(self.last_w[k])
        for k in w:
            if k in self.last_w:
                d.append(self.last_w[k])
            d.extend(self.readers.get(k, ()))
        return d

    def _record(self, r, w, tag):
        for k in r:
            self.readers.setdefault(k, []).append(tag)
        for k in w:
            self.last_w[k] = tag
            self.readers[k] = []

    def _skip(self):
        import os
        lim = os.environ.get("KB_LIMIT")
        if not lim:
            return False
        ph, n = (int(v) for v in lim.split(":"))
        if self.phase != ph:
            return False
        self.pops = getattr(self, "pops", 0) + 1
        return self.pops > n

    def op(self, en, fn, r=(), w=(), inc=True):
        if self._skip():
            return None
        w = list(w) + [k for k in r if k in self.excl and k not in w]
        self._wait(en, self._deps(r, w))
        ins = fn(self.eng[en])
        self.nins += 1
        if inc:
            ins.then_inc(self.sem[en], 1)
            self.cnt[en] += 1
            tag = (en, self.cnt[en])
        else:
            tag = (en, self.cnt[en] + 1)
        self._record(r, w, tag)
        return ins

    def pe(self, fn, r=(), w=(), inc=True):
        return self.op("pe", fn, r, w, inc)

    def act(self, fn, r=(), w=()):
        return self.op("act", fn, r, w)

    def dve(self, fn, r=(), w=()):
        return self.op("dve", fn, r, w)

    def pool(self, fn, r=(), w=()):
        return self.op("pool", fn, r, w)

    def dma(self, out, in_, r=(), w=(), q="sp", **kw):
        if self._skip():
            return None
        deps = self._deps(r, w)
        i = self.dnext
        self.dnext = (self.dnext + 1) % self.NDMA
        if self.dcnt[i] > 0:
            deps.append((("d", i), 16 * self.dcnt[i]))
        self._wait(q, deps)
        ins = self.eng[q].dma_start(out=out, in_=in_, **kw)
        ins.then_inc(self.dsem[i], 16)
        self.nins += 1
        self.dcnt[i] += 1
        self._record(r, w, (("d", i), 16 * self.dcnt[i]))
        return ins

    def finish(self):
        deps = [(k, self.cnt[k]) for k in ("pe", "act", "dve", "pool") if self.cnt[k] > 0]
        deps += [(("d", i), 16 * c) for i, c in enumerate(self.dcnt) if c > 0]
        self._wait("sp", deps)

    def barrier(self):
        deps = [(k, self.cnt[k]) for k in ("pe", "act", "dve", "pool") if self.cnt[k] > 0]
        deps += [(("d", i), 16 * c) for i, c in enumerate(self.dcnt) if c > 0]
        for en in ("pe", "act", "dve", "pool", "sp"):
            self._wait(en, [d for d in deps if not (d[0] == "pe" and en == "pe")] + ([("pe", self.cnt["pe"])] if False else []))
        self.last_w = {}
        self.readers = {}

    def push(self):
        self._saved = getattr(self, "_saved", [])
        self._saved.append(self.stack)
        self.stack = ExitStack()
        self.stack.__enter__()
        self.phase += 1

    def pop(self):
        self.barrier()
        self.stack.__exit__(None, None, None)
        self.stack = self._saved.pop()


T = 8192
D = 1024
NT = T // 128
NG = T // 512
NCOL = 4104
EPS = 1e-6
LAMBDA_INIT = 0.8 - 0.6 * math.exp(0.0)
NFM = 20
TM0 = NFM * 128
TWO_PI = 2.0 * math.pi
MAGIC = 12582912.0
CW1 = 6.28125
CW2 = 0.0019350051879882812
CW3 = TWO_PI - CW1 - CW2


def build_program(dbg=(), upto=9):
    nc = bass.Bass("TRN2", target_bir_lowering=False)

    def din(name, shape, dt=F32):
        return nc.dram_tensor(name, list(shape), dt, kind="ExternalInput").ap()

    def dscr(name, shape, dt=F32):
        return nc.dram_tensor(name, list(shape), dt, kind="ExternalOutput" if name in dbg else "Internal").ap()

    x = din("x", [T, D])
    xres = din("xres", [T, 512])
    cT = din("cT", [128, 8])
    posT = din("posT", [128, NT], I32)
    wada = din("wada", [D, 2048])
    wadag = din("wadag", [D, 512])
    badac = din("badac", [128, 16])
    badag = din("badag", [128, 512])
    normgc = din("normgc", [128, 8])
    win = din("win", [D, NCOL])
    convc = din("convc", [128, 12, 4])
    alog = din("alog", [128, 4])
    dtb = din("dtb", [128, 4])
    gng = din("gng", [128, 1])
    slg = din("slg", [128, 1])
    gqk = din("gqk", [128, 1024])
    lamv = din("lamv", [128, 4, 64])
    wout = din("wout", [D, 512])
    c_identf = din("c_identf", [128, 128])
    c_identb = din("c_identb", [128, 128], BF16)
    c_onesb = din("c_onesb", [128, 128], BF16)
    c_masks = din("c_masks", [128, 7, 128])
    c_negblk = din("c_negblk", [128, 128])
    c_invf = din("c_invf", [128, 8])
    out = nc.dram_tensor("out", [T, 512], F32, kind="ExternalOutput").ap()

    gqT = dscr("gqT", [4, 128, T])
    gkT = dscr("gkT", [4, 128, T])
    gktok = dscr("gktok", [4, T, 128])
    gvtok = dscr("gvtok", [4, T, 128])
    sgT = dscr("sgT", [8, 128, T])
    dqT = dscr("dqT", [4, 128, T], BF16)
    dkT = dscr("dkT", [4, 128, T], BF16)
    dv = dscr("dv", [4, T, 128], BF16)
    mixT = dscr("mixT", [8, 128, T], BF16)

    kb = KB(nc)
    with kb.ctx():
        identf = kb.sb("identf", [128, 128], F32)
        identb = kb.sb("identb", [128, 128], BF16)
        onesb = kb.sb("onesb", [128, 128], BF16)
        masks = kb.sb("masks", [128, 7, 128], F32)
        negblk = kb.sb("negblk", [128, 128], F32)
        onesf, ublk, uincl, negustr, onesblk = (masks[:, i, :] for i in range(5))
        sel = [masks[:, 5, :], masks[:, 6, :]]
        bgres = kb.sb("bgres", [128, NT, 8], F32)
        cosr = kb.sb("cosr", [128, NT, 8], F32)
        sinr = kb.sb("sinr", [128, NT, 8], F32)
        gaterep = kb.sb("gaterep", [128, 512], F32)
        mscale = kb.sb("mscale", [128, 8], F32)
        shiftc = kb.sb("shiftc", [128, 8], F32)
        negA = kb.sb("negA", [128, 4], F32)
        dtbs = kb.sb("dtbs", [128, 4], F32)
        gngs = kb.sb("gngs", [128, 1], F32)
        slgs = kb.sb("slgs", [128, 1], F32)
        neglam = kb.sb("neglam", [128, 1], F32)
        negC = kb.sb("negC", [128, 1], F32)
        gqks = kb.sb("gqks", [128, 1024], F32)
        convs = kb.sb("convs", [128, 12, 4], F32)
        wobf = kb.sb("wobf", [128, 8, 512], BF16)

        epst = kb.sb("epst", [128, 1], F32)
        onet = kb.sb("onet", [128, 1], F32)
        CONSTK = ["const"]
        kb.pool(lambda e: e.memset(epst[:], EPS), w=CONSTK)
        kb.pool(lambda e: e.memset(onet[:], 1.0), w=CONSTK)
        for dst, src in ((identf, c_identf), (identb, c_identb), (onesb, c_onesb), (masks, c_masks), (negblk, c_negblk),
                         (negA, alog), (dtbs, dtb), (gngs, gng), (slgs, slg), (gqks, gqk), (convs, convc)):
            kb.dma(dst[:], src, w=CONSTK)

        kb.push()
        sc = kb.sb("sc", [128, 8], F32)
        screp = kb.sb("screp", [128, 8, 128], F32)
        wch = [kb.sb(f"wch{i}", [128, 8, 512], F32) for i in range(2)]
        modcol = kb.sb("modcol", [128, 16], F32)
        bcol = kb.sb("bcol", [128, 16], F32)
        ngc = kb.sb("ngc", [128, 8], F32)
        bgate = kb.sb("bgate", [128, 512], F32)
        lams = kb.sb("lams", [128, 4, 64], F32)
        lprod = kb.sb("lprod", [128, 2, 64], F32)
        lsum = kb.sb("lsum", [128, 2], F32)
        posi = kb.sb("posi", [128, NT], I32)
        posf = kb.sb("posf", [128, NT], F32)
        invf = kb.sb("invf", [128, 8], F32)
        ang = kb.sb("ang", [128, NT, 8], F32)
        tk = kb.sb("tk", [128, NT, 8], F32)
        rr = kb.sb("rr", [128, NT, 8], F32)
        gmax = kb.sb("gmax", [128, 2], F32)
        psm = kb.ps("psm", [128, 512], F32)
        psg = kb.ps("psg", [128, 512], F32)

        kb.dma(sc[:], cT, w=["sc"])
        kb.dma(bcol[:], badac, w=["bcol"])
        kb.dma(ngc[:], normgc, w=["ngc"])
        kb.dma(bgate[:], badag, w=["bgate"])
        kb.dma(lams[:], lamv, w=["lams"])
        kb.dma(posi[:], posT, w=["posi"])
        kb.dma(invf[:], c_invf, w=["invf"])
        kb.act(lambda e: e.activation(out=sc[:], in_=sc[:], func=ACT.Silu), r=["sc"], w=["sc"])
        kb.dve(lambda e: e.tensor_copy(out=screp[:], in_=sc[:].unsqueeze(2).to_broadcast([128, 8, 128])), r=["sc"], w=["screp"])
        wada_v = wada.rearrange("(kc p) n -> p kc n", p=128)
        for cc in range(4):
            wt = wch[cc % 2]
            kb.dma(wt[:], wada_v[:, :, cc * 512:(cc + 1) * 512], w=[f"wch{cc%2}"])
            for jj in range(4):
                j = cc * 4 + jj
                for kc in range(8):
                    kb.pe(lambda e: e.matmul(psm[:, 2 * j:2 * j + 2], lhsT=wt[:, kc, jj * 128:(jj + 1) * 128], rhs=screp[:, kc, 0:2],
                                             start=(kc == 0), stop=(kc == 7)),
                          r=[f"wch{cc%2}", "screp"], w=["psm"], inc=(kc == 7))
        kb.dve(lambda e: e.tensor_tensor(out=modcol[:], in0=psm[:, 0:32].rearrange("p (j two) -> p j two", two=2)[:, :, 0], in1=bcol[:], op=ALU.add),
               r=["psm", "bcol"], w=["modcol"])
        kb.dve(lambda e: e.tensor_copy(out=shiftc[:], in_=modcol[:, 0:8]), r=["modcol"], w=["shiftc"])
        kb.dve(lambda e: e.scalar_tensor_tensor(out=mscale[:], in0=modcol[:, 8:16], scalar=1.0, in1=ngc[:], op0=ALU.add, op1=ALU.mult),
               r=["modcol", "ngc"], w=["mscale"])
        wt = wch[0]
        kb.dma(wt[:], wadag.rearrange("(kc p) n -> p kc n", p=128), w=["wch0"])
        for kc in range(8):
            kb.pe(lambda e: e.matmul(psg[:], lhsT=screp[:, kc, :], rhs=wt[:, kc, :], start=(kc == 0), stop=(kc == 7)),
                  r=["wch0", "screp"], w=["psg"], inc=(kc == 7))
        kb.dve(lambda e: e.tensor_tensor(out=gaterep[:], in0=psg[:], in1=bgate[:], op=ALU.add), r=["psg", "bgate"], w=["gaterep"])
        wt = wch[1]
        kb.dma(wt[:], wout.rearrange("(kc p) n -> p kc n", p=128), w=["wch1"])
        kb.dve(lambda e: e.tensor_copy(out=wobf[:], in_=wt[:]), r=["wch1"], w=["wobf"])
        kb.dve(lambda e: e.tensor_tensor(out=lprod[:, 0, :], in0=lams[:, 0, :], in1=lams[:, 1, :], op=ALU.mult), r=["lams"], w=["lprod"])
        kb.dve(lambda e: e.tensor_tensor(out=lprod[:, 1, :], in0=lams[:, 2, :], in1=lams[:, 3, :], op=ALU.mult), r=["lams", "lprod"], w=["lprod"])
        kb.dve(lambda e: e.tensor_reduce(out=lsum[:], in_=lprod[:], axis=AX.X, op=ALU.add), r=["lprod"], w=["lsum"])
        kb.act(lambda e: e.activation(out=lsum[:], in_=lsum[:], func=ACT.Exp), r=["lsum"], w=["lsum"])
        kb.dve(lambda e: e.scalar_tensor_tensor(out=neglam[:], in0=lsum[:, 1:2], scalar=-LAMBDA_INIT, in1=lsum[:, 0:1], op0=ALU.add, op1=ALU.subtract),
               r=["lsum"], w=["neglam"])
        kb.act(lambda e: e.activation(out=negA[:], in_=negA[:], func=ACT.Exp), r=CONSTK, w=["negA"])
        kb.dve(lambda e: e.tensor_scalar(out=negA[:], in0=negA[:], scalar1=-1.0, scalar2=None, op0=ALU.mult), r=["negA"], w=["negA"])
        kb.dve(lambda e: e.tensor_scalar(out=slgs[:], in0=slgs[:], scalar1=1.0 - LAMBDA_INIT, scalar2=None, op0=ALU.mult), r=CONSTK, w=["slgs"])
        kb.dve(lambda e: e.tensor_reduce(out=gmax[:], in_=gqks[:].rearrange("p (a n) -> p a n", a=2), axis=AX.X, op=ALU.max, apply_absolute_value=True),
               r=CONSTK, w=["gmax"])
        kb.dve(lambda e: e.scalar_tensor_tensor(out=negC[:], in0=gmax[:, 0:1], scalar=-8.0, in1=gmax[:, 1:2], op0=ALU.mult, op1=ALU.mult),
               r=["gmax"], w=["negC"])
        kb.dve(lambda e: e.tensor_copy(out=posf[:], in_=posi[:]), r=["posi"], w=["posf"])
        kb.dve(lambda e: e.tensor_tensor(out=ang[:], in0=posf[:].unsqueeze(2).to_broadcast([128, NT, 8]),
                                         in1=invf[:].unsqueeze(1).to_broadcast([128, NT, 8]), op=ALU.mult), r=["posf", "invf"], w=["ang"])
        for (dst, off, name) in ((sinr, 0.0, "sinr"), (cosr, 0.25, "cosr")):
            kb.dve(lambda e: e.tensor_scalar(out=tk[:], in0=ang[:], scalar1=1.0 / TWO_PI, scalar2=off, op0=ALU.mult, op1=ALU.add), r=["ang"], w=["tk"])
            kb.dve(lambda e: e.tensor_scalar(out=tk[:], in0=tk[:], scalar1=MAGIC, scalar2=None, op0=ALU.add), r=["tk"], w=["tk"])
            kb.dve(lambda e: e.tensor_scalar(out=tk[:], in0=tk[:], scalar1=-MAGIC, scalar2=None, op0=ALU.add), r=["tk"], w=["tk"])
            kb.dve(lambda e: e.scalar_tensor_tensor(out=rr[:], in0=tk[:], scalar=-CW1, in1=ang[:], op0=ALU.mult, op1=ALU.add), r=["ang", "tk"], w=["rr"])
            kb.dve(lambda e: e.scalar_tensor_tensor(out=rr[:], in0=tk[:], scalar=-CW2, in1=rr[:], op0=ALU.mult, op1=ALU.add), r=["rr", "tk"], w=["rr"])
            kb.dve(lambda e: e.scalar_tensor_tensor(out=rr[:], in0=tk[:], scalar=-CW3, in1=rr[:], op0=ALU.mult, op1=ALU.add), r=["rr", "tk"], w=["rr"])
            kb.dve(lambda e: e.tensor_scalar(out=rr[:], in0=rr[:], scalar1=off * TWO_PI, scalar2=math.pi, op0=ALU.add, op1=ALU.min), r=["rr"], w=["rr"])
            kb.dve(lambda e: e.tensor_scalar(out=rr[:], in0=rr[:], scalar1=-math.pi, scalar2=None, op0=ALU.max), r=["rr"], w=["rr"])
            kb.act(lambda e: e.activation(out=dst[:], in_=rr[:], func=ACT.Sin), r=["rr"], w=[name])
        kb.pop()
        kb.push()
        wbf = kb.sb("wbf", [128, 8, NCOL], BF16)
        wcb = [kb.sb(f"wcb{i}", [128, 8, 64], F32) for i in range(2)]
        xt = [kb.sb(f"xt{i}", [128, 1024], F32) for i in range(2)]
        xs = [kb.sb(f"xs{i}", [128, 1024], F32) for i in range(2)]
        st = [kb.sb(f"st{i}", [128, 4], F32) for i in range(2)]
        hT = [kb.sb(f"hT{i}", [128, 8, 512], BF16) for i in range(2)]
        cbuf = [kb.sb(f"cbuf{i}", [128, 515], F32) for i in range(12)]
        acc = [kb.sb(f"acc{i}", [128, 512], F32) for i in range(2)]
        yb = [kb.sb(f"yb{i}", [128, 512], F32) for i in range(3)]
        sqb = [kb.sb(f"sqb{i}", [128, 512], F32) for i in range(2)]
        sdb = [kb.sb(f"sdb{i}", [128, 512], F32) for i in range(2)]
        ynb = [kb.sb(f"ynb{i}", [128, 512], F32) for i in range(3)]
        trb = [kb.sb(f"trb{i}", [128, 512], F32) for i in range(2)]
        sgt = [kb.sb(f"sgt{i}", [128, 512], F32) for i in range(2)]
        qk = kb.sb("qk", [128, 1024], F32)
        ssq = kb.sb("ssq", [128, 16], F32)
        xn = kb.sb("xn", [128, 1024], F32)
        qkbs = [kb.sb(f"qkb{i}", [128, 1024], BF16) for i in range(2)]
        rt = [kb.sb(f"rt{i}", [128, 16, 8], F32) for i in range(4)]
        vb = [kb.sb(f"vb{i}", [128, 512], BF16) for i in range(2)]
        qkTg = [kb.sb("qkTg0", [128, 8, 512], BF16)]
        tmp4 = kb.sb("tmp4", [128, 4], F32)
        psT = kb.ps("psT", [128, 512], F32)
        psF = kb.ps("psF", [128, 512], F32)
        psQ = kb.ps("psQ", [128, 512], F32)
        psK = kb.ps("psK", [128, 512], F32)
        psV = kb.ps("psV", [128, 512], F32)
        psN = kb.ps("psN", [128, 512], F32)
        psXf = kb.ps("psXf", [128, 512], F32)
        psXb = kb.ps("psXb", [128, 1024], BF16)

        win_v = win.rearrange("(kc p) n -> p kc n", p=128)
        nch = (NCOL + 63) // 64
        for ci in range(nch):
            c0, c1 = ci * 64, min(NCOL, ci * 64 + 64)
            wt = wcb[ci % 2]
            kb.dma(wt[:, :, 0:c1 - c0], win_v[:, :, c0:c1], w=[f"wcb{ci%2}"])
            if ci % 2 == 0:
                kb.dve(lambda e: e.tensor_copy(out=wbf[:, :, c0:c1], in_=wt[:, :, 0:c1 - c0]), r=[f"wcb{ci%2}"], w=["wbf"])
            else:
                kb.pool(lambda e: e.tensor_copy(out=wbf[:, :, c0:c1], in_=wt[:, :, 0:c1 - c0]), r=[f"wcb{ci%2}"], w=["wbf"])
        for c in range(12):
            kb.pool(lambda e: e.memset(cbuf[c][:, 0:3], 0.0), w=[f"cbuf{c}"])

        dqT_v = dqT.rearrange("h p t -> p h t")
        dkT_v = dkT.rearrange("h p t -> p h t")
        dv_v = dv.rearrange("h t e -> t h e")
        nfm = 0
        import os as _os
        if _os.environ.get('SKIP_P1'):
            kb.pool(lambda e: e.memset(bgres[:], -0.05), w=["bgres"])
            for h in range(4):
                kb.dma(gqT[h, :, 0:512], x[0:128, 0:512], w=["gqkT"])
                kb.dma(gkT[h, :, 0:512], x[0:128, 0:512], w=["gqkT"])
                kb.dma(gktok[h, 0:512, :].rearrange("(a p) e -> p a e", p=128), x[0:128, 0:512].rearrange("p (a e) -> p a e", a=4), w=["gtok"])
                kb.dma(gvtok[h, 0:512, :].rearrange("(a p) e -> p a e", p=128), x[0:128, 0:512].rearrange("p (a e) -> p a e", a=4), w=["gtok"])
                kb.dma(sgT[h, :, 0:512], x[0:128, 0:512], w=["sgT"])
        for g in range(0 if _os.environ.get('SKIP_P1') else NG):
            hTg = hT[g % 2]
            hk = f"hT{g%2}"
            for a in range(4):
                tt = 4 * g + a
                i2 = tt % 2
                kb.dma(xt[i2][:], x[tt * 128:(tt + 1) * 128, :], w=[f"xt{i2}"])
                s = st[i2]
                kb.act(lambda e: e.activation(out=xs[i2][:], in_=xt[i2][:], func=ACT.Square, accum_out=s[:, 0:1]), r=[f"xt{i2}"], w=[f"xs{i2}", f"st{i2}"])
                kb.dve(lambda e: e.tensor_scalar(out=s[:, 1:2], in0=s[:, 0:1], scalar1=1.0 / D, scalar2=EPS, op0=ALU.mult, op1=ALU.add), r=[f"st{i2}"], w=[f"st{i2}"])
                kb.act(lambda e: e.activation(out=s[:, 2:3], in_=s[:, 1:2], func=ACT.Sqrt), r=[f"st{i2}"], w=[f"st{i2}"])
                kb.dve(lambda e: e.reciprocal(out=s[:, 3:4], in_=s[:, 2:3]), r=[f"st{i2}"], w=[f"st{i2}"])
                kb.act(lambda e: e.activation(out=xs[i2][:], in_=xt[i2][:], func=ACT.Copy, scale=s[:, 3:4]), r=[f"xt{i2}", f"st{i2}"], w=[f"xs{i2}"])
                for half in range(2):
                    for q4 in range(4):
                        kc = half * 4 + q4
                        kb.pe(lambda e: e.transpose(out=psT[:, q4 * 128:(q4 + 1) * 128], in_=xs[i2][:, kc * 128:(kc + 1) * 128], identity=identf[:]),
                              r=[f"xs{i2}", "const"], w=["psT"], inc=(q4 == 3))
                    for q4 in range(4):
                        kc = half * 4 + q4
                        kb.dve(lambda e: e.tensor_scalar(out=hTg[:, kc, a * 128:(a + 1) * 128], in0=psT[:, q4 * 128:(q4 + 1) * 128],
                                                         scalar1=mscale[:, kc:kc + 1], scalar2=shiftc[:, kc:kc + 1], op0=ALU.mult, op1=ALU.add),
                               r=["psT", "mscale", "shiftc"], w=[hk])
            pend = []
            tsl = slice(g * 512, (g + 1) * 512)
            for c in range(NFM):
                for kc in range(8):
                    kb.pe(lambda e: e.matmul(psF[:], lhsT=wbf[:, kc, c * 128:(c + 1) * 128], rhs=hTg[:, kc, :], start=(kc == 0), stop=(kc == 7)),
                          r=["wbf", hk], w=["psF"], inc=(kc == 7))
                new2, new3 = None, None
                if c >= 12:
                    sg = sgt[c % 2]
                    kb.act(lambda e: e.activation(out=sg[:], in_=psF[:], func=ACT.Silu), r=["psF"], w=[f"sgt{c%2}"])
                    kb.dma(sgT[c - 12, :, tsl], sg[:], r=[f"sgt{c%2}"], w=["sgT"], q="pool")
                else:
                    h, kind = c // 3, c % 3
                    cb = cbuf[c]
                    ck = f"cbuf{c}"
                    kb.act(lambda e: e.activation(out=cb[:, 3:515], in_=psF[:], func=ACT.Copy), r=["psF"], w=[ck])
                    ac = acc[nfm % 2]
                    ak = f"acc{nfm%2}"
                    y = yb[nfm % 3]
                    yk = f"yb{nfm%3}"
                    kb.dve(lambda e: e.tensor_scalar(out=ac[:], in0=cb[:, 0:512], scalar1=convs[:, c, 0:1], scalar2=None, op0=ALU.mult), r=[ck, "const"], w=[ak])
                    for j in range(1, 4):
                        kb.dve(lambda e: e.scalar_tensor_tensor(out=ac[:], in0=cb[:, j:j + 512], scalar=convs[:, c, j:j + 1], in1=ac[:], op0=ALU.mult, op1=ALU.add),
                               r=[ck, ak, "const"], w=[ak])
                    kb.pool(lambda e: e.tensor_copy(out=cb[:, 0:3], in_=cb[:, 512:515]), r=[ck], w=[ck])
                    kb.act(lambda e: e.activation(out=y[:], in_=ac[:], func=ACT.Silu), r=[ak], w=[yk])
                    sq_, sqk = sqb[nfm % 2], f"sqb{nfm%2}"
                    sd_, sdk = sdb[nfm % 2], f"sdb{nfm%2}"
                    yn, ynk = ynb[nfm % 3], f"ynb{nfm%3}"
                    tr, trk = trb[nfm % 2], f"trb{nfm%2}"
                    if kind < 2:
                        kb.pool(lambda e: e.tensor_tensor(out=sq_[:], in0=y[:], in1=y[:], op=ALU.mult), r=[yk], w=[sqk])

                    def part3(h=h, kind=kind, src=(yn if kind < 2 else y), srck=(ynk if kind < 2 else yk), tr=tr, trk=trk):
                        for a in range(4):
                            kb.pe(lambda e: e.transpose(out=psXf[:, a * 128:(a + 1) * 128], in_=src[:, a * 128:(a + 1) * 128], identity=identf[:]),
                                  r=[srck, "const"], w=["psXf"], inc=(a == 3))
                        kb.dve(lambda e: e.tensor_copy(out=tr[:], in_=psXf[:]), r=["psXf"], w=[trk])
                        dst = (gktok if kind == 1 else gvtok)[h, g * 512:(g + 1) * 512, :].rearrange("(a p) e -> p a e", p=128)
                        kb.dma(dst, tr[:].rearrange("p (a e) -> p a e", a=4), r=[trk], w=["gtok"], q="pool")

                    def part2(h=h, kind=kind, y=y, yk=yk, sq_=sq_, sqk=sqk, sd_=sd_, sdk=sdk, yn=yn, ynk=ynk):
                        kb.pe(lambda e: e.matmul(psN[:], lhsT=onesf, rhs=sq_[:], start=True, stop=True), r=[sqk, "const"], w=["psN"])
                        kb.act(lambda e: e.activation(out=sd_[:], in_=psN[:], func=ACT.Ln, bias=epst[:]), r=["psN", "const"], w=[sdk])
                        kb.act(lambda e: e.activation(out=sd_[:], in_=sd_[:], func=ACT.Exp, scale=-0.5), r=[sdk], w=[sdk])
                        cq = (128.0 ** -0.5) if kind == 0 else 1.0
                        kb.dve(lambda e: e.scalar_tensor_tensor(out=yn[:], in0=y[:], scalar=cq, in1=sd_[:], op0=ALU.mult, op1=ALU.mult), r=[yk, sdk], w=[ynk])
                        kb.dma((gqT if kind == 0 else gkT)[h, :, tsl], yn[:], r=[ynk], w=["gqkT"], q="pool")

                    if kind < 2:
                        new2 = part2
                    if kind >= 1:
                        new3 = part3
                    nfm += 1
                for item in [it for it in pend if it[0] <= c]:
                    pend.remove(item)
                    item[1]()
                if new2 is not None:
                    pend.append((c + 1, new2))
                if new3 is not None:
                    pend.append((c + 2 if new2 is not None else c + 1, new3))
            for item in sorted(pend, key=lambda it: it[0]):
                item[1]()
            qTg = qkTg[0]
            qTk = "qkTg0"
            pendC = None
            for a in range(4):
                tt = 4 * g + a
                qkb, qkk = qkbs[a % 2], f"qkb{a%2}"
                lhs = lambda kc: hTg[:, kc, a * 128:(a + 1) * 128]
                for (pst, pk, c0, n) in ((psQ, "psQ", TM0, 512), (psK, "psK", TM0 + 512, 512), (psV, "psV", TM0 + 1024, 512), (psXf, "psXf", TM0 + 1536, 8)):
                    for kc in range(8):
                        kb.pe(lambda e: e.matmul(pst[:, 0:n], lhsT=lhs(kc), rhs=wbf[:, kc, c0:c0 + n], start=(kc == 0), stop=(kc == 7)),
                              r=["wbf", hk], w=[pk], inc=(kc == 7))
                if pendC is not None:
                    pendC()
                    pendC = None
                kb.act(lambda e: e.activation(out=bgres[:, tt, 0:4], in_=psXf[:, 0:4], func=ACT.Sigmoid), r=["psXf"], w=["bgres"])
                for h in range(4):
                    kb.act(lambda e: e.activation(out=tmp4[:, h:h + 1], in_=psXf[:, 4 + h:5 + h], func=ACT.Exp, bias=dtbs[:, h:h + 1]), r=["psXf", "const"], w=["tmp4"])
                kb.act(lambda e: e.activation(out=tmp4[:], in_=tmp4[:], func=ACT.Ln, bias=onet[:]), r=["tmp4", "const"], w=["tmp4"])
                kb.dve(lambda e: e.tensor_tensor(out=bgres[:, tt, 4:8], in0=tmp4[:], in1=negA[:], op=ALU.mult), r=["tmp4", "negA"], w=["bgres"])
                v_ = vb[a % 2]
                vk = f"vb{a%2}"
                kb.act(lambda e: e.activation(out=v_[:], in_=psV[:], func=ACT.Copy), r=["psV"], w=[vk])
                kb.dma(dv_v[tt * 128:(tt + 1) * 128, :, :], v_[:].rearrange("p (h e) -> p h e", h=4), r=[vk], w=["dv"], q="pool")
                kb.dve(lambda e: e.tensor_copy(out=qk[:, 0:512], in_=psQ[:]), r=["psQ"], w=["qk"])
                kb.act(lambda e: e.activation(out=qk[:, 512:1024], in_=psK[:], func=ACT.Copy), r=["psK"], w=["qk"])
                kb.pool(lambda e: e.tensor_tensor(out=xn[:], in0=qk[:], in1=qk[:], op=ALU.mult), r=["qk"], w=["xn"])
                kb.dve(lambda e: e.tensor_reduce(out=ssq[:], in_=xn[:].rearrange("p (s d) -> p s d", d=64), axis=AX.X, op=ALU.add), r=["xn"], w=["ssq"])
                kb.dve(lambda e: e.tensor_scalar(out=ssq[:], in0=ssq[:], scalar1=1.0 / 64, scalar2=EPS, op0=ALU.mult, op1=ALU.add), r=["ssq"], w=["ssq"])
                kb.act(lambda e: e.activation(out=ssq[:], in_=ssq[:], func=ACT.Sqrt), r=["ssq"], w=["ssq"])
                kb.dve(lambda e: e.reciprocal(out=ssq[:], in_=ssq[:]), r=["ssq"], w=["ssq"])
                qk3 = qk[:].rearrange("p (s d) -> p s d", d=64)
                xn3 = xn[:].rearrange("p (s d) -> p s d", d=64)
                kb.dve(lambda e: e.tensor_tensor(out=xn3, in0=qk3, in1=ssq[:].unsqueeze(2).to_broadcast([128, 16, 64]), op=ALU.mult), r=["qk", "ssq"], w=["xn"])
                kb.pool(lambda e: e.tensor_tensor(out=xn[:], in0=xn[:], in1=gqks[:], op=ALU.mult), r=["xn", "const"], w=["xn"])
                kb.act(lambda e: e.activation(out=qkb[:], in_=xn[:], func=ACT.Copy), r=["xn"], w=[qkk])
                qkb3 = qkb[:].rearrange("p (s d) -> p s d", d=64)
                cb_ = cosr[:, tt, :].unsqueeze(1).to_broadcast([128, 16, 8])
                sb_ = sinr[:, tt, :].unsqueeze(1).to_broadcast([128, 16, 8])
                x1, x2 = xn3[:, :, 0:8], xn3[:, :, 8:16]
                kb.dve(lambda e: e.tensor_tensor(out=rt[0][:], in0=x1, in1=cb_, op=ALU.mult), r=["xn", "cosr"], w=["rt0"])
                kb.dve(lambda e: e.tensor_tensor(out=rt[1][:], in0=x2, in1=sb_, op=ALU.mult), r=["xn", "sinr"], w=["rt1"])
                kb.dve(lambda e: e.tensor_tensor(out=rt[2][:], in0=x2, in1=cb_, op=ALU.mult), r=["xn", "cosr"], w=["rt2"])
                kb.dve(lambda e: e.tensor_tensor(out=rt[3][:], in0=x1, in1=sb_, op=ALU.mult), r=["xn", "sinr"], w=["rt3"])
                kb.dve(lambda e: e.tensor_tensor(out=qkb3[:, :, 0:8], in0=rt[0][:], in1=rt[1][:], op=ALU.subtract), r=["rt0", "rt1", qkk], w=[qkk])
                kb.dve(lambda e: e.tensor_tensor(out=qkb3[:, :, 8:16], in0=rt[2][:], in1=rt[3][:], op=ALU.add), r=["rt2", "rt3", qkk], w=[qkk])
                def trC(a=a, qkb=qkb, qkk=qkk):
                    for bi in range(8):
                        kb.pe(lambda e: e.transpose(out=psXb[:, bi * 128:(bi + 1) * 128], in_=qkb[:, bi * 128:(bi + 1) * 128], identity=identb[:]),
                              r=[qkk, "const"], w=["psXb"], inc=(bi == 7))
                    kb.dve(lambda e: e.tensor_copy(out=qTg[:, :, a * 128:(a + 1) * 128], in_=psXb[:].rearrange("p (b t) -> p b t", b=8)), r=["psXb"], w=[qTk])
                pendC = trC
            if pendC is not None:
                pendC()
                pendC = None
            kb.dma(dqT_v[:, :, g * 512:(g + 1) * 512], qTg[:, 0:4, :], r=[qTk], w=["dqT"], q="pool")
            kb.dma(dkT_v[:, :, g * 512:(g + 1) * 512], qTg[:, 4:8, :], r=[qTk], w=["dkT"], q="pool")
        kb.pop()
        kb.push()
        Q = lambda t, i: t[:, i * 128:(i + 1) * 128]
        RR = (lambda ap: ap.bitcast(mybir.dt.float32r)) if not _os.environ.get('NO_F32R') else (lambda ap: ap)
        names = ["kTr", "qTr", "kTp", "qTp", "ktk", "vtk", "Ug", "E", "Ei", "Es", "aT", "Ya", "YaT", "Yb", "YbT", "Wa", "Wb", "dg", "qdT", "kdec", "rhs2", "vn"]
        gb = {}
        for h in range(4):
            for par in range(2):
                for n in names:
                    gb[(n, h, par)] = kb.sb(f"{n}{h}{par}", [128, 128], F32)
                gb[("sm", h, par)] = kb.sb(f"sm{h}{par}", [128, 8], F32)
                gb[("smc", h, par)] = kb.sb(f"smc{h}{par}", [128, 8], F32)
                gb[("g2", h, par)] = kb.sb(f"g2{h}{par}", [128, 2], F32)
        Sst = [[kb.sb(f"S{h}{i}", [128, 128], F32) for i in range(2)] for h in range(4)]
        obuf = [kb.sb(f"obuf{h}", [128, 512], F32) for h in range(4)]
        gsq = kb.sb("gsq", [128, 512], F32)
        gsd = kb.sb("gsd", [128, 512], F32)
        gon = kb.sb("gon", [128, 512], F32)
        gsg = [kb.sb(f"gsg{i}", [128, 512], F32) for i in range(2)]
        gmx = [kb.sb(f"gmx{i}", [128, 512], BF16) for i in range(2)]
        pA = kb.ps("pA", [128, 512], F32)
        pB = kb.ps("pB", [128, 512], F32)
        pC = kb.ps("pC", [128, 512], F32)
        pKS = kb.ps("pKS", [128, 512], F32)
        pVN = kb.ps("pVN", [128, 512], F32)
        pSS = kb.ps("pSS", [128, 512], F32)
        pO = kb.ps("pO", [128, 512], F32)
        pP = kb.ps("pP", [128, 512], F32)
        for h in range(4):
            kb.dve(lambda e: e.tensor_scalar(out=RR(Sst[h][0][:]), in0=identf[:], scalar1=0.0, scalar2=None, op0=ALU.mult), r=["const"], w=[f"S{h}0"])
        scur = [0, 0, 0, 0]
        H4 = range(4)

        def pre_stages(p):
            par = p % 2
            tok = slice(p * 128, (p + 1) * 128)
            B = lambda n, h: gb[(n, h, par)]
            K_ = lambda n, h: f"{n}{h}{par}"
            gcol = lambda h: bgres[:, p, 4 + h:5 + h]
            beta = lambda h: bgres[:, p, h:h + 1]
            st = []

            def s_load():
                for h in H4:
                    kb.dma(B("kTp", h)[:], gkT[h, :, tok], r=["gqkT"], w=[K_("kTp", h)])
                    kb.dma(B("qTp", h)[:], gqT[h, :, tok], r=["gqkT"], w=[K_("qTp", h)])
                    kb.dma(B("ktk", h)[:], gktok[h, tok, :], r=["gtok"], w=[K_("ktk", h)])
                    kb.dma(B("vtk", h)[:], gvtok[h, tok, :], r=["gtok"], w=[K_("vtk", h)])
            st.append(s_load)

            def s_ug():
                for h in H4:
                    kb.pool(lambda e: e.tensor_copy(out=RR(B("kTr", h)[:]), in_=B("kTp", h)[:]), r=[K_("kTp", h)], w=[K_("kTr", h)])
                    kb.pool(lambda e: e.tensor_copy(out=RR(B("qTr", h)[:]), in_=B("qTp", h)[:]), r=[K_("qTp", h)], w=[K_("qTr", h)])
                    kb.dve(lambda e: e.tensor_scalar(out=B("Ug", h)[:], in0=ublk, scalar1=gcol(h), scalar2=None, op0=ALU.mult), r=["const", "bgres"], w=[K_("Ug", h)])
                    kb.dve(lambda e: e.tensor_copy(out=B("g2", h)[:], in_=gcol(h).to_broadcast([128, 2])), r=["bgres"], w=[K_("g2", h)])
            st.append(s_ug)

            def s_g():
                for h in H4:
                    kb.pe(lambda e: e.matmul(Q(pA, h), lhsT=B("Ug", h)[:], rhs=onesblk, start=True, stop=False), r=[K_("Ug", h), "const"], w=["pA"], inc=False)
                    kb.pe(lambda e: e.matmul(Q(pA, h), lhsT=negblk[:], rhs=B("Ug", h)[:], start=False, stop=True), r=[K_("Ug", h), "const"], w=["pA"])
                for h in H4:
                    o8 = 8 * h
                    kb.pe(lambda e: e.matmul(pB[:, o8:o8 + 2], lhsT=B("Ug", h)[:], rhs=onesf[:, 0:2], start=True, stop=True), r=[K_("Ug", h), "const"], w=["pB"], inc=False)
                    kb.pe(lambda e: e.matmul(pB[:, o8 + 2:o8 + 4], lhsT=onesblk, rhs=B("g2", h)[:], start=True, stop=True), r=[K_("g2", h), "const"], w=["pB"], inc=False)
                    kb.pe(lambda e: e.matmul(pB[:, o8 + 4:o8 + 6], lhsT=sel[0], rhs=B("g2", h)[:], start=True, stop=True), r=[K_("g2", h), "const"], w=["pB"], inc=False)
                    kb.pe(lambda e: e.matmul(pB[:, o8 + 6:o8 + 8], lhsT=sel[1], rhs=B("g2", h)[:], start=True, stop=True), r=[K_("g2", h), "const"], w=["pB"])
            st.append(s_g)

            def s_e1():
                for h in H4:
                    kb.act(lambda e: e.activation(out=B("smc", h)[:], in_=pB[:, 8 * h:8 * h + 8], func=ACT.Copy), r=["pB"], w=[K_("smc", h)])
                    kb.dve(lambda e: e.tensor_scalar(out=B("E", h)[:], in0=Q(pA, h), scalar1=-1.0, scalar2=0.0, op0=ALU.mult, op1=ALU.min), r=["pA"], w=[K_("E", h)])
            st.append(s_e1)

            def s_e2():
                for h in H4:
                    sm, smc = B("sm", h), B("smc", h)
                    kb.act(lambda e: e.activation(out=sm[:, 0:1], in_=smc[:, 0:1], func=ACT.Exp), r=[K_("smc", h)], w=[K_("sm", h)])
                    kb.act(lambda e: e.activation(out=B("E", h)[:], in_=B("E", h)[:], func=ACT.Exp), r=[K_("E", h)], w=[K_("E", h)])
                    kb.dve(lambda e: e.tensor_tensor(out=sm[:, 2:3], in0=smc[:, 2:3], in1=smc[:, 0:1], op=ALU.subtract), r=[K_("smc", h), K_("sm", h)], w=[K_("sm", h)])
            st.append(s_e2)

            def s_kk():
                for h in H4:
                    kb.pe(lambda e: e.matmul(Q(pC, h), lhsT=RR(B("kTr", h)[:]), rhs=RR(B("kTr", h)[:]), start=True, stop=True), r=[K_("kTr", h)], w=["pC"])
                for h in H4:
                    kb.pe(lambda e: e.matmul(Q(pA, h), lhsT=RR(B("kTr", h)[:]), rhs=RR(B("qTr", h)[:]), start=True, stop=True), r=[K_("kTr", h), K_("qTr", h)], w=["pA"])
            st.append(s_kk)

            def s_e3():
                for h in H4:
                    sm, smc = B("sm", h), B("smc", h)
                    kb.dve(lambda e: e.tensor_scalar(out=sm[:, 1:2], in0=sm[:, 0:1], scalar1=-1.0, scalar2=None, op0=ALU.mult), r=[K_("sm", h)], w=[K_("sm", h)])
                    kb.act(lambda e: e.activation(out=sm[:, 2:3], in_=sm[:, 2:3], func=ACT.Exp), r=[K_("sm", h)], w=[K_("sm", h)])
                    kb.act(lambda e: e.activation(out=sm[:, 4:8], in_=smc[:, 4:8], func=ACT.Exp), r=[K_("smc", h), K_("sm", h)], w=[K_("sm", h)])
                    kb.pool(lambda e: e.tensor_tensor(out=B("Ei", h)[:], in0=B("E", h)[:], in1=uincl, op=ALU.mult), r=[K_("E", h), "const"], w=[K_("Ei", h)])
                    kb.pool(lambda e: e.tensor_tensor(out=B("Es", h)[:], in0=B("E", h)[:], in1=negustr, op=ALU.mult), r=[K_("E", h), "const"], w=[K_("Es", h)])
            st.append(s_e3)

            def s_ya():
                for h in H4:
                    kb.dve(lambda e: e.scalar_tensor_tensor(out=RR(B("Ya", h)[:]), in0=Q(pC, h), scalar=beta(h), in1=B("Es", h)[:], op0=ALU.mult, op1=ALU.mult),
                           r=["pC", K_("Es", h), "bgres"], w=[K_("Ya", h)])
                for h in H4:
                    kb.dve(lambda e: e.tensor_tensor(out=RR(B("aT", h)[:]), in0=Q(pA, h), in1=B("Ei", h)[:], op=ALU.mult), r=["pA", K_("Ei", h)], w=[K_("aT", h)])
            st.append(s_ya)

            def s_tr():
                for h in H4:
                    kb.pe(lambda e: e.transpose(out=Q(pB, h), in_=B("Ya", h)[:], identity=identf[:]), r=[K_("Ya", h), "const"], w=["pB"])
            st.append(s_tr)

            def s_tr2():
                for h in H4:
                    kb.act(lambda e: e.activation(out=RR(B("YaT", h)[:]), in_=Q(pB, h), func=ACT.Copy), r=["pB"], w=[K_("YaT", h)])
                    kb.dve(lambda e: e.tensor_tensor(out=RR(B("Wa", h)[:]), in0=B("Ya", h)[:], in1=identf[:], op=ALU.add), r=[K_("Ya", h), "const"], w=[K_("Wa", h)])
            st.append(s_tr2)

            cur = ["Ya", "YaT", "Wa"]
            for k in range(1, 6):
                Y, YT, W = cur
                Yn, YTn, Wn = ("Yb", "YbT", "Wb") if Y == "Ya" else ("Ya", "YaT", "Wa")

                def s_sq(k=k, Y=Y, YT=YT):
                    if k < 5:
                        for h in H4:
                            kb.pe(lambda e: e.matmul(Q(pA, h), lhsT=RR(B(YT, h)[:]), rhs=RR(B(Y, h)[:]), start=True, stop=True), r=[K_(Y, h), K_(YT, h)], w=["pA"])
                    for h in H4:
                        kb.pe(lambda e: e.matmul(Q(pC, h), lhsT=RR(B(Y, h)[:]), rhs=RR(B(YT, h)[:]), start=True, stop=True), r=[K_(Y, h), K_(YT, h)], w=["pC"])
                st.append(s_sq)

                def s_ev(k=k, Yn=Yn, YTn=YTn):
                    for h in H4:
                        if k < 5:
                            kb.act(lambda e: e.activation(out=RR(B(Yn, h)[:]), in_=Q(pA, h), func=ACT.Copy), r=["pA"], w=[K_(Yn, h)])
                        kb.dve(lambda e: e.tensor_copy(out=RR(B(YTn, h)[:]), in_=Q(pC, h)), r=["pC"], w=[K_(YTn, h)])
                st.append(s_ev)

                def s_w(YTn=YTn, W=W):
                    for h in H4:
                        kb.pe(lambda e: e.matmul(Q(pB, h), lhsT=RR(B(YTn, h)[:]), rhs=RR(B(W, h)[:]), start=True, stop=True), r=[K_(YTn, h), K_(W, h)], w=["pB"])
                st.append(s_w)

                def s_w2(W=W, Wn=Wn):
                    for h in H4:
                        kb.dve(lambda e: e.tensor_tensor(out=RR(B(Wn, h)[:]), in0=Q(pB, h), in1=B(W, h)[:], op=ALU.add), r=["pB", K_(W, h)], w=[K_(Wn, h)])
                st.append(s_w2)
                cur = [Yn, YTn, Wn]
            for h in H4:
                gb[("Wfin", h, par)] = cur[2]

            def s_dg():
                for h in H4:
                    sm = B("sm", h)
                    kb.dve(lambda e: e.tensor_scalar(out=B("dg", h)[:], in0=identf[:], scalar1=sm[:, 0:1], scalar2=None, op0=ALU.mult), r=["const", K_("sm", h)], w=[K_("dg", h)])
                    kb.act(lambda e: e.activation(out=RR(B("kdec", h)[:]), in_=B("ktk", h)[:], func=ACT.Copy, scale=sm[:, 2:3]), r=[K_("ktk", h), K_("sm", h)], w=[K_("kdec", h)])
            st.append(s_dg)

            def s_eg():
                for h in H4:
                    kb.pe(lambda e: e.matmul(Q(pA, h), lhsT=onesf, rhs=B("dg", h)[:], start=True, stop=True), r=[K_("dg", h), "const"], w=["pA"])
            st.append(s_eg)

            def s_qd():
                for h in H4:
                    kb.dve(lambda e: e.tensor_tensor(out=RR(B("qdT", h)[:]), in0=Q(pA, h), in1=B("qTp", h)[:], op=ALU.mult), r=["pA", K_("qTp", h)], w=[K_("qdT", h)])
            st.append(s_qd)
            return st

        def scan_stages(p):
            par = p % 2
            B = lambda n, h: gb[(n, h, par)]
            K_ = lambda n, h: f"{n}{h}{par}"
            st = []
            for c in range(2):
                hs = slice(64 * c, 64 * c + 64)

                def s1(hs=hs):
                    for h in H4:
                        S, Sk = Sst[h][scur[h]], f"S{h}{scur[h]}"
                        kb.pe(lambda e: e.matmul(Q(pKS, h), lhsT=RR(B("kTr", h)[:]), rhs=RR(S[:]), start=True, stop=True), r=[K_("kTr", h), Sk], w=["pKS"])
                st.append(s1)

                def s2(hs=hs):
                    for h in H4:
                        kb.dve(lambda e: e.scalar_tensor_tensor(out=RR(B("rhs2", h)[hs, :]), in0=Q(pKS, h)[hs, :], scalar=B("sm", h)[hs, 1:2], in1=B("vtk", h)[hs, :],
                                                                op0=ALU.mult, op1=ALU.add), r=["pKS", K_("sm", h), K_("vtk", h)], w=[K_("rhs2", h)])
                st.append(s2)

                def s3(hs=hs):
                    for h in H4:
                        Wf = gb[("Wfin", h, par)]
                        kb.pe(lambda e: e.matmul(Q(pVN, h), lhsT=RR(B(Wf, h)[hs, :]), rhs=RR(B("rhs2", h)[hs, :]), start=True, stop=True), r=[K_(Wf, h), K_("rhs2", h)], w=["pVN"])
                st.append(s3)

                def s4(hs=hs):
                    for h in H4:
                        kb.dve(lambda e: e.tensor_scalar(out=RR(B("vn", h)[hs, :]), in0=Q(pVN, h)[hs, :], scalar1=bgres[hs, p, h:h + 1], scalar2=None, op0=ALU.mult),
                               r=["pVN", "bgres"], w=[K_("vn", h)])
                st.append(s4)

                def s5(hs=hs, c=c):
                    for h in H4:
                        S, Sk = Sst[h][scur[h]], f"S{h}{scur[h]}"
                        oc = slice(h * 128 + 64 * c, h * 128 + 64 * c + 64)
                        kb.pe(lambda e: e.matmul(pO[:, oc], lhsT=RR(S[:]), rhs=RR(B("qdT", h)[:, hs]), start=True, stop=False), r=[Sk, K_("qdT", h)], w=["pO"], inc=False)
                        kb.pe(lambda e: e.matmul(pO[:, oc], lhsT=RR(B("vn", h)[hs, :]), rhs=RR(B("aT", h)[hs, hs]), start=False, stop=True), r=[K_("vn", h), K_("aT", h)], w=["pO"])
                        kb.pe(lambda e: e.matmul(Q(pSS, h), lhsT=RR(B("kdec", h)[hs, :]), rhs=RR(B("vn", h)[hs, :]), start=True, stop=True), r=[K_("kdec", h), K_("vn", h)], w=["pSS"])
                st.append(s5)

                def s6(c=c):
                    for h in H4:
                        S, Sk = Sst[h][scur[h]], f"S{h}{scur[h]}"
                        Sn, Snk = Sst[h][1 - scur[h]], f"S{h}{1-scur[h]}"
                        kb.dve(lambda e: e.scalar_tensor_tensor(out=RR(Sn[:]), in0=S[:], scalar=B("sm", h)[:, 4 + 2 * c:5 + 2 * c], in1=Q(pSS, h), op0=ALU.mult, op1=ALU.add),
                               r=[Sk, K_("sm", h), "pSS"], w=[Snk])
                        scur[h] = 1 - scur[h]
                st.append(s6)

            def s7():
                for h in H4:
                    kb.act(lambda e: e.activation(out=obuf[h][:, (p % 4) * 128:(p % 4 + 1) * 128], in_=Q(pO, h), func=ACT.Copy), r=["pO"], w=[f"obuf{h}"])
            st.append(s7)
            return st

        npost = [0]

        def post(g):
            tsl = slice(g * 512, (g + 1) * 512)
            for h in H4:
                i2 = npost[0] % 2
                kb.dma(gsg[i2][:], sgT[h, :, tsl], r=["sgT"], w=[f"gsg{i2}"])
                kb.pool(lambda e: e.tensor_tensor(out=gsq[:], in0=obuf[h][:], in1=obuf[h][:], op=ALU.mult), r=[f"obuf{h}"], w=["gsq"])
                kb.pe(lambda e: e.matmul(pP[:], lhsT=onesf, rhs=gsq[:], start=True, stop=True), r=["gsq", "const"], w=["pP"])
                kb.act(lambda e: e.activation(out=gsd[:], in_=pP[:], func=ACT.Ln, bias=epst[:], scale=1.0 / 128), r=["pP", "const"], w=["gsd"])
                kb.act(lambda e: e.activation(out=gsd[:], in_=gsd[:], func=ACT.Exp, scale=-0.5), r=["gsd"], w=["gsd"])
                kb.dve(lambda e: e.tensor_tensor(out=gon[:], in0=obuf[h][:], in1=gsd[:], op=ALU.mult), r=[f"obuf{h}", "gsd"], w=["gon"])
                kb.dve(lambda e: e.scalar_tensor_tensor(out=gmx[i2][:], in0=gon[:], scalar=gngs[:, 0:1], in1=gsg[i2][:], op0=ALU.mult, op1=ALU.mult),
                       r=["gon", "const", f"gsg{i2}"], w=[f"gmx{i2}"])
                kb.dma(mixT[h, :, tsl], gmx[i2][:], r=[f"gmx{i2}"], w=["mixT"], q="pool")
                npost[0] += 1

        for f in pre_stages(0):
            f()
        for p in range(NT):
            sc_ = scan_stages(p)
            pr_ = pre_stages(p + 1) if p + 1 < NT else []
            ratio = (len(pr_) + len(sc_) - 1) // len(sc_)
            pi = 0
            for f in sc_:
                f()
                for _ in range(ratio):
                    if pi < len(pr_):
                        pr_[pi]()
                        pi += 1
            while pi < len(pr_):
                pr_[pi]()
                pi += 1
            if p % 4 == 3:
                post(p // 4)
        kb.pop()
        if upto <= 2:
            kb.finish()
            return nc, kb
        kb.push()
        akT = [kb.sb(f"akT{i}", [128, T], BF16) for i in range(2)]
        aqT = [kb.sb(f"aqT{i}", [128, T], BF16) for i in range(2)]
        avv = [kb.sb(f"avv{i}", [128, NT, 128], BF16) for i in range(2)]
        pTb = [[kb.sb(f"pT{i}{j}", [128, 512], BF16) for j in range(2)] for i in range(2)]
        arl = [kb.sb(f"arl{i}", [128, 512], F32) for i in range(2)]
        lacc = [kb.sb(f"lacc{i}", [128, 512], F32) for i in range(2)]
        araw = [kb.sb(f"araw{i}", [128, 512], F32) for i in range(2)]
        ao = [kb.sb(f"ao{i}", [128, 512], F32) for i in range(2)]
        aod = kb.sb("aod", [128, 512], F32)
        asq = kb.sb("asq", [128, 512], F32)
        asd = kb.sb("asd", [128, 512], F32)
        aon = kb.sb("aon", [128, 512], F32)
        asg = [kb.sb(f"asg{i}", [128, 512], F32) for i in range(2)]
        amx = [kb.sb(f"amx{i}", [128, 512], BF16) for i in range(2)]
        pS = [[kb.ps(f"pS{i}{j}", [128, 512], F32) for j in range(2)] for i in range(2)]
        pOa = [kb.ps(f"pOa{i}", [128, 512], F32) for i in range(2)]
        pLa = [kb.ps(f"pLa{i}", [128, 512], F32) for i in range(2)]
        nstep = 0
        for h in range(4):
            hb = h % 2
            for part in range(2):
                sl = slice(part * (T // 2), (part + 1) * (T // 2))
                kb.dma(akT[hb][:, sl], dkT[h, :, sl], r=["dkT"], w=[f"akT{hb}"])
                kb.dma(aqT[hb][:, sl], dqT[h, :, sl], r=["dqT"], w=[f"aqT{hb}"])
            dvh = dv[h].rearrange("(t p) e -> p t e", p=128)
            for part in range(4):
                kb.dma(avv[hb][:, part * 16:(part + 1) * 16, :], dvh[:, part * 16:(part + 1) * 16, :], r=["dv"], w=[f"avv{hb}"])
            steps = [(g, kt) for g in range(NG) for kt in range(4 * g + 4)]

            def emit_S(n):
                g, kt = steps[n]
                j = kt - 4 * g
                qlo = 128 * max(j, 0)
                qs = slice(qlo, 512)
                sb2 = n % 2
                for i in range(2):
                    rows = slice(64 * i, 64 * i + 64)
                    kb.pe(lambda e: e.matmul(pS[i][sb2][:, qs], lhsT=akT[hb][rows, kt * 128:(kt + 1) * 128], rhs=aqT[hb][rows, g * 512 + qlo:(g + 1) * 512],
                                             start=True, stop=True), r=[f"akT{hb}", f"aqT{hb}"], w=[f"pS{i}{sb2}"])

            def emit_rest(n):
                g, kt = steps[n]
                nk = 4 * g + 4
                j = kt - 4 * g
                qlo = 128 * max(j, 0)
                qs = slice(qlo, 512)
                sb2 = n % 2
                for i in range(2):
                    pt = pTb[i][sb2]
                    kb.act(lambda e: e.activation(out=pt[:, qs], in_=pS[i][sb2][:, qs], func=ACT.Exp, bias=negC[:], scale=0.125),
                           r=[f"pS{i}{sb2}", "negC"], w=[f"pT{i}{sb2}"])
                    if j >= 0:
                        kb.pool(lambda e: e.memset(pt[64:128, qlo:qlo + 64], 0.0), r=[], w=[f"pT{i}{sb2}"])
                for i in range(2):
                    pt = pTb[i][sb2]
                    kb.pe(lambda e: e.matmul(pOa[i][:, qs], lhsT=avv[hb][:, kt, :], rhs=pt[:, qs], start=(kt == 0), stop=(kt == nk - 1)),
                          r=[f"avv{hb}", f"pT{i}{sb2}"], w=[f"pOa{i}"])
                    ae = kb.pool if i == 0 else kb.dve
                    if kt == 0:
                        ae(lambda e: e.tensor_copy(out=lacc[i][:, qs], in_=pt[:, qs]), r=[f"pT{i}{sb2}"], w=[f"lacc{i}"])
                    else:
                        ae(lambda e: e.tensor_tensor(out=lacc[i][:, qs], in0=lacc[i][:, qs], in1=pt[:, qs], op=ALU.add), r=[f"pT{i}{sb2}", f"lacc{i}"], w=[f"lacc{i}"])
                if kt != nk - 1:
                    return
                tsl = slice(g * 512, (g + 1) * 512)
                i2 = g % 2
                kb.dma(asg[i2][:], sgT[4 + h, :, tsl], r=["sgT"], w=[f"asg{i2}"])
                for i in range(2):
                    kb.dve(lambda e: e.tensor_copy(out=araw[i][:], in_=pOa[i][:]), r=[f"pOa{i}"], w=[f"araw{i}"])
                for i in range(2):
                    kb.pe(lambda e: e.matmul(pLa[i][:], lhsT=onesf, rhs=lacc[i][:], start=True, stop=True), r=["const", f"lacc{i}"], w=[f"pLa{i}"])
                for i in range(2):
                    kb.act(lambda e: e.activation(out=arl[i][:], in_=pLa[i][:], func=ACT.Ln), r=[f"pLa{i}"], w=[f"arl{i}"])
                    kb.act(lambda e: e.activation(out=arl[i][:], in_=arl[i][:], func=ACT.Exp, scale=-1.0), r=[f"arl{i}"], w=[f"arl{i}"])
                    kb.pool(lambda e: e.tensor_tensor(out=ao[i][:], in0=araw[i][:], in1=arl[i][:], op=ALU.mult), r=[f"araw{i}", f"arl{i}"], w=[f"ao{i}"])
                kb.dve(lambda e: e.scalar_tensor_tensor(out=aod[:], in0=ao[1][:], scalar=neglam[:, 0:1], in1=ao[0][:], op0=ALU.mult, op1=ALU.add),
                       r=["ao0", "ao1", "neglam"], w=["aod"])
                kb.pool(lambda e: e.tensor_tensor(out=asq[:], in0=aod[:], in1=aod[:], op=ALU.mult), r=["aod"], w=["asq"])
                pN_ = pLa[0]
                kb.pe(lambda e: e.matmul(pN_[:], lhsT=onesf, rhs=asq[:], start=True, stop=True), r=["asq", "const"], w=["pLa0"])
                kb.act(lambda e: e.activation(out=asd[:], in_=pN_[:], func=ACT.Ln, bias=epst[:], scale=1.0 / 128), r=["pLa0", "const"], w=["asd"])
                kb.act(lambda e: e.activation(out=asd[:], in_=asd[:], func=ACT.Exp, scale=-0.5), r=["asd"], w=["asd"])
                kb.dve(lambda e: e.tensor_tensor(out=aon[:], in0=aod[:], in1=asd[:], op=ALU.mult), r=["aod", "asd"], w=["aon"])
                kb.dve(lambda e: e.scalar_tensor_tensor(out=amx[i2][:], in0=aon[:], scalar=slgs[:, 0:1], in1=asg[i2][:], op0=ALU.mult, op1=ALU.mult),
                       r=["aon", "slgs", f"asg{i2}"], w=[f"amx{i2}"])
                kb.dma(mixT[4 + h, :, tsl], amx[i2][:], r=[f"amx{i2}"], w=["mixT"], q="pool")

            emit_S(0)
            for n in range(len(steps)):
                if n + 1 < len(steps):
                    emit_S(n + 1)
                emit_rest(n)
        kb.pop()
        if upto <= 3:
            kb.finish()
            return nc, kb
        kb.push()
        mt = [kb.sb(f"mt{i}", [128, 8, 512], BF16) for i in range(2)]
        xr = [kb.sb(f"xr{i}", [128, 512], F32) for i in range(2)]
        y1 = [kb.sb(f"y1{i}", [128, 512], F32) for i in range(2)]
        y2 = [kb.sb(f"y2{i}", [128, 512], F32) for i in range(2)]
        pY = [kb.ps(f"pY{i}", [128, 512], F32) for i in range(2)]
        mixT_v = mixT.rearrange("m p t -> p m t")
        for g in range(NG):
            m_ = mt[g % 2]
            mk = f"mt{g%2}"
            kb.dma(m_[:], mixT_v[:, :, g * 512:(g + 1) * 512], r=["mixT"], w=[mk])
            for a in range(4):
                tt = 4 * g + a
                i2 = tt % 2
                kb.dma(xr[i2][:], xres[tt * 128:(tt + 1) * 128, :], w=[f"xr{i2}"])
                for m in range(8):
                    kb.pe(lambda e: e.matmul(pY[i2][:], lhsT=m_[:, m, a * 128:(a + 1) * 128], rhs=wobf[:, m, :], start=(m == 0), stop=(m == 7)),
                          r=[mk, "wobf"], w=[f"pY{i2}"], inc=(m == 7))
                kb.dve(lambda e: e.tensor_tensor(out=y1[i2][:], in0=pY[i2][:], in1=gaterep[:], op=ALU.mult), r=[f"pY{i2}", "gaterep"], w=[f"y1{i2}"])
                kb.pool(lambda e: e.tensor_tensor(out=y2[i2][:], in0=y1[i2][:], in1=xr[i2][:], op=ALU.add), r=[f"y1{i2}", f"xr{i2}"], w=[f"y2{i2}"])
                kb.dma(out[tt * 128:(tt + 1) * 128, :], y2[i2][:], r=[f"y2{i2}"], w=["out"], q="pool")
        kb.pop()
        kb.finish()
    return nc, kb


def _consts():
    i = np.arange(128)
    same = (i[:, None] // 64) == (i[None, :] // 64)
    onesf = np.ones((128, 128), np.float32)
    ublk = (same & (i[:, None] <= i[None, :])).astype(np.float32)
    uincl = ublk.copy()
    negustr = -(same & (i[None, :] > i[:, None])).astype(np.float32)
    onesblk = same.astype(np.float32)
    sel0 = np.repeat((i < 64).astype(np.float32)[:, None], 128, 1)
    sel1 = np.repeat((i >= 64).astype(np.float32)[:, None], 128, 1)
    masks = np.stack([onesf, ublk, uincl, negustr, onesblk, sel0, sel1], 1).astype(np.float32)
    invf = (np.float32(500000.0) ** (-np.arange(0, 16, 2, dtype=np.float32) / np.float32(16))).astype(np.float32)
    return {
        "c_identf": np.eye(128, dtype=np.float32),
        "c_identb": np.eye(128, dtype=np.float32).astype(ml_dtypes.bfloat16),
        "c_onesb": np.ones((128, 128), np.float32).astype(ml_dtypes.bfloat16),
        "c_masks": np.ascontiguousarray(masks),
        "c_negblk": -onesblk,
        "c_invf": np.ascontiguousarray(np.broadcast_to(invf[None, :], (128, 8))),
    }


def _rep(v, n=128):
    return np.ascontiguousarray(np.broadcast_to(np.asarray(v, np.float32).reshape(1, -1), (n, np.asarray(v).size)))


def prep_core(inp, b, half):
    f = lambda a: np.ascontiguousarray(np.asarray(a, np.float32))
    w_in = np.asarray(inp["w_in"][0], np.float32)
    off = np.cumsum([0, 512, 512, 512, 4, 4, 512, 512, 512, 512, 512])
    aq, ak, av, abeta, adec, agate, bq, bk, bv, bgate = [np.arange(off[i], off[i + 1]) for i in range(10)]
    cols = []
    for h in range(4):
        cols += [aq[h * 128:(h + 1) * 128], ak[h * 128:(h + 1) * 128], av[h * 128:(h + 1) * 128]]
    cols += [agate, bgate, bq, bk, bv, abeta, adec]
    cols = np.concatenate(cols)
    conv_w = np.asarray(inp["conv_w"][0], np.float32)
    convc = np.zeros((128, 12, 4), np.float32)
    for h in range(4):
        for kind in range(3):
            ch = kind * 512 + h * 128 + np.arange(128)
            convc[:, 3 * h + kind, :] = conv_w[:, ch].T
    w_ada = np.asarray(inp["w_ada"][0], np.float32)
    b_ada = np.asarray(inp["b_ada"][0], np.float32)
    gsl = slice(2048 + half * 512, 2048 + (half + 1) * 512)
    d = {
        "x": f(inp["x"][b]),
        "xres": f(inp["x"][b][:, half * 512:(half + 1) * 512]),
        "cT": f(np.asarray(inp["c"][b]).reshape(8, 128).T),
        "posT": np.ascontiguousarray(np.asarray(inp["positions"][b], np.int32).reshape(NT, 128).T),
        "wada": f(w_ada[:, 0:2048]),
        "wadag": f(w_ada[:, gsl]),
        "badac": f(b_ada[0:2048].reshape(16, 128).T),
        "badag": _rep(b_ada[gsl]),
        "normgc": f(np.asarray(inp["norm_g"][0]).reshape(8, 128).T),
        "win": f(w_in[:, cols]),
        "convc": convc,
        "alog": _rep(inp["a_log"][0]),
        "dtb": _rep(inp["dt_bias"][0]),
        "gng": f(np.asarray(inp["gdn_norm_g"][0]).reshape(128, 1)),
        "slg": f(np.asarray(inp["subln_g"][0]).reshape(128, 1)),
        "gqk": _rep(np.concatenate([np.tile(np.asarray(inp["q_norm_g"][0]), 8), np.tile(np.asarray(inp["k_norm_g"][0]), 8)])),
        "lamv": np.ascontiguousarray(np.broadcast_to(np.stack([np.asarray(inp[k][0], np.float32) for k in
                                                               ("lambda_q1", "lambda_k1", "lambda_q2", "lambda_k2")])[None], (128, 4, 64))),
        "wout": f(np.asarray(inp["w_out"][0])[:, half * 512:(half + 1) * 512]),
    }
    d.update(_consts())
    return d


_NC_CACHE = {}


def kernel(**inputs):
    if "nc" not in _NC_CACHE:
        _NC_CACHE["nc"] = build_program()[0]
    nc = _NC_CACHE["nc"]
    in_maps = [prep_core(inputs, core // 2, core % 2) for core in range(8)]
    res = run_bass_kernel_spmd(nc, in_maps, core_ids=list(range(8)))
    out = np.empty((4, T, D), np.float32)
    for core in range(8):
        out[core // 2, :, (core % 2) * 512:(core % 2 + 1) * 512] = res.results[core]["out"]
    return out
```

```python
import math
from contextlib import ExitStack

import numpy as np
import ml_dtypes

import concourse.bass as bass
import concourse.mybir as mybir
from concourse.bass_utils import run_bass_kernel_spmd

F32 = mybir.dt.float32
BF16 = mybir.dt.bfloat16
I32 = mybir.dt.int32
ACT = mybir.ActivationFunctionType
ALU = mybir.AluOpType
AX = mybir.AxisListType


class KB:
    NDMA = 40

    def __init__(self, nc):
        self.nc = nc
        self.eng = {"pe": nc.tensor, "act": nc.scalar, "dve": nc.vector, "pool": nc.gpsimd, "sp": nc.sync}
        self.stack = ExitStack()
        self.sem = {}
        self.cnt = {k: 0 for k in self.eng}
        self.waited = {k: {} for k in self.eng}
        self.last_w = {}
        self.readers = {}
        self.dsem = []
        self.dcnt = []
        self.dnext = 0
        self.nins = 0
        self.phase = 0
        self.excl = set()

    def ctx(self):
        for k in ("pe", "act", "dve", "pool"):
            self.sem[k] = self.stack.enter_context(self.nc.semaphore("s_" + k))
        for i in range(self.NDMA):
            self.dsem.append(self.stack.enter_context(self.nc.semaphore(f"s_dma{i}")))
            self.dcnt.append(0)
        return self.stack

    def sb(self, name, shape, dtype):
        return self.stack.enter_context(self.nc.sbuf_tensor(f"{name}_{self.phase}", shape, dtype))

    def ps(self, name, shape, dtype):
        self.excl.add(name)
        return self.stack.enter_context(self.nc.psum_tensor(f"{name}_{self.phase}", shape, dtype))

    def _handle(self, s):
        return self.dsem[s[1]] if isinstance(s, tuple) else self.sem[s]

    def _wait(self, en, deps):
        e = self.eng[en]
        need = {}
        for s, v in deps:
            if s == "pe" and en == "pe":
                continue
            if v > need.get(s, 0):
                need[s] = v
        for s, v in need.items():
            if self.waited[en].get(s, 0) < v:
                e.wait_ge(self._handle(s), v)
                self.waited[en][s] = v
                self.nins += 1

    def _deps(self, r, w):
        d = []
        for k in r:
            if k in self.last_w:
                d.append(self.last_w[k])
        for k in w:
            if k in self.last_w:
                d.append(self.last_w[k])
            d.extend(self.readers.get(k, ()))
        return d

    def _record(self, r, w, tag):
        for k in r:
            self.readers.setdefault(k, []).append(tag)
        for k in w:
            self.last_w[k] = tag
            self.readers[k] = []

    def _skip(self):
        import os
        lim = os.environ.get("KB_LIMIT")
        if not lim:
            return False
        ph, n = (int(v) for v in lim.split(":"))
        if self.phase != ph:
            return False
        self.pops = getattr(self, "pops", 0) + 1
        return self.pops > n

    def op(self, en, fn, r=(), w=(), inc=True):
        if self._skip():
            return None
        w = list(w) + [k for k in r if k in self.excl and k not in w]
        self._wait(en, self._deps(r, w))
        ins = fn(self.eng[en])
        self.nins += 1
        if inc:
            ins.then_inc(self.sem[en], 1)
            self.cnt[en] += 1
            tag = (en, self.cnt[en])
        else:
            tag = (en, self.cnt[en] + 1)
        self._record(r, w, tag)
        return ins

    def pe(self, fn, r=(), w=(), inc=True):
        return self.op("pe", fn, r, w, inc)

    def act(self, fn, r=(), w=()):
        return self.op("act", fn, r, w)

    def dve(self, fn, r=(), w=()):
        return self.op("dve", fn, r, w)

    def pool(self, fn, r=(), w=()):
        return self.op("pool", fn, r, w)

    def dma(self, out, in_, r=(), w=(), q="sp", **kw):
        if self._skip():
            return None
        deps = self._deps(r, w)
        i = self.dnext
        self.dnext = (self.dnext + 1) % self.NDMA
        if self.dcnt[i] > 0:
            deps.append((("d", i), 16 * self.dcnt[i]))
        self._wait(q, deps)
        ins = self.eng[q].dma_start(out=out, in_=in_, **kw)
        ins.then_inc(self.dsem[i], 16)
        self.nins += 1
        self.dcnt[i] += 1
        self._record(r, w, (("d", i), 16 * self.dcnt[i]))
        return ins

    def finish(self):
        deps = [(k, self.cnt[k]) for k in ("pe", "act", "dve", "pool") if self.cnt[k] > 0]
        deps += [(("d", i), 16 * c) for i, c in enumerate(self.dcnt) if c > 0]
        self._wait("sp", deps)

    def barrier(self):
        deps = [(k, self.cnt[k]) for k in ("pe", "act", "dve", "pool") if self.cnt[k] > 0]
        deps += [(("d", i), 16 * c) for i, c in enumerate(self.dcnt) if c > 0]
        for en in ("pe", "act", "dve", "pool", "sp"):
            self._wait(en, [d for d in deps if not (d[0] == "pe" and en == "pe")] + ([("pe", self.cnt["pe"])] if False else []))
        self.last_w = {}
        self.readers = {}

    def push(self):
        self._saved = getattr(self, "_saved", [])
        self._saved.append(self.stack)
        self.stack = ExitStack()
        self.stack.__enter__()
        self.phase += 1

    def pop(self):
        self.barrier()
        self.stack.__exit__(None, None, None)
        self.stack = self._saved.pop()


T = 8192
D = 1024
NT = T // 128
NG = T // 512
NCOL = 4104
EPS = 1e-6
LAMBDA_INIT = 0.8 - 0.6 * math.exp(0.0)
NFM = 20
TM0 = NFM * 128
TWO_PI = 2.0 * math.pi
MAGIC = 12582912.0
CW1 = 6.28125
CW2 = 0.0019350051879882812
CW3 = TWO_PI - CW1 - CW2


def build_program(dbg=(), upto=9):
    nc = bass.Bass("TRN2", target_bir_lowering=False)

    def din(name, shape, dt=F32):
        return nc.dram_tensor(name, list(shape), dt, kind="ExternalInput").ap()

    def dscr(name, shape, dt=F32):
        return nc.dram_tensor(name, list(shape), dt, kind="ExternalOutput" if name in dbg else "Internal").ap()

    x = din("x", [T, D])
    xres = din("xres", [T, 512])
    cT = din("cT", [128, 8])
    posT = din("posT", [128, NT], I32)
    wada = din("wada", [D, 2048])
    wadag = din("wadag", [D, 512])
    badac = din("badac", [128, 16])
    badag = din("badag", [128, 512])
    normgc = din("normgc", [128, 8])
    win = din("win", [D, NCOL])
    convc = din("convc", [128, 12, 4])
    alog = din("alog", [128, 4])
    dtb = din("dtb", [128, 4])
    gng = din("gng", [128, 1])
    slg = din("slg", [128, 1])
    gqk = din("gqk", [128, 1024])
    lamv = din("lamv", [128, 4, 64])
    wout = din("wout", [D, 512])
    c_identf = din("c_identf", [128, 128])
    c_identb = din("c_identb", [128, 128], BF16)
    c_onesb = din("c_onesb", [128, 128], BF16)
    c_masks = din("c_masks", [128, 7, 128])
    c_negblk = din("c_negblk", [128, 128])
    c_invf = din("c_invf", [128, 8])
    out = nc.dram_tensor("out", [T, 512], F32, kind="ExternalOutput").ap()

    gqT = dscr("gqT", [4, 128, T])
    gkT = dscr("gkT", [4, 128, T])
    gktok = dscr("gktok", [4, T, 128])
    gvtok = dscr("gvtok", [4, T, 128])
    sgT = dscr("sgT", [8, 128, T])
    dqT = dscr("dqT", [4, 128, T], BF16)
    dkT = dscr("dkT", [4, 128, T], BF16)
    dv = dscr("dv", [4, T, 128], BF16)
    mixT = dscr("mixT", [8, 128, T], BF16)

    kb = KB(nc)
    with kb.ctx():
        identf = kb.sb("identf", [128, 128], F32)
        identb = kb.sb("identb", [128, 128], BF16)
        onesb = kb.sb("onesb", [128, 128], BF16)
        masks = kb.sb("masks", [128, 7, 128], F32)
        negblk = kb.sb("negblk", [128, 128], F32)
        onesf, ublk, uincl, negustr, onesblk = (masks[:, i, :] for i in range(5))
        sel = [masks[:, 5, :], masks[:, 6, :]]
        bgres = kb.sb("bgres", [128, NT, 8], F32)
        cosr = kb.sb("cosr", [128, NT, 8], F32)
        sinr = kb.sb("sinr", [128, NT, 8], F32)
        gaterep = kb.sb("gaterep", [128, 512], F32)
        mscale = kb.sb("mscale", [128, 8], F32)
        shiftc = kb.sb("shiftc", [128, 8], F32)
        negA = kb.sb("negA", [128, 4], F32)
        dtbs = kb.sb("dtbs", [128, 4], F32)
        gngs = kb.sb("gngs", [128, 1], F32)
        slgs = kb.sb("slgs", [128, 1], F32)
        neglam = kb.sb("neglam", [128, 1], F32)
        negC = kb.sb("negC", [128, 1], F32)
        gqks = kb.sb("gqks", [128, 1024], F32)
        convs = kb.sb("convs", [128, 12, 4], F32)
        wobf = kb.sb("wobf", [128, 8, 512], BF16)

        epst = kb.sb("epst", [128, 1], F32)
        onet = kb.sb("onet", [128, 1], F32)
        CONSTK = ["const"]
        kb.pool(lambda e: e.memset(epst[:], EPS), w=CONSTK)
        kb.pool(lambda e: e.memset(onet[:], 1.0), w=CONSTK)
        for dst, src in ((identf, c_identf), (identb, c_identb), (onesb, c_onesb), (masks, c_masks), (negblk, c_negblk),
                         (negA, alog), (dtbs, dtb), (gngs, gng), (slgs, slg), (gqks, gqk), (convs, convc)):
            kb.dma(dst[:], src, w=CONSTK)

        kb.push()
        sc = kb.sb("sc", [128, 8], F32)
        screp = kb.sb("screp", [128, 8, 128], F32)
        wch = [kb.sb(f"wch{i}", [128, 8, 512], F32) for i in range(2)]
        modcol = kb.sb("modcol", [128, 16], F32)
        bcol = kb.sb("bcol", [128, 16], F32)
        ngc = kb.sb("ngc", [128, 8], F32)
        bgate = kb.sb("bgate", [128, 512], F32)
        lams = kb.sb("lams", [128, 4, 64], F32)
        lprod = kb.sb("lprod", [128, 2, 64], F32)
        lsum = kb.sb("lsum", [128, 2], F32)
        posi = kb.sb("posi", [128, NT], I32)
        posf = kb.sb("posf", [128, NT], F32)
        invf = kb.sb("invf", [128, 8], F32)
        ang = kb.sb("ang", [128, NT, 8], F32)
        tk = kb.sb("tk", [128, NT, 8], F32)
        rr = kb.sb("rr", [128, NT, 8], F32)
        gmax = kb.sb("gmax", [128, 2], F32)
        psm = kb.ps("psm", [128, 512], F32)
        psg = kb.ps("psg", [128, 512], F32)

        kb.dma(sc[:], cT, w=["sc"])
        kb.dma(bcol[:], badac, w=["bcol"])
        kb.dma(ngc[:], normgc, w=["ngc"])
        kb.dma(bgate[:], badag, w=["bgate"])
        kb.dma(lams[:], lamv, w=["lams"])
        kb.dma(posi[:], posT, w=["posi"])
        kb.dma(invf[:], c_invf, w=["invf"])
        kb.act(lambda e: e.activation(out=sc[:], in_=sc[:], func=ACT.Silu), r=["sc"], w=["sc"])
        kb.dve(lambda e: e.tensor_copy(out=screp[:], in_=sc[:].unsqueeze(2).to_broadcast([128, 8, 128])), r=["sc"], w=["screp"])
        wada_v = wada.rearrange("(kc p) n -> p kc n", p=128)
        for cc in range(4):
            wt = wch[cc % 2]
            kb.dma(wt[:], wada_v[:, :, cc * 512:(cc + 1) * 512], w=[f"wch{cc%2}"])
            for jj in range(4):
                j = cc * 4 + jj
                for kc in range(8):
                    kb.pe(lambda e: e.matmul(psm[:, 2 * j:2 * j + 2], lhsT=wt[:, kc, jj * 128:(jj + 1) * 128], rhs=screp[:, kc, 0:2],
                                             start=(kc == 0), stop=(kc == 7)),
                          r=[f"wch{cc%2}", "screp"], w=["psm"], inc=(kc == 7))
        kb.dve(lambda e: e.tensor_tensor(out=modcol[:], in0=psm[:, 0:32].rearrange("p (j two) -> p j two", two=2)[:, :, 0], in1=bcol[:], op=ALU.add),
               r=["psm", "bcol"], w=["modcol"])
        kb.dve(lambda e: e.tensor_copy(out=shiftc[:], in_=modcol[:, 0:8]), r=["modcol"], w=["shiftc"])
        kb.dve(lambda e: e.scalar_tensor_tensor(out=mscale[:], in0=modcol[:, 8:16], scalar=1.0, in1=ngc[:], op0=ALU.add, op1=ALU.mult),
               r=["modcol", "ngc"], w=["mscale"])
        wt = wch[0]
        kb.dma(wt[:], wadag.rearrange("(kc p) n -> p kc n", p=128), w=["wch0"])
        for kc in range(8):
            kb.pe(lambda e: e.matmul(psg[:], lhsT=screp[:, kc, :], rhs=wt[:, kc, :], start=(kc == 0), stop=(kc == 7)),
                  r=["wch0", "screp"], w=["psg"], inc=(kc == 7))
        kb.dve(lambda e: e.tensor_tensor(out=gaterep[:], in0=psg[:], in1=bgate[:], op=ALU.add), r=["psg", "bgate"], w=["gaterep"])
        wt = wch[1]
        kb.dma(wt[:], wout.rearrange("(kc p) n -> p kc n", p=128), w=["wch1"])
        kb.dve(lambda e: e.tensor_copy(out=wobf[:], in_=wt[:]), r=["wch1"], w=["wobf"])
        kb.dve(lambda e: e.tensor_tensor(out=lprod[:, 0, :], in0=lams[:, 0, :], in1=lams[:, 1, :], op=ALU.mult), r=["lams"], w=["lprod"])
        kb.dve(lambda e: e.tensor_tensor(out=lprod[:, 1, :], in0=lams[:, 2, :], in1=lams[:, 3, :], op=ALU.mult), r=["lams", "lprod"], w=["lprod"])
        kb.dve(lambda e: e.tensor_reduce(out=lsum[:], in_=lprod[:], axis=AX.X, op=ALU.add), r=["lprod"], w=["lsum"])
        kb.act(lambda e: e.activation(out=lsum[:], in_=lsum[:], func=ACT.Exp), r=["lsum"], w=["lsum"])
        kb.dve(lambda e: e.scalar_tensor_tensor(out=neglam[:], in0=lsum[:, 1:2], scalar=-LAMBDA_INIT, in1=lsum[:, 0:1], op0=ALU.add, op1=ALU.subtract),
               r=["lsum"], w=["neglam"])
        kb.act(lambda e: e.activation(out=negA[:], in_=negA[:], func=ACT.Exp), r=CONSTK, w=["negA"])
        kb.dve(lambda e: e.tensor_scalar(out=negA[:], in0=negA[:], scalar1=-1.0, scalar2=None, op0=ALU.mult), r=["negA"], w=["negA"])
        kb.dve(lambda e: e.tensor_scalar(out=slgs[:], in0=slgs[:], scalar1=1.0 - LAMBDA_INIT, scalar2=None, op0=ALU.mult), r=CONSTK, w=["slgs"])
        kb.dve(lambda e: e.tensor_reduce(out=gmax[:], in_=gqks[:].rearrange("p (a n) -> p a n", a=2), axis=AX.X, op=ALU.max, apply_absolute_value=True),
               r=CONSTK, w=["gmax"])
        kb.dve(lambda e: e.scalar_tensor_tensor(out=negC[:], in0=gmax[:, 0:1], scalar=-8.0, in1=gmax[:, 1:2], op0=ALU.mult, op1=ALU.mult),
               r=["gmax"], w=["negC"])
        kb.dve(lambda e: e.tensor_copy(out=posf[:], in_=posi[:]), r=["posi"], w=["posf"])
        kb.dve(lambda e: e.tensor_tensor(out=ang[:], in0=posf[:].unsqueeze(2).to_broadcast([128, NT, 8]),
                                         in1=invf[:].unsqueeze(1).to_broadcast([128, NT, 8]), op=ALU.mult), r=["posf", "invf"], w=["ang"])
        for (dst, off, name) in ((sinr, 0.0, "sinr"), (cosr, 0.25, "cosr")):
            kb.dve(lambda e: e.tensor_scalar(out=tk[:], in0=ang[:], scalar1=1.0 / TWO_PI, scalar2=off, op0=ALU.mult, op1=ALU.add), r=["ang"], w=["tk"])
            kb.dve(lambda e: e.tensor_scalar(out=tk[:], in0=tk[:], scalar1=MAGIC, scalar2=None, op0=ALU.add), r=["tk"], w=["tk"])
            kb.dve(lambda e: e.tensor_scalar(out=tk[:], in0=tk[:], scalar1=-MAGIC, scalar2=None, op0=ALU.add), r=["tk"], w=["tk"])
            kb.dve(lambda e: e.scalar_tensor_tensor(out=rr[:], in0=tk[:], scalar=-CW1, in1=ang[:], op0=ALU.mult, op1=ALU.add), r=["ang", "tk"], w=["rr"])
            kb.dve(lambda e: e.scalar_tensor_tensor(out=rr[:], in0=tk[:], scalar=-CW2, in1=rr[:], op0=ALU.mult, op1=ALU.add), r=["rr", "tk"], w=["rr"])
            kb.dve(lambda e: e.scalar_tensor_tensor(out=rr[:], in0=tk[:], scalar=-CW3, in1=rr[:], op0=ALU.mult, op1=ALU.add), r=["rr", "tk"], w=["rr"])
            kb.dve(lambda e: e.tensor_scalar(out=rr[:], in0=rr[:], scalar1=off * TWO_PI, scalar2=math.pi, op0=ALU.add, op1=ALU.min), r=["rr"], w=["rr"])
            kb.dve(lambda e: e.tensor_scalar(out=rr[:], in0=rr[:], scalar1=-math.pi, scalar2=None, op0=ALU.max), r=["rr"], w=["rr"])
            kb.act(lambda e: e.activation(out=dst[:], in_=rr[:], func=ACT.Sin), r=["rr"], w=[name])
        kb.pop()
        kb.push()
        wbf = kb.sb("wbf", [128, 8, NCOL], BF16)
        wcb = [kb.sb(f"wcb{i}", [128, 8, 64], F32) for i in range(2)]
        xt = [kb.sb(f"xt{i}", [128, 1024], F32) for i in range(2)]
        xs = [kb.sb(f"xs{i}", [128, 1024], F32) for i in range(2)]
        st = [kb.sb(f"st{i}", [128, 4], F32) for i in range(2)]
        hT = [kb.sb(f"hT{i}", [128, 8, 512], BF16) for i in range(2)]
        cbuf = [kb.sb(f"cbuf{i}", [128, 515], F32) for i in range(12)]
        acc = [kb.sb(f"acc{i}", [128, 512], F32) for i in range(2)]
        yb = [kb.sb(f"yb{i}", [128, 512], F32) for i in range(3)]
        sqb = [kb.sb(f"sqb{i}", [128, 512], F32) for i in range(2)]
        sdb = [kb.sb(f"sdb{i}", [128, 512], F32) for i in range(2)]
        ynb = [kb.sb(f"ynb{i}", [128, 512], F32) for i in range(3)]
        trb = [kb.sb(f"trb{i}", [128, 512], F32) for i in range(2)]
        sgt = [kb.sb(f"sgt{i}", [128, 512], F32) for i in range(2)]
        qk = kb.sb("qk", [128, 1024], F32)
        ssq = kb.sb("ssq", [128, 16], F32)
        xn = kb.sb("xn", [128, 1024], F32)
        qkbs = [kb.sb(f"qkb{i}", [128, 1024], BF16) for i in range(2)]
        rt = [kb.sb(f"rt{i}", [128, 16, 8], F32) for i in range(4)]
        vb = [kb.sb(f"vb{i}", [128, 512], BF16) for i in range(2)]
        qkTg = [kb.sb("qkTg0", [128, 8, 512], BF16)]
        tmp4 = kb.sb("tmp4", [128, 4], F32)
        psT = kb.ps("psT", [128, 512], F32)
        psF = kb.ps("psF", [128, 512], F32)
        psQ = kb.ps("psQ", [128, 512], F32)
        psK = kb.ps("psK", [128, 512], F32)
        psV = kb.ps("psV", [128, 512], F32)
        psN = kb.ps("psN", [128, 512], F32)
        psXf = kb.ps("psXf", [128, 512], F32)
        psXb = kb.ps("psXb", [128, 1024], BF16)

        win_v = win.rearrange("(kc p) n -> p kc n", p=128)
        nch = (NCOL + 63) // 64
        for ci in range(nch):
            c0, c1 = ci * 64, min(NCOL, ci * 64 + 64)
            wt = wcb[ci % 2]
            kb.dma(wt[:, :, 0:c1 - c0], win_v[:, :, c0:c1], w=[f"wcb{ci%2}"])
            if ci % 2 == 0:
                kb.dve(lambda e: e.tensor_copy(out=wbf[:, :, c0:c1], in_=wt[:, :, 0:c1 - c0]), r=[f"wcb{ci%2}"], w=["wbf"])
            else:
                kb.pool(lambda e: e.tensor_copy(out=wbf[:, :, c0:c1], in_=wt[:, :, 0:c1 - c0]), r=[f"wcb{ci%2}"], w=["wbf"])
        for c in range(12):
            kb.pool(lambda e: e.memset(cbuf[c][:, 0:3], 0.0), w=[f"cbuf{c}"])

        dqT_v = dqT.rearrange("h p t -> p h t")
        dkT_v = dkT.rearrange("h p t -> p h t")
        dv_v = dv.rearrange("h t e -> t h e")
        nfm = 0
        import os as _os
        if _os.environ.get('SKIP_P1'):
            kb.pool(lambda e: e.memset(bgres[:], -0.05), w=["bgres"])
            for h in range(4):
                kb.dma(gqT[h, :, 0:512], x[0:128, 0:512], w=["gqkT"])
                kb.dma(gkT[h, :, 0:512], x[0:128, 0:512], w=["gqkT"])
                kb.dma(gktok[h, 0:512, :].rearrange("(a p) e -> p a e", p=128), x[0:128, 0:512].rearrange("p (a e) -> p a e", a=4), w=["gtok"])
                kb.dma(gvtok[h, 0:512, :].rearrange("(a p) e -> p a e", p=128), x[0:128, 0:512].rearrange("p (a e) -> p a e", a=4), w=["gtok"])
                kb.dma(sgT[h, :, 0:512], x[0:128, 0:512], w=["sgT"])
        for g in range(0 if _os.environ.get('SKIP_P1') else NG):
            hTg = hT[g % 2]
            hk = f"hT{g%2}"
            for a in range(4):
                tt = 4 * g + a
                i2 = tt % 2
                kb.dma(xt[i2][:], x[tt * 128:(tt + 1) * 128, :], w=[f"xt{i2}"])
                s = st[i2]
                kb.act(lambda e: e.activation(out=xs[i2][:], in_=xt[i2][:], func=ACT.Square, accum_out=s[:, 0:1]), r=[f"xt{i2}"], w=[f"xs{i2}", f"st{i2}"])
                kb.dve(lambda e: e.tensor_scalar(out=s[:, 1:2], in0=s[:, 0:1], scalar1=1.0 / D, scalar2=EPS, op0=ALU.mult, op1=ALU.add), r=[f"st{i2}"], w=[f"st{i2}"])
                kb.act(lambda e: e.activation(out=s[:, 2:3], in_=s[:, 1:2], func=ACT.Sqrt), r=[f"st{i2}"], w=[f"st{i2}"])
                kb.dve(lambda e: e.reciprocal(out=s[:, 3:4], in_=s[:, 2:3]), r=[f"st{i2}"], w=[f"st{i2}"])
                kb.act(lambda e: e.activation(out=xs[i2][:], in_=xt[i2][:], func=ACT.Copy, scale=s[:, 3:4]), r=[f"xt{i2}", f"st{i2}"], w=[f"xs{i2}"])
                for half in range(2):
                    for q4 in range(4):
                        kc = half * 4 + q4
                        kb.pe(lambda e: e.transpose(out=psT[:, q4 * 128:(q4 + 1) * 128], in_=xs[i2][:, kc * 128:(kc + 1) * 128], identity=identf[:]),
                              r=[f"xs{i2}", "const"], w=["psT"], inc=(q4 == 3))
                    for q4 in range(4):
                        kc = half * 4 + q4
                        kb.dve(lambda e: e.tensor_scalar(out=hTg[:, kc, a * 128:(a + 1) * 128], in0=psT[:, q4 * 128:(q4 + 1) * 128],
                                                         scalar1=mscale[:, kc:kc + 1], scalar2=shiftc[:, kc:kc + 1], op0=ALU.mult, op1=ALU.add),
                               r=["psT", "mscale", "shiftc"], w=[hk])
            pend = []
            tsl = slice(g * 512, (g + 1) * 512)
            for c in range(NFM):
                for kc in range(8):
                    kb.pe(lambda e: e.matmul(psF[:], lhsT=wbf[:, kc, c * 128:(c + 1) * 128], rhs=hTg[:, kc, :], start=(kc == 0), stop=(kc == 7)),
                          r=["wbf", hk], w=["psF"], inc=(kc == 7))
                new2, new3 = None, None
                if c >= 12:
                    sg = sgt[c % 2]
                    kb.act(lambda e: e.activation(out=sg[:], in_=psF[:], func=ACT.Silu), r=["psF"], w=[f"sgt{c%2}"])
                    kb.dma(sgT[c - 12, :, tsl], sg[:], r=[f"sgt{c%2}"], w=["sgT"], q="pool")
                else:
                    h, kind = c // 3, c % 3
                    cb = cbuf[c]
                    ck = f"cbuf{c}"
                    kb.act(lambda e: e.activation(out=cb[:, 3:515], in_=psF[:], func=ACT.Copy), r=["psF"], w=[ck])
                    ac = acc[nfm % 2]
                    ak = f"acc{nfm%2}"
                    y = yb[nfm % 3]
                    yk = f"yb{nfm%3}"
                    kb.dve(lambda e: e.tensor_scalar(out=ac[:], in0=cb[:, 0:512], scalar1=convs[:, c, 0:1], scalar2=None, op0=ALU.mult), r=[ck, "const"], w=[ak])
                    for j in range(1, 4):
                        kb.dve(lambda e: e.scalar_tensor_tensor(out=ac[:], in0=cb[:, j:j + 512], scalar=convs[:, c, j:j + 1], in1=ac[:], op0=ALU.mult, op1=ALU.add),
                               r=[ck, ak, "const"], w=[ak])
                    kb.pool(lambda e: e.tensor_copy(out=cb[:, 0:3], in_=cb[:, 512:515]), r=[ck], w=[ck])
                    kb.act(lambda e: e.activation(out=y[:], in_=ac[:], func=ACT.Silu), r=[ak], w=[yk])
                    sq_, sqk = sqb[nfm % 2], f"sqb{nfm%2}"
                    sd_, sdk = sdb[nfm % 2], f"sdb{nfm%2}"
                    yn, ynk = ynb[nfm % 3], f"ynb{nfm%3}"
                    tr, trk = trb[nfm % 2], f"trb{nfm%2}"
                    if kind < 2:
                        kb.pool(lambda e: e.tensor_tensor(out=sq_[:], in0=y[:], in1=y[:], op=ALU.mult), r=[yk], w=[sqk])

                    def part3(h=h, kind=kind, src=(yn if kind < 2 else y), srck=(ynk if kind < 2 else yk), tr=tr, trk=trk):
                        for a in range(4):
                            kb.pe(lambda e: e.transpose(out=psXf[:, a * 128:(a + 1) * 128], in_=src[:, a * 128:(a + 1) * 128], identity=identf[:]),
                                  r=[srck, "const"], w=["psXf"], inc=(a == 3))
                        kb.dve(lambda e: e.tensor_copy(out=tr[:], in_=psXf[:]), r=["psXf"], w=[trk])
                        dst = (gktok if kind == 1 else gvtok)[h, g * 512:(g + 1) * 512, :].rearrange("(a p) e -> p a e", p=128)
                        kb.dma(dst, tr[:].rearrange("p (a e) -> p a e", a=4), r=[trk], w=["gtok"], q="pool")

                    def part2(h=h, kind=kind, y=y, yk=yk, sq_=sq_, sqk=sqk, sd_=sd_, sdk=sdk, yn=yn, ynk=ynk):
                        kb.pe(lambda e: e.matmul(psN[:], lhsT=onesf, rhs=sq_[:], start=True, stop=True), r=[sqk, "const"], w=["psN"])
                        kb.act(lambda e: e.activation(out=sd_[:], in_=psN[:], func=ACT.Ln, bias=epst[:]), r=["psN", "const"], w=[sdk])
                        kb.act(lambda e: e.activation(out=sd_[:], in_=sd_[:], func=ACT.Exp, scale=-0.5), r=[sdk], w=[sdk])
                        cq = (128.0 ** -0.5) if kind == 0 else 1.0
                        kb.dve(lambda e: e.scalar_tensor_tensor(out=yn[:], in0=y[:], scalar=cq, in1=sd_[:], op0=ALU.mult, op1=ALU.mult), r=[yk, sdk], w=[ynk])
                        kb.dma((gqT if kind == 0 else gkT)[h, :, tsl], yn[:], r=[ynk], w=["gqkT"], q="pool")

                    if kind < 2:
                        new2 = part2
                    if kind >= 1:
                        new3 = part3
                    nfm += 1
                for item in [it for it in pend if it[0] <= c]:
                    pend.remove(item)
                    item[1]()
                if new2 is not None:
                    pend.append((c + 1, new2))
                if new3 is not None:
                    pend.append((c + 2 if new2 is not None else c + 1, new3))
            for item in sorted(pend, key=lambda it: it[0]):
                item[1]()
            qTg = qkTg[0]
            qTk = "qkTg0"
            pendC = None
            for a in range(4):
                tt = 4 * g + a
                qkb, qkk = qkbs[a % 2], f"qkb{a%2}"
                lhs = lambda kc: hTg[:, kc, a * 128:(a + 1) * 128]
                for (pst, pk, c0, n) in ((psQ, "psQ", TM0, 512), (psK, "psK", TM0 + 512, 512), (psV, "psV", TM0 + 1024, 512), (psXf, "psXf", TM0 + 1536, 8)):
                    for kc in range(8):
                        kb.pe(lambda e: e.matmul(pst[:, 0:n], lhsT=lhs(kc), rhs=wbf[:, kc, c0:c0 + n], start=(kc == 0), stop=(kc == 7)),
                              r=["wbf", hk], w=[pk], inc=(kc == 7))
                if pendC is not None:
                    pendC()
                    pendC = None
                kb.act(lambda e: e.activation(out=bgres[:, tt, 0:4], in_=psXf[:, 0:4], func=ACT.Sigmoid), r=["psXf"], w=["bgres"])
                for h in range(4):
                    kb.act(lambda e: e.activation(out=tmp4[:, h:h + 1], in_=psXf[:, 4 + h:5 + h], func=ACT.Exp, bias=dtbs[:, h:h + 1]), r=["psXf", "const"], w=["tmp4"])
                kb.act(lambda e: e.activation(out=tmp4[:], in_=tmp4[:], func=ACT.Ln, bias=onet[:]), r=["tmp4", "const"], w=["tmp4"])
                kb.dve(lambda e: e.tensor_tensor(out=bgres[:, tt, 4:8], in0=tmp4[:], in1=negA[:], op=ALU.mult), r=["tmp4", "negA"], w=["bgres"])
                v_ = vb[a % 2]
                vk = f"vb{a%2}"
                kb.act(lambda e: e.activation(out=v_[:], in_=psV[:], func=ACT.Copy), r=["psV"], w=[vk])
                kb.dma(dv_v[tt * 128:(tt + 1) * 128, :, :], v_[:].rearrange("p (h e) -> p h e", h=4), r=[vk], w=["dv"], q="pool")
                kb.dve(lambda e: e.tensor_copy(out=qk[:, 0:512], in_=psQ[:]), r=["psQ"], w=["qk"])
                kb.act(lambda e: e.activation(out=qk[:, 512:1024], in_=psK[:], func=ACT.Copy), r=["psK"], w=["qk"])
                kb.pool(lambda e: e.tensor_tensor(out=xn[:], in0=qk[:], in1=qk[:], op=ALU.mult), r=["qk"], w=["xn"])
                kb.dve(lambda e: e.tensor_reduce(out=ssq[:], in_=xn[:].rearrange("p (s d) -> p s d", d=64), axis=AX.X, op=ALU.add), r=["xn"], w=["ssq"])
                kb.dve(lambda e: e.tensor_scalar(out=ssq[:], in0=ssq[:], scalar1=1.0 / 64, scalar2=EPS, op0=ALU.mult, op1=ALU.add), r=["ssq"], w=["ssq"])
                kb.act(lambda e: e.activation(out=ssq[:], in_=ssq[:], func=ACT.Sqrt), r=["ssq"], w=["ssq"])
                kb.dve(lambda e: e.reciprocal(out=ssq[:], in_=ssq[:]), r=["ssq"], w=["ssq"])
                qk3 = qk[:].rearrange("p (s d) -> p s d", d=64)
                xn3 = xn[:].rearrange("p (s d) -> p s d", d=64)
                kb.dve(lambda e: e.tensor_tensor(out=xn3, in0=qk3, in1=ssq[:].unsqueeze(2).to_broadcast([128, 16, 64]), op=ALU.mult), r=["qk", "ssq"], w=["xn"])
                kb.pool(lambda e: e.tensor_tensor(out=xn[:], in0=xn[:], in1=gqks[:], op=ALU.mult), r=["xn", "const"], w=["xn"])
                kb.act(lambda e: e.activation(out=qkb[:], in_=xn[:], func=ACT.Copy), r=["xn"], w=[qkk])
                qkb3 = qkb[:].rearrange("p (s d) -> p s d", d=64)
                cb_ = cosr[:, tt, :].unsqueeze(1).to_broadcast([128, 16, 8])
                sb_ = sinr[:, tt, :].unsqueeze(1).to_broadcast([128, 16, 8])
                x1, x2 = xn3[:, :, 0:8], xn3[:, :, 8:16]
                kb.dve(lambda e: e.tensor_tensor(out=rt[0][:], in0=x1, in1=cb_, op=ALU.mult), r=["xn", "cosr"], w=["rt0"])
                kb.dve(lambda e: e.tensor_tensor(out=rt[1][:], in0=x2, in1=sb_, op=ALU.mult), r=["xn", "sinr"], w=["rt1"])
                kb.dve(lambda e: e.tensor_tensor(out=rt[2][:], in0=x2, in1=cb_, op=ALU.mult), r=["xn", "cosr"], w=["rt2"])
                kb.dve(lambda e: e.tensor_tensor(out=rt[3][:], in0=x1, in1=sb_, op=ALU.mult), r=["xn", "sinr"], w=["rt3"])
                kb.dve(lambda e: e.tensor_tensor(out=qkb3[:, :, 0:8], in0=rt[0][:], in1=rt[1][:], op=ALU.subtract), r=["rt0", "rt1", qkk], w=[qkk])
                kb.dve(lambda e: e.tensor_tensor(out=qkb3[:, :, 8:16], in0=rt[2][:], in1=rt[3][:], op=ALU.add), r=["rt2", "rt3", qkk], w=[qkk])
                def trC(a=a, qkb=qkb, qkk=qkk):
                    for bi in range(8):
                        kb.pe(lambda e: e.transpose(out=psXb[:, bi * 128:(bi + 1) * 128], in_=qkb[:, bi * 128:(bi + 1) * 128], identity=identb[:]),
                              r=[qkk, "const"], w=["psXb"], inc=(bi == 7))
                    kb.dve(lambda e: e.tensor_copy(out=qTg[:, :, a * 128:(a + 1) * 128], in_=psXb[:].rearrange("p (b t) -> p b t", b=8)), r=["psXb"], w=[qTk])
                pendC = trC
            if pendC is not None:
                pendC()
                pendC = None
            kb.dma(dqT_v[:, :, g * 512:(g + 1) * 512], qTg[:, 0:4, :], r=[qTk], w=["dqT"], q="pool")
            kb.dma(dkT_v[:, :, g * 512:(g + 1) * 512], qTg[:, 4:8, :], r=[qTk], w=["dkT"], q="pool")
        kb.pop()
        kb.push()
        Q = lambda t, i: t[:, i * 128:(i + 1) * 128]
        RR = (lambda ap: ap.bitcast(mybir.dt.float32r)) if not _os.environ.get('NO_F32R') else (lambda ap: ap)
        names = ["kTr", "qTr", "kTp", "qTp", "ktk", "vtk", "Ug", "E", "Ei", "Es", "aT", "Ya", "YaT", "Yb", "YbT", "Wa", "Wb", "dg", "qdT", "kdec", "rhs2", "vn"]
        gb = {}
        for h in range(4):
            for par in range(2):
                for n in names:
                    gb[(n, h, par)] = kb.sb(f"{n}{h}{par}", [128, 128], F32)
                gb[("sm", h, par)] = kb.sb(f"sm{h}{par}", [128, 8], F32)
                gb[("smc", h, par)] = kb.sb(f"smc{h}{par}", [128, 8], F32)
                gb[("g2", h, par)] = kb.sb(f"g2{h}{par}", [128, 2], F32)
        Sst = [[kb.sb(f"S{h}{i}", [128, 128], F32) for i in range(2)] for h in range(4)]
        obuf = [kb.sb(f"obuf{h}", [128, 512], F32) for h in range(4)]
        gsq = kb.sb("gsq", [128, 512], F32)
        gsd = kb.sb("gsd", [128, 512], F32)
        gon = kb.sb("gon", [128, 512], F32)
        gsg = [kb.sb(f"gsg{i}", [128, 512], F32) for i in range(2)]
        gmx = [kb.sb(f"gmx{i}", [128, 512], BF16) for i in range(2)]
        pA = kb.ps("pA", [128, 512], F32)
        pB = kb.ps("pB", [128, 512], F32)
        pC = kb.ps("pC", [128, 512], F32)
        pKS = kb.ps("pKS", [128, 512], F32)
        pVN = kb.ps("pVN", [128, 512], F32)
        pSS = kb.ps("pSS", [128, 512], F32)
        pO = kb.ps("pO", [128, 512], F32)
        pP = kb.ps("pP", [128, 512], F32)
        for h in range(4):
            kb.dve(lambda e: e.tensor_scalar(out=RR(Sst[h][0][:]), in0=identf[:], scalar1=0.0, scalar2=None, op0=ALU.mult), r=["const"], w=[f"S{h}0"])
        scur = [0, 0, 0, 0]
        H4 = range(4)

        def pre_stages(p):
            par = p % 2
            tok = slice(p * 128, (p + 1) * 128)
            B = lambda n, h: gb[(n, h, par)]
            K_ = lambda n, h: f"{n}{h}{par}"
            gcol = lambda h: bgres[:, p, 4 + h:5 + h]
            beta = lambda h: bgres[:, p, h:h + 1]
            st = []

            def s_load():
                for h in H4:
                    kb.dma(B("kTp", h)[:], gkT[h, :, tok], r=["gqkT"], w=[K_("kTp", h)])
                    kb.dma(B("qTp", h)[:], gqT[h, :, tok], r=["gqkT"], w=[K_("qTp", h)])
                    kb.dma(B("ktk", h)[:], gktok[h, tok, :], r=["gtok"], w=[K_("ktk", h)])
                    kb.dma(B("vtk", h)[:], gvtok[h, tok, :], r=["gtok"], w=[K_("vtk", h)])
            st.append(s_load)

            def s_ug():
                for h in H4:
                    kb.pool(lambda e: e.tensor_copy(out=RR(B("kTr", h)[:]), in_=B("kTp", h)[:]), r=[K_("kTp", h)], w=[K_("kTr", h)])
                    kb.pool(lambda e: e.tensor_copy(out=RR(B("qTr", h)[:]), in_=B("qTp", h)[:]), r=[K_("qTp", h)], w=[K_("qTr", h)])
                    kb.dve(lambda e: e.tensor_scalar(out=B("Ug", h)[:], in0=ublk, scalar1=gcol(h), scalar2=None, op0=ALU.mult), r=["const", "bgres"], w=[K_("Ug", h)])
                    kb.dve(lambda e: e.tensor_copy(out=B("g2", h)[:], in_=gcol(h).to_broadcast([128, 2])), r=["bgres"], w=[K_("g2", h)])
            st.append(s_ug)

            def s_g():
                for h in H4:
                    kb.pe(lambda e: e.matmul(Q(pA, h), lhsT=B("Ug", h)[:], rhs=onesblk, start=True, stop=False), r=[K_("Ug", h), "const"], w=["pA"], inc=False)
                    kb.pe(lambda e: e.matmul(Q(pA, h), lhsT=negblk[:], rhs=B("Ug", h)[:], start=False, stop=True), r=[K_("Ug", h), "const"], w=["pA"])
                for h in H4:
                    o8 = 8 * h
                    kb.pe(lambda e: e.matmul(pB[:, o8:o8 + 2], lhsT=B("Ug", h)[:], rhs=onesf[:, 0:2], start=True, stop=True), r=[K_("Ug", h), "const"], w=["pB"], inc=False)
                    kb.pe(lambda e: e.matmul(pB[:, o8 + 2:o8 + 4], lhsT=onesblk, rhs=B("g2", h)[:], start=True, stop=True), r=[K_("g2", h), "const"], w=["pB"], inc=False)
                    kb.pe(lambda e: e.matmul(pB[:, o8 + 4:o8 + 6], lhsT=sel[0], rhs=B("g2", h)[:], start=True, stop=True), r=[K_("g2", h), "const"], w=["pB"], inc=False)
                    kb.pe(lambda e: e.matmul(pB[:, o8 + 6:o8 + 8], lhsT=sel[1], rhs=B("g2", h)[:], start=True, stop=True), r=[K_("g2", h), "const"], w=["pB"])
            st.append(s_g)

            def s_e1():
                for h in H4:
                    kb.act(lambda e: e.activation(out=B("smc", h)[:], in_=pB[:, 8 * h:8 * h + 8], func=ACT.Copy), r=["pB"], w=[K_("smc", h)])
                    kb.dve(lambda e: e.tensor_scalar(out=B("E", h)[:], in0=Q(pA, h), scalar1=-1.0, scalar2=0.0, op0=ALU.mult, op1=ALU.min), r=["pA"], w=[K_("E", h)])
            st.append(s_e1)

            def s_e2():
                for h in H4:
                    sm, smc = B("sm", h), B("smc", h)
                    kb.act(lambda e: e.activation(out=sm[:, 0:1], in_=smc[:, 0:1], func=ACT.Exp), r=[K_("smc", h)], w=[K_("sm", h)])
                    kb.act(lambda e: e.activation(out=B("E", h)[:], in_=B("E", h)[:], func=ACT.Exp), r=[K_("E", h)], w=[K_("E", h)])
                    kb.dve(lambda e: e.tensor_tensor(out=sm[:, 2:3], in0=smc[:, 2:3], in1=smc[:, 0:1], op=ALU.subtract), r=[K_("smc", h), K_("sm", h)], w=[K_("sm", h)])
            st.append(s_e2)

            def s_kk():
                for h in H4:
                    kb.pe(lambda e: e.matmul(Q(pC, h), lhsT=RR(B("kTr", h)[:]), rhs=RR(B("kTr", h)[:]), start=True, stop=True), r=[K_("kTr", h)], w=["pC"])
                for h in H4:
                    kb.pe(lambda e: e.matmul(Q(pA, h), lhsT=RR(B("kTr", h)[:]), rhs=RR(B("qTr", h)[:]), start=True, stop=True), r=[K_("kTr", h), K_("qTr", h)], w=["pA"])
            st.append(s_kk)

            def s_e3():
                for h in H4:
                    sm, smc = B("sm", h), B("smc", h)
                    kb.dve(lambda e: e.tensor_scalar(out=sm[:, 1:2], in0=sm[:, 0:1], scalar1=-1.0, scalar2=None, op0=ALU.mult), r=[K_("sm", h)], w=[K_("sm", h)])
                    kb.act(lambda e: e.activation(out=sm[:, 2:3], in_=sm[:, 2:3], func=ACT.Exp), r=[K_("sm", h)], w=[K_("sm", h)])
                    kb.act(lambda e: e.activation(out=sm[:, 4:8], in_=smc[:, 4:8], func=ACT.Exp), r=[K_("smc", h), K_("sm", h)], w=[K_("sm", h)])
                    kb.pool(lambda e: e.tensor_tensor(out=B("Ei", h)[:], in0=B("E", h)[:], in1=uincl, op=ALU.mult), r=[K_("E", h), "const"], w=[K_("Ei", h)])
                    kb.pool(lambda e: e.tensor_tensor(out=B("Es", h)[:], in0=B("E", h)[:], in1=negustr, op=ALU.mult), r=[K_("E", h), "const"], w=[K_("Es", h)])
            st.append(s_e3)

            def s_ya():
                for h in H4:
                    kb.dve(lambda e: e.scalar_tensor_tensor(out=RR(B("Ya", h)[:]), in0=Q(pC, h), scalar=beta(h), in1=B("Es", h)[:], op0=ALU.mult, op1=ALU.mult),
                           r=["pC", K_("Es", h), "bgres"], w=[K_("Ya", h)])
                for h in H4:
                    kb.dve(lambda e: e.tensor_tensor(out=RR(B("aT", h)[:]), in0=Q(pA, h), in1=B("Ei", h)[:], op=ALU.mult), r=["pA", K_("Ei", h)], w=[K_("aT", h)])
            st.append(s_ya)

            def s_tr():
                for h in H4:
                    kb.pe(lambda e: e.transpose(out=Q(pB, h), in_=B("Ya", h)[:], identity=identf[:]), r=[K_("Ya", h), "const"], w=["pB"])
            st.append(s_tr)

            def s_tr2():
                for h in H4:
                    kb.act(lambda e: e.activation(out=RR(B("YaT", h)[:]), in_=Q(pB, h), func=ACT.Copy), r=["pB"], w=[K_("YaT", h)])
                    kb.dve(lambda e: e.tensor_tensor(out=RR(B("Wa", h)[:]), in0=B("Ya", h)[:], in1=identf[:], op=ALU.add), r=[K_("Ya", h), "const"], w=[K_("Wa", h)])
            st.append(s_tr2)

            cur = ["Ya", "YaT", "Wa"]
            for k in range(1, 6):
                Y, YT, W = cur
                Yn, YTn, Wn = ("Yb", "YbT", "Wb") if Y == "Ya" else ("Ya", "YaT", "Wa")

                def s_sq(k=k, Y=Y, YT=YT):
                    if k < 5:
                        for h in H4:
                            kb.pe(lambda e: e.matmul(Q(pA, h), lhsT=RR(B(YT, h)[:]), rhs=RR(B(Y, h)[:]), start=True, stop=True), r=[K_(Y, h), K_(YT, h)], w=["pA"])
                    for h in H4:
                        kb.pe(lambda e: e.matmul(Q(pC, h), lhsT=RR(B(Y, h)[:]), rhs=RR(B(YT, h)[:]), start=True, stop=True), r=[K_(Y, h), K_(YT, h)], w=["pC"])
                st.append(s_sq)

                def s_ev(k=k, Yn=Yn, YTn=YTn):
                    for h in H4:
                        if k < 5:
                            kb.act(lambda e: e.activation(out=RR(B(Yn, h)[:]), in_=Q(pA, h), func=ACT.Copy), r=["pA"], w=[K_(Yn, h)])
                        kb.dve(lambda e: e.tensor_copy(out=RR(B(YTn, h)[:]), in_=Q(pC, h)), r=["pC"], w=[K_(YTn, h)])
                st.append(s_ev)

                def s_w(YTn=YTn, W=W):
                    for h in H4:
                        kb.pe(lambda e: e.matmul(Q(pB, h), lhsT=RR(B(YTn, h)[:]), rhs=RR(B(W, h)[:]), start=True, stop=True), r=[K_(YTn, h), K_(W, h)], w=["pB"])
                st.append(s_w)

                def s_w2(W=W, Wn=Wn):
                    for h in H4:
                        kb.dve(lambda e: e.tensor_tensor(out=RR(B(Wn, h)[:]), in0=Q(pB, h), in1=B(W, h)[:], op=ALU.add), r=["pB", K_(W, h)], w=[K_(Wn, h)])
                st.append(s_w2)
                cur = [Yn, YTn, Wn]
            for h in H4:
                gb[("Wfin", h, par)] = cur[2]

            def s_dg():
                for h in H4:
                    sm = B("sm", h)
                    kb.dve(lambda e: e.tensor_scalar(out=B("dg", h)[:], in0=identf[:], scalar1=sm[:, 0:1], scalar2=None, op0=ALU.mult), r=["const", K_("sm", h)], w=[K_("dg", h)])
                    kb.act(lambda e: e.activation(out=RR(B("kdec", h)[:]), in_=B("ktk", h)[:], func=ACT.Copy, scale=sm[:, 2:3]), r=[K_("ktk", h), K_("sm", h)], w=[K_("kdec", h)])
            st.append(s_dg)

            def s_eg():
                for h in H4:
                    kb.pe(lambda e: e.matmul(Q(pA, h), lhsT=onesf, rhs=B("dg", h)[:], start=True, stop=True), r=[K_("dg", h), "const"], w=["pA"])
            st.append(s_eg)

            def s_qd():
                for h in H4:
                    kb.dve(lambda e: e.tensor_tensor(out=RR(B("qdT", h)[:]), in0=Q(pA, h), in1=B("qTp", h)[:], op=ALU.mult), r=["pA", K_("qTp", h)], w=[K_("qdT", h)])
            st.append(s_qd)
            return st

        def scan_stages(p):
            par = p % 2
            B = lambda n, h: gb[(n, h, par)]
            K_ = lambda n, h: f"{n}{h}{par}"
            st = []
            for c in range(2):
                hs = slice(64 * c, 64 * c + 64)

                def s1(hs=hs):
                    for h in H4:
                        S, Sk = Sst[h][scur[h]], f"S{h}{scur[h]}"
                        kb.pe(lambda e: e.matmul(Q(pKS, h), lhsT=RR(B("kTr", h)[:]), rhs=RR(S[:]), start=True, stop=True), r=[K_("kTr", h), Sk], w=["pKS"])
                st.append(s1)

                def s2(hs=hs):
                    for h in H4:
                        kb.dve(lambda e: e.scalar_tensor_tensor(out=RR(B("rhs2", h)[hs, :]), in0=Q(pKS, h)[hs, :], scalar=B("sm", h)[hs, 1:2], in1=B("vtk", h)[hs, :],
                                                                op0=ALU.mult, op1=ALU.add), r=["pKS", K_("sm", h), K_("vtk", h)], w=[K_("rhs2", h)])
                st.append(s2)

                def s3(hs=hs):
                    for h in H4:
                        Wf = gb[("Wfin", h, par)]
                        kb.pe(lambda e: e.matmul(Q(pVN, h), lhsT=RR(B(Wf, h)[hs, :]), rhs=RR(B("rhs2", h)[hs, :]), start=True, stop=True), r=[K_(Wf, h), K_("rhs2", h)], w=["pVN"])
                st.append(s3)

                def s4(hs=hs):
                    for h in H4:
                        kb.dve(lambda e: e.tensor_scalar(out=RR(B("vn", h)[hs, :]), in0=Q(pVN, h)[hs, :], scalar1=bgres[hs, p, h:h + 1], scalar2=None, op0=ALU.mult),
                               r=["pVN", "bgres"], w=[K_("vn", h)])
                st.append(s4)

                def s5(hs=hs, c=c):
                    for h in H4:
                        S, Sk = Sst[h][scur[h]], f"S{h}{scur[h]}"
                        oc = slice(h * 128 + 64 * c, h * 128 + 64 * c + 64)
                        kb.pe(lambda e: e.matmul(pO[:, oc], lhsT=RR(S[:]), rhs=RR(B("qdT", h)[:, hs]), start=True, stop=False), r=[Sk, K_("qdT", h)], w=["pO"], inc=False)
                        kb.pe(lambda e: e.matmul(pO[:, oc], lhsT=RR(B("vn", h)[hs, :]), rhs=RR(B("aT", h)[hs, hs]), start=False, stop=True), r=[K_("vn", h), K_("aT", h)], w=["pO"])
                        kb.pe(lambda e: e.matmul(Q(pSS, h), lhsT=RR(B("kdec", h)[hs, :]), rhs=RR(B("vn", h)[hs, :]), start=True, stop=True), r=[K_("kdec", h), K_("vn", h)], w=["pSS"])
                st.append(s5)

                def s6(c=c):
                    for h in H4:
                        S, Sk = Sst[h][scur[h]], f"S{h}{scur[h]}"
                        Sn, Snk = Sst[h][1 - scur[h]], f"S{h}{1-scur[h]}"
                        kb.dve(lambda e: e.scalar_tensor_tensor(out=RR(Sn[:]), in0=S[:], scalar=B("sm", h)[:, 4 + 2 * c:5 + 2 * c], in1=Q(pSS, h), op0=ALU.mult, op1=ALU.add),
                               r=[Sk, K_("sm", h), "pSS"], w=[Snk])
                        scur[h] = 1 - scur[h]
                st.append(s6)

            def s7():
                for h in H4:
                    kb.act(lambda e: e.activation(out=obuf[h][:, (p % 4) * 128:(p % 4 + 1) * 128], in_=Q(pO, h), func=ACT.Copy), r=["pO"], w=[f"obuf{h}"])
            st.append(s7)
            return st

        npost = [0]

        def post(g):
            tsl = slice(g * 512, (g + 1) * 512)
            for h in H4:
                i2 = npost[0] % 2
                kb.dma(gsg[i2][:], sgT[h, :, tsl], r=["sgT"], w=[f"gsg{i2}"])
                kb.pool(lambda e: e.tensor_tensor(out=gsq[:], in0=obuf[h][:], in1=obuf[h][:], op=ALU.mult), r=[f"obuf{h}"], w=["gsq"])
                kb.pe(lambda e: e.matmul(pP[:], lhsT=onesf, rhs=gsq[:], start=True, stop=True), r=["gsq", "const"], w=["pP"])
                kb.act(lambda e: e.activation(out=gsd[:], in_=pP[:], func=ACT.Ln, bias=epst[:], scale=1.0 / 128), r=["pP", "const"], w=["gsd"])
                kb.act(lambda e: e.activation(out=gsd[:], in_=gsd[:], func=ACT.Exp, scale=-0.5), r=["gsd"], w=["gsd"])
                kb.dve(lambda e: e.tensor_tensor(out=gon[:], in0=obuf[h][:], in1=gsd[:], op=ALU.mult), r=[f"obuf{h}", "gsd"], w=["gon"])
                kb.dve(lambda e: e.scalar_tensor_tensor(out=gmx[i2][:], in0=gon[:], scalar=gngs[:, 0:1], in1=gsg[i2][:], op0=ALU.mult, op1=ALU.mult),
                       r=["gon", "const", f"gsg{i2}"], w=[f"gmx{i2}"])
                kb.dma(mixT[h, :, tsl], gmx[i2][:], r=[f"gmx{i2}"], w=["mixT"], q="pool")
                npost[0] += 1

        for f in pre_stages(0):
            f()
        for p in range(NT):
            sc_ = scan_stages(p)
            pr_ = pre_stages(p + 1) if p + 1 < NT else []
            ratio = (len(pr_) + len(sc_) - 1) // len(sc_)
            pi = 0
            for f in sc_:
                f()
                for _ in range(ratio):
                    if pi < len(pr_):
                        pr_[pi]()
                        pi += 1
            while pi < len(pr_):
                pr_[pi]()
                pi += 1
            if p % 4 == 3:
                post(p // 4)
        kb.pop()
        if upto <= 2:
            kb.finish()
            return nc, kb
        kb.push()
        akT = [kb.sb(f"akT{i}", [128, T], BF16) for i in range(2)]
        aqT = [kb.sb(f"aqT{i}", [128, T], BF16) for i in range(2)]
        avv = [kb.sb(f"avv{i}", [128, NT, 128], BF16) for i in range(2)]
        pTb = [[kb.sb(f"pT{i}{j}", [128, 512], BF16) for j in range(2)] for i in range(2)]
        arl = [kb.sb(f"arl{i}", [128, 512], F32) for i in range(2)]
        lacc = [kb.sb(f"lacc{i}", [128, 512], F32) for i in range(2)]
        araw = [kb.sb(f"araw{i}", [128, 512], F32) for i in range(2)]
        ao = [kb.sb(f"ao{i}", [128, 512], F32) for i in range(2)]
        aod = kb.sb("aod", [128, 512], F32)
        asq = kb.sb("asq", [128, 512], F32)
        asd = kb.sb("asd", [128, 512], F32)
        aon = kb.sb("aon", [128, 512], F32)
        asg = [kb.sb(f"asg{i}", [128, 512], F32) for i in range(2)]
        amx = [kb.sb(f"amx{i}", [128, 512], BF16) for i in range(2)]
        pS = [[kb.ps(f"pS{i}{j}", [128, 512], F32) for j in range(2)] for i in range(2)]
        pOa = [kb.ps(f"pOa{i}", [128, 512], F32) for i in range(2)]
        pLa = [kb.ps(f"pLa{i}", [128, 512], F32) for i in range(2)]
        nstep = 0
        for h in range(4):
            hb = h % 2
            for part in range(2):
                sl = slice(part * (T // 2), (part + 1) * (T // 2))
                kb.dma(akT[hb][:, sl], dkT[h, :, sl], r=["dkT"], w=[f"akT{hb}"])
                kb.dma(aqT[hb][:, sl], dqT[h, :, sl], r=["dqT"], w=[f"aqT{hb}"])
            dvh = dv[h].rearrange("(t p) e -> p t e", p=128)
            for part in range(4):
                kb.dma(avv[hb][:, part * 16:(part + 1) * 16, :], dvh[:, part * 16:(part + 1) * 16, :], r=["dv"], w=[f"avv{hb}"])
            steps = [(g, kt) for g in range(NG) for kt in range(4 * g + 4)]

            def emit_S(n):
                g, kt = steps[n]
                j = kt - 4 * g
                qlo = 128 * max(j, 0)
                qs = slice(qlo, 512)
                sb2 = n % 2
                for i in range(2):
                    rows = slice(64 * i, 64 * i + 64)
                    kb.pe(lambda e: e.matmul(pS[i][sb2][:, qs], lhsT=akT[hb][rows, kt * 128:(kt + 1) * 128], rhs=aqT[hb][rows, g * 512 + qlo:(g + 1) * 512],
                                             start=True, stop=True), r=[f"akT{hb}", f"aqT{hb}"], w=[f"pS{i}{sb2}"])

            def emit_rest(n):
                g, kt = steps[n]
                nk = 4 * g + 4
                j = kt - 4 * g
                qlo = 128 * max(j, 0)
                qs = slice(qlo, 512)
                sb2 = n % 2
                for i in range(2):
                    pt = pTb[i][sb2]
                    kb.act(lambda e: e.activation(out=pt[:, qs], in_=pS[i][sb2][:, qs], func=ACT.Exp, bias=negC[:], scale=0.125),
                           r=[f"pS{i}{sb2}", "negC"], w=[f"pT{i}{sb2}"])
                    if j >= 0:
                        kb.pool(lambda e: e.memset(pt[64:128, qlo:qlo + 64], 0.0), r=[], w=[f"pT{i}{sb2}"])
                for i in range(2):
                    pt = pTb[i][sb2]
                    kb.pe(lambda e: e.matmul(pOa[i][:, qs], lhsT=avv[hb][:, kt, :], rhs=pt[:, qs], start=(kt == 0), stop=(kt == nk - 1)),
                          r=[f"avv{hb}", f"pT{i}{sb2}"], w=[f"pOa{i}"])
                    ae = kb.pool if i == 0 else kb.dve
                    if kt == 0:
                        ae(lambda e: e.tensor_copy(out=lacc[i][:, qs], in_=pt[:, qs]), r=[f"pT{i}{sb2}"], w=[f"lacc{i}"])
                    else:
                        ae(lambda e: e.tensor_tensor(out=lacc[i][:, qs], in0=lacc[i][:, qs], in1=pt[:, qs], op=ALU.add), r=[f"pT{i}{sb2}", f"lacc{i}"], w=[f"lacc{i}"])
                if kt != nk - 1:
                    return None
                tsl = slice(g * 512, (g + 1) * 512)
                i2 = g % 2
                kb.dma(asg[i2][:], sgT[4 + h, :, tsl], r=["sgT"], w=[f"asg{i2}"])
                for i in range(2):
                    kb.dve(lambda e: e.tensor_copy(out=araw[i][:], in_=pOa[i][:]), r=[f"pOa{i}"], w=[f"araw{i}"])
                for i in range(2):
                    kb.pe(lambda e: e.matmul(pLa[i][:], lhsT=onesf, rhs=lacc[i][:], start=True, stop=True), r=["const", f"lacc{i}"], w=[f"pLa{i}"])
                for i in range(2):
                    kb.act(lambda e: e.activation(out=arl[i][:], in_=pLa[i][:], func=ACT.Ln), r=[f"pLa{i}"], w=[f"arl{i}"])
                    kb.act(lambda e: e.activation(out=arl[i][:], in_=arl[i][:], func=ACT.Exp, scale=-1.0), r=[f"arl{i}"], w=[f"arl{i}"])
                    kb.pool(lambda e: e.tensor_tensor(out=ao[i][:], in0=araw[i][:], in1=arl[i][:], op=ALU.mult), r=[f"araw{i}", f"arl{i}"], w=[f"ao{i}"])
                kb.dve(lambda e: e.scalar_tensor_tensor(out=aod[:], in0=ao[1][:], scalar=neglam[:, 0:1], in1=ao[0][:], op0=ALU.mult, op1=ALU.add),
                       r=["ao0", "ao1", "neglam"], w=["aod"])
                kb.pool(lambda e: e.tensor_tensor(out=asq[:], in0=aod[:], in1=aod[:], op=ALU.mult), r=["aod"], w=["asq"])
                return lambda: post2(g, i2, tsl)

            def post2(g, i2, tsl):
                pN_ = pLa[0]
                kb.pe(lambda e: e.matmul(pN_[:], lhsT=onesf, rhs=asq[:], start=True, stop=True), r=["asq", "const"], w=["pLa0"])
                kb.act(lambda e: e.activation(out=asd[:], in_=pN_[:], func=ACT.Ln, bias=epst[:], scale=1.0 / 128), r=["pLa0", "const"], w=["asd"])
                kb.act(lambda e: e.activation(out=asd[:], in_=asd[:], func=ACT.Exp, scale=-0.5), r=["asd"], w=["asd"])
                kb.dve(lambda e: e.tensor_tensor(out=aon[:], in0=aod[:], in1=asd[:], op=ALU.mult), r=["aod", "asd"], w=["aon"])
                kb.dve(lambda e: e.scalar_tensor_tensor(out=amx[i2][:], in0=aon[:], scalar=slgs[:, 0:1], in1=asg[i2][:], op0=ALU.mult, op1=ALU.mult),
                       r=["aon", "slgs", f"asg{i2}"], w=[f"amx{i2}"])
                kb.dma(mixT[4 + h, :, tsl], amx[i2][:], r=[f"amx{i2}"], w=["mixT"], q="pool")

            emit_S(0)
            pp = None
            for n in range(len(steps)):
                if n + 1 < len(steps):
                    emit_S(n + 1)
                newpp = emit_rest(n)
                if pp is not None:
                    pp()
                pp = newpp
            if pp is not None:
                pp()
        kb.pop()
        if upto <= 3:
            kb.finish()
            return nc, kb
        kb.push()
        mt = [kb.sb(f"mt{i}", [128, 8, 512], BF16) for i in range(2)]
        xr = [kb.sb(f"xr{i}", [128, 512], F32) for i in range(2)]
        y1 = [kb.sb(f"y1{i}", [128, 512], F32) for i in range(2)]
        y2 = [kb.sb(f"y2{i}", [128, 512], F32) for i in range(2)]
        pY = [kb.ps(f"pY{i}", [128, 512], F32) for i in range(2)]
        mixT_v = mixT.rearrange("m p t -> p m t")
        for g in range(NG):
            m_ = mt[g % 2]
            mk = f"mt{g%2}"
            kb.dma(m_[:], mixT_v[:, :, g * 512:(g + 1) * 512], r=["mixT"], w=[mk])
            for a in range(4):
                tt = 4 * g + a
                i2 = tt % 2
                kb.dma(xr[i2][:], xres[tt * 128:(tt + 1) * 128, :], w=[f"xr{i2}"])
                for m in range(8):
                    kb.pe(lambda e: e.matmul(pY[i2][:], lhsT=m_[:, m, a * 128:(a + 1) * 128], rhs=wobf[:, m, :], start=(m == 0), stop=(m == 7)),
                          r=[mk, "wobf"], w=[f"pY{i2}"], inc=(m == 7))
                kb.dve(lambda e: e.tensor_tensor(out=y1[i2][:], in0=pY[i2][:], in1=gaterep[:], op=ALU.mult), r=[f"pY{i2}", "gaterep"], w=[f"y1{i2}"])
                kb.pool(lambda e: e.tensor_tensor(out=y2[i2][:], in0=y1[i2][:], in1=xr[i2][:], op=ALU.add), r=[f"y1{i2}", f"xr{i2}"], w=[f"y2{i2}"])
                kb.dma(out[tt * 128:(tt + 1) * 128, :], y2[i2][:], r=[f"y2{i2}"], w=["out"], q="pool")
        kb.pop()
        kb.finish()
    return nc, kb


def _consts():
    i = np.arange(128)
    same = (i[:, None] // 64) == (i[None, :] // 64)
    onesf = np.ones((128, 128), np.float32)
    ublk = (same & (i[:, None] <= i[None, :])).astype(np.float32)
    uincl = ublk.copy()
    negustr = -(same & (i[None, :] > i[:, None])).astype(np.float32)
    onesblk = same.astype(np.float32)
    sel0 = np.repeat((i < 64).astype(np.float32)[:, None], 128, 1)
    sel1 = np.repeat((i >= 64).astype(np.float32)[:, None], 128, 1)
    masks = np.stack([onesf, ublk, uincl, negustr, onesblk, sel0, sel1], 1).astype(np.float32)
    invf = (np.float32(500000.0) ** (-np.arange(0, 16, 2, dtype=np.float32) / np.float32(16))).astype(np.float32)
    return {
        "c_identf": np.eye(128, dtype=np.float32),
        "c_identb": np.eye(128, dtype=np.float32).astype(ml_dtypes.bfloat16),
        "c_onesb": np.ones((128, 128), np.float32).astype(ml_dtypes.bfloat16),
        "c_masks": np.ascontiguousarray(masks),
        "c_negblk": -onesblk,
        "c_invf": np.ascontiguousarray(np.broadcast_to(invf[None, :], (128, 8))),
    }


def _rep(v, n=128):
    return np.ascontiguousarray(np.broadcast_to(np.asarray(v, np.float32).reshape(1, -1), (n, np.asarray(v).size)))


def prep_core(inp, b, half):
    f = lambda a: np.ascontiguousarray(np.asarray(a, np.float32))
    w_in = np.asarray(inp["w_in"][0], np.float32)
    off = np.cumsum([0, 512, 512, 512, 4, 4, 512, 512, 512, 512, 512])
    aq, ak, av, abeta, adec, agate, bq, bk, bv, bgate = [np.arange(off[i], off[i + 1]) for i in range(10)]
    cols = []
    for h in range(4):
        cols += [aq[h * 128:(h + 1) * 128], ak[h * 128:(h + 1) * 128], av[h * 128:(h + 1) * 128]]
    cols += [agate, bgate, bq, bk, bv, abeta, adec]
    cols = np.concatenate(cols)
    conv_w = np.asarray(inp["conv_w"][0], np.float32)
    convc = np.zeros((128, 12, 4), np.float32)
    for h in range(4):
        for kind in range(3):
            ch = kind * 512 + h * 128 + np.arange(128)
            convc[:, 3 * h + kind, :] = conv_w[:, ch].T
    w_ada = np.asarray(inp["w_ada"][0], np.float32)
    b_ada = np.asarray(inp["b_ada"][0], np.float32)
    gsl = slice(2048 + half * 512, 2048 + (half + 1) * 512)
    d = {
        "x": f(inp["x"][b]),
        "xres": f(inp["x"][b][:, half * 512:(half + 1) * 512]),
        "cT": f(np.asarray(inp["c"][b]).reshape(8, 128).T),
        "posT": np.ascontiguousarray(np.asarray(inp["positions"][b], np.int32).reshape(NT, 128).T),
        "wada": f(w_ada[:, 0:2048]),
        "wadag": f(w_ada[:, gsl]),
        "badac": f(b_ada[0:2048].reshape(16, 128).T),
        "badag": _rep(b_ada[gsl]),
        "normgc": f(np.asarray(inp["norm_g"][0]).reshape(8, 128).T),
        "win": f(w_in[:, cols]),
        "convc": convc,
        "alog": _rep(inp["a_log"][0]),
        "dtb": _rep(inp["dt_bias"][0]),
        "gng": f(np.asarray(inp["gdn_norm_g"][0]).reshape(128, 1)),
        "slg": f(np.asarray(inp["subln_g"][0]).reshape(128, 1)),
        "gqk": _rep(np.concatenate([np.tile(np.asarray(inp["q_norm_g"][0]), 8), np.tile(np.asarray(inp["k_norm_g"][0]), 8)])),
        "lamv": np.ascontiguousarray(np.broadcast_to(np.stack([np.asarray(inp[k][0], np.float32) for k in
                                                               ("lambda_q1", "lambda_k1", "lambda_q2", "lambda_k2")])[None], (128, 4, 64))),
        "wout": f(np.asarray(inp["w_out"][0])[:, half * 512:(half + 1) * 512]),
    }
    d.update(_consts())
    return d


_NC_CACHE = {}


def kernel(**inputs):
    if "nc" not in _NC_CACHE:
        _NC_CACHE["nc"] = build_program()[0]
    nc = _NC_CACHE["nc"]
    in_maps = [prep_core(inputs, core // 2, core % 2) for core in range(8)]
    res = run_bass_kernel_spmd(nc, in_maps, core_ids=list(range(8)))
    out = np.empty((4, T, D), np.float32)
    for core in range(8):
        out[core // 2, :, (core % 2) * 512:(core % 2 + 1) * 512] = res.results[core]["out"]
    return out
```

```python
import math
from contextlib import ExitStack

import numpy as np
import ml_dtypes

import concourse.bass as bass
import concourse.mybir as mybir
from concourse.bass_utils import run_bass_kernel_spmd

F32 = mybir.dt.float32
BF16 = mybir.dt.bfloat16
I32 = mybir.dt.int32
ACT = mybir.ActivationFunctionType
ALU = mybir.AluOpType
AX = mybir.AxisListType


class KB:
    NDMA = 40

    def __init__(self, nc):
        self.nc = nc
        self.eng = {"pe": nc.tensor, "act": nc.scalar, "dve": nc.vector, "pool": nc.gpsimd, "sp": nc.sync}
        self.stack = ExitStack()
        self.sem = {}
        self.cnt = {k: 0 for k in self.eng}
        self.waited = {k: {} for k in self.eng}
        self.last_w = {}
        self.readers = {}
        self.dsem = []
        self.dcnt = []
        self.dnext = 0
        self.nins = 0
        self.phase = 0
        self.excl = set()

    def ctx(self):
        for k in ("pe", "act", "dve", "pool"):
            self.sem[k] = self.stack.enter_context(self.nc.semaphore("s_" + k))
        for i in range(self.NDMA):
            self.dsem.append(self.stack.enter_context(self.nc.semaphore(f"s_dma{i}")))
            self.dcnt.append(0)
        return self.stack

    def sb(self, name, shape, dtype):
        return self.stack.enter_context(self.nc.sbuf_tensor(f"{name}_{self.phase}", shape, dtype))

    def ps(self, name, shape, dtype):
        self.excl.add(name)
        return self.stack.enter_context(self.nc.psum_tensor(f"{name}_{self.phase}", shape, dtype))

    def _handle(self, s):
        return self.dsem[s[1]] if isinstance(s, tuple) else self.sem[s]

    def _wait(self, en, deps):
        e = self.eng[en]
        need = {}
        for s, v in deps:
            if s == "pe" and en == "pe":
                continue
            if v > need.get(s, 0):
                need[s] = v
        for s, v in need.items():
            if self.waited[en].get(s, 0) < v:
                e.wait_ge(self._handle(s), v)
                self.waited[en][s] = v
                self.nins += 1

    def _deps(self, r, w):
        d = []
        for k in r:
            if k in self.last_w:
                d.append(self.last_w[k])
        for k in w:
            if k in self.last_w:
                d.append(self.last_w[k])
            d.extend(self.readers.get(k, ()))
        return d

    def _record(self, r, w, tag):
        for k in r:
            self.readers.setdefault(k, []).append(tag)
        for k in w:
            self.last_w[k] = tag
            self.readers[k] = []

    def _skip(self):
        import os
        lim = os.environ.get("KB_LIMIT")
        if not lim:
            return False
        ph, n = (int(v) for v in lim.split(":"))
        if self.phase != ph:
            return False
        self.pops = getattr(self, "pops", 0) + 1
        return self.pops > n

    def op(self, en, fn, r=(), w=(), inc=True):
        if self._skip():
            return None
        w = list(w) + [k for k in r if k in self.excl and k not in w]
        self._wait(en, self._deps(r, w))
        ins = fn(self.eng[en])
        self.nins += 1
        if inc:
            ins.then_inc(self.sem[en], 1)
            self.cnt[en] += 1
            tag = (en, self.cnt[en])
        else:
            tag = (en, self.cnt[en] + 1)
        self._record(r, w, tag)
        return ins

    def pe(self, fn, r=(), w=(), inc=True):
        return self.op("pe", fn, r, w, inc)

    def act(self, fn, r=(), w=()):
        return self.op("act", fn, r, w)

    def dve(self, fn, r=(), w=()):
        return self.op("dve", fn, r, w)

    def pool(self, fn, r=(), w=()):
        return self.op("pool", fn, r, w)

    def dma(self, out, in_, r=(), w=(), q="sp", **kw):
        if self._skip():
            return None
        deps = self._deps(r, w)
        i = self.dnext
        self.dnext = (self.dnext + 1) % self.NDMA
        if self.dcnt[i] > 0:
            deps.append((("d", i), 16 * self.dcnt[i]))
        self._wait(q, deps)
        ins = self.eng[q].dma_start(out=out, in_=in_, **kw)
        ins.then_inc(self.dsem[i], 16)
        self.nins += 1
        self.dcnt[i] += 1
        self._record(r, w, (("d", i), 16 * self.dcnt[i]))
        return ins

    def finish(self):
        deps = [(k, self.cnt[k]) for k in ("pe", "act", "dve", "pool") if self.cnt[k] > 0]
        deps += [(("d", i), 16 * c) for i, c in enumerate(self.dcnt) if c > 0]
        self._wait("sp", deps)

    def barrier(self):
        deps = [(k, self.cnt[k]) for k in ("pe", "act", "dve", "pool") if self.cnt[k] > 0]
        deps += [(("d", i), 16 * c) for i, c in enumerate(self.dcnt) if c > 0]
        for en in ("pe", "act", "dve", "pool", "sp"):
            self._wait(en, [d for d in deps if not (d[0] == "pe" and en == "pe")] + ([("pe", self.cnt["pe"])] if False else []))
        self.last_w = {}
        self.readers = {}

    def push(self):
        self._saved = getattr(self, "_saved", [])
        self._saved.append(self.stack)
        self.stack = ExitStack()
        self.stack.__enter__()
        self.phase += 1

    def pop(self):
        self.barrier()
        self.stack.__exit__(None, None, None)
        self.stack = self._saved.pop()


T = 8192
D = 1024
NT = T // 128
NG = T // 512
NCOL = 4104
EPS = 1e-6
LAMBDA_INIT = 0.8 - 0.6 * math.exp(0.0)
NFM = 20
TM0 = NFM * 128
TWO_PI = 2.0 * math.pi
MAGIC = 12582912.0
CW1 = 6.28125
CW2 = 0.0019350051879882812
CW3 = TWO_PI - CW1 - CW2


def build_program(dbg=(), upto=9):
    nc = bass.Bass("TRN2", target_bir_lowering=False)

    def din(name, shape, dt=F32):
        return nc.dram_tensor(name, list(shape), dt, kind="ExternalInput").ap()

    def dscr(name, shape, dt=F32):
        return nc.dram_tensor(name, list(shape), dt, kind="ExternalOutput" if name in dbg else "Internal").ap()

    x = din("x", [T, D])
    xres = din("xres", [T, 512])
    cT = din("cT", [128, 8])
    posT = din("posT", [128, NT], I32)
    wada = din("wada", [D, 2048])
    wadag = din("wadag", [D, 512])
    badac = din("badac", [128, 16])
    badag = din("badag", [128, 512])
    normgc = din("normgc", [128, 8])
    win = din("win", [D, NCOL])
    convc = din("convc", [128, 12, 4])
    alog = din("alog", [128, 4])
    dtb = din("dtb", [128, 4])
    gng = din("gng", [128, 1])
    slg = din("slg", [128, 1])
    gqk = din("gqk", [128, 1024])
    lamv = din("lamv", [128, 4, 64])
    wout = din("wout", [D, 512])
    c_identf = din("c_identf", [128, 128])
    c_identb = din("c_identb", [128, 128], BF16)
    c_onesb = din("c_onesb", [128, 128], BF16)
    c_masks = din("c_masks", [128, 7, 128])
    c_negblk = din("c_negblk", [128, 128])
    c_invf = din("c_invf", [128, 8])
    out = nc.dram_tensor("out", [T, 512], F32, kind="ExternalOutput").ap()

    gqT = dscr("gqT", [4, 128, T])
    gkT = dscr("gkT", [4, 128, T])
    gktok = dscr("gktok", [4, T, 128])
    gvtok = dscr("gvtok", [4, T, 128])
    sgT = dscr("sgT", [8, 128, T])
    dqT = dscr("dqT", [4, 128, T], BF16)
    dkT = dscr("dkT", [4, 128, T], BF16)
    dv = dscr("dv", [4, T, 128], BF16)
    mixT = dscr("mixT", [8, 128, T], BF16)

    kb = KB(nc)
    with kb.ctx():
        identf = kb.sb("identf", [128, 128], F32)
        identb = kb.sb("identb", [128, 128], BF16)
        onesb = kb.sb("onesb", [128, 128], BF16)
        masks = kb.sb("masks", [128, 7, 128], F32)
        negblk = kb.sb("negblk", [128, 128], F32)
        onesf, ublk, uincl, negustr, onesblk = (masks[:, i, :] for i in range(5))
        sel = [masks[:, 5, :], masks[:, 6, :]]
        bgres = kb.sb("bgres", [128, NT, 8], F32)
        cosr = kb.sb("cosr", [128, NT, 8], F32)
        sinr = kb.sb("sinr", [128, NT, 8], F32)
        gaterep = kb.sb("gaterep", [128, 512], F32)
        mscale = kb.sb("mscale", [128, 8], F32)
        shiftc = kb.sb("shiftc", [128, 8], F32)
        negA = kb.sb("negA", [128, 4], F32)
        dtbs = kb.sb("dtbs", [128, 4], F32)
        gngs = kb.sb("gngs", [128, 1], F32)
        slgs = kb.sb("slgs", [128, 1], F32)
        neglam = kb.sb("neglam", [128, 1], F32)
        negC = kb.sb("negC", [128, 1], F32)
        gqks = kb.sb("gqks", [128, 1024], F32)
        convs = kb.sb("convs", [128, 12, 4], F32)
        wobf = kb.sb("wobf", [128, 8, 512], BF16)

        epst = kb.sb("epst", [128, 1], F32)
        onet = kb.sb("onet", [128, 1], F32)
        CONSTK = ["const"]
        kb.pool(lambda e: e.memset(epst[:], EPS), w=CONSTK)
        kb.pool(lambda e: e.memset(onet[:], 1.0), w=CONSTK)
        for dst, src in ((identf, c_identf), (identb, c_identb), (onesb, c_onesb), (masks, c_masks), (negblk, c_negblk),
                         (negA, alog), (dtbs, dtb), (gngs, gng), (slgs, slg), (gqks, gqk), (convs, convc)):
            kb.dma(dst[:], src, w=CONSTK)

        kb.push()
        sc = kb.sb("sc", [128, 8], F32)
        screp = kb.sb("screp", [128, 8, 128], F32)
        wch = [kb.sb(f"wch{i}", [128, 8, 512], F32) for i in range(2)]
        modcol = kb.sb("modcol", [128, 16], F32)
        bcol = kb.sb("bcol", [128, 16], F32)
        ngc = kb.sb("ngc", [128, 8], F32)
        bgate = kb.sb("bgate", [128, 512], F32)
        lams = kb.sb("lams", [128, 4, 64], F32)
        lprod = kb.sb("lprod", [128, 2, 64], F32)
        lsum = kb.sb("lsum", [128, 2], F32)
        posi = kb.sb("posi", [128, NT], I32)
        posf = kb.sb("posf", [128, NT], F32)
        invf = kb.sb("invf", [128, 8], F32)
        ang = kb.sb("ang", [128, NT, 8], F32)
        tk = kb.sb("tk", [128, NT, 8], F32)
        rr = kb.sb("rr", [128, NT, 8], F32)
        gmax = kb.sb("gmax", [128, 2], F32)
        psm = kb.ps("psm", [128, 512], F32)
        psg = kb.ps("psg", [128, 512], F32)

        kb.dma(sc[:], cT, w=["sc"])
        kb.dma(bcol[:], badac, w=["bcol"])
        kb.dma(ngc[:], normgc, w=["ngc"])
        kb.dma(bgate[:], badag, w=["bgate"])
        kb.dma(lams[:], lamv, w=["lams"])
        kb.dma(posi[:], posT, w=["posi"])
        kb.dma(invf[:], c_invf, w=["invf"])
        kb.act(lambda e: e.activation(out=sc[:], in_=sc[:], func=ACT.Silu), r=["sc"], w=["sc"])
        kb.dve(lambda e: e.tensor_copy(out=screp[:], in_=sc[:].unsqueeze(2).to_broadcast([128, 8, 128])), r=["sc"], w=["screp"])
        wada_v = wada.rearrange("(kc p) n -> p kc n", p=128)
        for cc in range(4):
            wt = wch[cc % 2]
            kb.dma(wt[:], wada_v[:, :, cc * 512:(cc + 1) * 512], w=[f"wch{cc%2}"])
            for jj in range(4):
                j = cc * 4 + jj
                for kc in range(8):
                    kb.pe(lambda e: e.matmul(psm[:, 2 * j:2 * j + 2], lhsT=wt[:, kc, jj * 128:(jj + 1) * 128], rhs=screp[:, kc, 0:2],
                                             start=(kc == 0), stop=(kc == 7)),
                          r=[f"wch{cc%2}", "screp"], w=["psm"], inc=(kc == 7))
        kb.dve(lambda e: e.tensor_tensor(out=modcol[:], in0=psm[:, 0:32].rearrange("p (j two) -> p j two", two=2)[:, :, 0], in1=bcol[:], op=ALU.add),
               r=["psm", "bcol"], w=["modcol"])
        kb.dve(lambda e: e.tensor_copy(out=shiftc[:], in_=modcol[:, 0:8]), r=["modcol"], w=["shiftc"])
        kb.dve(lambda e: e.scalar_tensor_tensor(out=mscale[:], in0=modcol[:, 8:16], scalar=1.0, in1=ngc[:], op0=ALU.add, op1=ALU.mult),
               r=["modcol", "ngc"], w=["mscale"])
        wt = wch[0]
        kb.dma(wt[:], wadag.rearrange("(kc p) n -> p kc n", p=128), w=["wch0"])
        for kc in range(8):
            kb.pe(lambda e: e.matmul(psg[:], lhsT=screp[:, kc, :], rhs=wt[:, kc, :], start=(kc == 0), stop=(kc == 7)),
                  r=["wch0", "screp"], w=["psg"], inc=(kc == 7))
        kb.dve(lambda e: e.tensor_tensor(out=gaterep[:], in0=psg[:], in1=bgate[:], op=ALU.add), r=["psg", "bgate"], w=["gaterep"])
        wt = wch[1]
        kb.dma(wt[:], wout.rearrange("(kc p) n -> p kc n", p=128), w=["wch1"])
        kb.dve(lambda e: e.tensor_copy(out=wobf[:], in_=wt[:]), r=["wch1"], w=["wobf"])
        kb.dve(lambda e: e.tensor_tensor(out=lprod[:, 0, :], in0=lams[:, 0, :], in1=lams[:, 1, :], op=ALU.mult), r=["lams"], w=["lprod"])
        kb.dve(lambda e: e.tensor_tensor(out=lprod[:, 1, :], in0=lams[:, 2, :], in1=lams[:, 3, :], op=ALU.mult), r=["lams", "lprod"], w=["lprod"])
        kb.dve(lambda e: e.tensor_reduce(out=lsum[:], in_=lprod[:], axis=AX.X, op=ALU.add), r=["lprod"], w=["lsum"])
        kb.act(lambda e: e.activation(out=lsum[:], in_=lsum[:], func=ACT.Exp), r=["lsum"], w=["lsum"])
        kb.dve(lambda e: e.scalar_tensor_tensor(out=neglam[:], in0=lsum[:, 1:2], scalar=-LAMBDA_INIT, in1=lsum[:, 0:1], op0=ALU.add, op1=ALU.subtract),
               r=["lsum"], w=["neglam"])
        kb.act(lambda e: e.activation(out=negA[:], in_=negA[:], func=ACT.Exp), r=CONSTK, w=["negA"])
        kb.dve(lambda e: e.tensor_scalar(out=negA[:], in0=negA[:], scalar1=-1.0, scalar2=None, op0=ALU.mult), r=["negA"], w=["negA"])
        kb.dve(lambda e: e.tensor_scalar(out=slgs[:], in0=slgs[:], scalar1=1.0 - LAMBDA_INIT, scalar2=None, op0=ALU.mult), r=CONSTK, w=["slgs"])
        kb.dve(lambda e: e.tensor_reduce(out=gmax[:], in_=gqks[:].rearrange("p (a n) -> p a n", a=2), axis=AX.X, op=ALU.max, apply_absolute_value=True),
               r=CONSTK, w=["gmax"])
        kb.dve(lambda e: e.scalar_tensor_tensor(out=negC[:], in0=gmax[:, 0:1], scalar=-8.0, in1=gmax[:, 1:2], op0=ALU.mult, op1=ALU.mult),
               r=["gmax"], w=["negC"])
        kb.dve(lambda e: e.tensor_copy(out=posf[:], in_=posi[:]), r=["posi"], w=["posf"])
        kb.dve(lambda e: e.tensor_tensor(out=ang[:], in0=posf[:].unsqueeze(2).to_broadcast([128, NT, 8]),
                                         in1=invf[:].unsqueeze(1).to_broadcast([128, NT, 8]), op=ALU.mult), r=["posf", "invf"], w=["ang"])
        for (dst, off, name) in ((sinr, 0.0, "sinr"), (cosr, 0.25, "cosr")):
            kb.dve(lambda e: e.tensor_scalar(out=tk[:], in0=ang[:], scalar1=1.0 / TWO_PI, scalar2=off, op0=ALU.mult, op1=ALU.add), r=["ang"], w=["tk"])
            kb.dve(lambda e: e.tensor_scalar(out=tk[:], in0=tk[:], scalar1=MAGIC, scalar2=None, op0=ALU.add), r=["tk"], w=["tk"])
            kb.dve(lambda e: e.tensor_scalar(out=tk[:], in0=tk[:], scalar1=-MAGIC, scalar2=None, op0=ALU.add), r=["tk"], w=["tk"])
            kb.dve(lambda e: e.scalar_tensor_tensor(out=rr[:], in0=tk[:], scalar=-CW1, in1=ang[:], op0=ALU.mult, op1=ALU.add), r=["ang", "tk"], w=["rr"])
            kb.dve(lambda e: e.scalar_tensor_tensor(out=rr[:], in0=tk[:], scalar=-CW2, in1=rr[:], op0=ALU.mult, op1=ALU.add), r=["rr", "tk"], w=["rr"])
            kb.dve(lambda e: e.scalar_tensor_tensor(out=rr[:], in0=tk[:], scalar=-CW3, in1=rr[:], op0=ALU.mult, op1=ALU.add), r=["rr", "tk"], w=["rr"])
            kb.dve(lambda e: e.tensor_scalar(out=rr[:], in0=rr[:], scalar1=off * TWO_PI, scalar2=math.pi, op0=ALU.add, op1=ALU.min), r=["rr"], w=["rr"])
            kb.dve(lambda e: e.tensor_scalar(out=rr[:], in0=rr[:], scalar1=-math.pi, scalar2=None, op0=ALU.max), r=["rr"], w=["rr"])
            kb.act(lambda e: e.activation(out=dst[:], in_=rr[:], func=ACT.Sin), r=["rr"], w=[name])
        kb.pop()
        kb.push()
        wbf = kb.sb("wbf", [128, 8, NCOL], BF16)
        wcb = [kb.sb(f"wcb{i}", [128, 8, 64], F32) for i in range(2)]
        xt = [kb.sb(f"xt{i}", [128, 1024], F32) for i in range(2)]
        xs = [kb.sb(f"xs{i}", [128, 1024], F32) for i in range(2)]
        st = [kb.sb(f"st{i}", [128, 4], F32) for i in range(2)]
        hT = [kb.sb(f"hT{i}", [128, 8, 512], BF16) for i in range(2)]
        cbuf = [kb.sb(f"cbuf{i}", [128, 515], F32) for i in range(12)]
        acc = [kb.sb(f"acc{i}", [128, 512], F32) for i in range(2)]
        yb = [kb.sb(f"yb{i}", [128, 512], F32) for i in range(3)]
        sqb = [kb.sb(f"sqb{i}", [128, 512], F32) for i in range(2)]
        sdb = [kb.sb(f"sdb{i}", [128, 512], F32) for i in range(2)]
        ynb = [kb.sb(f"ynb{i}", [128, 512], F32) for i in range(3)]
        trb = [kb.sb(f"trb{i}", [128, 512], F32) for i in range(2)]
        sgt = [kb.sb(f"sgt{i}", [128, 512], F32) for i in range(2)]
        qk = kb.sb("qk", [128, 1024], F32)
        ssq = kb.sb("ssq", [128, 16], F32)
        xn = kb.sb("xn", [128, 1024], F32)
        qkbs = [kb.sb(f"qkb{i}", [128, 1024], BF16) for i in range(2)]
        rt = [kb.sb(f"rt{i}", [128, 16, 8], F32) for i in range(4)]
        vb = [kb.sb(f"vb{i}", [128, 512], BF16) for i in range(2)]
        qkTg = [kb.sb("qkTg0", [128, 8, 512], BF16)]
        tmp4 = kb.sb("tmp4", [128, 4], F32)
        psT = kb.ps("psT", [128, 512], F32)
        psF = kb.ps("psF", [128, 512], F32)
        psQ = kb.ps("psQ", [128, 512], F32)
        psK = kb.ps("psK", [128, 512], F32)
        psV = kb.ps("psV", [128, 512], F32)
        psN = kb.ps("psN", [128, 512], F32)
        psXf = kb.ps("psXf", [128, 512], F32)
        psXb = kb.ps("psXb", [128, 1024], BF16)

        win_v = win.rearrange("(kc p) n -> p kc n", p=128)
        nch = (NCOL + 63) // 64
        for ci in range(nch):
            c0, c1 = ci * 64, min(NCOL, ci * 64 + 64)
            wt = wcb[ci % 2]
            kb.dma(wt[:, :, 0:c1 - c0], win_v[:, :, c0:c1], w=[f"wcb{ci%2}"])
            if ci % 2 == 0:
                kb.dve(lambda e: e.tensor_copy(out=wbf[:, :, c0:c1], in_=wt[:, :, 0:c1 - c0]), r=[f"wcb{ci%2}"], w=["wbf"])
            else:
                kb.pool(lambda e: e.tensor_copy(out=wbf[:, :, c0:c1], in_=wt[:, :, 0:c1 - c0]), r=[f"wcb{ci%2}"], w=["wbf"])
        for c in range(12):
            kb.pool(lambda e: e.memset(cbuf[c][:, 0:3], 0.0), w=[f"cbuf{c}"])

        dqT_v = dqT.rearrange("h p t -> p h t")
        dkT_v = dkT.rearrange("h p t -> p h t")
        dv_v = dv.rearrange("h t e -> t h e")
        nfm = 0
        import os as _os
        if _os.environ.get('SKIP_P1'):
            kb.pool(lambda e: e.memset(bgres[:], -0.05), w=["bgres"])
            for h in range(4):
                kb.dma(gqT[h, :, 0:512], x[0:128, 0:512], w=["gqkT"])
                kb.dma(gkT[h, :, 0:512], x[0:128, 0:512], w=["gqkT"])
                kb.dma(gktok[h, 0:512, :].rearrange("(a p) e -> p a e", p=128), x[0:128, 0:512].rearrange("p (a e) -> p a e", a=4), w=["gtok"])
                kb.dma(gvtok[h, 0:512, :].rearrange("(a p) e -> p a e", p=128), x[0:128, 0:512].rearrange("p (a e) -> p a e", a=4), w=["gtok"])
                kb.dma(sgT[h, :, 0:512], x[0:128, 0:512], w=["sgT"])
        for g in range(0 if _os.environ.get('SKIP_P1') else NG):
            hTg = hT[g % 2]
            hk = f"hT{g%2}"
            for a in range(4):
                tt = 4 * g + a
                i2 = tt % 2
                kb.dma(xt[i2][:], x[tt * 128:(tt + 1) * 128, :], w=[f"xt{i2}"])
                s = st[i2]
                kb.act(lambda e: e.activation(out=xs[i2][:], in_=xt[i2][:], func=ACT.Square, accum_out=s[:, 0:1]), r=[f"xt{i2}"], w=[f"xs{i2}", f"st{i2}"])
                kb.act(lambda e: e.activation(out=s[:, 2:3], in_=s[:, 0:1], func=ACT.Ln, bias=epst[:], scale=1.0 / D), r=[f"st{i2}", "const"], w=[f"st{i2}"])
                kb.act(lambda e: e.activation(out=s[:, 3:4], in_=s[:, 2:3], func=ACT.Exp, scale=-0.5), r=[f"st{i2}"], w=[f"st{i2}"])
                kb.act(lambda e: e.activation(out=xs[i2][:], in_=xt[i2][:], func=ACT.Copy, scale=s[:, 3:4]), r=[f"xt{i2}", f"st{i2}"], w=[f"xs{i2}"])
                for half in range(2):
                    for q4 in range(4):
                        kc = half * 4 + q4
                        kb.pe(lambda e: e.transpose(out=psT[:, q4 * 128:(q4 + 1) * 128], in_=xs[i2][:, kc * 128:(kc + 1) * 128], identity=identf[:]),
                              r=[f"xs{i2}", "const"], w=["psT"], inc=(q4 == 3))
                    for q4 in range(4):
                        kc = half * 4 + q4
                        kb.dve(lambda e: e.tensor_scalar(out=hTg[:, kc, a * 128:(a + 1) * 128], in0=psT[:, q4 * 128:(q4 + 1) * 128],
                                                         scalar1=mscale[:, kc:kc + 1], scalar2=shiftc[:, kc:kc + 1], op0=ALU.mult, op1=ALU.add),
                               r=["psT", "mscale", "shiftc"], w=[hk])
            pend = []
            tsl = slice(g * 512, (g + 1) * 512)
            for c in range(NFM):
                for kc in range(8):
                    kb.pe(lambda e: e.matmul(psF[:], lhsT=wbf[:, kc, c * 128:(c + 1) * 128], rhs=hTg[:, kc, :], start=(kc == 0), stop=(kc == 7)),
                          r=["wbf", hk], w=["psF"], inc=(kc == 7))
                new2, new3 = None, None
                if c >= 12:
                    sg = sgt[c % 2]
                    kb.act(lambda e: e.activation(out=sg[:], in_=psF[:], func=ACT.Silu), r=["psF"], w=[f"sgt{c%2}"])
                    kb.dma(sgT[c - 12, :, tsl], sg[:], r=[f"sgt{c%2}"], w=["sgT"], q="pool")
                else:
                    h, kind = c // 3, c % 3
                    cb = cbuf[c]
                    ck = f"cbuf{c}"
                    kb.act(lambda e: e.activation(out=cb[:, 3:515], in_=psF[:], func=ACT.Copy), r=["psF"], w=[ck])
                    ac = acc[nfm % 2]
                    ak = f"acc{nfm%2}"
                    y = yb[nfm % 3]
                    yk = f"yb{nfm%3}"
                    kb.dve(lambda e: e.tensor_scalar(out=ac[:], in0=cb[:, 0:512], scalar1=convs[:, c, 0:1], scalar2=None, op0=ALU.mult), r=[ck, "const"], w=[ak])
                    for j in range(1, 4):
                        kb.dve(lambda e: e.scalar_tensor_tensor(out=ac[:], in0=cb[:, j:j + 512], scalar=convs[:, c, j:j + 1], in1=ac[:], op0=ALU.mult, op1=ALU.add),
                               r=[ck, ak, "const"], w=[ak])
                    kb.pool(lambda e: e.tensor_copy(out=cb[:, 0:3], in_=cb[:, 512:515]), r=[ck], w=[ck])
                    kb.act(lambda e: e.activation(out=y[:], in_=ac[:], func=ACT.Silu), r=[ak], w=[yk])
                    sq_, sqk = sqb[nfm % 2], f"sqb{nfm%2}"
                    sd_, sdk = sdb[nfm % 2], f"sdb{nfm%2}"
                    yn, ynk = ynb[nfm % 3], f"ynb{nfm%3}"
                    tr, trk = trb[nfm % 2], f"trb{nfm%2}"
                    if kind < 2:
                        kb.pool(lambda e: e.tensor_tensor(out=sq_[:], in0=y[:], in1=y[:], op=ALU.mult), r=[yk], w=[sqk])

                    def part3(h=h, kind=kind, src=(yn if kind < 2 else y), srck=(ynk if kind < 2 else yk), tr=tr, trk=trk):
                        for a in range(4):
                            kb.pe(lambda e: e.transpose(out=psXf[:, a * 128:(a + 1) * 128], in_=src[:, a * 128:(a + 1) * 128], identity=identf[:]),
                                  r=[srck, "const"], w=["psXf"], inc=(a == 3))
                        kb.dve(lambda e: e.tensor_copy(out=tr[:], in_=psXf[:]), r=["psXf"], w=[trk])
                        dst = (gktok if kind == 1 else gvtok)[h, g * 512:(g + 1) * 512, :].rearrange("(a p) e -> p a e", p=128)
                        kb.dma(dst, tr[:].rearrange("p (a e) -> p a e", a=4), r=[trk], w=["gtok"], q="pool")

                    def part2(h=h, kind=kind, y=y, yk=yk, sq_=sq_, sqk=sqk, sd_=sd_, sdk=sdk, yn=yn, ynk=ynk):
                        kb.pe(lambda e: e.matmul(psN[:], lhsT=onesf, rhs=sq_[:], start=True, stop=True), r=[sqk, "const"], w=["psN"])
                        kb.act(lambda e: e.activation(out=sd_[:], in_=psN[:], func=ACT.Ln, bias=epst[:]), r=["psN", "const"], w=[sdk])
                        kb.act(lambda e: e.activation(out=sd_[:], in_=sd_[:], func=ACT.Exp, scale=-0.5), r=[sdk], w=[sdk])
                        cq = (128.0 ** -0.5) if kind == 0 else 1.0
                        kb.dve(lambda e: e.scalar_tensor_tensor(out=yn[:], in0=y[:], scalar=cq, in1=sd_[:], op0=ALU.mult, op1=ALU.mult), r=[yk, sdk], w=[ynk])
                        kb.dma((gqT if kind == 0 else gkT)[h, :, tsl], yn[:], r=[ynk], w=["gqkT"], q="pool")

                    if kind < 2:
                        new2 = part2
                    if kind >= 1:
                        new3 = part3
                    nfm += 1
                for item in [it for it in pend if it[0] <= c]:
                    pend.remove(item)
                    item[1]()
                if new2 is not None:
                    pend.append((c + 1, new2))
                if new3 is not None:
                    pend.append((c + 2 if new2 is not None else c + 1, new3))
            for item in sorted(pend, key=lambda it: it[0]):
                item[1]()
            qTg = qkTg[0]
            qTk = "qkTg0"
            pendC = None
            for a in range(4):
                tt = 4 * g + a
                qkb, qkk = qkbs[a % 2], f"qkb{a%2}"
                lhs = lambda kc: hTg[:, kc, a * 128:(a + 1) * 128]
                for (pst, pk, c0, n) in ((psQ, "psQ", TM0, 512), (psK, "psK", TM0 + 512, 512), (psV, "psV", TM0 + 1024, 512), (psXf, "psXf", TM0 + 1536, 8)):
                    for kc in range(8):
                        kb.pe(lambda e: e.matmul(pst[:, 0:n], lhsT=lhs(kc), rhs=wbf[:, kc, c0:c0 + n], start=(kc == 0), stop=(kc == 7)),
                              r=["wbf", hk], w=[pk], inc=(kc == 7))
                if pendC is not None:
                    pendC()
                    pendC = None
                kb.act(lambda e: e.activation(out=bgres[:, tt, 0:4], in_=psXf[:, 0:4], func=ACT.Sigmoid), r=["psXf"], w=["bgres"])
                for h in range(4):
                    kb.act(lambda e: e.activation(out=tmp4[:, h:h + 1], in_=psXf[:, 4 + h:5 + h], func=ACT.Exp, bias=dtbs[:, h:h + 1]), r=["psXf", "const"], w=["tmp4"])
                kb.act(lambda e: e.activation(out=tmp4[:], in_=tmp4[:], func=ACT.Ln, bias=onet[:]), r=["tmp4", "const"], w=["tmp4"])
                kb.dve(lambda e: e.tensor_tensor(out=bgres[:, tt, 4:8], in0=tmp4[:], in1=negA[:], op=ALU.mult), r=["tmp4", "negA"], w=["bgres"])
                v_ = vb[a % 2]
                vk = f"vb{a%2}"
                kb.act(lambda e: e.activation(out=v_[:], in_=psV[:], func=ACT.Copy), r=["psV"], w=[vk])
                kb.dma(dv_v[tt * 128:(tt + 1) * 128, :, :], v_[:].rearrange("p (h e) -> p h e", h=4), r=[vk], w=["dv"], q="pool")
                kb.dve(lambda e: e.tensor_copy(out=qk[:, 0:512], in_=psQ[:]), r=["psQ"], w=["qk"])
                kb.act(lambda e: e.activation(out=qk[:, 512:1024], in_=psK[:], func=ACT.Copy), r=["psK"], w=["qk"])
                kb.pool(lambda e: e.tensor_tensor(out=xn[:], in0=qk[:], in1=qk[:], op=ALU.mult), r=["qk"], w=["xn"])
                kb.dve(lambda e: e.tensor_reduce(out=ssq[:], in_=xn[:].rearrange("p (s d) -> p s d", d=64), axis=AX.X, op=ALU.add), r=["xn"], w=["ssq"])
                kb.act(lambda e: e.activation(out=ssq[:], in_=ssq[:], func=ACT.Ln, bias=epst[:], scale=1.0 / 64), r=["ssq", "const"], w=["ssq"])
                kb.act(lambda e: e.activation(out=ssq[:], in_=ssq[:], func=ACT.Exp, scale=-0.5), r=["ssq"], w=["ssq"])
                qk3 = qk[:].rearrange("p (s d) -> p s d", d=64)
                xn3 = xn[:].rearrange("p (s d) -> p s d", d=64)
                kb.dve(lambda e: e.tensor_tensor(out=xn3, in0=qk3, in1=ssq[:].unsqueeze(2).to_broadcast([128, 16, 64]), op=ALU.mult), r=["qk", "ssq"], w=["xn"])
                kb.pool(lambda e: e.tensor_tensor(out=xn[:], in0=xn[:], in1=gqks[:], op=ALU.mult), r=["xn", "const"], w=["xn"])
                kb.act(lambda e: e.activation(out=qkb[:], in_=xn[:], func=ACT.Copy), r=["xn"], w=[qkk])
                qkb3 = qkb[:].rearrange("p (s d) -> p s d", d=64)
                cb_ = cosr[:, tt, :].unsqueeze(1).to_broadcast([128, 16, 8])
                sb_ = sinr[:, tt, :].unsqueeze(1).to_broadcast([128, 16, 8])
                x1, x2 = xn3[:, :, 0:8], xn3[:, :, 8:16]
                kb.dve(lambda e: e.tensor_tensor(out=rt[0][:], in0=x1, in1=cb_, op=ALU.mult), r=["xn", "cosr"], w=["rt0"])
                kb.dve(lambda e: e.tensor_tensor(out=rt[1][:], in0=x2, in1=sb_, op=ALU.mult), r=["xn", "sinr"], w=["rt1"])
                kb.dve(lambda e: e.tensor_tensor(out=rt[2][:], in0=x2, in1=cb_, op=ALU.mult), r=["xn", "cosr"], w=["rt2"])
                kb.dve(lambda e: e.tensor_tensor(out=rt[3][:], in0=x1, in1=sb_, op=ALU.mult), r=["xn", "sinr"], w=["rt3"])
                kb.dve(lambda e: e.tensor_tensor(out=qkb3[:, :, 0:8], in0=rt[0][:], in1=rt[1][:], op=ALU.subtract), r=["rt0", "rt1", qkk], w=[qkk])
                kb.dve(lambda e: e.tensor_tensor(out=qkb3[:, :, 8:16], in0=rt[2][:], in1=rt[3][:], op=ALU.add), r=["rt2", "rt3", qkk], w=[qkk])
                def trC(a=a, qkb=qkb, qkk=qkk):
                    for bi in range(8):
                        kb.pe(lambda e: e.transpose(out=psXb[:, bi * 128:(bi + 1) * 128], in_=qkb[:, bi * 128:(bi + 1) * 128], identity=identb[:]),
                              r=[qkk, "const"], w=["psXb"], inc=(bi == 7))
                    kb.dve(lambda e: e.tensor_copy(out=qTg[:, :, a * 128:(a + 1) * 128], in_=psXb[:].rearrange("p (b t) -> p b t", b=8)), r=["psXb"], w=[qTk])
                pendC = trC
            if pendC is not None:
                pendC()
                pendC = None
            kb.dma(dqT_v[:, :, g * 512:(g + 1) * 512], qTg[:, 0:4, :], r=[qTk], w=["dqT"], q="pool")
            kb.dma(dkT_v[:, :, g * 512:(g + 1) * 512], qTg[:, 4:8, :], r=[qTk], w=["dkT"], q="pool")
        kb.pop()
        kb.push()
        Q = lambda t, i: t[:, i * 128:(i + 1) * 128]
        RR = (lambda ap: ap.bitcast(mybir.dt.float32r)) if not _os.environ.get('NO_F32R') else (lambda ap: ap)
        names = ["kTr", "qTr", "kTp", "qTp", "ktk", "vtk", "Ug", "E", "Ei", "Es", "aT", "Ya", "YaT", "Yb", "YbT", "Wa", "Wb", "dg", "qdT", "kdec", "rhs2", "vn"]
        gb = {}
        for h in range(4):
            for par in range(2):
                for n in names:
                    gb[(n, h, par)] = kb.sb(f"{n}{h}{par}", [128, 128], F32)
                gb[("sm", h, par)] = kb.sb(f"sm{h}{par}", [128, 8], F32)
                gb[("smc", h, par)] = kb.sb(f"smc{h}{par}", [128, 8], F32)
                gb[("g2", h, par)] = kb.sb(f"g2{h}{par}", [128, 2], F32)
        Sst = [[kb.sb(f"S{h}{i}", [128, 128], F32) for i in range(2)] for h in range(4)]
        obuf = [kb.sb(f"obuf{h}", [128, 512], F32) for h in range(4)]
        gsq = kb.sb("gsq", [128, 512], F32)
        gsd = kb.sb("gsd", [128, 512], F32)
        gon = kb.sb("gon", [128, 512], F32)
        gsg = [kb.sb(f"gsg{i}", [128, 512], F32) for i in range(2)]
        gmx = [kb.sb(f"gmx{i}", [128, 512], BF16) for i in range(2)]
        pA = kb.ps("pA", [128, 512], F32)
        pB = kb.ps("pB", [128, 512], F32)
        pC = kb.ps("pC", [128, 512], F32)
        pKS = kb.ps("pKS", [128, 512], F32)
        pVN = kb.ps("pVN", [128, 512], F32)
        pSS = kb.ps("pSS", [128, 512], F32)
        pO = kb.ps("pO", [128, 512], F32)
        pP = kb.ps("pP", [128, 512], F32)
        for h in range(4):
            kb.dve(lambda e: e.tensor_scalar(out=RR(Sst[h][0][:]), in0=identf[:], scalar1=0.0, scalar2=None, op0=ALU.mult), r=["const"], w=[f"S{h}0"])
        scur = [0, 0, 0, 0]
        H4 = range(4)

        def pre_stages(p):
            par = p % 2
            tok = slice(p * 128, (p + 1) * 128)
            B = lambda n, h: gb[(n, h, par)]
            K_ = lambda n, h: f"{n}{h}{par}"
            gcol = lambda h: bgres[:, p, 4 + h:5 + h]
            beta = lambda h: bgres[:, p, h:h + 1]
            st = []

            def s_load():
                for h in H4:
                    kb.dma(B("kTp", h)[:], gkT[h, :, tok], r=["gqkT"], w=[K_("kTp", h)])
                    kb.dma(B("qTp", h)[:], gqT[h, :, tok], r=["gqkT"], w=[K_("qTp", h)])
                    kb.dma(B("ktk", h)[:], gktok[h, tok, :], r=["gtok"], w=[K_("ktk", h)])
                    kb.dma(B("vtk", h)[:], gvtok[h, tok, :], r=["gtok"], w=[K_("vtk", h)])
            st.append(s_load)

            def s_ug():
                for h in H4:
                    kb.pool(lambda e: e.tensor_copy(out=RR(B("kTr", h)[:]), in_=B("kTp", h)[:]), r=[K_("kTp", h)], w=[K_("kTr", h)])
                    kb.pool(lambda e: e.tensor_copy(out=RR(B("qTr", h)[:]), in_=B("qTp", h)[:]), r=[K_("qTp", h)], w=[K_("qTr", h)])
                    kb.dve(lambda e: e.tensor_scalar(out=B("Ug", h)[:], in0=ublk, scalar1=gcol(h), scalar2=None, op0=ALU.mult), r=["const", "bgres"], w=[K_("Ug", h)])
                    kb.dve(lambda e: e.tensor_copy(out=B("g2", h)[:], in_=gcol(h).to_broadcast([128, 2])), r=["bgres"], w=[K_("g2", h)])
            st.append(s_ug)

            def s_g():
                for h in H4:
                    kb.pe(lambda e: e.matmul(Q(pA, h), lhsT=B("Ug", h)[:], rhs=onesblk, start=True, stop=False), r=[K_("Ug", h), "const"], w=["pA"], inc=False)
                    kb.pe(lambda e: e.matmul(Q(pA, h), lhsT=negblk[:], rhs=B("Ug", h)[:], start=False, stop=True), r=[K_("Ug", h), "const"], w=["pA"])
                for h in H4:
                    o8 = 8 * h
                    kb.pe(lambda e: e.matmul(pB[:, o8:o8 + 2], lhsT=B("Ug", h)[:], rhs=onesf[:, 0:2], start=True, stop=True), r=[K_("Ug", h), "const"], w=["pB"], inc=False)
                    kb.pe(lambda e: e.matmul(pB[:, o8 + 2:o8 + 4], lhsT=onesblk, rhs=B("g2", h)[:], start=True, stop=True), r=[K_("g2", h), "const"], w=["pB"], inc=False)
                    kb.pe(lambda e: e.matmul(pB[:, o8 + 4:o8 + 6], lhsT=sel[0], rhs=B("g2", h)[:], start=True, stop=True), r=[K_("g2", h), "const"], w=["pB"], inc=False)
                    kb.pe(lambda e: e.matmul(pB[:, o8 + 6:o8 + 8], lhsT=sel[1], rhs=B("g2", h)[:], start=True, stop=True), r=[K_("g2", h), "const"], w=["pB"])
            st.append(s_g)

            def s_e1():
                for h in H4:
                    kb.act(lambda e: e.activation(out=B("smc", h)[:], in_=pB[:, 8 * h:8 * h + 8], func=ACT.Copy), r=["pB"], w=[K_("smc", h)])
                    kb.dve(lambda e: e.tensor_scalar(out=B("E", h)[:], in0=Q(pA, h), scalar1=-1.0, scalar2=0.0, op0=ALU.mult, op1=ALU.min), r=["pA"], w=[K_("E", h)])
            st.append(s_e1)

            def s_e2():
                for h in H4:
                    sm, smc = B("sm", h), B("smc", h)
                    kb.act(lambda e: e.activation(out=sm[:, 0:1], in_=smc[:, 0:1], func=ACT.Exp), r=[K_("smc", h)], w=[K_("sm", h)])
                    kb.act(lambda e: e.activation(out=B("E", h)[:], in_=B("E", h)[:], func=ACT.Exp), r=[K_("E", h)], w=[K_("E", h)])
                    kb.dve(lambda e: e.tensor_tensor(out=sm[:, 2:3], in0=smc[:, 2:3], in1=smc[:, 0:1], op=ALU.subtract), r=[K_("smc", h), K_("sm", h)], w=[K_("sm", h)])
            st.append(s_e2)

            def s_kk():
                for h in H4:
                    kb.pe(lambda e: e.matmul(Q(pC, h), lhsT=RR(B("kTr", h)[:]), rhs=RR(B("kTr", h)[:]), start=True, stop=True), r=[K_("kTr", h)], w=["pC"])
                for h in H4:
                    kb.pe(lambda e: e.matmul(Q(pA, h), lhsT=RR(B("kTr", h)[:]), rhs=RR(B("qTr", h)[:]), start=True, stop=True), r=[K_("kTr", h), K_("qTr", h)], w=["pA"])
            st.append(s_kk)

            def s_e3():
                for h in H4:
                    sm, smc = B("sm", h), B("smc", h)
                    kb.dve(lambda e: e.tensor_scalar(out=sm[:, 1:2], in0=sm[:, 0:1], scalar1=-1.0, scalar2=None, op0=ALU.mult), r=[K_("sm", h)], w=[K_("sm", h)])
                    kb.act(lambda e: e.activation(out=sm[:, 2:3], in_=sm[:, 2:3], func=ACT.Exp), r=[K_("sm", h)], w=[K_("sm", h)])
                    kb.act(lambda e: e.activation(out=sm[:, 4:8], in_=smc[:, 4:8], func=ACT.Exp), r=[K_("smc", h), K_("sm", h)], w=[K_("sm", h)])
                    kb.pool(lambda e: e.tensor_tensor(out=B("Ei", h)[:], in0=B("E", h)[:], in1=uincl, op=ALU.mult), r=[K_("E", h), "const"], w=[K_("Ei", h)])
                    kb.pool(lambda e: e.tensor_tensor(out=B("Es", h)[:], in0=B("E", h)[:], in1=negustr, op=ALU.mult), r=[K_("E", h), "const"], w=[K_("Es", h)])
            st.append(s_e3)

            def s_ya():
                for h in H4:
                    kb.dve(lambda e: e.scalar_tensor_tensor(out=RR(B("Ya", h)[:]), in0=Q(pC, h), scalar=beta(h), in1=B("Es", h)[:], op0=ALU.mult, op1=ALU.mult),
                           r=["pC", K_("Es", h), "bgres"], w=[K_("Ya", h)])
                for h in H4:
                    kb.dve(lambda e: e.tensor_tensor(out=RR(B("aT", h)[:]), in0=Q(pA, h), in1=B("Ei", h)[:], op=ALU.mult), r=["pA", K_("Ei", h)], w=[K_("aT", h)])
            st.append(s_ya)

            def s_tr():
                for h in H4:
                    kb.pe(lambda e: e.transpose(out=Q(pB, h), in_=B("Ya", h)[:], identity=identf[:]), r=[K_("Ya", h), "const"], w=["pB"])
            st.append(s_tr)

            def s_tr2():
                for h in H4:
                    kb.act(lambda e: e.activation(out=RR(B("YaT", h)[:]), in_=Q(pB, h), func=ACT.Copy), r=["pB"], w=[K_("YaT", h)])
                    kb.dve(lambda e: e.tensor_tensor(out=RR(B("Wa", h)[:]), in0=B("Ya", h)[:], in1=identf[:], op=ALU.add), r=[K_("Ya", h), "const"], w=[K_("Wa", h)])
            st.append(s_tr2)

            cur = ["Ya", "YaT", "Wa"]
            for k in range(1, 6):
                Y, YT, W = cur
                Yn, YTn, Wn = ("Yb", "YbT", "Wb") if Y == "Ya" else ("Ya", "YaT", "Wa")

                def s_sq(k=k, Y=Y, YT=YT):
                    if k < 5:
                        for h in H4:
                            kb.pe(lambda e: e.matmul(Q(pA, h), lhsT=RR(B(YT, h)[:]), rhs=RR(B(Y, h)[:]), start=True, stop=True), r=[K_(Y, h), K_(YT, h)], w=["pA"])
                    for h in H4:
                        kb.pe(lambda e: e.matmul(Q(pC, h), lhsT=RR(B(Y, h)[:]), rhs=RR(B(YT, h)[:]), start=True, stop=True), r=[K_(Y, h), K_(YT, h)], w=["pC"])
                st.append(s_sq)

                def s_ev(k=k, Yn=Yn, YTn=YTn):
                    for h in H4:
                        if k < 5:
                            kb.act(lambda e: e.activation(out=RR(B(Yn, h)[:]), in_=Q(pA, h), func=ACT.Copy), r=["pA"], w=[K_(Yn, h)])
                        kb.dve(lambda e: e.tensor_copy(out=RR(B(YTn, h)[:]), in_=Q(pC, h)), r=["pC"], w=[K_(YTn, h)])
                st.append(s_ev)

                def s_w(YTn=YTn, W=W):
                    for h in H4:
                        kb.pe(lambda e: e.matmul(Q(pB, h), lhsT=RR(B(YTn, h)[:]), rhs=RR(B(W, h)[:]), start=True, stop=True), r=[K_(YTn, h), K_(W, h)], w=["pB"])
                st.append(s_w)

                def s_w2(W=W, Wn=Wn):
                    for h in H4:
                        kb.dve(lambda e: e.tensor_tensor(out=RR(B(Wn, h)[:]), in0=Q(pB, h), in1=B(W, h)[:], op=ALU.add), r=["pB", K_(W, h)], w=[K_(Wn, h)])
                st.append(s_w2)
                cur = [Yn, YTn, Wn]
            for h in H4:
                gb[("Wfin", h, par)] = cur[2]

            def s_dg():
                for h in H4:
                    sm = B("sm", h)
                    kb.dve(lambda e: e.tensor_scalar(out=B("dg", h)[:], in0=identf[:], scalar1=sm[:, 0:1], scalar2=None, op0=ALU.mult), r=["const", K_("sm", h)], w=[K_("dg", h)])
                    kb.act(lambda e: e.activation(out=RR(B("kdec", h)[:]), in_=B("ktk", h)[:], func=ACT.Copy, scale=sm[:, 2:3]), r=[K_("ktk", h), K_("sm", h)], w=[K_("kdec", h)])
            st.append(s_dg)

            def s_eg():
                for h in H4:
                    kb.pe(lambda e: e.matmul(Q(pA, h), lhsT=onesf, rhs=B("dg", h)[:], start=True, stop=True), r=[K_("dg", h), "const"], w=["pA"])
            st.append(s_eg)

            def s_qd():
                for h in H4:
                    kb.dve(lambda e: e.tensor_tensor(out=RR(B("qdT", h)[:]), in0=Q(pA, h), in1=B("qTp", h)[:], op=ALU.mult), r=["pA", K_("qTp", h)], w=[K_("qdT", h)])
            st.append(s_qd)
            return st

        def scan_stages(p):
            par = p % 2
            B = lambda n, h: gb[(n, h, par)]
            K_ = lambda n, h: f"{n}{h}{par}"
            st = []
            for c in range(2):
                hs = slice(64 * c, 64 * c + 64)

                def s1(hs=hs):
                    for h in H4:
                        S, Sk = Sst[h][scur[h]], f"S{h}{scur[h]}"
                        kb.pe(lambda e: e.matmul(Q(pKS, h), lhsT=RR(B("kTr", h)[:]), rhs=RR(S[:]), start=True, stop=True), r=[K_("kTr", h), Sk], w=["pKS"])
                st.append(s1)

                def s2(hs=hs):
                    for h in H4:
                        kb.dve(lambda e: e.scalar_tensor_tensor(out=RR(B("rhs2", h)[hs, :]), in0=Q(pKS, h)[hs, :], scalar=B("sm", h)[hs, 1:2], in1=B("vtk", h)[hs, :],
                                                                op0=ALU.mult, op1=ALU.add), r=["pKS", K_("sm", h), K_("vtk", h)], w=[K_("rhs2", h)])
                st.append(s2)

                def s3(hs=hs):
                    for h in H4:
                        Wf = gb[("Wfin", h, par)]
                        kb.pe(lambda e: e.matmul(Q(pVN, h), lhsT=RR(B(Wf, h)[hs, :]), rhs=RR(B("rhs2", h)[hs, :]), start=True, stop=True), r=[K_(Wf, h), K_("rhs2", h)], w=["pVN"])
                st.append(s3)

                def s4(hs=hs):
                    for h in H4:
                        kb.dve(lambda e: e.tensor_scalar(out=RR(B("vn", h)[hs, :]), in0=Q(pVN, h)[hs, :], scalar1=bgres[hs, p, h:h + 1], scalar2=None, op0=ALU.mult),
                               r=["pVN", "bgres"], w=[K_("vn", h)])
                st.append(s4)

                def s5(hs=hs, c=c):
                    for h in H4:
                        S, Sk = Sst[h][scur[h]], f"S{h}{scur[h]}"
                        oc = slice(h * 128 + 64 * c, h * 128 + 64 * c + 64)
                        kb.pe(lambda e: e.matmul(pO[:, oc], lhsT=RR(S[:]), rhs=RR(B("qdT", h)[:, hs]), start=True, stop=False), r=[Sk, K_("qdT", h)], w=["pO"], inc=False)
                        kb.pe(lambda e: e.matmul(pO[:, oc], lhsT=RR(B("vn", h)[hs, :]), rhs=RR(B("aT", h)[hs, hs]), start=False, stop=True), r=[K_("vn", h), K_("aT", h)], w=["pO"])
                        kb.pe(lambda e: e.matmul(Q(pSS, h), lhsT=RR(B("kdec", h)[hs, :]), rhs=RR(B("vn", h)[hs, :]), start=True, stop=True), r=[K_("kdec", h), K_("vn", h)], w=["pSS"])
                st.append(s5)

                def s6(c=c):
                    for h in H4:
                        S, Sk = Sst[h][scur[h]], f"S{h}{scur[h]}"
                        Sn, Snk = Sst[h][1 - scur[h]], f"S{h}{1-scur[h]}"
                        kb.dve(lambda e: e.scalar_tensor_tensor(out=RR(Sn[:]), in0=S[:], scalar=B("sm", h)[:, 4 + 2 * c:5 + 2 * c], in1=Q(pSS, h), op0=ALU.mult, op1=ALU.add),
                               r=[Sk, K_("sm", h), "pSS"], w=[Snk])
                        scur[h] = 1 - scur[h]
                st.append(s6)

            def s7():
                for h in H4:
                    kb.act(lambda e: e.activation(out=obuf[h][:, (p % 4) * 128:(p % 4 + 1) * 128], in_=Q(pO, h), func=ACT.Copy), r=["pO"], w=[f"obuf{h}"])
            st.append(s7)
            return st

        npost = [0]

        def post(g):
            tsl = slice(g * 512, (g + 1) * 512)
            for h in H4:
                i2 = npost[0] % 2
                kb.dma(gsg[i2][:], sgT[h, :, tsl], r=["sgT"], w=[f"gsg{i2}"])
                kb.pool(lambda e: e.tensor_tensor(out=gsq[:], in0=obuf[h][:], in1=obuf[h][:], op=ALU.mult), r=[f"obuf{h}"], w=["gsq"])
                kb.pe(lambda e: e.matmul(pP[:], lhsT=onesf, rhs=gsq[:], start=True, stop=True), r=["gsq", "const"], w=["pP"])
                kb.act(lambda e: e.activation(out=gsd[:], in_=pP[:], func=ACT.Ln, bias=epst[:], scale=1.0 / 128), r=["pP", "const"], w=["gsd"])
                kb.act(lambda e: e.activation(out=gsd[:], in_=gsd[:], func=ACT.Exp, scale=-0.5), r=["gsd"], w=["gsd"])
                kb.dve(lambda e: e.tensor_tensor(out=gon[:], in0=obuf[h][:], in1=gsd[:], op=ALU.mult), r=[f"obuf{h}", "gsd"], w=["gon"])
                kb.dve(lambda e: e.scalar_tensor_tensor(out=gmx[i2][:], in0=gon[:], scalar=gngs[:, 0:1], in1=gsg[i2][:], op0=ALU.mult, op1=ALU.mult),
                       r=["gon", "const", f"gsg{i2}"], w=[f"gmx{i2}"])
                kb.dma(mixT[h, :, tsl], gmx[i2][:], r=[f"gmx{i2}"], w=["mixT"], q="pool")
                npost[0] += 1

        for f in pre_stages(0):
            f()
        for p in range(NT):
            sc_ = scan_stages(p)
            pr_ = pre_stages(p + 1) if p + 1 < NT else []
            ratio = (len(pr_) + len(sc_) - 1) // len(sc_)
            pi = 0
            for f in sc_:
                f()
                for _ in range(ratio):
                    if pi < len(pr_):
                        pr_[pi]()
                        pi += 1
            while pi < len(pr_):
                pr_[pi]()
                pi += 1
            if p % 4 == 3:
                post(p // 4)
        kb.pop()
        if upto <= 2:
            kb.finish()
            return nc, kb
        kb.push()
        akT = [kb.sb(f"akT{i}", [128, T], BF16) for i in range(2)]
        aqT = [kb.sb(f"aqT{i}", [128, T], BF16) for i in range(2)]
        avv = [kb.sb(f"avv{i}", [128, NT, 128], BF16) for i in range(2)]
        pTb = [[kb.sb(f"pT{i}{j}", [128, 512], BF16) for j in range(2)] for i in range(2)]
        arl = [kb.sb(f"arl{i}", [128, 512], F32) for i in range(2)]
        lacc = [kb.sb(f"lacc{i}", [128, 512], F32) for i in range(2)]
        araw = [kb.sb(f"araw{i}", [128, 512], F32) for i in range(2)]
        ao = [kb.sb(f"ao{i}", [128, 512], F32) for i in range(2)]
        aod = kb.sb("aod", [128, 512], F32)
        asq = kb.sb("asq", [128, 512], F32)
        asd = kb.sb("asd", [128, 512], F32)
        aon = kb.sb("aon", [128, 512], F32)
        asg = [kb.sb(f"asg{i}", [128, 512], F32) for i in range(2)]
        amx = [kb.sb(f"amx{i}", [128, 512], BF16) for i in range(2)]
        pS = [[kb.ps(f"pS{i}{j}", [128, 512], F32) for j in range(2)] for i in range(2)]
        pOa = [kb.ps(f"pOa{i}", [128, 512], F32) for i in range(2)]
        pLa = [kb.ps(f"pLa{i}", [128, 512], F32) for i in range(2)]
        nstep = 0
        for h in range(4):
            hb = h % 2
            for part in range(2):
                sl = slice(part * (T // 2), (part + 1) * (T // 2))
                kb.dma(akT[hb][:, sl], dkT[h, :, sl], r=["dkT"], w=[f"akT{hb}"])
                kb.dma(aqT[hb][:, sl], dqT[h, :, sl], r=["dqT"], w=[f"aqT{hb}"])
            dvh = dv[h].rearrange("(t p) e -> p t e", p=128)
            for part in range(4):
                kb.dma(avv[hb][:, part * 16:(part + 1) * 16, :], dvh[:, part * 16:(part + 1) * 16, :], r=["dv"], w=[f"avv{hb}"])
            steps = [(g, kt) for g in range(NG) for kt in range(4 * g + 4)]

            def emit_S(n):
                g, kt = steps[n]
                j = kt - 4 * g
                qlo = 128 * max(j, 0)
                qs = slice(qlo, 512)
                sb2 = n % 2
                for i in range(2):
                    rows = slice(64 * i, 64 * i + 64)
                    kb.pe(lambda e: e.matmul(pS[i][sb2][:, qs], lhsT=akT[hb][rows, kt * 128:(kt + 1) * 128], rhs=aqT[hb][rows, g * 512 + qlo:(g + 1) * 512],
                                             start=True, stop=True), r=[f"akT{hb}", f"aqT{hb}"], w=[f"pS{i}{sb2}"])

            def emit_rest(n):
                g, kt = steps[n]
                nk = 4 * g + 4
                j = kt - 4 * g
                qlo = 128 * max(j, 0)
                qs = slice(qlo, 512)
                sb2 = n % 2
                for i in range(2):
                    pt = pTb[i][sb2]
                    kb.act(lambda e: e.activation(out=pt[:, qs], in_=pS[i][sb2][:, qs], func=ACT.Exp, bias=negC[:], scale=0.125),
                           r=[f"pS{i}{sb2}", "negC"], w=[f"pT{i}{sb2}"])
                    if j >= 0:
                        kb.pool(lambda e: e.memset(pt[64:128, qlo:qlo + 64], 0.0), r=[], w=[f"pT{i}{sb2}"])
                for i in range(2):
                    pt = pTb[i][sb2]
                    kb.pe(lambda e: e.matmul(pOa[i][:, qs], lhsT=avv[hb][:, kt, :], rhs=pt[:, qs], start=(kt == 0), stop=(kt == nk - 1)),
                          r=[f"avv{hb}", f"pT{i}{sb2}"], w=[f"pOa{i}"])
                    ae = kb.pool if i == 0 else kb.dve
                    if kt == 0:
                        ae(lambda e: e.tensor_copy(out=lacc[i][:, qs], in_=pt[:, qs]), r=[f"pT{i}{sb2}"], w=[f"lacc{i}"])
                    else:
                        ae(lambda e: e.tensor_tensor(out=lacc[i][:, qs], in0=lacc[i][:, qs], in1=pt[:, qs], op=ALU.add), r=[f"pT{i}{sb2}", f"lacc{i}"], w=[f"lacc{i}"])
                if kt != nk - 1:
                    return None
                tsl = slice(g * 512, (g + 1) * 512)
                i2 = g % 2
                kb.dma(asg[i2][:], sgT[4 + h, :, tsl], r=["sgT"], w=[f"asg{i2}"])
                for i in range(2):
                    kb.dve(lambda e: e.tensor_copy(out=araw[i][:], in_=pOa[i][:]), r=[f"pOa{i}"], w=[f"araw{i}"])
                for i in range(2):
                    kb.pe(lambda e: e.matmul(pLa[i][:], lhsT=onesf, rhs=lacc[i][:], start=True, stop=True), r=["const", f"lacc{i}"], w=[f"pLa{i}"])
                for i in range(2):
                    kb.act(lambda e: e.activation(out=arl[i][:], in_=pLa[i][:], func=ACT.Ln), r=[f"pLa{i}"], w=[f"arl{i}"])
                    kb.act(lambda e: e.activation(out=arl[i][:], in_=arl[i][:], func=ACT.Exp, scale=-1.0), r=[f"arl{i}"], w=[f"arl{i}"])
                    kb.pool(lambda e: e.tensor_tensor(out=ao[i][:], in0=araw[i][:], in1=arl[i][:], op=ALU.mult), r=[f"araw{i}", f"arl{i}"], w=[f"ao{i}"])
                kb.dve(lambda e: e.scalar_tensor_tensor(out=aod[:], in0=ao[1][:], scalar=neglam[:, 0:1], in1=ao[0][:], op0=ALU.mult, op1=ALU.add),
                       r=["ao0", "ao1", "neglam"], w=["aod"])
                kb.pool(lambda e: e.tensor_tensor(out=asq[:], in0=aod[:], in1=aod[:], op=ALU.mult), r=["aod"], w=["asq"])
                return lambda: post2(g, i2, tsl)

            def post2(g, i2, tsl):
                pN_ = pLa[0]
                kb.pe(lambda e: e.matmul(pN_[:], lhsT=onesf, rhs=asq[:], start=True, stop=True), r=["asq", "const"], w=["pLa0"])
                kb.act(lambda e: e.activation(out=asd[:], in_=pN_[:], func=ACT.Ln, bias=epst[:], scale=1.0 / 128), r=["pLa0", "const"], w=["asd"])
                kb.act(lambda e: e.activation(out=asd[:], in_=asd[:], func=ACT.Exp, scale=-0.5), r=["asd"], w=["asd"])
                kb.dve(lambda e: e.tensor_tensor(out=aon[:], in0=aod[:], in1=asd[:], op=ALU.mult), r=["aod", "asd"], w=["aon"])
                kb.dve(lambda e: e.scalar_tensor_tensor(out=amx[i2][:], in0=aon[:], scalar=slgs[:, 0:1], in1=asg[i2][:], op0=ALU.mult, op1=ALU.mult),
                       r=["aon", "slgs", f"asg{i2}"], w=[f"amx{i2}"])
                kb.dma(mixT[4 + h, :, tsl], amx[i2][:], r=[f"amx{i2}"], w=["mixT"], q="pool")

            emit_S(0)
            pp = None
            for n in range(len(steps)):
                if n + 1 < len(steps):
                    emit_S(n + 1)
                newpp = emit_rest(n)
                if pp is not None:
                    pp()
                pp = newpp
            if pp is not None:
                pp()
        kb.pop()
        if upto <= 3:
            kb.finish()
            return nc, kb
        kb.push()
        mt = [kb.sb(f"mt{i}", [128, 8, 512], BF16) for i in range(2)]
        xr = [kb.sb(f"xr{i}", [128, 512], F32) for i in range(2)]
        y1 = [kb.sb(f"y1{i}", [128, 512], F32) for i in range(2)]
        y2 = [kb.sb(f"y2{i}", [128, 512], F32) for i in range(2)]
        pY = [kb.ps(f"pY{i}", [128, 512], F32) for i in range(2)]
        mixT_v = mixT.rearrange("m p t -> p m t")
        for g in range(NG):
            m_ = mt[g % 2]
            mk = f"mt{g%2}"
            kb.dma(m_[:], mixT_v[:, :, g * 512:(g + 1) * 512], r=["mixT"], w=[mk])
            for a in range(4):
                tt = 4 * g + a
                i2 = tt % 2
                kb.dma(xr[i2][:], xres[tt * 128:(tt + 1) * 128, :], w=[f"xr{i2}"])
                for m in range(8):
                    kb.pe(lambda e: e.matmul(pY[i2][:], lhsT=m_[:, m, a * 128:(a + 1) * 128], rhs=wobf[:, m, :], start=(m == 0), stop=(m == 7)),
                          r=[mk, "wobf"], w=[f"pY{i2}"], inc=(m == 7))
                kb.dve(lambda e: e.tensor_tensor(out=y1[i2][:], in0=pY[i2][:], in1=gaterep[:], op=ALU.mult), r=[f"pY{i2}", "gaterep"], w=[f"y1{i2}"])
                kb.pool(lambda e: e.tensor_tensor(out=y2[i2][:], in0=y1[i2][:], in1=xr[i2][:], op=ALU.add), r=[f"y1{i2}", f"xr{i2}"], w=[f"y2{i2}"])
                kb.dma(out[tt * 128:(tt + 1) * 128, :], y2[i2][:], r=[f"y2{i2}"], w=["out"], q="pool")
        kb.pop()
        kb.finish()
    return nc, kb


def _consts():
    i = np.arange(128)
    same = (i[:, None] // 64) == (i[None, :] // 64)
    onesf = np.ones((128, 128), np.float32)
    ublk = (same & (i[:, None] <= i[None, :])).astype(np.float32)
    uincl = ublk.copy()
    negustr = -(same & (i[None, :] > i[:, None])).astype(np.float32)
    onesblk = same.astype(np.float32)
    sel0 = np.repeat((i < 64).astype(np.float32)[:, None], 128, 1)
    sel1 = np.repeat((i >= 64).astype(np.float32)[:, None], 128, 1)
    masks = np.stack([onesf, ublk, uincl, negustr, onesblk, sel0, sel1], 1).astype(np.float32)
    invf = (np.float32(500000.0) ** (-np.arange(0, 16, 2, dtype=np.float32) / np.float32(16))).astype(np.float32)
    return {
        "c_identf": np.eye(128, dtype=np.float32),
        "c_identb": np.eye(128, dtype=np.float32).astype(ml_dtypes.bfloat16),
        "c_onesb": np.ones((128, 128), np.float32).astype(ml_dtypes.bfloat16),
        "c_masks": np.ascontiguousarray(masks),
        "c_negblk": -onesblk,
        "c_invf": np.ascontiguousarray(np.broadcast_to(invf[None, :], (128, 8))),
    }


def _rep(v, n=128):
    return np.ascontiguousarray(np.broadcast_to(np.asarray(v, np.float32).reshape(1, -1), (n, np.asarray(v).size)))


def prep_core(inp, b, half):
    f = lambda a: np.ascontiguousarray(np.asarray(a, np.float32))
    w_in = np.asarray(inp["w_in"][0], np.float32)
    off = np.cumsum([0, 512, 512, 512, 4, 4, 512, 512, 512, 512, 512])
    aq, ak, av, abeta, adec, agate, bq, bk, bv, bgate = [np.arange(off[i], off[i + 1]) for i in range(10)]
    cols = []
    for h in range(4):
        cols += [aq[h * 128:(h + 1) * 128], ak[h * 128:(h + 1) * 128], av[h * 128:(h + 1) * 128]]
    cols += [agate, bgate, bq, bk, bv, abeta, adec]
    cols = np.concatenate(cols)
    conv_w = np.asarray(inp["conv_w"][0], np.float32)
    convc = np.zeros((128, 12, 4), np.float32)
    for h in range(4):
        for kind in range(3):
            ch = kind * 512 + h * 128 + np.arange(128)
            convc[:, 3 * h + kind, :] = conv_w[:, ch].T
    w_ada = np.asarray(inp["w_ada"][0], np.float32)
    b_ada = np.asarray(inp["b_ada"][0], np.float32)
    gsl = slice(2048 + half * 512, 2048 + (half + 1) * 512)
    d = {
        "x": f(inp["x"][b]),
        "xres": f(inp["x"][b][:, half * 512:(half + 1) * 512]),
        "cT": f(np.asarray(inp["c"][b]).reshape(8, 128).T),
        "posT": np.ascontiguousarray(np.asarray(inp["positions"][b], np.int32).reshape(NT, 128).T),
        "wada": f(w_ada[:, 0:2048]),
        "wadag": f(w_ada[:, gsl]),
        "badac": f(b_ada[0:2048].reshape(16, 128).T),
        "badag": _rep(b_ada[gsl]),
        "normgc": f(np.asarray(inp["norm_g"][0]).reshape(8, 128).T),
        "win": f(w_in[:, cols]),
        "convc": convc,
        "alog": _rep(inp["a_log"][0]),
        "dtb": _rep(inp["dt_bias"][0]),
        "gng": f(np.asarray(inp["gdn_norm_g"][0]).reshape(128, 1)),
        "slg": f(np.asarray(inp["subln_g"][0]).reshape(128, 1)),
        "gqk": _rep(np.concatenate([np.tile(np.asarray(inp["q_norm_g"][0]), 8), np.tile(np.asarray(inp["k_norm_g"][0]), 8)])),
        "lamv": np.ascontiguousarray(np.broadcast_to(np.stack([np.asarray(inp[k][0], np.float32) for k in
                                                               ("lambda_q1", "lambda_k1", "lambda_q2", "lambda_k2")])[None], (128, 4, 64))),
        "wout": f(np.asarray(inp["w_out"][0])[:, half * 512:(half + 1) * 512]),
    }
    d.update(_consts())
    return d


_NC_CACHE = {}


def kernel(**inputs):
    if "nc" not in _NC_CACHE:
        _NC_CACHE["nc"] = build_program()[0]
    nc = _NC_CACHE["nc"]
    in_maps = [prep_core(inputs, core // 2, core % 2) for core in range(8)]
    res = run_bass_kernel_spmd(nc, in_maps, core_ids=list(range(8)))
    out = np.empty((4, T, D), np.float32)
    for core in range(8):
        out[core // 2, :, (core % 2) * 512:(core % 2 + 1) * 512] = res.results[core]["out"]
    return out
```
